# Optimizing a Trainium2 kernel written in Bass

```python
import math
import jax, jax.numpy as jnp
from jax import lax
import numpy as np

D_MODEL = 1024
BATCH = 16
SEQ = 2048
DEPTH = 4

CHUNK = 64
Q_BLOCK = 128
N_HEADS = 8
QK_NOPE = 64
QK_ROPE = 32
V_HEAD = 64
Q_LORA = 384
KV_LORA = 256
ROPE_THETA = 10000.0
SSM_WIDTH = 512
SSM_GROUP = 16
SSM_GROUPS = SSM_WIDTH // SSM_GROUP
SSM_STATE = 64
DT_MIN = 1e-3
DT_MAX = 1e-1
D_FF = 2816
CONV_W = 3
EPS = 1e-6
N_BRANCH = 2
IN_COLS = Q_LORA + KV_LORA + QK_ROPE + SSM_WIDTH + N_BRANCH * D_MODEL

kernel_name = "hybrid_mla_s5_convglu_trunk"


def rmsnorm(x, g):
    xf = x.astype(jnp.float32)
    var = jnp.mean(xf * xf, axis=-1, keepdims=True)
    return (xf * lax.rsqrt(var + EPS) * g.astype(jnp.float32)).astype(x.dtype)


def rope_tables(seq):
    pos = jnp.arange(seq, dtype=jnp.float32)
    inv_freq = ROPE_THETA ** (-jnp.arange(0, QK_ROPE, 2, dtype=jnp.float32) / QK_ROPE)
    ang = pos[:, None] * inv_freq[None, :]
    return jnp.cos(ang), jnp.sin(ang)


def apply_rope(x, cos, sin):
    xf = x.astype(jnp.float32)
    x1, x2 = jnp.split(xf, 2, axis=-1)
    out = jnp.concatenate([x1 * cos - x2 * sin, x2 * cos + x1 * sin], axis=-1)
    return out.astype(x.dtype)


def mla_branch(c_q, c_kv, k_pe, g_q, w_uq, g_kv, w_ukv, w_o_att, cos, sin):
    b, s, _ = c_q.shape
    q = (rmsnorm(c_q, g_q) @ w_uq).reshape(b, s, N_HEADS, QK_NOPE + QK_ROPE)
    q_nope, q_pe = q[..., :QK_NOPE], q[..., QK_NOPE:]
    q_pe = apply_rope(q_pe, cos[None, :, None, :], sin[None, :, None, :])
    kv = (rmsnorm(c_kv, g_kv) @ w_ukv).reshape(b, s, N_HEADS, QK_NOPE + V_HEAD)
    k_nope, v = kv[..., :QK_NOPE], kv[..., QK_NOPE:]
    k_pe = apply_rope(k_pe, cos[None], sin[None])
    scale = 1.0 / math.sqrt(QK_NOPE + QK_ROPE)
    outs = []
    for i in range(s // Q_BLOCK):
        q0 = i * Q_BLOCK
        kend = q0 + Q_BLOCK
        sc = (jnp.einsum('bqhd,bkhd->bhqk', q_nope[:, q0:kend], k_nope[:, :kend])
              + jnp.einsum('bqhr,bkr->bhqk', q_pe[:, q0:kend], k_pe[:, :kend]))
        sc = sc.astype(jnp.float32) * scale
        q_chunk = (q0 + jnp.arange(Q_BLOCK)) // CHUNK
        k_chunk = jnp.arange(kend) // CHUNK
        mask = q_chunk[:, None] >= k_chunk[None, :]
        sc = jnp.where(mask[None, None], sc, -jnp.inf)
        p = jax.nn.softmax(sc, axis=-1).astype(v.dtype)
        outs.append(jnp.einsum('bhqk,bkhd->bqhd', p, v[:, :kend]))
    o = jnp.concatenate(outs, axis=1).reshape(b, s, N_HEADS * V_HEAD)
    return o @ w_o_att


def s5_combine(e1, e2):
    a1r, a1i, b1r, b1i = e1
    a2r, a2i, b2r, b2i = e2
    return (a2r * a1r - a2i * a1i,
            a2r * a1i + a2i * a1r,
            a2r * b1r - a2i * b1i + b2r,
            a2r * b1i + a2i * b1r + b2i)


def s5_branch(u, lam_re, lam_im, log_step, b_re, b_im, c_re, c_im, d_skip, w_glu):
    b, s, _ = u.shape
    u32 = u.astype(jnp.float32)
    ug = u32.reshape(b, s, SSM_GROUPS, SSM_GROUP)
    lr = lam_re.astype(jnp.float32)
    li = lam_im.astype(jnp.float32)
    delta = jnp.exp(log_step.astype(jnp.float32))[:, None]
    mag = jnp.exp(lr * delta)
    abar_re = mag * jnp.cos(li * delta)
    abar_im = mag * jnp.sin(li * delta)
    nr, ni = abar_re - 1.0, abar_im
    den = lr * lr + li * li
    f_re = (nr * lr + ni * li) / den
    f_im = (ni * lr - nr * li) / den
    br, bi = b_re.astype(jnp.float32), b_im.astype(jnp.float32)
    bbar_re = f_re[..., None] * br - f_im[..., None] * bi
    bbar_im = f_re[..., None] * bi + f_im[..., None] * br
    bu_re = jnp.einsum('bsgh,gph->bsgp', ug, bbar_re)
    bu_im = jnp.einsum('bsgh,gph->bsgp', ug, bbar_im)
    a_re = jnp.broadcast_to(abar_re[None, None], (1, s, SSM_GROUPS, SSM_STATE))
    a_im = jnp.broadcast_to(abar_im[None, None], (1, s, SSM_GROUPS, SSM_STATE))
    _, _, xr, xi = lax.associative_scan(s5_combine, (a_re, a_im, bu_re, bu_im), axis=1)
    y = (jnp.einsum('bsgp,ghp->bsgh', xr, c_re.astype(jnp.float32))
         - jnp.einsum('bsgp,ghp->bsgh', xi, c_im.astype(jnp.float32)))
    y = y.reshape(b, s, SSM_WIDTH) + d_skip.astype(jnp.float32) * u32
    y = jax.nn.gelu(y).astype(u.dtype)
    ga, gb = jnp.split(y @ w_glu, 2, axis=-1)
    return ga * jax.nn.sigmoid(gb)


def conv_glu_ffn(h, w_up, conv_w, conv_b, w_down):
    s = h.shape[1]
    gate, val = jnp.split(h @ w_up, 2, axis=-1)
    padded = jnp.pad(gate, ((0, 0), (CONV_W - 1, 0), (0, 0)))
    conv = conv_b
    for k in range(CONV_W):
        conv = conv + conv_w[k] * padded[:, k:k + s]
    return (jax.nn.gelu(conv) * val) @ w_down


def setup_inputs(seed: int = 0) -> dict:
    key = jax.random.key(seed)
    ks = jax.random.split(key, 24)
    f32 = jnp.float32
    nrm = lambda k, shape, sc: jax.random.normal(k, shape, f32) * sc
    gain = lambda k, dim: 1.0 + 0.05 * jax.random.normal(k, (DEPTH, dim), f32)
    n_idx = jnp.arange(SSM_STATE, dtype=f32)
    return {
        "x": jax.random.normal(ks[0], (BATCH, SEQ, D_MODEL), f32),
        "w_in": nrm(ks[1], (DEPTH, D_MODEL, IN_COLS), D_MODEL ** -0.5),
        "b_gate": nrm(ks[2], (DEPTH, N_BRANCH * D_MODEL), 0.1),
        "g_mix_pre": gain(ks[3], D_MODEL),
        "g_q": gain(ks[4], Q_LORA),
        "w_uq": nrm(ks[5], (DEPTH, Q_LORA, N_HEADS * (QK_NOPE + QK_ROPE)), Q_LORA ** -0.5),
        "g_kv": gain(ks[6], KV_LORA),
        "w_ukv": nrm(ks[7], (DEPTH, KV_LORA, N_HEADS * (QK_NOPE + V_HEAD)), KV_LORA ** -0.5),
        "w_o_att": nrm(ks[8], (DEPTH, N_HEADS * V_HEAD, D_MODEL), (N_HEADS * V_HEAD) ** -0.5),
        "lam_re": -0.5 + 0.01 * jax.random.normal(ks[9], (DEPTH, SSM_GROUPS, SSM_STATE), f32),
        "lam_im": math.pi * n_idx + 0.01 * jax.random.normal(ks[10], (DEPTH, SSM_GROUPS, SSM_STATE), f32),
        "log_step": jax.random.uniform(ks[11], (DEPTH, SSM_GROUPS), f32,
                                       math.log(DT_MIN), math.log(DT_MAX)),
        "b_re": nrm(ks[12], (DEPTH, SSM_GROUPS, SSM_STATE, SSM_GROUP), (2 * SSM_GROUP) ** -0.5),
        "b_im": nrm(ks[13], (DEPTH, SSM_GROUPS, SSM_STATE, SSM_GROUP), (2 * SSM_GROUP) ** -0.5),
        "c_re": nrm(ks[14], (DEPTH, SSM_GROUPS, SSM_GROUP, SSM_STATE), (2 * SSM_STATE) ** -0.5),
        "c_im": nrm(ks[15], (DEPTH, SSM_GROUPS, SSM_GROUP, SSM_STATE), (2 * SSM_STATE) ** -0.5),
        "d_skip": nrm(ks[16], (DEPTH, SSM_WIDTH), 1.0),
        "w_glu": nrm(ks[17], (DEPTH, SSM_WIDTH, 2 * D_MODEL), SSM_WIDTH ** -0.5),
        "w_out": nrm(ks[18], (DEPTH, D_MODEL, D_MODEL), D_MODEL ** -0.5),
        "g_mix_post": gain(ks[19], D_MODEL),
        "g_ffn_pre": gain(ks[20], D_MODEL),
        "w_up": nrm(ks[21], (DEPTH, D_MODEL, 2 * D_FF), D_MODEL ** -0.5),
        "conv_w": nrm(ks[22], (DEPTH, CONV_W, D_FF), CONV_W ** -0.5),
        "conv_b": nrm(ks[23], (DEPTH, D_FF), 0.02),
        "w_down": nrm(jax.random.fold_in(key, 101), (DEPTH, D_FF, D_MODEL), D_FF ** -0.5),
        "g_ffn_post": gain(jax.random.fold_in(key, 102), D_MODEL),
    }


def reference(x, w_in, b_gate, g_mix_pre, g_q, w_uq, g_kv, w_ukv, w_o_att,
              lam_re, lam_im, log_step, b_re, b_im, c_re, c_im, d_skip, w_glu,
              w_out, g_mix_post, g_ffn_pre, w_up, conv_w, conv_b, w_down, g_ffn_post):
    s = x.shape[1]
    cos, sin = rope_tables(s)
    o1 = Q_LORA
    o2 = o1 + KV_LORA
    o3 = o2 + QK_ROPE
    o4 = o3 + SSM_WIDTH
    for l in range(DEPTH):
        h = rmsnorm(x, g_mix_pre[l])
        z = h @ w_in[l]
        c_q, c_kv, k_pe, u = z[..., :o1], z[..., o1:o2], z[..., o2:o3], z[..., o3:o4]
        gates = jax.nn.sigmoid((z[..., o4:] + b_gate[l]).astype(jnp.float32)).astype(x.dtype)
        g_att, g_ssm = gates[..., :D_MODEL], gates[..., D_MODEL:]
        y_att = mla_branch(c_q, c_kv, k_pe, g_q[l], w_uq[l], g_kv[l], w_ukv[l],
                           w_o_att[l], cos, sin)
        y_ssm = s5_branch(u, lam_re[l], lam_im[l], log_step[l], b_re[l], b_im[l],
                          c_re[l], c_im[l], d_skip[l], w_glu[l])
        merged = g_att * y_att + g_ssm * y_ssm
        x = x + rmsnorm(merged @ w_out[l], g_mix_post[l])
        h = rmsnorm(x, g_ffn_pre[l])
        x = x + rmsnorm(conv_glu_ffn(h, w_up[l], conv_w[l], conv_b[l], w_down[l]), g_ffn_post[l])
    return x
```

```python
import contextlib
import math
import numpy as np
import ml_dtypes
import concourse.bass as bass
import concourse.mybir as mybir
from concourse.bass_utils import run_bass_kernel_spmd
from concourse.alu_op_type import AluOpType as ALU

AF = mybir.ActivationFunctionType
F32 = mybir.dt.float32
BF16 = mybir.dt.bfloat16
I32 = mybir.dt.int32
ENGS = ['sp', 'act', 'pool', 'dve', 'pe']

D_MODEL = 1024
N_HEADS = 8
D_FF = 2816
NFC = 22
EPS = 1e-6
NV = 145
GPRE, BG, GQ, GKV, GPOST, GFPRE, GFPOST, CW, CB, DSK = 0, 8, 24, 27, 29, 37, 45, 53, 119, 141
WINX = 3264
ARENA_ELEMS = 196 * 512


class Res:
    def __init__(self, name, dsem=None, multi=False):
        self.name = name
        self.writers = {}
        self.readers = {}
        self.dsem = dsem
        self.nw = 0
        self.multi = multi
        self.a = None


class Op:
    __slots__ = ('eng', 'fn', 'deps', 'needed', 'sig', 'dma', 'key')


class Prog:
    def __init__(self, nc, st):
        self.nc = nc
        self.st = st
        self.ops = []
        self.esem = {e: st.enter_context(nc.semaphore("s_" + e)) for e in ENGS}
        self.allsems = list(self.esem.values())
        self.nsem = len(ENGS)
        self.last = {e: None for e in ENGS}
        self.dma_res = []
        self.bar_deps = []

    def res(self, name, dma=False, multi=False):
        dsem = None
        if dma:
            dsem = self.st.enter_context(self.nc.semaphore("d_" + name))
            self.allsems.append(dsem)
            self.nsem += 1
        r = Res(name, dsem, multi)
        if dma:
            self.dma_res.append(r)
            r.last_dma = None
        return r

    def sb(self, name, shape, dtype, dma=False, multi=False):
        r = self.res(name, dma=dma, multi=multi)
        t = self.st.enter_context(self.nc.sbuf_tensor(name, shape, dtype))
        r.a = t[:]
        return r

    def ps(self, name, shape, dtype):
        r = self.res(name)
        t = self.st.enter_context(self.nc.psum_tensor(name, shape, dtype))
        r.a = t[:]
        return r

    def barrier(self):
        deps = [o for o in self.last.values() if o is not None]
        for r in self.dma_res:
            if r.last_dma is not None:
                deps.append(r.last_dma)
        self.bar_deps = deps

    def add(self, eng, fn, reads=(), writes=(), dma=False):
        op = Op()
        op.eng = eng
        op.fn = fn
        op.dma = dma
        op.needed = False
        op.sig = None
        deps = {}
        for o in self.bar_deps:
            deps[id(o)] = o
        for r in reads:
            for o in r.writers.values():
                deps[id(o)] = o
        for w in writes:
            for o in w.writers.values():
                deps[id(o)] = o
            for o in w.readers.values():
                deps[id(o)] = o
        if dma:
            dres = [w for w in writes if w.dsem is not None]
            assert len(dres) == 1, [w.name for w in writes]
            dres[0].nw += 1
            op.sig = (dres[0].dsem, 16 * dres[0].nw)
            dres[0].last_dma = op
            key = id(dres[0].dsem)
        else:
            key = eng
            self.last[eng] = op
        op.key = key
        op.deps = [o for o in deps.values()
                   if not (o.eng == 'pe' and eng == 'pe' and not o.dma and not dma)]
        for o in op.deps:
            o.needed = True
        for r in reads:
            r.readers[key] = op
        for w in writes:
            if w.multi:
                w.writers[key] = op
            else:
                w.writers = {key: op}
                w.readers = {}
        self.ops.append(op)
        return op

    def finish(self, final_res):
        nc = self.nc
        self.add('sp', None, reads=final_res)
        cnt = {e: 0 for e in ENGS}
        for op in self.ops:
            if not op.dma and op.needed:
                cnt[op.eng] += 1
                op.sig = (self.esem[op.eng], cnt[op.eng])
        per = {e: [o for o in self.ops if o.eng == e] for e in ENGS}
        self.stats = {e: len(per[e]) for e in ENGS}

        def mk(e):
            def body(engobj):
                waited = {}
                for op in per[e]:
                    need = {}
                    for d in op.deps:
                        s, v = d.sig
                        k = id(s)
                        if v > need.get(k, (None, 0))[1]:
                            need[k] = (s, v)
                    for k, (s, v) in need.items():
                        if v > waited.get(k, 0):
                            engobj.wait_ge(s, v)
                            waited[k] = v
                    if op.fn is None:
                        continue
                    ins = op.fn(engobj)
                    if op.dma:
                        ins.then_inc(op.sig[0], 16)
                    elif op.needed:
                        ins.then_inc(op.sig[0], 1)
            return body

        import os
        if os.environ.get("NOCLEAR") != "1":
            for sm in self.allsems:
                nc.gpsimd.sem_clear(sm)
            nc.all_engine_barrier()
        with nc.Block() as block:
            block.sync(mk('sp'))
            block.scalar(mk('act'))
            block.gpsimd(mk('pool'))
            block.vector(mk('dve'))
            block.tensor(mk('pe'))


class Arena:
    def __init__(self, K, name):
        self.K = K
        self.name = name
        self.off = 0

    def alloc(self, name, shape, dtype, dma=False, multi=False, at=None):
        n = 1
        for s in shape:
            n *= s
        nel = n * (2 if dtype in (F32, I32) else 1)
        nel = (nel + 15) // 16 * 16
        off = self.off if at is None else at
        if at is None:
            self.off += nel
        assert off + nel <= ARENA_ELEMS, (self.name, name, off + nel, ARENA_ELEMS)
        v = self.K.arena_t[:, off:off + n * (2 if dtype in (F32, I32) else 1)]
        if dtype in (F32, I32):
            v = v.bitcast(dtype)
        if len(shape) >= 2:
            names = "abcdefg"[:len(shape)]
            pat = "p (" + " ".join(names) + ") -> p " + " ".join(names)
            v = v.rearrange(pat, **{n: sz for n, sz in zip(names[:-1], shape[:-1])})
        r = self.K.P.res(self.name + "_" + name, dma=dma, multi=multi)
        r.a = v
        r.off = off
        return r


class NS:
    pass


class K:
    def __init__(self, nseq=2, seq=2048, depth=4, dump=False, phases=None):
        self.phases = phases
        self.NSEQ = nseq
        self.S = seq
        self.L = depth
        self.T = nseq * seq
        self.NT = self.T // 512
        self.TPS = seq // 512
        self.NCH = self.T // 8
        self.CPS = seq // 8
        self.dump = dump
        self.wl_i = 0
        assert self.NCH <= 512

    def build(self):
        nc = bass.Bass("TRN2", target_bir_lowering=False)
        self.nc = nc
        L, T, S = self.L, self.T, self.S
        d = {}

        def inp(name, shape):
            d[name] = nc.dram_tensor(name, shape, F32, kind="ExternalInput").ap()

        inp("xT", [1024, T])
        inp("w_inx", [L, 1024, WINX])
        inp("w_uq", [L, 384, 768])
        inp("w_uqsw", [L, 384, 256])
        inp("w_uk", [L, 256, 512])
        inp("w_uv", [L, 256, 512])
        inp("w_oatt", [L, 512, 1024])
        inp("w_glu", [L, 512, 2048])
        inp("w_out", [L, 1024, 1024])
        inp("w_up", [L, 1024, 2 * D_FF])
        inp("w_down", [L, D_FF, 1024])
        inp("vecs", [128, L * NV])
        inp("s5v", [128, L * 48])
        inp("s5m", [L, 128, 2048])
        inp("rope", [2, 128, S])
        inp("ident", [128, 128])
        inp("bmask", [128, 128])
        d["yT"] = nc.dram_tensor("yT", [1024, T], F32, kind="ExternalOutput").ap()
        skind = "ExternalOutput" if self.dump else "Internal"

        def scr(name, shape, dt):
            d[name] = nc.dram_tensor(name, shape, dt, kind=skind).ap()

        scr("s1", [1024, T], F32)
        scr("s2", [1024, T], F32)
        scr("qT", [8, 96, T], BF16)
        scr("kT", [8, 96, T], BF16)
        scr("vA", [T, 768], BF16)
        scr("uT", [512, T], BF16)
        scr("gT", [2048, T], BF16)
        scr("oT", [512, T], BF16)
        scr("ysT", [512, T], BF16)
        scr("actT", [D_FF, T], BF16)
        self.d = d
        with contextlib.ExitStack() as st:
            P = Prog(nc, st)
            self.P = P
            self.dr = {n: P.res("dr_" + n, dma=True, multi=True) for n in
                       ["yT", "s1", "s2", "qT", "kT", "vA", "uT", "gT", "oT", "ysT", "actT"]}
            self.dr["xT"] = P.res("dr_xT")
            self.arena_t = st.enter_context(nc.sbuf_tensor("arena", [128, ARENA_ELEMS], BF16))
            self.psb = [P.ps("psb%d" % i, [128, 512], F32) for i in range(8)]
            self.vecs = P.sb("vecs_sb", [128, L * NV], F32, dma=True)
            self.s5v = P.sb("s5v_sb", [128, L * 48], F32, dma=True)
            self.ident = P.sb("identb", [128, 128], BF16, dma=True)
            self.ones = P.sb("onesb", [128, 128], BF16)
            self.bmask = P.sb("bmask_sb", [128, 128], F32, dma=True)
            P.add('sp', lambda e: e.dma_start(out=self.bmask.a, in_=d["bmask"]), writes=[self.bmask], dma=True)
            self.epst = P.sb("epst", [128, 1], F32)
            P.add('sp', lambda e: e.dma_start(out=self.vecs.a, in_=d["vecs"]), writes=[self.vecs], dma=True)
            P.add('sp', lambda e: e.dma_start(out=self.s5v.a, in_=d["s5v"]), writes=[self.s5v], dma=True)
            P.add('pool', lambda e: e.dma_start(out=self.ident.a, in_=d["ident"]), writes=[self.ident], dma=True)
            P.add('dve', lambda e: e.memset(self.ones.a, 1.0), writes=[self.ones])
            P.add('dve', lambda e: e.memset(self.epst.a, EPS), writes=[self.epst])
            self.mk_arenas()
            nupd = 2 * L
            for l in range(L):
                for half in range(2):
                    u = 2 * l + half
                    src = "xT" if u == 0 else ("s1", "s2")[(u - 1) % 2]
                    dst = "yT" if u == nupd - 1 else ("s1", "s2")[u % 2]
                    on = lambda ph: self.phases is None or ph in self.phases
                    if half == 0:
                        if on('m1'):
                            self.m1(l, src)
                            P.barrier()
                        if on('m3'):
                            self.m3_prep(l)
                        if on('m2'):
                            self.m2(l)
                            P.barrier()
                        if on('m3'):
                            self.m3(l)
                            P.barrier()
                        if on('m4'):
                            self.m4(l, src, dst)
                            P.barrier()
                    else:
                        if on('f1'):
                            self.f1(l, src)
                            P.barrier()
                        if on('f2'):
                            self.f2(l, src, dst)
                            P.barrier()
            P.finish([self.dr["yT"]])
        return nc

    def vec(self, l, off, n=1):
        return self.vecs.a[:, l * NV + off:l * NV + off + n]

    def mk_arenas(self):
        T, S, NCH = self.T, self.S, self.NCH
        A = Arena(self, "m1")
        a = NS()
        a.win = A.alloc("win", [8, WINX], BF16, multi=True)
        self.wblocks(a.win, WINX, 1024)
        a.wuq = A.alloc("wuq", [3, 768], BF16, multi=True)
        a.wuqsw = A.alloc("wuqsw", [3, 256], BF16, multi=True)
        a.wuk = A.alloc("wuk", [2, 512], BF16, multi=True)
        a.wuv = A.alloc("wuv", [2, 512], BF16, multi=True)
        a.stg = [A.alloc("stg%d" % i, [1024], F32, dma=True) for i in range(2)]
        a.X = [A.alloc("X%d" % i, [8, 512], F32, dma=True) for i in range(2)]
        a.cs = [A.alloc("cs%d" % i, [2, 512], F32, dma=True) for i in range(2)]
        a.xsq = A.alloc("xsq", [8, 512], BF16)
        a.xg = A.alloc("xg", [8, 512], BF16)
        a.rt = A.alloc("rt", [512], F32)
        a.rstd = A.alloc("rstd", [512], F32)
        a.cq = A.alloc("cq", [3, 512], F32)
        a.ckv = A.alloc("ckv", [2, 512], F32)
        a.sq = A.alloc("sq", [3, 512], BF16)
        a.rq = A.alloc("rq", [512], F32)
        a.cqn = A.alloc("cqn", [3, 512], BF16)
        a.ckvn = A.alloc("ckvn", [2, 512], BF16)
        a.tmp = [A.alloc("tmp%d" % i, [512], F32) for i in range(4)]
        a.kp = A.alloc("kp", [512], F32)
        a.kps = A.alloc("kps", [512], F32)
        a.kr = A.alloc("kr", [512], BF16)
        a.ut = A.alloc("ut", [4, 512], BF16)
        a.gts = [A.alloc("gts%d" % i, [4, 512], BF16) for i in range(2)]
        a.qh = [A.alloc("qh%d" % i, [512], BF16) for i in range(3)]
        a.kh = [A.alloc("kh%d" % i, [512], BF16) for i in range(3)]
        a.vt = A.alloc("vt", [4, 8, 96], BF16)
        self.A1 = a
        A = Arena(self, "m2")
        a = NS()
        a.q = [A.alloc("q%d" % i, [S], BF16, dma=True) for i in range(2)]
        a.k = [A.alloc("k%d" % i, [S], BF16, dma=True) for i in range(2)]
        a.v = [A.alloc("v%d" % i, [S // 128, 768], BF16, dma=True) for i in range(2)]
        a.pT = [A.alloc("pT%d" % i, [512], BF16) for i in range(3)]
        a.rc = [A.alloc("rc%d" % i, [512], F32) for i in range(3)]
        a.hi = [A.alloc("hi%d" % i, [512], BF16) for i in range(3)]
        a.lo = [A.alloc("lo%d" % i, [512], BF16) for i in range(3)]
        a.bc = [A.alloc("bc%d" % i, [512], F32) for i in range(2)]
        a.ot = [A.alloc("ot%d" % i, [512], BF16) for i in range(2)]
        self.A2 = a
        A = Arena(self, "m3")
        a = NS()
        a.u = A.alloc("u", [4, T], BF16, dma=True, multi=True)
        a.Sb = A.alloc("Sb", [16, 2, NCH], BF16)
        a.Xb = A.alloc("Xb", [16, 2, NCH], BF16)
        a.Wp = A.alloc("Wp", [4, 8, 2, 128], BF16)
        a.PCb = A.alloc("PCb", [9, 2, 16, 32], BF16)
        a.Kb = A.alloc("Kb", [4, 8, 128], BF16)
        a.BbQb = A.alloc("BbQb", [2, 16, 32], BF16)
        a.m = A.alloc("m", [4, 16, 32], F32, dma=True)
        a.Bb = A.alloc("Bb", [2, 16, 32], F32)
        a.Pw = A.alloc("Pw", [9, 2, 16], F32)
        a.sm = [A.alloc("sm%d" % i, [16], F32) for i in range(14)]
        a.smi = A.alloc("smi", [16], I32)
        a.fc = A.alloc("fcoef", [2, 16], F32)
        a.nAi = A.alloc("nAi", [16], F32)
        a.nPr = A.alloc("nPr", [9, 16], F32)
        a.td = [A.alloc("td%d" % i, [16, 32], F32) for i in range(4)]
        a.tp = [A.alloc("tp%d" % i, [16, 32], F32) for i in range(4)]
        a.xs = [A.alloc("xs%d" % i, [16, 2, self.NSEQ], F32) for i in range(2)]
        a.t1 = A.alloc("t1", [16, 2, self.NSEQ], F32)
        a.t2 = A.alloc("t2", [16, 2, self.NSEQ], F32)
        off_ys = A.off
        a.ys = A.alloc("ys", [T], BF16)
        a.PBb = A.alloc("PBb", [4, 8, 2, 4, 32], BF16, at=off_ys)
        if 16 * 8 * 2 * 32 > T:
            A.off = off_ys + 16 * 8 * 2 * 32
        self.A3 = a
        A = Arena(self, "m4")
        a = NS()
        a.woatt = A.alloc("woatt", [4, 1024], BF16, multi=True)
        a.stg = [A.alloc("stg%d" % i, [2048], F32, dma=True) for i in range(2)]
        a.wglu = A.alloc("wglu", [4, 2048], BF16, multi=True)
        a.wout = A.alloc("wout", [8, 1024], BF16, multi=True)
        a.o = [A.alloc("o%d" % i, [4, 512], BF16, dma=True) for i in range(2)]
        a.ysb = [A.alloc("ysb%d" % i, [4, 512], BF16, dma=True) for i in range(2)]
        a.g = [A.alloc("g%d" % i, [16, 512], BF16, dma=True) for i in range(2)]
        a.X = [A.alloc("X%d" % i, [8, 512], F32, dma=True) for i in range(2)]
        a.sg = [A.alloc("sg%d" % i, [512], F32) for i in range(2)]
        a.yss = [A.alloc("yss%d" % i, [512], F32) for i in range(2)]
        a.m1 = [A.alloc("m1%d" % i, [512], F32) for i in range(2)]
        a.m2 = [A.alloc("m2%d" % i, [512], F32) for i in range(2)]
        a.mg = A.alloc("mg", [8, 512], BF16)
        a.Y = A.alloc("Y", [8, 512], F32)
        a.ysq = A.alloc("ysq", [8, 512], BF16)
        a.rt = A.alloc("rt", [512], F32)
        a.rstd = A.alloc("rstd", [512], F32)
        a.tt = [A.alloc("tt%d" % i, [512], F32) for i in range(2)]
        self.A4 = a
        A = Arena(self, "f1")
        a = NS()
        rstdF = A.alloc("rstdF", [T], F32)
        a.rstdF = rstdF
        a.wup = A.alloc("wup", [8, 2 * D_FF], BF16, multi=True)
        a.X = [A.alloc("X%d" % i, [8, 512], F32, dma=True) for i in range(2)]
        a.sqg = A.alloc("sqg", [8, 512], BF16)
        a.rt = A.alloc("rt", [512], F32)
        a.Gb = [A.alloc("Gb%d" % i, [514], BF16) for i in range(3)]
        a.Gh = A.alloc("Gh", [NFC, 2], BF16)
        a.ge = [A.alloc("ge%d" % i, [512], BF16) for i in range(3)]
        a.dgall = A.alloc("dgall", [NFC, 3, 128], BF16)
        a.act = [A.alloc("act%d" % i, [NFC // 2, 512], BF16) for i in range(2)]
        a.stg = [A.alloc("stg%d" % i, [1024], F32, dma=True, at=a.act[i].off) for i in range(2)]
        self.wblocks(a.wup, 2 * D_FF, 1024)
        self.A5 = a
        A = Arena(self, "f2")
        a = NS()
        A.alloc("rstdF_pad", [T], F32)
        a.rstdF = rstdF
        a.wdown = A.alloc("wdown", [NFC, 1024], BF16, multi=True)
        a.stg = [A.alloc("stg%d" % i, [512], F32, dma=True) for i in range(4)]
        self.wblocks(a.wdown, 1024, 512)
        a.a = [A.alloc("a%d" % i, [NFC, 512], BF16, dma=True) for i in range(2)]
        a.X = [A.alloc("X%d" % i, [8, 512], F32, dma=True) for i in range(2)]
        a.Y = A.alloc("Y", [8, 512], F32)
        a.ysq = A.alloc("ysq", [8, 512], BF16)
        a.rt = A.alloc("rt", [512], F32)
        a.rstd = A.alloc("rstd", [512], F32)
        a.tt = [A.alloc("tt%d" % i, [512], F32) for i in range(2)]
        self.A6 = a

    def wblocks(self, dst, n, W):
        nb = (n + W - 1) // W
        rs = []
        for b in range(nb):
            r = self.P.res(dst.name + "_b%d" % b, multi=True)
            r.a = dst.a
            rs.append(r)
        dst.blocks = rs
        dst.W = W
        return rs

    def wr(self, dst, c0, c1):
        if not hasattr(dst, 'blocks'):
            return [dst]
        return dst.blocks[c0 // dst.W:(c1 - 1) // dst.W + 1]

    def load_w(self, dst, src2d, kc, n, stg):
        W = stg[0].a.shape[-1]
        if hasattr(dst, 'blocks'):
            assert dst.W % W == 0 or W % dst.W == 0
            W = min(W, dst.W)
        for n0 in range(0, n, W):
            n1 = min(n, n0 + W)
            wres = self.wr(dst, n0, n1)
            assert len(wres) == 1
            for c in range(kc):
                i = self.wl_i
                self.wl_i += 1
                sl = stg[i % len(stg)]
                self.dma('sp', sl, sl.a[:, 0:n1 - n0], None, src2d[c * 128:(c + 1) * 128, n0:n1])
                if i % 2 == 0:
                    self.act(wres[0], dst.a[:, c, n0:n1], sl, sl.a[:, 0:n1 - n0], AF.Copy)
                else:
                    self.cp('dve', wres[0], dst.a[:, c, n0:n1], sl, sl.a[:, 0:n1 - n0])

    def mm(self, outr, out_ap, lr, lhsT, rr, rhs, start, stop, tp=None):
        if tp is None:
            fn = lambda e: e.matmul(out_ap, lhsT, rhs, start=start, stop=stop)
        else:
            fn = lambda e: e.matmul(out_ap, lhsT, rhs, start=start, stop=stop, tile_position=tp)
        rd = (lr if isinstance(lr, list) else [lr]) + (rr if isinstance(rr, list) else [rr])
        self.P.add('pe', fn, reads=rd, writes=[outr])

    def act(self, outr, out_ap, inr, in_ap, func, bias=None, scale=None, extra_reads=()):
        kw = {}
        if bias is not None:
            kw['bias'] = bias
        if scale is not None:
            kw['scale'] = scale
        self.P.add('act', lambda e: e.activation(out=out_ap, in_=in_ap, func=func, **kw),
                   reads=[inr] + list(extra_reads), writes=[outr])

    def tt(self, eng, outr, out_ap, r0, in0, r1, in1, op):
        self.P.add(eng, lambda e: e.tensor_tensor(out=out_ap, in0=in0, in1=in1, op=op),
                   reads=[r0, r1], writes=[outr])

    def ts(self, eng, outr, out_ap, r0, in0, s1, s2, op0, op1=None, extra_reads=()):
        if op1 is None:
            fn = lambda e: e.tensor_scalar(out=out_ap, in0=in0, scalar1=s1, scalar2=None, op0=op0)
        else:
            fn = lambda e: e.tensor_scalar(out=out_ap, in0=in0, scalar1=s1, scalar2=s2, op0=op0, op1=op1)
        self.P.add(eng, fn, reads=[r0] + list(extra_reads), writes=[outr])

    def stt(self, outr, out_ap, r0, in0, scalar, r1, in1, op0, op1, extra_reads=()):
        self.P.add('dve', lambda e: e.scalar_tensor_tensor(out=out_ap, in0=in0, scalar=scalar, in1=in1,
                                                           op0=op0, op1=op1),
                   reads=[r0, r1] + list(extra_reads), writes=[outr])

    def cp(self, eng, outr, out_ap, inr, in_ap):
        self.P.add(eng, lambda e: e.tensor_copy(out=out_ap, in_=in_ap), reads=[inr], writes=[outr])

    def dma(self, eng, outr, out_ap, inr, in_ap):
        self.P.add(eng, lambda e: e.dma_start(out=out_ap, in_=in_ap),
                   reads=[inr] if inr is not None else [], writes=[outr], dma=True)

    def rms_rstd(self, ps, sq_res, sq_ap_fn, nchunk, dim, rt, rstd):
        for c in range(nchunk):
            self.mm(ps, ps.a, self.ones, self.ones.a, sq_res, sq_ap_fn(c), c == 0, c == nchunk - 1)
        self.act(rt, rt.a, ps, ps.a, AF.Sqrt, bias=self.epst.a[:, 0:1], scale=1.0 / dim, extra_reads=[self.epst])
        self.P.add('dve', lambda e: e.reciprocal(out=rstd.a, in_=rt.a), reads=[rt], writes=[rstd])

    def m1(self, l, src):
        P, a, d, psb = self.P, self.A1, self.d, self.psb
        xin = d[src].rearrange("(c p) t -> p c t", p=128)
        xr = self.dr[src]
        self.load_w(a.win, d["w_inx"][l], 8, WINX, a.stg)
        self.load_w(a.wuq, d["w_uq"][l], 3, 768, a.stg)
        self.load_w(a.wuqsw, d["w_uqsw"][l], 3, 256, a.stg)
        self.load_w(a.wuk, d["w_uk"][l], 2, 512, a.stg)
        self.load_w(a.wuv, d["w_uv"][l], 2, 512, a.stg)
        P.add('pool', lambda e: e.memset(a.vt.a[:, :, :, 64:96], 1.0), writes=[a.vt])
        ropev = d["rope"].rearrange("k p s -> p k s")

        def load(tt):
            X = a.X[tt % 2]
            cs = a.cs[tt % 2]
            self.dma('sp', X, X.a, xr, xin[:, :, tt * 512:(tt + 1) * 512])
            p0 = (tt % self.TPS) * 512
            self.dma('sp', cs, cs.a[64:96, :, :], None, ropev[64:96, :, p0:p0 + 512])

        load(0)
        zb = [1, 2, 3, 4]
        zi = [0]

        def nextbank():
            b = psb[zb[zi[0] % 4]]
            zi[0] += 1
            return b

        for tt in range(self.NT):
            if tt + 1 < self.NT:
                load(tt + 1)
            X, cs = a.X[tt % 2], a.cs[tt % 2]
            t0 = tt * 512
            cosr = cs.a[64:96, 0, :]
            sinr = cs.a[64:96, 1, :]
            self.act(a.xsq, a.xsq.a, X, X.a, AF.Square)
            for dc in range(8):
                self.ts('dve', a.xg, a.xg.a[:, dc, :], X, X.a[:, dc, :], self.vec(l, GPRE + dc), None, ALU.mult,
                        extra_reads=[self.vecs])
            self.rms_rstd(psb[0], a.xsq, lambda c: a.xsq.a[:, c, :], 8, 1024.0, a.rt, a.rstd)

            def zchunk(ps, c0, m, tp=None, out_ap=None):
                oa = ps.a[0:m, :] if out_ap is None else out_ap
                for dc in range(8):
                    self.mm(ps, oa, self.wr(a.win, c0, c0 + m), a.win.a[:, dc, c0:c0 + m], a.xg, a.xg.a[:, dc, :], dc == 0, dc == 7, tp)

            for c in range(3):
                ps = nextbank()
                zchunk(ps, c * 128, 128)
                self.tt('dve', a.cq, a.cq.a[:, c, :], ps, ps.a, a.rstd, a.rstd.a, ALU.mult)
            for c in range(2):
                ps = nextbank()
                zchunk(ps, 384 + c * 128, 128)
                self.tt('dve', a.ckv, a.ckv.a[:, c, :], ps, ps.a, a.rstd, a.rstd.a, ALU.mult)
            self.act(a.sq, a.sq.a, a.cq, a.cq.a, AF.Square)
            self.rms_rstd(psb[7], a.sq, lambda c: a.sq.a[:, c, :], 3, 384.0, a.rt, a.rq)
            for c in range(3):
                self.stt(a.cqn, a.cqn.a[:, c, :], a.cq, a.cq.a[:, c, :], self.vec(l, GQ + c), a.rq, a.rq.a,
                         ALU.mult, ALU.mult, extra_reads=[self.vecs])
            self.act(a.sq, a.sq.a[:, 0:2, :], a.ckv, a.ckv.a, AF.Square)
            self.rms_rstd(psb[7], a.sq, lambda c: a.sq.a[:, c, :], 2, 256.0, a.rt, a.rq)
            for c in range(2):
                self.stt(a.ckvn, a.ckvn.a[:, c, :], a.ckv, a.ckv.a[:, c, :], self.vec(l, GKV + c), a.rq, a.rq.a,
                         ALU.mult, ALU.mult, extra_reads=[self.vecs])
            zchunk(psb[5], 640, 32, tp=(0, 64), out_ap=psb[5].a[64:96, :])
            zchunk(psb[6], 672, 32, tp=(0, 64), out_ap=psb[6].a[64:96, :])
            self.tt('dve', a.kp, a.kp.a[64:96, :], psb[5], psb[5].a[64:96, :], a.rstd, a.rstd.a[64:96, :], ALU.mult)
            self.tt('dve', a.kps, a.kps.a[64:96, :], psb[6], psb[6].a[64:96, :], a.rstd, a.rstd.a[64:96, :], ALU.mult)
            self.tt('dve', a.kp, a.kp.a[64:96, :], a.kp, a.kp.a[64:96, :], cs, cosr, ALU.mult)
            self.tt('dve', a.kps, a.kps.a[64:96, :], a.kps, a.kps.a[64:96, :], cs, sinr, ALU.mult)
            self.tt('pool', a.kr, a.kr.a[64:96, :], a.kp, a.kp.a[64:96, :], a.kps, a.kps.a[64:96, :], ALU.add)
            for c in range(4):
                ps = nextbank()
                zchunk(ps, 704 + c * 128, 128)
                self.tt('dve', a.ut, a.ut.a[:, c, :], ps, ps.a, a.rstd, a.rstd.a, ALU.mult)
            self.dma('sp', self.dr["uT"], d["uT"].rearrange("(c p) t -> p c t", p=128)[:, :, t0:t0 + 512],
                     a.ut, a.ut.a)
            for gb in range(4):
                gts = a.gts[gb % 2]
                for gi in range(4):
                    gc = gb * 4 + gi
                    ps = nextbank()
                    zchunk(ps, 1216 + gc * 128, 128)
                    tmp = a.tmp[gc % 4]
                    self.tt('dve', tmp, tmp.a, ps, ps.a, a.rstd, a.rstd.a, ALU.mult)
                    self.act(gts, gts.a[:, gi, :], tmp, tmp.a, AF.Sigmoid, bias=self.vec(l, BG + gc),
                             extra_reads=[self.vecs])
                self.dma('sp', self.dr["gT"],
                         d["gT"].rearrange("(c p) t -> p c t", p=128)[:, gb * 4:(gb + 1) * 4, t0:t0 + 512],
                         gts, gts.a)
            for h in range(8):
                ps = nextbank()
                psw = psb[5 + h % 2]
                for c in range(3):
                    self.mm(ps, ps.a[0:96, :], a.wuq, a.wuq.a[:, c, 96 * h:96 * h + 96], a.cqn, a.cqn.a[:, c, :],
                            c == 0, c == 2)
                for c in range(3):
                    self.mm(psw, psw.a[64:96, :], a.wuqsw, a.wuqsw.a[:, c, 32 * h:32 * h + 32], a.cqn,
                            a.cqn.a[:, c, :], c == 0, c == 2, tp=(0, 64))
                qh = a.qh[h % 3]
                t1, t2 = a.tmp[(2 * h) % 4], a.tmp[(2 * h + 1) % 4]
                self.act(qh, qh.a[0:64, :], ps, ps.a[0:64, :], AF.Copy)
                self.tt('dve', t1, t1.a[64:96, :], ps, ps.a[64:96, :], cs, cosr, ALU.mult)
                self.tt('dve', t2, t2.a[64:96, :], psw, psw.a[64:96, :], cs, sinr, ALU.mult)
                self.tt('pool', qh, qh.a[64:96, :], t1, t1.a[64:96, :], t2, t2.a[64:96, :], ALU.add)
                self.dma('sp', self.dr["qT"], d["qT"][h, :, t0:t0 + 512], qh, qh.a[0:96, :])
            for h in range(8):
                ps = nextbank()
                for c in range(2):
                    self.mm(ps, ps.a[0:64, :], a.wuk, a.wuk.a[:, c, 64 * h:64 * h + 64], a.ckvn, a.ckvn.a[:, c, :],
                            c == 0, c == 1)
                kh = a.kh[h % 3]
                self.act(kh, kh.a[0:64, :], ps, ps.a[0:64, :], AF.Copy)
                self.cp('pool', kh, kh.a[64:96, :], a.kr, a.kr.a[64:96, :])
                self.dma('sp', self.dr["kT"], d["kT"][h, :, t0:t0 + 512], kh, kh.a[0:96, :])
            for tb in range(4):
                ps = nextbank()
                for c in range(2):
                    self.mm(ps, ps.a, a.ckvn, a.ckvn.a[:, c, tb * 128:(tb + 1) * 128], a.wuv, a.wuv.a[:, c, :],
                            c == 0, c == 1)
                self.cp('dve' if tb % 2 else 'act', a.vt, a.vt.a[:, tb, :, 0:64],
                        ps, ps.a.rearrange("p (h c) -> p h c", h=8)) if tb % 2 else \
                    self.act(a.vt, a.vt.a[:, tb, :, 0:64], ps, ps.a.rearrange("p (h c) -> p h c", h=8), AF.Copy)
            self.dma('sp', self.dr["vA"],
                     d["vA"].rearrange("(n p) c -> p n c", p=128)[:, 4 * tt:4 * tt + 4, :],
                     a.vt, a.vt.a.rearrange("p n h c -> p n (h c)"))

    def m2(self, l):
        P, a, d, psb = self.P, self.A2, self.d, self.psb
        S = self.S
        scale = 1.0 / math.sqrt(96.0)
        vAv = d["vA"].rearrange("(n p) c -> p n c", p=128)
        nsb = S // 128
        pairs = [(s, h) for s in range(self.NSEQ) for h in range(8)]

        def loadv(s):
            v = a.v[s % 2]
            self.dma('sp', v, v.a, self.dr["vA"], vAv[:, s * nsb:(s + 1) * nsb, :])

        def loadqk(i):
            s, h = pairs[i]
            q, k = a.q[i % 2], a.k[i % 2]
            self.dma('sp', q, q.a[0:96, :], self.dr["qT"], d["qT"][h, :, s * S:(s + 1) * S])
            self.dma('sp', k, k.a[0:96, :], self.dr["kT"], d["kT"][h, :, s * S:(s + 1) * S])

        blocks = []
        g = 0
        for i, (s, h) in enumerate(pairs):
            for qa in range(S // 512):
                nblk = 4 * qa + 4
                for j in range(nblk):
                    blocks.append((i, s, h, qa, j, nblk, g))
                g += 1
        N = len(blocks)

        def geom(b):
            i, s, h, qa, j, nblk, g = b
            r = j - 4 * qa
            qoff = 128 * r if r > 0 else 0
            return r, qoff, 512 - qoff

        def emitS(idx):
            i, s, h, qa, j, nblk, g = blocks[idx]
            r, qoff, nq = geom(blocks[idx])
            q, k = a.q[i % 2], a.k[i % 2]
            pss = psb[idx % 3]
            self.mm(pss, pss.a[:, 0:nq], k, k.a[0:96, j * 128:(j + 1) * 128],
                    q, q.a[0:96, qa * 512 + qoff:qa * 512 + 512], True, True)

        def emitEP(idx):
            r, qoff, nq = geom(blocks[idx])
            pss, pT = psb[idx % 3], a.pT[idx % 3]
            self.act(pT, pT.a[:, 0:nq], pss, pss.a[:, 0:nq], AF.Exp, scale=scale)
            if r >= 0:
                P.add('dve', lambda e, pT=pT: e.memset(pT.a[64:128, 0:64], 0.0), writes=[pT])

        def emitPV(idx):
            i, s, h, qa, j, nblk, g = blocks[idx]
            r, qoff, nq = geom(blocks[idx])
            v, pT, po = a.v[s % 2], a.pT[idx % 3], psb[POB[g % 3]]
            self.mm(po, po.a[0:96, qoff:512], v, v.a[:, j, h * 96:(h + 1) * 96], pT, pT.a[:, 0:nq],
                    j == 0, j == nblk - 1)

        def tail_front(b):
            g = b[6]
            po, rc, hi, lo = psb[POB[g % 3]], a.rc[g % 3], a.hi[g % 3], a.lo[g % 3]
            P.add('dve', lambda e, rc=rc, po=po: e.reciprocal(out=rc.a[64:65, :], in_=po.a[64:65, :]),
                  reads=[po], writes=[rc])
            self.cp('dve', hi, hi.a[64:65, :], rc, rc.a[64:65, :])
            self.tt('dve', lo, lo.a[64:65, :], rc, rc.a[64:65, :], hi, hi.a[64:65, :], ALU.subtract)

        def tail_back(b):
            i, s, h, qa, j, nblk, g = b
            po, pb = psb[POB[g % 3]], psb[5 + g % 2]
            hi, lo, bc, ot = a.hi[g % 3], a.lo[g % 3], a.bc[g % 2], a.ot[g % 2]
            self.mm(pb, pb.a[0:64, :], self.ones, self.ones.a[64:65, 0:64], hi, hi.a[64:65, :], True, False,
                    tp=(64, 0))
            self.mm(pb, pb.a[0:64, :], self.ones, self.ones.a[64:65, 0:64], lo, lo.a[64:65, :], False, True,
                    tp=(64, 0))
            self.act(bc, bc.a[0:64, :], pb, pb.a[0:64, :], AF.Copy)
            self.tt('dve', ot, ot.a[0:64, :], po, po.a[0:64, :], bc, bc.a[0:64, :], ALU.mult)
            tq = s * S + qa * 512
            self.dma('sp', self.dr["oT"], d["oT"][h * 64:(h + 1) * 64, tq:tq + 512], ot, ot.a[0:64, :])

        POB = [3, 4, 7]
        DEFER = 6
        loadv(0)
        loadqk(0)
        if len(pairs) > 1:
            loadqk(1)
        emitS(0)
        if N > 1:
            emitS(1)
        pending = []
        for idx in range(N):
            b = blocks[idx]
            i, s, h, qa, j, nblk, g = b
            if qa == 0 and j == 0:
                if h == 0 and s + 1 < self.NSEQ:
                    loadv(s + 1)
            emitEP(idx)
            if idx + 2 < N:
                b2 = blocks[idx + 2]
                if b2[3] == 0 and b2[4] == 0 and b2[0] + 1 < len(pairs) and b2[0] >= 1:
                    pass
                emitS(idx + 2)
            emitPV(idx)
            pending = [(pb_, c_ - 1) for (pb_, c_) in pending]
            while pending and pending[0][1] <= 0:
                tail_back(pending.pop(0)[0])
            if j == nblk - 1:
                tail_front(b)
                pending.append((b, DEFER))
                if qa == S // 512 - 1 and i + 2 < len(pairs):
                    loadqk(i + 2)
        for pb_, c_ in pending:
            tail_back(pb_)

    def m3_prep(self, l):
        P, a, d, psb = self.P, self.A3, self.d, self.psb
        T, NCH, CPS, NSEQ = self.T, self.NCH, self.CPS, self.NSEQ
        sv = self.s5v.a[:, l * 48:(l + 1) * 48]
        lre, lim, lst = sv[:, 0:16], sv[:, 16:32], sv[:, 32:48]
        svr = self.s5v
        self.dma('sp', a.m, a.m.a, None, d["s5m"][l].rearrange("p (k g c) -> p k g c", k=4, g=16))
        sm = a.sm
        dv, ac = 'dve', 'act'
        delta, lrd, mag, ang, rr, kf, fr, s1, s2, ch, tA, tB, ca, den = sm
        self.act(delta, delta.a, svr, lst, AF.Exp)
        self.tt(dv, lrd, lrd.a, svr, lre, delta, delta.a, ALU.mult)
        self.act(mag, mag.a, lrd, lrd.a, AF.Exp)
        self.tt(dv, ang, ang.a, svr, lim, delta, delta.a, ALU.mult)
        self.ts(dv, rr, rr.a, ang, ang.a, 1.0 / (2.0 * math.pi), None, ALU.mult)
        self.cp(dv, a.smi, a.smi.a, rr, rr.a)
        self.cp(dv, kf, kf.a, a.smi, a.smi.a)
        self.tt(dv, fr, fr.a, rr, rr.a, kf, kf.a, ALU.subtract)
        self.act(s1, s1.a, fr, fr.a, AF.Sin, scale=math.pi)
        self.act(s2, s2.a, fr, fr.a, AF.Sin, scale=math.pi / 2.0)
        self.tt(dv, tA, tA.a, s2, s2.a, s2, s2.a, ALU.mult)
        self.ts(dv, ch, ch.a, tA, tA.a, -2.0, 1.0, ALU.mult, ALU.add)
        self.tt(dv, tA, tA.a, s1, s1.a, ch, ch.a, ALU.mult)
        Pw = a.Pw
        self.stt(Pw, Pw.a[:, 1, 1, :], tA, tA.a, 2.0, mag, mag.a, ALU.mult, ALU.mult)
        self.tt(dv, tB, tB.a, s1, s1.a, s1, s1.a, ALU.mult)
        self.ts(dv, ca, ca.a, tB, tB.a, -2.0, 1.0, ALU.mult, ALU.add)
        self.tt(dv, Pw, Pw.a[:, 1, 0, :], ca, ca.a, mag, mag.a, ALU.mult)
        P.add(dv, lambda e: e.memset(Pw.a[:, 0, 0, :], 1.0), writes=[Pw])
        P.add(dv, lambda e: e.memset(Pw.a[:, 0, 1, :], 0.0), writes=[Pw])
        ar, ai = Pw.a[:, 1, 0, :], Pw.a[:, 1, 1, :]
        nr = s1
        self.ts(dv, nr, nr.a, Pw, ar, -1.0, None, ALU.add)
        self.tt(dv, tA, tA.a, svr, lre, svr, lre, ALU.mult)
        self.tt(dv, tB, tB.a, svr, lim, svr, lim, ALU.mult)
        self.tt(dv, den, den.a, tA, tA.a, tB, tB.a, ALU.add)
        P.add(dv, lambda e: e.reciprocal(out=den.a, in_=den.a), reads=[den], writes=[den])
        self.tt(dv, tA, tA.a, nr, nr.a, svr, lre, ALU.mult)
        self.tt(dv, tB, tB.a, Pw, ai, svr, lim, ALU.mult)
        self.tt(dv, tA, tA.a, tA, tA.a, tB, tB.a, ALU.add)
        self.tt(dv, a.fc, a.fc.a[:, 0, :], tA, tA.a, den, den.a, ALU.mult)
        self.tt(dv, tA, tA.a, Pw, ai, svr, lre, ALU.mult)
        self.tt(dv, tB, tB.a, nr, nr.a, svr, lim, ALU.mult)
        self.tt(dv, tA, tA.a, tA, tA.a, tB, tB.a, ALU.subtract)
        self.tt(dv, a.fc, a.fc.a[:, 1, :], tA, tA.a, den, den.a, ALU.mult)
        for k in range(2, 9):
            pr, pi = Pw.a[:, k - 1, 0, :], Pw.a[:, k - 1, 1, :]
            self.tt(dv, tA, tA.a, Pw, pr, Pw, ar, ALU.mult)
            self.tt(dv, tB, tB.a, Pw, pi, Pw, ai, ALU.mult)
            self.tt(dv, Pw, Pw.a[:, k, 0, :], tA, tA.a, tB, tB.a, ALU.subtract)
            self.tt(dv, tA, tA.a, Pw, pr, Pw, ai, ALU.mult)
            self.tt(dv, tB, tB.a, Pw, pi, Pw, ar, ALU.mult)
            self.tt(dv, Pw, Pw.a[:, k, 1, :], tA, tA.a, tB, tB.a, ALU.add)
        self.ts(dv, a.nAi, a.nAi.a, Pw, Pw.a[:, 8, 1, :], -1.0, None, ALU.mult)

        def bc32(ap16):
            return ap16.unsqueeze(2).broadcast_to([128, 16, 32])

        Cr, Ci, Br, Bi = a.m.a[:, 0], a.m.a[:, 1], a.m.a[:, 2], a.m.a[:, 3]
        fre, fim = bc32(a.fc.a[:, 0, :]), bc32(a.fc.a[:, 1, :])
        td, tp = a.td, a.tp
        for k in range(9):
            self.ts(dv, a.nPr, a.nPr.a[:, k, :], Pw, Pw.a[:, k, 0, :], -1.0, None, ALU.mult)
        pl = 'pool'
        self.tt(pl, td[0], td[0].a, a.m, Br, a.fc, fre, ALU.mult)
        self.tt(pl, td[1], td[1].a, a.m, Bi, a.fc, fim, ALU.mult)
        self.tt(pl, a.Bb, a.Bb.a[:, 0], td[0], td[0].a, td[1], td[1].a, ALU.subtract)
        self.tt(pl, td[2], td[2].a, a.m, Bi, a.fc, fre, ALU.mult)
        self.tt(pl, td[3], td[3].a, a.m, Br, a.fc, fim, ALU.mult)
        self.tt(pl, a.Bb, a.Bb.a[:, 1], td[2], td[2].a, td[3], td[3].a, ALU.add)
        self.cp('pool', a.BbQb, a.BbQb.a[:, 0], a.Bb, a.Bb.a[:, 0])
        self.cp('pool', a.BbQb, a.BbQb.a[:, 1], a.Bb, a.Bb.a[:, 1])
        for k in range(9):
            pr, pi = bc32(Pw.a[:, k, 0, :]), bc32(Pw.a[:, k, 1, :])
            npr = bc32(a.nPr.a[:, k, :])
            self.tt(pl, td[0], td[0].a, a.m, Cr, Pw, pr, ALU.mult)
            self.tt(pl, td[1], td[1].a, a.m, Ci, Pw, pi, ALU.mult)
            self.tt(pl, a.PCb, a.PCb.a[:, k, 0], td[0], td[0].a, td[1], td[1].a, ALU.subtract)
            self.tt(pl, td[2], td[2].a, a.m, Ci, a.nPr, npr, ALU.mult)
            self.tt(pl, td[3], td[3].a, a.m, Cr, Pw, pi, ALU.mult)
            self.tt(pl, a.PCb, a.PCb.a[:, k, 1], td[2], td[2].a, td[3], td[3].a, ALU.subtract)
        for j in range(8):
            pr, pi = bc32(Pw.a[:, 7 - j, 0, :]), bc32(Pw.a[:, 7 - j, 1, :])
            self.tt(pl, tp[0], tp[0].a, a.Bb, a.Bb.a[:, 0], Pw, pr, ALU.mult)
            self.tt(pl, tp[1], tp[1].a, a.Bb, a.Bb.a[:, 1], Pw, pi, ALU.mult)
            v4 = lambda r: r.a.rearrange("p (t m) c -> p t m c", t=4)
            self.tt(pl, a.PBb, a.PBb.a[:, :, j, 0, :, :], tp[0], v4(tp[0]), tp[1], v4(tp[1]), ALU.subtract)
            self.tt(pl, tp[2], tp[2].a, a.Bb, a.Bb.a[:, 1], Pw, pr, ALU.mult)
            self.tt(pl, tp[3], tp[3].a, a.Bb, a.Bb.a[:, 0], Pw, pi, ALU.mult)
            self.tt(pl, a.PBb, a.PBb.a[:, :, j, 1, :, :], tp[2], v4(tp[2]), tp[3], v4(tp[3]), ALU.add)

    def m3(self, l):
        P, a, d, psb = self.P, self.A3, self.d, self.psb
        T, NCH, CPS, NSEQ = self.T, self.NCH, self.CPS, self.NSEQ
        Pw = a.Pw
        dv = 'dve'
        for Tt in range(4):
            self.dma('sp', a.u, a.u.a[:, Tt, :], self.dr["uT"], d["uT"][Tt * 128:(Tt + 1) * 128, :])
        nb = 0
        for Tt in range(4):
            for jh in range(2):
                ps = psb[nb % 2]
                nb += 1
                psv = ps.a.bitcast(BF16)
                for jj in range(4):
                    for ri in range(2):
                        j = jh * 4 + jj
                        idx = jj * 2 + ri
                        P.add('pe', lambda e, psv=psv, idx=idx, Tt=Tt, j=j, ri=ri: e.transpose(
                            psv[:, idx * 128:(idx + 1) * 128], a.PBb.a[:, Tt, j, ri, :, :].rearrange("p m c -> p (m c)"),
                            self.ident.a), reads=[a.PBb, self.ident], writes=[ps])
                self.cp('dve' if nb % 2 else 'act', a.Wp,
                        a.Wp.a[:, Tt, jh * 4:jh * 4 + 4, :, :].rearrange("p j r c -> p (j r c)"),
                        ps, psv) if nb % 2 else \
                    self.act(a.Wp, a.Wp.a[:, Tt, jh * 4:jh * 4 + 4, :, :].rearrange("p j r c -> p (j r c)"),
                             ps, psv, AF.Copy)
        fl = lambda ap: ap.rearrange("p m c -> p (m c)")
        for Tt in range(4):
            for kh in range(2):
                ps = psb[2 + (nb % 2)]
                nb += 1
                for kk in range(4):
                    k = kh * 4 + kk
                    oa = ps.a[:, kk * 128:(kk + 1) * 128]
                    for ri in range(2):
                        self.mm(ps, oa, a.BbQb, fl(a.BbQb.a[:, ri, 4 * Tt:4 * Tt + 4, :]),
                                a.PCb, fl(a.PCb.a[:, k, ri, 4 * Tt:4 * Tt + 4, :]), ri == 0, ri == 1)
                self.tt('dve', a.Kb, a.Kb.a[:, Tt, kh * 4:kh * 4 + 4, :], ps,
                        ps.a.rearrange("p (k c) -> p k c", k=4), self.bmask,
                        self.bmask.a.unsqueeze(1).broadcast_to([128, 4, 128]), ALU.mult)
            self.stt(a.Kb, a.Kb.a[:, Tt, 0, :], self.ident, self.ident.a, self.vec(l, DSK + Tt), a.Kb,
                     a.Kb.a[:, Tt, 0, :], ALU.mult, ALU.add, extra_reads=[self.vecs])
        for gp in range(16):
            Tt, m = gp // 4, gp % 4
            for ri in range(2):
                ps = psb[4 + (nb % 4)]
                nb += 1
                for j in range(8):
                    rhs = a.u.a[32 * m:32 * m + 32, Tt, :].rearrange("p (c j) -> p j c", j=8)[:, j, :]
                    self.mm(ps, ps.a[:, 0:NCH], a.Wp, a.Wp.a[32 * m:32 * m + 32, Tt, j, ri, :], a.u, rhs,
                            j == 0, j == 7, tp=(32 * m, 0))
                if nb % 2:
                    self.cp('dve', a.Sb, a.Sb.a[:, gp, ri, :], ps, ps.a[:, 0:NCH])
                else:
                    self.act(a.Sb, a.Sb.a[:, gp, ri, :], ps, ps.a[:, 0:NCH], AF.Copy)
        Sv = a.Sb.a.rearrange("p g r (s c) -> p g r s c", s=NSEQ)
        Xv = a.Xb.a.rearrange("p g r (s c) -> p g r s c", s=NSEQ)
        Ar4 = Pw.a[:, 8, 0, :].unsqueeze(2).unsqueeze(3).broadcast_to([128, 16, 2, NSEQ])
        Ai3 = Pw.a[:, 8, 1, :].unsqueeze(2).broadcast_to([128, 16, NSEQ])
        nAi3 = a.nAi.a.unsqueeze(2).broadcast_to([128, 16, NSEQ])
        P.add(dv, lambda e: e.memset(a.xs[0].a, 0.0), writes=[a.xs[0]])
        for c in range(CPS):
            xc, xn = a.xs[c % 2], a.xs[(c + 1) % 2]
            self.act(a.Xb, Xv[:, :, :, :, c], xc, xc.a, AF.Copy)
            if c == CPS - 1:
                break
            self.tt(dv, a.t1, a.t1.a, xc, xc.a, Pw, Ar4, ALU.mult)
            self.tt(dv, a.t2, a.t2.a[:, :, 0, :], xc, xc.a[:, :, 1, :], a.nAi, nAi3, ALU.mult)
            self.tt(dv, a.t2, a.t2.a[:, :, 1, :], xc, xc.a[:, :, 0, :], Pw, Ai3, ALU.mult)
            self.tt(dv, a.t1, a.t1.a, a.t1, a.t1.a, a.t2, a.t2.a, ALU.add)
            self.tt(dv, xn, xn.a, a.t1, a.t1.a, a.Sb, Sv[:, :, :, :, c], ALU.add)
        for Tt in range(4):
            for j in range(8):
                ps = psb[nb % 4]
                nb += 1
                for k in range(j + 1):
                    rhs = a.u.a[:, Tt, :].rearrange("p (c j) -> p j c", j=8)[:, j - k, :]
                    self.mm(ps, ps.a[:, 0:NCH], a.Kb, a.Kb.a[:, Tt, k, :], a.u, rhs, k == 0, False)
                for m in range(4):
                    gp = 4 * Tt + m
                    for ri in range(2):
                        self.mm(ps, ps.a[32 * m:32 * m + 32, 0:NCH], a.PCb, a.PCb.a[:, j + 1, ri, gp, :],
                                a.Xb, a.Xb.a[:, gp, ri, :], False, (m == 3 and ri == 1), tp=(0, 32 * m))
                self.act(a.ys, a.ys.a.rearrange("p (c j) -> p j c", j=8)[:, j, :], ps, ps.a[:, 0:NCH],
                         AF.Gelu_apprx_tanh)
            self.dma('sp', self.dr["ysT"], d["ysT"][Tt * 128:(Tt + 1) * 128, :], a.ys, a.ys.a)

    def post_norm_residual(self, l, a, X, goff, dst, t0, square_done=False):
        psb, d = self.psb, self.d
        if not square_done:
            self.act(a.ysq, a.ysq.a, a.Y, a.Y.a, AF.Square)
        self.rms_rstd(psb[7], a.ysq, lambda c: a.ysq.a[:, c, :], 8, 1024.0, a.rt, a.rstd)
        for oc in range(8):
            t = a.tt[oc % 2]
            self.tt('dve', t, t.a, a.Y, a.Y.a[:, oc, :], a.rstd, a.rstd.a, ALU.mult)
            self.stt(X, X.a[:, oc, :], t, t.a, self.vec(l, goff + oc), X, X.a[:, oc, :], ALU.mult, ALU.add,
                     extra_reads=[self.vecs])
        self.dma('sp', self.dr[dst], d[dst].rearrange("(c p) t -> p c t", p=128)[:, :, t0:t0 + 512], X, X.a)

    def m4(self, l, src, dst):
        P, a, d, psb = self.P, self.A4, self.d, self.psb
        self.load_w(a.woatt, d["w_oatt"][l], 4, 1024, a.stg)
        self.load_w(a.wglu, d["w_glu"][l], 4, 2048, a.stg)
        self.load_w(a.wout, d["w_out"][l], 8, 1024, a.stg)
        xin = d[src].rearrange("(c p) t -> p c t", p=128)
        oin = d["oT"].rearrange("(c p) t -> p c t", p=128)
        yin = d["ysT"].rearrange("(c p) t -> p c t", p=128)
        gin = d["gT"].rearrange("(c p) t -> p c t", p=128)

        def load(tt):
            sl = slice(tt * 512, (tt + 1) * 512)
            self.dma('sp', a.o[tt % 2], a.o[tt % 2].a, self.dr["oT"], oin[:, :, sl])
            self.dma('sp', a.ysb[tt % 2], a.ysb[tt % 2].a, self.dr["ysT"], yin[:, :, sl])
            self.dma('sp', a.g[tt % 2], a.g[tt % 2].a, self.dr["gT"], gin[:, :, sl])
            self.dma('sp', a.X[tt % 2], a.X[tt % 2].a, self.dr[src], xin[:, :, sl])

        nbc = [0]

        def stageA(tt):
            o, ysb, g = a.o[tt % 2], a.ysb[tt % 2], a.g[tt % 2]
            for oc in range(8):
                nb = nbc[0]
                pa, pga, pgb = psb[(3 * nb) % 6], psb[(3 * nb + 1) % 6], psb[(3 * nb + 2) % 6]
                nbc[0] += 1
                sg, yss, m1, m2 = a.sg[oc % 2], a.yss[oc % 2], a.m1[oc % 2], a.m2[oc % 2]
                cs = slice(oc * 128, (oc + 1) * 128)
                for k in range(4):
                    self.mm(pa, pa.a, a.woatt, a.woatt.a[:, k, cs], o, o.a[:, k, :], k == 0, k == 3)
                for k in range(4):
                    self.mm(pga, pga.a, a.wglu, a.wglu.a[:, k, cs], ysb, ysb.a[:, k, :], k == 0, k == 3)
                for k in range(4):
                    self.mm(pgb, pgb.a, a.wglu, a.wglu.a[:, k, 1024 + oc * 128:1024 + (oc + 1) * 128], ysb,
                            ysb.a[:, k, :], k == 0, k == 3)
                self.act(sg, sg.a, pgb, pgb.a, AF.Sigmoid)
                self.tt('dve', yss, yss.a, pga, pga.a, sg, sg.a, ALU.mult)
                self.tt('dve', m1, m1.a, pa, pa.a, g, g.a[:, oc, :], ALU.mult)
                self.tt('dve', m2, m2.a, yss, yss.a, g, g.a[:, 8 + oc, :], ALU.mult)
                self.tt('dve', a.mg, a.mg.a[:, oc, :], m1, m1.a, m2, m2.a, ALU.add)

        def stageB(tt):
            for oc in range(8):
                py = psb[(3 * nbc[0]) % 6]
                nbc[0] += 1
                for k in range(8):
                    self.mm(py, py.a, a.wout, a.wout.a[:, k, oc * 128:(oc + 1) * 128], a.mg, a.mg.a[:, k, :],
                            k == 0, k == 7)
                self.act(a.Y, a.Y.a[:, oc, :], py, py.a, AF.Copy)
            self.act(a.ysq, a.ysq.a, a.Y, a.Y.a, AF.Square)

        def stageC(tt):
            self.post_norm_residual(l, a, a.X[tt % 2], GPOST, dst, tt * 512, square_done=True)

        NT = self.NT
        load(0)
        if NT > 1:
            load(1)
        stageA(0)
        stageB(0)
        for tt in range(1, NT):
            stageA(tt)
            stageC(tt - 1)
            if tt + 1 < NT:
                load(tt + 1)
            stageB(tt)
        stageC(NT - 1)

    def f1(self, l, src):
        P, a, d, psb = self.P, self.A5, self.d, self.psb
        self.load_w(a.wup, d["w_up"][l], 8, 2 * D_FF, a.stg)
        for fc in range(NFC):
            for k in range(3):
                self.ts('dve', a.dgall, a.dgall.a[:, fc, k, :], self.ident, self.ident.a,
                        self.vec(l, CW + fc * 3 + k), None, ALU.mult, extra_reads=[self.vecs])
        xin = d[src].rearrange("(c p) t -> p c t", p=128)
        aout = d["actT"].rearrange("(c p) t -> p c t", p=128)

        def load(tt):
            self.dma('sp', a.X[tt % 2], a.X[tt % 2].a, self.dr[src], xin[:, :, tt * 512:(tt + 1) * 512])

        load(0)
        gi = 0
        for tt in range(self.NT):
            if tt + 1 < self.NT:
                load(tt + 1)
            X = a.X[tt % 2]
            t0 = tt * 512
            rs = a.rstdF.a[:, t0:t0 + 512]
            self.act(a.sqg, a.sqg.a, X, X.a, AF.Square)
            for c in range(8):
                self.mm(psb[0], psb[0].a, self.ones, self.ones.a, a.sqg, a.sqg.a[:, c, :], c == 0, c == 7)
            self.act(a.rt, a.rt.a, psb[0], psb[0].a, AF.Sqrt, bias=self.epst.a[:, 0:1], scale=1.0 / 1024.0,
                     extra_reads=[self.epst])
            P.add('dve', lambda e, rs=rs: e.reciprocal(out=rs, in_=a.rt.a), reads=[a.rt], writes=[a.rstdF])
            for dc in range(8):
                self.ts('dve', a.sqg, a.sqg.a[:, dc, :], X, X.a[:, dc, :], self.vec(l, GFPRE + dc), None, ALU.mult,
                        extra_reads=[self.vecs])
            seq_start = (tt % self.TPS == 0)
            pend = None

            def conv_stage(fc, Gb, psv, ge_i):
                psc = psb[5 + fc % 2]
                for k in range(3):
                    self.mm(psc, psc.a, a.dgall, a.dgall.a[:, fc, k, :], Gb, Gb.a[:, k:k + 512], k == 0, k == 2)
                ge = a.ge[ge_i % 3]
                self.act(ge, ge.a, psc, psc.a, AF.Gelu_apprx_tanh, bias=self.vec(l, CB + fc),
                         extra_reads=[self.vecs])
                ar = a.act[fc // (NFC // 2)]
                self.tt('dve', ar, ar.a[:, fc % (NFC // 2), :], psv, psv.a, ge, ge.a, ALU.mult)
                if fc % (NFC // 2) == NFC // 2 - 1:
                    hh = fc // (NFC // 2)
                    self.dma('sp', self.dr["actT"], aout[:, hh * 11:(hh + 1) * 11, t0:t0 + 512], ar, ar.a)

            for fc in range(NFC):
                psg, psv = psb[1 + fc % 2], psb[3 + fc % 2]
                Gb = a.Gb[gi % 3]
                gi += 1
                for dc in range(8):
                    self.mm(psg, psg.a, self.wr(a.wup, fc * 256, fc * 256 + 128),
                            a.wup.a[:, dc, fc * 256:fc * 256 + 128], a.sqg, a.sqg.a[:, dc, :], dc == 0, dc == 7)
                for dc in range(8):
                    self.mm(psv, psv.a, self.wr(a.wup, fc * 256 + 128, fc * 256 + 256),
                            a.wup.a[:, dc, fc * 256 + 128:fc * 256 + 256], a.sqg, a.sqg.a[:, dc, :], dc == 0, dc == 7)
                if seq_start:
                    P.add('pool', lambda e, Gb=Gb: e.memset(Gb.a[:, 0:2], 0.0), writes=[Gb])
                else:
                    self.act(Gb, Gb.a[:, 0:2], a.Gh, a.Gh.a[:, fc, :], AF.Copy)
                self.tt('dve', Gb, Gb.a[:, 2:514], psg, psg.a, a.rstdF, rs, ALU.mult)
                self.act(a.Gh, a.Gh.a[:, fc, :], Gb, Gb.a[:, 512:514], AF.Copy)
                if pend is not None:
                    conv_stage(*pend)
                pend = (fc, Gb, psv, gi)
            conv_stage(*pend)

    def f2(self, l, src, dst):
        P, a, d, psb = self.P, self.A6, self.d, self.psb
        self.load_w(a.wdown, d["w_down"][l], NFC, 1024, a.stg)
        xin = d[src].rearrange("(c p) t -> p c t", p=128)
        ain = d["actT"].rearrange("(c p) t -> p c t", p=128)

        def load(tt):
            sl = slice(tt * 512, (tt + 1) * 512)
            self.dma('sp', a.a[tt % 2], a.a[tt % 2].a, self.dr["actT"], ain[:, :, sl])
            self.dma('sp', a.X[tt % 2], a.X[tt % 2].a, self.dr[src], xin[:, :, sl])

        load(0)
        nb = 0
        for tt in range(self.NT):
            if tt + 1 < self.NT:
                load(tt + 1)
            A_, X = a.a[tt % 2], a.X[tt % 2]
            rs = a.rstdF.a[:, tt * 512:(tt + 1) * 512]
            for oc in range(8):
                py = psb[nb % 6]
                nb += 1
                for k in range(NFC):
                    self.mm(py, py.a, self.wr(a.wdown, oc * 128, (oc + 1) * 128), a.wdown.a[:, k, oc * 128:(oc + 1) * 128], A_, A_.a[:, k, :],
                            k == 0, k == NFC - 1)
                self.tt('dve', a.Y, a.Y.a[:, oc, :], py, py.a, a.rstdF, rs, ALU.mult)
            self.post_norm_residual(l, a, X, GFPOST, dst, tt * 512)


def _rope_tables(seq):
    pos = np.arange(seq, dtype=np.float32)
    inv_freq = (np.float32(10000.0) ** (-np.arange(0, 32, 2, dtype=np.float32) / np.float32(32))).astype(np.float32)
    ang = (pos[:, None] * inv_freq[None, :]).astype(np.float32)
    cos, sin = np.cos(ang).astype(np.float32), np.sin(ang).astype(np.float32)
    tab = np.zeros((2, 128, seq), np.float32)
    tab[0, 64:80] = cos.T
    tab[0, 80:96] = cos.T
    tab[1, 64:80] = -sin.T
    tab[1, 80:96] = sin.T
    return tab


def _interleave_up(w):
    L = w.shape[0]
    g = w[:, :, :D_FF].reshape(L, 1024, NFC, 128)
    v = w[:, :, D_FF:].reshape(L, 1024, NFC, 128)
    return np.ascontiguousarray(np.stack([g, v], axis=3).reshape(L, 1024, 2 * D_FF))


def prep_weights(inp, depth, seq):
    f = lambda a: np.ascontiguousarray(np.asarray(a, dtype=np.float32))
    L = depth
    w_in = f(inp["w_in"])[:L]
    o1, o2, o3, o4 = 384, 640, 672, 1184
    kpe = w_in[:, :, o2:o3]
    kpe_sw = np.concatenate([kpe[:, :, 16:32], kpe[:, :, 0:16]], axis=2)
    w_inx = np.concatenate([w_in[:, :, 0:o2], kpe, kpe_sw, w_in[:, :, o3:o4], w_in[:, :, o4:]], axis=2)
    assert w_inx.shape[2] == WINX
    w_uq = f(inp["w_uq"])[:L]
    wq = w_uq.reshape(L, 384, 8, 96)
    w_uqsw = np.concatenate([wq[..., 80:96], wq[..., 64:80]], axis=3).reshape(L, 384, 256)
    w_ukv = f(inp["w_ukv"])[:L].reshape(L, 256, 8, 128)
    w_uk = w_ukv[..., 0:64].reshape(L, 256, 512)
    w_uv = w_ukv[..., 64:128].reshape(L, 256, 512)
    vecs = np.zeros((128, L, NV), np.float32)

    def put(off, arr, n):
        vecs[:, :, off:off + n] = f(arr)[:L].reshape(L, n, 128).transpose(2, 0, 1)

    put(GPRE, inp["g_mix_pre"], 8)
    put(BG, inp["b_gate"], 16)
    put(GQ, inp["g_q"], 3)
    put(GKV, inp["g_kv"], 2)
    put(GPOST, inp["g_mix_post"], 8)
    put(GFPRE, inp["g_ffn_pre"], 8)
    put(GFPOST, inp["g_ffn_post"], 8)
    cw = f(inp["conv_w"])[:L].reshape(L, 3, NFC, 128).transpose(3, 0, 2, 1)
    vecs[:, :, CW:CW + 66] = cw.reshape(128, L, 66)
    put(CB, inp["conv_b"], NFC)
    put(DSK, inp["d_skip"], 4)
    s5v = np.zeros((128, L, 48), np.float32)

    def gl(arr):
        return f(arr)[:L].reshape(L, 16, 2, 64).transpose(2, 3, 0, 1).reshape(128, L, 16)

    s5v[:, :, 0:16] = gl(inp["lam_re"])
    s5v[:, :, 16:32] = gl(inp["lam_im"])
    ls = f(inp["log_step"])[:L].reshape(L, 16, 2)
    s5v[:, :, 32:48] = np.repeat(ls.transpose(2, 0, 1)[:, None], 64, axis=1).reshape(128, L, 16)
    s5m = np.zeros((L, 2, 64, 4, 16, 2, 16), np.float32)
    cr = f(inp["c_re"])[:L].reshape(L, 16, 2, 16, 64)
    ci = f(inp["c_im"])[:L].reshape(L, 16, 2, 16, 64)
    br = f(inp["b_re"])[:L].reshape(L, 16, 2, 64, 16)
    bi = f(inp["b_im"])[:L].reshape(L, 16, 2, 64, 16)
    for g2 in range(2):
        s5m[:, g2, :, 0, :, g2, :] = cr[:, :, g2].transpose(0, 3, 1, 2)
        s5m[:, g2, :, 1, :, g2, :] = ci[:, :, g2].transpose(0, 3, 1, 2)
        s5m[:, g2, :, 2, :, g2, :] = br[:, :, g2].transpose(0, 2, 1, 3)
        s5m[:, g2, :, 3, :, g2, :] = bi[:, :, g2].transpose(0, 2, 1, 3)
    return {
        "w_inx": np.ascontiguousarray(w_inx),
        "w_uq": w_uq, "w_uqsw": np.ascontiguousarray(w_uqsw),
        "w_uk": np.ascontiguousarray(w_uk), "w_uv": np.ascontiguousarray(w_uv),
        "w_oatt": f(inp["w_o_att"])[:L], "w_glu": f(inp["w_glu"])[:L], "w_out": f(inp["w_out"])[:L],
        "w_up": _interleave_up(f(inp["w_up"])[:L]), "w_down": f(inp["w_down"])[:L],
        "vecs": np.ascontiguousarray(vecs.reshape(128, L * NV)),
        "s5v": np.ascontiguousarray(s5v.reshape(128, L * 48)),
        "s5m": np.ascontiguousarray(s5m.reshape(L, 128, 2048)),
        "rope": _rope_tables(seq),
        "ident": np.eye(128, dtype=np.float32),
        "bmask": np.kron(np.eye(4, dtype=np.float32), np.ones((32, 32), np.float32)),
    }


_CACHE = {}


def kernel(**inputs):
    x = np.asarray(inputs["x"], dtype=np.float32)
    B, S, D = x.shape
    ncores = 8
    nseq = B // ncores
    depth = int(np.asarray(inputs["w_in"]).shape[0])
    key = (nseq, S, depth)
    if key not in _CACHE:
        _CACHE[key] = K(nseq=nseq, seq=S, depth=depth).build()
    nc = _CACHE[key]
    wts = prep_weights(inputs, depth, S)
    in_maps = []
    for c in range(ncores):
        xs = x[c * nseq:(c + 1) * nseq].reshape(nseq * S, D)
        m = dict(wts)
        m["xT"] = np.ascontiguousarray(xs.T)
        in_maps.append(m)
    res = run_bass_kernel_spmd(nc, in_maps, core_ids=list(range(ncores)))
    out = np.empty((B, S, D), np.float32)
    for c in range(ncores):
        yT = np.asarray(res.results[c]["yT"], dtype=np.float32)
        out[c * nseq:(c + 1) * nseq] = yT.T.reshape(nseq, S, D)
    return out
```

```python
import contextlib
import math
import numpy as np
import ml_dtypes
import concourse.bass as bass
import concourse.mybir as mybir
from concourse.bass_utils import run_bass_kernel_spmd
from concourse.alu_op_type import AluOpType as ALU

AF = mybir.ActivationFunctionType
F32 = mybir.dt.float32
BF16 = mybir.dt.bfloat16
I32 = mybir.dt.int32
ENGS = ['sp', 'act', 'pool', 'dve', 'pe']

D_MODEL = 1024
N_HEADS = 8
D_FF = 2816
NFC = 22
EPS = 1e-6
NV = 145
GPRE, BG, GQ, GKV, GPOST, GFPRE, GFPOST, CW, CB, DSK = 0, 8, 24, 27, 29, 37, 45, 53, 119, 141
WINX = 3264
ARENA_ELEMS = 196 * 512


class Res:
    def __init__(self, name, dsem=None, multi=False):
        self.name = name
        self.writers = {}
        self.readers = {}
        self.dsem = dsem
        self.nw = 0
        self.multi = multi
        self.a = None


class Op:
    __slots__ = ('eng', 'fn', 'deps', 'needed', 'sig', 'dma', 'key')


class Prog:
    def __init__(self, nc, st):
        self.nc = nc
        self.st = st
        self.ops = []
        self.esem = {e: st.enter_context(nc.semaphore("s_" + e)) for e in ENGS}
        self.allsems = list(self.esem.values())
        self.nsem = len(ENGS)
        self.last = {e: None for e in ENGS}
        self.dma_res = []
        self.bar_deps = []

    def res(self, name, dma=False, multi=False):
        dsem = None
        if dma:
            dsem = self.st.enter_context(self.nc.semaphore("d_" + name))
            self.allsems.append(dsem)
            self.nsem += 1
        r = Res(name, dsem, multi)
        if dma:
            self.dma_res.append(r)
            r.last_dma = None
        return r

    def sb(self, name, shape, dtype, dma=False, multi=False):
        r = self.res(name, dma=dma, multi=multi)
        t = self.st.enter_context(self.nc.sbuf_tensor(name, shape, dtype))
        r.a = t[:]
        return r

    def ps(self, name, shape, dtype):
        r = self.res(name)
        t = self.st.enter_context(self.nc.psum_tensor(name, shape, dtype))
        r.a = t[:]
        return r

    def barrier(self):
        deps = [o for o in self.last.values() if o is not None]
        for r in self.dma_res:
            if r.last_dma is not None:
                deps.append(r.last_dma)
        self.bar_deps = deps

    def add(self, eng, fn, reads=(), writes=(), dma=False):
        op = Op()
        op.eng = eng
        op.fn = fn
        op.dma = dma
        op.needed = False
        op.sig = None
        deps = {}
        for o in self.bar_deps:
            deps[id(o)] = o
        for r in reads:
            for o in r.writers.values():
                deps[id(o)] = o
        for w in writes:
            for o in w.writers.values():
                deps[id(o)] = o
            for o in w.readers.values():
                deps[id(o)] = o
        if dma:
            dres = [w for w in writes if w.dsem is not None]
            assert len(dres) == 1, [w.name for w in writes]
            dres[0].nw += 1
            op.sig = (dres[0].dsem, 16 * dres[0].nw)
            dres[0].last_dma = op
            key = id(dres[0].dsem)
        else:
            key = eng
            self.last[eng] = op
        op.key = key
        op.deps = [o for o in deps.values()
                   if not (o.eng == 'pe' and eng == 'pe' and not o.dma and not dma)]
        for o in op.deps:
            o.needed = True
        for r in reads:
            r.readers[key] = op
        for w in writes:
            if w.multi:
                w.writers[key] = op
            else:
                w.writers = {key: op}
                w.readers = {}
        self.ops.append(op)
        return op

    def finish(self, final_res):
        nc = self.nc
        self.add('sp', None, reads=final_res)
        cnt = {e: 0 for e in ENGS}
        for op in self.ops:
            if not op.dma and op.needed:
                cnt[op.eng] += 1
                op.sig = (self.esem[op.eng], cnt[op.eng])
        per = {e: [o for o in self.ops if o.eng == e] for e in ENGS}
        self.stats = {e: len(per[e]) for e in ENGS}

        def mk(e):
            def body(engobj):
                waited = {}
                for op in per[e]:
                    need = {}
                    for d in op.deps:
                        s, v = d.sig
                        k = id(s)
                        if v > need.get(k, (None, 0))[1]:
                            need[k] = (s, v)
                    for k, (s, v) in need.items():
                        if v > waited.get(k, 0):
                            engobj.wait_ge(s, v)
                            waited[k] = v
                    if op.fn is None:
                        continue
                    ins = op.fn(engobj)
                    if op.dma:
                        ins.then_inc(op.sig[0], 16)
                    elif op.needed:
                        ins.then_inc(op.sig[0], 1)
            return body

        import os
        if os.environ.get("NOCLEAR") != "1":
            for sm in self.allsems:
                nc.gpsimd.sem_clear(sm)
            nc.all_engine_barrier()
        with nc.Block() as block:
            block.sync(mk('sp'))
            block.scalar(mk('act'))
            block.gpsimd(mk('pool'))
            block.vector(mk('dve'))
            block.tensor(mk('pe'))


class Arena:
    def __init__(self, K, name):
        self.K = K
        self.name = name
        self.off = 0

    def alloc(self, name, shape, dtype, dma=False, multi=False, at=None):
        n = 1
        for s in shape:
            n *= s
        nel = n * (2 if dtype in (F32, I32) else 1)
        nel = (nel + 15) // 16 * 16
        off = self.off if at is None else at
        if at is None:
            self.off += nel
        assert off + nel <= ARENA_ELEMS, (self.name, name, off + nel, ARENA_ELEMS)
        v = self.K.arena_t[:, off:off + n * (2 if dtype in (F32, I32) else 1)]
        if dtype in (F32, I32):
            v = v.bitcast(dtype)
        if len(shape) >= 2:
            names = "abcdefg"[:len(shape)]
            pat = "p (" + " ".join(names) + ") -> p " + " ".join(names)
            v = v.rearrange(pat, **{n: sz for n, sz in zip(names[:-1], shape[:-1])})
        r = self.K.P.res(self.name + "_" + name, dma=dma, multi=multi)
        r.a = v
        r.off = off
        return r


class NS:
    pass


class K:
    def __init__(self, nseq=2, seq=2048, depth=4, dump=False, phases=None):
        self.phases = phases
        self.NSEQ = nseq
        self.S = seq
        self.L = depth
        self.T = nseq * seq
        self.NT = self.T // 512
        self.TPS = seq // 512
        self.NCH = self.T // 8
        self.CPS = seq // 8
        self.dump = dump
        self.wl_i = 0
        assert self.NCH <= 512

    def build(self):
        nc = bass.Bass("TRN2", target_bir_lowering=False)
        self.nc = nc
        L, T, S = self.L, self.T, self.S
        d = {}

        def inp(name, shape):
            d[name] = nc.dram_tensor(name, shape, F32, kind="ExternalInput").ap()

        inp("xT", [1024, T])
        inp("w_inx", [L, 1024, WINX])
        inp("w_uq", [L, 384, 768])
        inp("w_uqsw", [L, 384, 256])
        inp("w_uk", [L, 256, 512])
        inp("w_uv", [L, 256, 512])
        inp("w_oatt", [L, 512, 1024])
        inp("w_glu", [L, 512, 2048])
        inp("w_out", [L, 1024, 1024])
        inp("w_up", [L, 1024, 2 * D_FF])
        inp("w_down", [L, D_FF, 1024])
        inp("vecs", [128, L * NV])
        inp("s5v", [128, L * 48])
        inp("s5m", [L, 128, 2048])
        inp("rope", [2, 128, S])
        inp("ident", [128, 128])
        inp("bmask", [128, 128])
        d["yT"] = nc.dram_tensor("yT", [1024, T], F32, kind="ExternalOutput").ap()
        skind = "ExternalOutput" if self.dump else "Internal"

        def scr(name, shape, dt):
            d[name] = nc.dram_tensor(name, shape, dt, kind=skind).ap()

        scr("s1", [1024, T], F32)
        scr("s2", [1024, T], F32)
        scr("qT", [8, 96, T], BF16)
        scr("kT", [8, 96, T], BF16)
        scr("vA", [T, 768], BF16)
        scr("uT", [512, T], BF16)
        scr("gT", [2048, T], BF16)
        scr("oT", [512, T], BF16)
        scr("ysT", [512, T], BF16)
        scr("actT", [D_FF, T], BF16)
        self.d = d
        with contextlib.ExitStack() as st:
            P = Prog(nc, st)
            self.P = P
            self.dr = {n: P.res("dr_" + n, dma=True, multi=True) for n in
                       ["yT", "s1", "s2", "qT", "kT", "vA", "uT", "gT", "oT", "ysT", "actT"]}
            self.dr["xT"] = P.res("dr_xT")
            self.arena_t = st.enter_context(nc.sbuf_tensor("arena", [128, ARENA_ELEMS], BF16))
            self.psb = [P.ps("psb%d" % i, [128, 512], F32) for i in range(8)]
            self.vecs = P.sb("vecs_sb", [128, L * NV], F32, dma=True)
            self.s5v = P.sb("s5v_sb", [128, L * 48], F32, dma=True)
            self.ident = P.sb("identb", [128, 128], BF16, dma=True)
            self.ones = P.sb("onesb", [128, 128], BF16)
            self.bmask = P.sb("bmask_sb", [128, 128], F32, dma=True)
            P.add('sp', lambda e: e.dma_start(out=self.bmask.a, in_=d["bmask"]), writes=[self.bmask], dma=True)
            self.epst = P.sb("epst", [128, 1], F32)
            P.add('sp', lambda e: e.dma_start(out=self.vecs.a, in_=d["vecs"]), writes=[self.vecs], dma=True)
            P.add('sp', lambda e: e.dma_start(out=self.s5v.a, in_=d["s5v"]), writes=[self.s5v], dma=True)
            P.add('pool', lambda e: e.dma_start(out=self.ident.a, in_=d["ident"]), writes=[self.ident], dma=True)
            P.add('dve', lambda e: e.memset(self.ones.a, 1.0), writes=[self.ones])
            P.add('dve', lambda e: e.memset(self.epst.a, EPS), writes=[self.epst])
            self.mk_arenas()
            nupd = 2 * L
            for l in range(L):
                for half in range(2):
                    u = 2 * l + half
                    src = "xT" if u == 0 else ("s1", "s2")[(u - 1) % 2]
                    dst = "yT" if u == nupd - 1 else ("s1", "s2")[u % 2]
                    on = lambda ph: self.phases is None or ph in self.phases
                    if half == 0:
                        if on('m1'):
                            self.m1(l, src)
                            P.barrier()
                        if on('m3'):
                            self.m3_prep(l)
                        if on('m2'):
                            self.m2(l)
                            P.barrier()
                        if on('m3'):
                            self.m3(l)
                            P.barrier()
                        if on('m4'):
                            self.m4(l, src, dst)
                            P.barrier()
                    else:
                        if on('f1'):
                            self.f1(l, src)
                            P.barrier()
                        if on('f2'):
                            self.f2(l, src, dst)
                            P.barrier()
            P.finish([self.dr["yT"]])
        return nc

    def vec(self, l, off, n=1):
        return self.vecs.a[:, l * NV + off:l * NV + off + n]

    def mk_arenas(self):
        T, S, NCH = self.T, self.S, self.NCH
        A = Arena(self, "m1")
        a = NS()
        a.win = A.alloc("win", [8, WINX], BF16, multi=True)
        self.wblocks(a.win, WINX, 1024)
        a.wuq = A.alloc("wuq", [3, 768], BF16, multi=True)
        a.wuqsw = A.alloc("wuqsw", [3, 256], BF16, multi=True)
        a.wuk = A.alloc("wuk", [2, 512], BF16, multi=True)
        a.wuv = A.alloc("wuv", [2, 512], BF16, multi=True)
        a.stg = [A.alloc("stg%d" % i, [1024], F32, dma=True) for i in range(4)]
        a.X = [A.alloc("X%d" % i, [8, 512], F32, dma=True) for i in range(2)]
        a.cs = [A.alloc("cs%d" % i, [2, 512], F32, dma=True) for i in range(2)]
        a.xsq = A.alloc("xsq", [8, 512], BF16)
        a.xg = A.alloc("xg", [8, 512], BF16)
        a.rt = A.alloc("rt", [512], F32)
        a.rstd = A.alloc("rstd", [512], F32)
        a.cq = A.alloc("cq", [3, 512], F32)
        a.ckv = A.alloc("ckv", [2, 512], F32)
        a.sq = A.alloc("sq", [3, 512], BF16)
        a.rq = A.alloc("rq", [512], F32)
        a.cqn = A.alloc("cqn", [3, 512], BF16)
        a.ckvn = A.alloc("ckvn", [2, 512], BF16)
        a.tmp = [A.alloc("tmp%d" % i, [512], F32) for i in range(4)]
        a.kp = A.alloc("kp", [512], F32)
        a.kps = A.alloc("kps", [512], F32)
        a.kr = A.alloc("kr", [512], BF16)
        a.ut = A.alloc("ut", [4, 512], BF16)
        a.gts = [A.alloc("gts%d" % i, [4, 512], BF16) for i in range(2)]
        a.qh = [A.alloc("qh%d" % i, [512], BF16) for i in range(3)]
        a.kh = [A.alloc("kh%d" % i, [512], BF16) for i in range(3)]
        a.vt = A.alloc("vt", [4, 8, 96], BF16)
        self.A1 = a
        A = Arena(self, "m2")
        a = NS()
        a.q = [A.alloc("q%d" % i, [S], BF16, dma=True) for i in range(2)]
        a.k = [A.alloc("k%d" % i, [S], BF16, dma=True) for i in range(2)]
        a.v = [A.alloc("v%d" % i, [S // 128, 768], BF16, dma=True) for i in range(2)]
        a.pT = [A.alloc("pT%d" % i, [512], BF16) for i in range(3)]
        a.rc = [A.alloc("rc%d" % i, [512], F32) for i in range(3)]
        a.hi = [A.alloc("hi%d" % i, [512], BF16) for i in range(3)]
        a.lo = [A.alloc("lo%d" % i, [512], BF16) for i in range(3)]
        a.bc = [A.alloc("bc%d" % i, [512], F32) for i in range(2)]
        a.ot = [A.alloc("ot%d" % i, [512], BF16) for i in range(2)]
        self.A2 = a
        A = Arena(self, "m3")
        a = NS()
        a.u = A.alloc("u", [4, T], BF16, dma=True, multi=True)
        a.Sb = A.alloc("Sb", [16, 2, NCH], BF16)
        a.Xb = A.alloc("Xb", [16, 2, NCH], BF16)
        a.Wp = A.alloc("Wp", [4, 8, 2, 128], BF16)
        a.PCb = A.alloc("PCb", [9, 2, 16, 32], BF16)
        a.Kb = A.alloc("Kb", [4, 8, 128], BF16)
        a.BbQb = A.alloc("BbQb", [2, 16, 32], BF16)
        a.m = A.alloc("m", [4, 16, 32], F32, dma=True)
        a.Bb = A.alloc("Bb", [2, 16, 32], F32)
        a.Pw = A.alloc("Pw", [9, 2, 16], F32)
        a.sm = [A.alloc("sm%d" % i, [16], F32) for i in range(14)]
        a.smi = A.alloc("smi", [16], I32)
        a.fc = A.alloc("fcoef", [2, 16], F32)
        a.nAi = A.alloc("nAi", [16], F32)
        a.nPr = A.alloc("nPr", [9, 16], F32)
        a.td = [A.alloc("td%d" % i, [16, 32], F32) for i in range(4)]
        a.tp = [A.alloc("tp%d" % i, [16, 32], F32) for i in range(4)]
        a.xs = [A.alloc("xs%d" % i, [16, 2, self.NSEQ], F32) for i in range(2)]
        a.t1 = A.alloc("t1", [16, 2, self.NSEQ], F32)
        a.t2 = A.alloc("t2", [16, 2, self.NSEQ], F32)
        off_ys = A.off
        a.ys = A.alloc("ys", [T], BF16)
        a.PBb = A.alloc("PBb", [4, 8, 2, 4, 32], BF16, at=off_ys)
        if 16 * 8 * 2 * 32 > T:
            A.off = off_ys + 16 * 8 * 2 * 32
        self.A3 = a
        A = Arena(self, "m4")
        a = NS()
        a.woatt = A.alloc("woatt", [4, 1024], BF16, multi=True)
        a.stg = [A.alloc("stg%d" % i, [1024], F32, dma=True) for i in range(4)]
        a.wglu = A.alloc("wglu", [4, 2048], BF16, multi=True)
        a.wout = A.alloc("wout", [8, 1024], BF16, multi=True)
        a.o = [A.alloc("o%d" % i, [4, 512], BF16, dma=True) for i in range(2)]
        a.ysb = [A.alloc("ysb%d" % i, [4, 512], BF16, dma=True) for i in range(2)]
        a.g = [A.alloc("g%d" % i, [16, 512], BF16, dma=True) for i in range(2)]
        a.X = [A.alloc("X%d" % i, [8, 512], F32, dma=True) for i in range(2)]
        a.sg = [A.alloc("sg%d" % i, [512], F32) for i in range(2)]
        a.yss = [A.alloc("yss%d" % i, [512], F32) for i in range(2)]
        a.m1 = [A.alloc("m1%d" % i, [512], F32) for i in range(2)]
        a.m2 = [A.alloc("m2%d" % i, [512], F32) for i in range(2)]
        a.mg = A.alloc("mg", [8, 512], BF16)
        a.Y = A.alloc("Y", [8, 512], F32)
        a.ysq = A.alloc("ysq", [8, 512], BF16)
        a.rt = A.alloc("rt", [512], F32)
        a.rstd = A.alloc("rstd", [512], F32)
        a.tt = [A.alloc("tt%d" % i, [512], F32) for i in range(2)]
        self.A4 = a
        A = Arena(self, "f1")
        a = NS()
        rstdF = A.alloc("rstdF", [T], F32)
        a.rstdF = rstdF
        a.wup = A.alloc("wup", [8, 2 * D_FF], BF16, multi=True)
        a.X = [A.alloc("X%d" % i, [8, 512], F32, dma=True) for i in range(2)]
        a.sqg = A.alloc("sqg", [8, 512], BF16)
        a.rt = A.alloc("rt", [512], F32)
        a.Gb = [A.alloc("Gb%d" % i, [514], BF16) for i in range(3)]
        a.Gh = A.alloc("Gh", [NFC, 2], BF16)
        a.ge = [A.alloc("ge%d" % i, [512], BF16) for i in range(3)]
        a.dgall = A.alloc("dgall", [NFC, 3, 128], BF16)
        a.act = [A.alloc("act%d" % i, [NFC // 2, 512], BF16) for i in range(2)]
        a.stg = [A.alloc("stg%d" % i, [1024], F32, dma=True, at=a.act[i // 2].off + (i % 2) * 2048) for i in range(4)]
        self.wblocks(a.wup, 2 * D_FF, 1024)
        self.A5 = a
        A = Arena(self, "f2")
        a = NS()
        A.alloc("rstdF_pad", [T], F32)
        a.rstdF = rstdF
        a.wdown = A.alloc("wdown", [NFC, 1024], BF16, multi=True)
        a.stg = [A.alloc("stg%d" % i, [512], F32, dma=True) for i in range(4)]
        self.wblocks(a.wdown, 1024, 512)
        a.a = [A.alloc("a%d" % i, [NFC, 512], BF16, dma=True) for i in range(2)]
        a.X = [A.alloc("X%d" % i, [8, 512], F32, dma=True) for i in range(2)]
        a.Y = A.alloc("Y", [8, 512], F32)
        a.ysq = A.alloc("ysq", [8, 512], BF16)
        a.rt = A.alloc("rt", [512], F32)
        a.rstd = A.alloc("rstd", [512], F32)
        a.tt = [A.alloc("tt%d" % i, [512], F32) for i in range(2)]
        self.A6 = a

    def wblocks(self, dst, n, W):
        nb = (n + W - 1) // W
        rs = []
        for b in range(nb):
            r = self.P.res(dst.name + "_b%d" % b, multi=True)
            r.a = dst.a
            rs.append(r)
        dst.blocks = rs
        dst.W = W
        return rs

    def wr(self, dst, c0, c1):
        if not hasattr(dst, 'blocks'):
            return [dst]
        return dst.blocks[c0 // dst.W:(c1 - 1) // dst.W + 1]

    def load_w(self, dst, src2d, kc, n, stg):
        W = stg[0].a.shape[-1]
        if hasattr(dst, 'blocks'):
            assert dst.W % W == 0 or W % dst.W == 0
            W = min(W, dst.W)
        for n0 in range(0, n, W):
            n1 = min(n, n0 + W)
            wres = self.wr(dst, n0, n1)
            assert len(wres) == 1
            for c in range(kc):
                i = self.wl_i
                self.wl_i += 1
                sl = stg[i % len(stg)]
                self.dma('sp', sl, sl.a[:, 0:n1 - n0], None, src2d[c * 128:(c + 1) * 128, n0:n1])
                if i % 2 == 0:
                    self.act(wres[0], dst.a[:, c, n0:n1], sl, sl.a[:, 0:n1 - n0], AF.Copy)
                else:
                    self.cp('dve', wres[0], dst.a[:, c, n0:n1], sl, sl.a[:, 0:n1 - n0])

    def mm(self, outr, out_ap, lr, lhsT, rr, rhs, start, stop, tp=None):
        if tp is None:
            fn = lambda e: e.matmul(out_ap, lhsT, rhs, start=start, stop=stop)
        else:
            fn = lambda e: e.matmul(out_ap, lhsT, rhs, start=start, stop=stop, tile_position=tp)
        rd = (lr if isinstance(lr, list) else [lr]) + (rr if isinstance(rr, list) else [rr])
        self.P.add('pe', fn, reads=rd, writes=[outr])

    def act(self, outr, out_ap, inr, in_ap, func, bias=None, scale=None, extra_reads=()):
        kw = {}
        if bias is not None:
            kw['bias'] = bias
        if scale is not None:
            kw['scale'] = scale
        self.P.add('act', lambda e: e.activation(out=out_ap, in_=in_ap, func=func, **kw),
                   reads=[inr] + list(extra_reads), writes=[outr])

    def tt(self, eng, outr, out_ap, r0, in0, r1, in1, op):
        self.P.add(eng, lambda e: e.tensor_tensor(out=out_ap, in0=in0, in1=in1, op=op),
                   reads=[r0, r1], writes=[outr])

    def ts(self, eng, outr, out_ap, r0, in0, s1, s2, op0, op1=None, extra_reads=()):
        if op1 is None:
            fn = lambda e: e.tensor_scalar(out=out_ap, in0=in0, scalar1=s1, scalar2=None, op0=op0)
        else:
            fn = lambda e: e.tensor_scalar(out=out_ap, in0=in0, scalar1=s1, scalar2=s2, op0=op0, op1=op1)
        self.P.add(eng, fn, reads=[r0] + list(extra_reads), writes=[outr])

    def stt(self, outr, out_ap, r0, in0, scalar, r1, in1, op0, op1, extra_reads=()):
        self.P.add('dve', lambda e: e.scalar_tensor_tensor(out=out_ap, in0=in0, scalar=scalar, in1=in1,
                                                           op0=op0, op1=op1),
                   reads=[r0, r1] + list(extra_reads), writes=[outr])

    def cp(self, eng, outr, out_ap, inr, in_ap):
        self.P.add(eng, lambda e: e.tensor_copy(out=out_ap, in_=in_ap), reads=[inr], writes=[outr])

    def dma(self, eng, outr, out_ap, inr, in_ap):
        self.P.add(eng, lambda e: e.dma_start(out=out_ap, in_=in_ap),
                   reads=[inr] if inr is not None else [], writes=[outr], dma=True)

    def rms_rstd(self, ps, sq_res, sq_ap_fn, nchunk, dim, rt, rstd):
        for c in range(nchunk):
            self.mm(ps, ps.a, self.ones, self.ones.a, sq_res, sq_ap_fn(c), c == 0, c == nchunk - 1)
        self.act(rt, rt.a, ps, ps.a, AF.Sqrt, bias=self.epst.a[:, 0:1], scale=1.0 / dim, extra_reads=[self.epst])
        self.P.add('dve', lambda e: e.reciprocal(out=rstd.a, in_=rt.a), reads=[rt], writes=[rstd])

    def m1(self, l, src):
        P, a, d, psb = self.P, self.A1, self.d, self.psb
        xin = d[src].rearrange("(c p) t -> p c t", p=128)
        xr = self.dr[src]
        P.add('pool', lambda e: e.memset(a.vt.a[:, :, :, 64:96], 1.0), writes=[a.vt])
        ropev = d["rope"].rearrange("k p s -> p k s")

        def load(tt):
            X = a.X[tt % 2]
            cs = a.cs[tt % 2]
            self.dma('sp', X, X.a, xr, xin[:, :, tt * 512:(tt + 1) * 512])
            p0 = (tt % self.TPS) * 512
            self.dma('sp', cs, cs.a[64:96, :, :], None, ropev[64:96, :, p0:p0 + 512])

        load(0)
        self.load_w(a.win, d["w_inx"][l], 8, WINX, a.stg)
        self.load_w(a.wuq, d["w_uq"][l], 3, 768, a.stg)
        self.load_w(a.wuqsw, d["w_uqsw"][l], 3, 256, a.stg)
        self.load_w(a.wuk, d["w_uk"][l], 2, 512, a.stg)
        self.load_w(a.wuv, d["w_uv"][l], 2, 512, a.stg)
        zb = [1, 2, 3, 4]
        zi = [0]

        def nextbank():
            b = psb[zb[zi[0] % 4]]
            zi[0] += 1
            return b

        for tt in range(self.NT):
            if tt + 1 < self.NT:
                load(tt + 1)
            X, cs = a.X[tt % 2], a.cs[tt % 2]
            t0 = tt * 512
            cosr = cs.a[64:96, 0, :]
            sinr = cs.a[64:96, 1, :]
            self.act(a.xsq, a.xsq.a, X, X.a, AF.Square)
            for dc in range(8):
                self.ts('dve', a.xg, a.xg.a[:, dc, :], X, X.a[:, dc, :], self.vec(l, GPRE + dc), None, ALU.mult,
                        extra_reads=[self.vecs])
            self.rms_rstd(psb[0], a.xsq, lambda c: a.xsq.a[:, c, :], 8, 1024.0, a.rt, a.rstd)

            def zchunk(ps, c0, m, tp=None, out_ap=None):
                oa = ps.a[0:m, :] if out_ap is None else out_ap
                for dc in range(8):
                    self.mm(ps, oa, self.wr(a.win, c0, c0 + m), a.win.a[:, dc, c0:c0 + m], a.xg, a.xg.a[:, dc, :], dc == 0, dc == 7, tp)

            for c in range(3):
                ps = nextbank()
                zchunk(ps, c * 128, 128)
                self.tt('dve', a.cq, a.cq.a[:, c, :], ps, ps.a, a.rstd, a.rstd.a, ALU.mult)
            for c in range(2):
                ps = nextbank()
                zchunk(ps, 384 + c * 128, 128)
                self.tt('dve', a.ckv, a.ckv.a[:, c, :], ps, ps.a, a.rstd, a.rstd.a, ALU.mult)
            self.act(a.sq, a.sq.a, a.cq, a.cq.a, AF.Square)
            self.rms_rstd(psb[7], a.sq, lambda c: a.sq.a[:, c, :], 3, 384.0, a.rt, a.rq)
            for c in range(3):
                self.stt(a.cqn, a.cqn.a[:, c, :], a.cq, a.cq.a[:, c, :], self.vec(l, GQ + c), a.rq, a.rq.a,
                         ALU.mult, ALU.mult, extra_reads=[self.vecs])
            self.act(a.sq, a.sq.a[:, 0:2, :], a.ckv, a.ckv.a, AF.Square)
            self.rms_rstd(psb[7], a.sq, lambda c: a.sq.a[:, c, :], 2, 256.0, a.rt, a.rq)
            for c in range(2):
                self.stt(a.ckvn, a.ckvn.a[:, c, :], a.ckv, a.ckv.a[:, c, :], self.vec(l, GKV + c), a.rq, a.rq.a,
                         ALU.mult, ALU.mult, extra_reads=[self.vecs])
            zchunk(psb[5], 640, 32, tp=(0, 64), out_ap=psb[5].a[64:96, :])
            zchunk(psb[6], 672, 32, tp=(0, 64), out_ap=psb[6].a[64:96, :])
            self.tt('dve', a.kp, a.kp.a[64:96, :], psb[5], psb[5].a[64:96, :], a.rstd, a.rstd.a[64:96, :], ALU.mult)
            self.tt('dve', a.kps, a.kps.a[64:96, :], psb[6], psb[6].a[64:96, :], a.rstd, a.rstd.a[64:96, :], ALU.mult)
            self.tt('dve', a.kp, a.kp.a[64:96, :], a.kp, a.kp.a[64:96, :], cs, cosr, ALU.mult)
            self.tt('dve', a.kps, a.kps.a[64:96, :], a.kps, a.kps.a[64:96, :], cs, sinr, ALU.mult)
            self.tt('pool', a.kr, a.kr.a[64:96, :], a.kp, a.kp.a[64:96, :], a.kps, a.kps.a[64:96, :], ALU.add)
            for c in range(4):
                ps = nextbank()
                zchunk(ps, 704 + c * 128, 128)
                self.tt('dve', a.ut, a.ut.a[:, c, :], ps, ps.a, a.rstd, a.rstd.a, ALU.mult)
            self.dma('sp', self.dr["uT"], d["uT"].rearrange("(c p) t -> p c t", p=128)[:, :, t0:t0 + 512],
                     a.ut, a.ut.a)
            for gb in range(4):
                gts = a.gts[gb % 2]
                for gi in range(4):
                    gc = gb * 4 + gi
                    ps = nextbank()
                    zchunk(ps, 1216 + gc * 128, 128)
                    tmp = a.tmp[gc % 4]
                    self.tt('dve', tmp, tmp.a, ps, ps.a, a.rstd, a.rstd.a, ALU.mult)
                    self.act(gts, gts.a[:, gi, :], tmp, tmp.a, AF.Sigmoid, bias=self.vec(l, BG + gc),
                             extra_reads=[self.vecs])
                self.dma('sp', self.dr["gT"],
                         d["gT"].rearrange("(c p) t -> p c t", p=128)[:, gb * 4:(gb + 1) * 4, t0:t0 + 512],
                         gts, gts.a)
            for h in range(8):
                ps = nextbank()
                psw = psb[5 + h % 2]
                for c in range(3):
                    self.mm(ps, ps.a[0:96, :], a.wuq, a.wuq.a[:, c, 96 * h:96 * h + 96], a.cqn, a.cqn.a[:, c, :],
                            c == 0, c == 2)
                for c in range(3):
                    self.mm(psw, psw.a[64:96, :], a.wuqsw, a.wuqsw.a[:, c, 32 * h:32 * h + 32], a.cqn,
                            a.cqn.a[:, c, :], c == 0, c == 2, tp=(0, 64))
                qh = a.qh[h % 3]
                t1, t2 = a.tmp[(2 * h) % 4], a.tmp[(2 * h + 1) % 4]
                self.act(qh, qh.a[0:64, :], ps, ps.a[0:64, :], AF.Copy)
                self.tt('dve', t1, t1.a[64:96, :], ps, ps.a[64:96, :], cs, cosr, ALU.mult)
                self.tt('dve', t2, t2.a[64:96, :], psw, psw.a[64:96, :], cs, sinr, ALU.mult)
                self.tt('pool', qh, qh.a[64:96, :], t1, t1.a[64:96, :], t2, t2.a[64:96, :], ALU.add)
                self.dma('sp', self.dr["qT"], d["qT"][h, :, t0:t0 + 512], qh, qh.a[0:96, :])
            for h in range(8):
                ps = nextbank()
                for c in range(2):
                    self.mm(ps, ps.a[0:64, :], a.wuk, a.wuk.a[:, c, 64 * h:64 * h + 64], a.ckvn, a.ckvn.a[:, c, :],
                            c == 0, c == 1)
                kh = a.kh[h % 3]
                self.act(kh, kh.a[0:64, :], ps, ps.a[0:64, :], AF.Copy)
                self.cp('pool', kh, kh.a[64:96, :], a.kr, a.kr.a[64:96, :])
                self.dma('sp', self.dr["kT"], d["kT"][h, :, t0:t0 + 512], kh, kh.a[0:96, :])
            for tb in range(4):
                ps = nextbank()
                for c in range(2):
                    self.mm(ps, ps.a, a.ckvn, a.ckvn.a[:, c, tb * 128:(tb + 1) * 128], a.wuv, a.wuv.a[:, c, :],
                            c == 0, c == 1)
                self.cp('dve' if tb % 2 else 'act', a.vt, a.vt.a[:, tb, :, 0:64],
                        ps, ps.a.rearrange("p (h c) -> p h c", h=8)) if tb % 2 else \
                    self.act(a.vt, a.vt.a[:, tb, :, 0:64], ps, ps.a.rearrange("p (h c) -> p h c", h=8), AF.Copy)
            self.dma('sp', self.dr["vA"],
                     d["vA"].rearrange("(n p) c -> p n c", p=128)[:, 4 * tt:4 * tt + 4, :],
                     a.vt, a.vt.a.rearrange("p n h c -> p n (h c)"))

    def m2(self, l):
        P, a, d, psb = self.P, self.A2, self.d, self.psb
        S = self.S
        scale = 1.0 / math.sqrt(96.0)
        vAv = d["vA"].rearrange("(n p) c -> p n c", p=128)
        nsb = S // 128
        pairs = [(s, h) for s in range(self.NSEQ) for h in range(8)]

        def loadv(s):
            v = a.v[s % 2]
            self.dma('sp', v, v.a, self.dr["vA"], vAv[:, s * nsb:(s + 1) * nsb, :])

        def loadqk(i):
            s, h = pairs[i]
            q, k = a.q[i % 2], a.k[i % 2]
            self.dma('sp', q, q.a[0:96, :], self.dr["qT"], d["qT"][h, :, s * S:(s + 1) * S])
            self.dma('sp', k, k.a[0:96, :], self.dr["kT"], d["kT"][h, :, s * S:(s + 1) * S])

        blocks = []
        g = 0
        for i, (s, h) in enumerate(pairs):
            for qa in range(S // 512):
                nblk = 4 * qa + 4
                for j in range(nblk):
                    blocks.append((i, s, h, qa, j, nblk, g))
                g += 1
        N = len(blocks)

        def geom(b):
            i, s, h, qa, j, nblk, g = b
            r = j - 4 * qa
            qoff = 128 * r if r > 0 else 0
            return r, qoff, 512 - qoff

        def emitS(idx):
            i, s, h, qa, j, nblk, g = blocks[idx]
            r, qoff, nq = geom(blocks[idx])
            q, k = a.q[i % 2], a.k[i % 2]
            pss = psb[idx % 3]
            self.mm(pss, pss.a[:, 0:nq], k, k.a[0:96, j * 128:(j + 1) * 128],
                    q, q.a[0:96, qa * 512 + qoff:qa * 512 + 512], True, True)

        def emitEP(idx):
            r, qoff, nq = geom(blocks[idx])
            pss, pT = psb[idx % 3], a.pT[idx % 3]
            self.act(pT, pT.a[:, 0:nq], pss, pss.a[:, 0:nq], AF.Exp, scale=scale)
            if r >= 0:
                P.add('dve', lambda e, pT=pT: e.memset(pT.a[64:128, 0:64], 0.0), writes=[pT])

        def emitPV(idx):
            i, s, h, qa, j, nblk, g = blocks[idx]
            r, qoff, nq = geom(blocks[idx])
            v, pT, po = a.v[s % 2], a.pT[idx % 3], psb[POB[g % 3]]
            self.mm(po, po.a[0:96, qoff:512], v, v.a[:, j, h * 96:(h + 1) * 96], pT, pT.a[:, 0:nq],
                    j == 0, j == nblk - 1)

        def tail_front(b):
            g = b[6]
            po, rc, hi, lo = psb[POB[g % 3]], a.rc[g % 3], a.hi[g % 3], a.lo[g % 3]
            P.add('dve', lambda e, rc=rc, po=po: e.reciprocal(out=rc.a[64:65, :], in_=po.a[64:65, :]),
                  reads=[po], writes=[rc])
            self.cp('dve', hi, hi.a[64:65, :], rc, rc.a[64:65, :])
            self.tt('dve', lo, lo.a[64:65, :], rc, rc.a[64:65, :], hi, hi.a[64:65, :], ALU.subtract)

        def tail_back(b):
            i, s, h, qa, j, nblk, g = b
            po, pb = psb[POB[g % 3]], psb[5 + g % 2]
            hi, lo, bc, ot = a.hi[g % 3], a.lo[g % 3], a.bc[g % 2], a.ot[g % 2]
            self.mm(pb, pb.a[0:64, :], self.ones, self.ones.a[64:65, 0:64], hi, hi.a[64:65, :], True, False,
                    tp=(64, 0))
            self.mm(pb, pb.a[0:64, :], self.ones, self.ones.a[64:65, 0:64], lo, lo.a[64:65, :], False, True,
                    tp=(64, 0))
            self.act(bc, bc.a[0:64, :], pb, pb.a[0:64, :], AF.Copy)
            self.tt('dve', ot, ot.a[0:64, :], po, po.a[0:64, :], bc, bc.a[0:64, :], ALU.mult)
            tq = s * S + qa * 512
            self.dma('sp', self.dr["oT"], d["oT"][h * 64:(h + 1) * 64, tq:tq + 512], ot, ot.a[0:64, :])

        POB = [3, 4, 7]
        DEFER = 6
        loadv(0)
        loadqk(0)
        if len(pairs) > 1:
            loadqk(1)
        emitS(0)
        if N > 1:
            emitS(1)
        pending = []
        for idx in range(N):
            b = blocks[idx]
            i, s, h, qa, j, nblk, g = b
            if qa == 0 and j == 0:
                if h == 0 and s + 1 < self.NSEQ:
                    loadv(s + 1)
            emitEP(idx)
            if idx + 2 < N:
                b2 = blocks[idx + 2]
                if b2[3] == 0 and b2[4] == 0 and b2[0] + 1 < len(pairs) and b2[0] >= 1:
                    pass
                emitS(idx + 2)
            emitPV(idx)
            pending = [(pb_, c_ - 1) for (pb_, c_) in pending]
            while pending and pending[0][1] <= 0:
                tail_back(pending.pop(0)[0])
            if j == nblk - 1:
                tail_front(b)
                pending.append((b, DEFER))
                if qa == S // 512 - 1 and i + 2 < len(pairs):
                    loadqk(i + 2)
        for pb_, c_ in pending:
            tail_back(pb_)

    def m3_prep(self, l):
        P, a, d, psb = self.P, self.A3, self.d, self.psb
        T, NCH, CPS, NSEQ = self.T, self.NCH, self.CPS, self.NSEQ
        sv = self.s5v.a[:, l * 48:(l + 1) * 48]
        lre, lim, lst = sv[:, 0:16], sv[:, 16:32], sv[:, 32:48]
        svr = self.s5v
        self.dma('sp', a.m, a.m.a, None, d["s5m"][l].rearrange("p (k g c) -> p k g c", k=4, g=16))
        sm = a.sm
        dv, ac = 'dve', 'act'
        delta, lrd, mag, ang, rr, kf, fr, s1, s2, ch, tA, tB, ca, den = sm
        self.act(delta, delta.a, svr, lst, AF.Exp)
        self.tt(dv, lrd, lrd.a, svr, lre, delta, delta.a, ALU.mult)
        self.act(mag, mag.a, lrd, lrd.a, AF.Exp)
        self.tt(dv, ang, ang.a, svr, lim, delta, delta.a, ALU.mult)
        self.ts(dv, rr, rr.a, ang, ang.a, 1.0 / (2.0 * math.pi), None, ALU.mult)
        self.cp(dv, a.smi, a.smi.a, rr, rr.a)
        self.cp(dv, kf, kf.a, a.smi, a.smi.a)
        self.tt(dv, fr, fr.a, rr, rr.a, kf, kf.a, ALU.subtract)
        self.act(s1, s1.a, fr, fr.a, AF.Sin, scale=math.pi)
        self.act(s2, s2.a, fr, fr.a, AF.Sin, scale=math.pi / 2.0)
        self.tt(dv, tA, tA.a, s2, s2.a, s2, s2.a, ALU.mult)
        self.ts(dv, ch, ch.a, tA, tA.a, -2.0, 1.0, ALU.mult, ALU.add)
        self.tt(dv, tA, tA.a, s1, s1.a, ch, ch.a, ALU.mult)
        Pw = a.Pw
        self.stt(Pw, Pw.a[:, 1, 1, :], tA, tA.a, 2.0, mag, mag.a, ALU.mult, ALU.mult)
        self.tt(dv, tB, tB.a, s1, s1.a, s1, s1.a, ALU.mult)
        self.ts(dv, ca, ca.a, tB, tB.a, -2.0, 1.0, ALU.mult, ALU.add)
        self.tt(dv, Pw, Pw.a[:, 1, 0, :], ca, ca.a, mag, mag.a, ALU.mult)
        P.add(dv, lambda e: e.memset(Pw.a[:, 0, 0, :], 1.0), writes=[Pw])
        P.add(dv, lambda e: e.memset(Pw.a[:, 0, 1, :], 0.0), writes=[Pw])
        ar, ai = Pw.a[:, 1, 0, :], Pw.a[:, 1, 1, :]
        nr = s1
        self.ts(dv, nr, nr.a, Pw, ar, -1.0, None, ALU.add)
        self.tt(dv, tA, tA.a, svr, lre, svr, lre, ALU.mult)
        self.tt(dv, tB, tB.a, svr, lim, svr, lim, ALU.mult)
        self.tt(dv, den, den.a, tA, tA.a, tB, tB.a, ALU.add)
        P.add(dv, lambda e: e.reciprocal(out=den.a, in_=den.a), reads=[den], writes=[den])
        self.tt(dv, tA, tA.a, nr, nr.a, svr, lre, ALU.mult)
        self.tt(dv, tB, tB.a, Pw, ai, svr, lim, ALU.mult)
        self.tt(dv, tA, tA.a, tA, tA.a, tB, tB.a, ALU.add)
        self.tt(dv, a.fc, a.fc.a[:, 0, :], tA, tA.a, den, den.a, ALU.mult)
        self.tt(dv, tA, tA.a, Pw, ai, svr, lre, ALU.mult)
        self.tt(dv, tB, tB.a, nr, nr.a, svr, lim, ALU.mult)
        self.tt(dv, tA, tA.a, tA, tA.a, tB, tB.a, ALU.subtract)
        self.tt(dv, a.fc, a.fc.a[:, 1, :], tA, tA.a, den, den.a, ALU.mult)
        for k in range(2, 9):
            pr, pi = Pw.a[:, k - 1, 0, :], Pw.a[:, k - 1, 1, :]
            self.tt(dv, tA, tA.a, Pw, pr, Pw, ar, ALU.mult)
            self.tt(dv, tB, tB.a, Pw, pi, Pw, ai, ALU.mult)
            self.tt(dv, Pw, Pw.a[:, k, 0, :], tA, tA.a, tB, tB.a, ALU.subtract)
            self.tt(dv, tA, tA.a, Pw, pr, Pw, ai, ALU.mult)
            self.tt(dv, tB, tB.a, Pw, pi, Pw, ar, ALU.mult)
            self.tt(dv, Pw, Pw.a[:, k, 1, :], tA, tA.a, tB, tB.a, ALU.add)
        self.ts(dv, a.nAi, a.nAi.a, Pw, Pw.a[:, 8, 1, :], -1.0, None, ALU.mult)

        def bc32(ap16):
            return ap16.unsqueeze(2).broadcast_to([128, 16, 32])

        Cr, Ci, Br, Bi = a.m.a[:, 0], a.m.a[:, 1], a.m.a[:, 2], a.m.a[:, 3]
        fre, fim = bc32(a.fc.a[:, 0, :]), bc32(a.fc.a[:, 1, :])
        td, tp = a.td, a.tp
        for k in range(9):
            self.ts(dv, a.nPr, a.nPr.a[:, k, :], Pw, Pw.a[:, k, 0, :], -1.0, None, ALU.mult)
        pl = 'pool'
        self.tt(pl, td[0], td[0].a, a.m, Br, a.fc, fre, ALU.mult)
        self.tt(pl, td[1], td[1].a, a.m, Bi, a.fc, fim, ALU.mult)
        self.tt(pl, a.Bb, a.Bb.a[:, 0], td[0], td[0].a, td[1], td[1].a, ALU.subtract)
        self.tt(pl, td[2], td[2].a, a.m, Bi, a.fc, fre, ALU.mult)
        self.tt(pl, td[3], td[3].a, a.m, Br, a.fc, fim, ALU.mult)
        self.tt(pl, a.Bb, a.Bb.a[:, 1], td[2], td[2].a, td[3], td[3].a, ALU.add)
        self.cp('pool', a.BbQb, a.BbQb.a[:, 0], a.Bb, a.Bb.a[:, 0])
        self.cp('pool', a.BbQb, a.BbQb.a[:, 1], a.Bb, a.Bb.a[:, 1])
        for k in range(9):
            pr, pi = bc32(Pw.a[:, k, 0, :]), bc32(Pw.a[:, k, 1, :])
            npr = bc32(a.nPr.a[:, k, :])
            self.tt(pl, td[0], td[0].a, a.m, Cr, Pw, pr, ALU.mult)
            self.tt(pl, td[1], td[1].a, a.m, Ci, Pw, pi, ALU.mult)
            self.tt(pl, a.PCb, a.PCb.a[:, k, 0], td[0], td[0].a, td[1], td[1].a, ALU.subtract)
            self.tt(pl, td[2], td[2].a, a.m, Ci, a.nPr, npr, ALU.mult)
            self.tt(pl, td[3], td[3].a, a.m, Cr, Pw, pi, ALU.mult)
            self.tt(pl, a.PCb, a.PCb.a[:, k, 1], td[2], td[2].a, td[3], td[3].a, ALU.subtract)
        for j in range(8):
            pr, pi = bc32(Pw.a[:, 7 - j, 0, :]), bc32(Pw.a[:, 7 - j, 1, :])
            self.tt(pl, tp[0], tp[0].a, a.Bb, a.Bb.a[:, 0], Pw, pr, ALU.mult)
            self.tt(pl, tp[1], tp[1].a, a.Bb, a.Bb.a[:, 1], Pw, pi, ALU.mult)
            v4 = lambda r: r.a.rearrange("p (t m) c -> p t m c", t=4)
            self.tt(pl, a.PBb, a.PBb.a[:, :, j, 0, :, :], tp[0], v4(tp[0]), tp[1], v4(tp[1]), ALU.subtract)
            self.tt(pl, tp[2], tp[2].a, a.Bb, a.Bb.a[:, 1], Pw, pr, ALU.mult)
            self.tt(pl, tp[3], tp[3].a, a.Bb, a.Bb.a[:, 0], Pw, pi, ALU.mult)
            self.tt(pl, a.PBb, a.PBb.a[:, :, j, 1, :, :], tp[2], v4(tp[2]), tp[3], v4(tp[3]), ALU.add)

    def m3(self, l):
        P, a, d, psb = self.P, self.A3, self.d, self.psb
        T, NCH, CPS, NSEQ = self.T, self.NCH, self.CPS, self.NSEQ
        Pw = a.Pw
        dv = 'dve'
        for Tt in range(4):
            self.dma('sp', a.u, a.u.a[:, Tt, :], self.dr["uT"], d["uT"][Tt * 128:(Tt + 1) * 128, :])
        nb = 0
        for Tt in range(4):
            for jh in range(2):
                ps = psb[nb % 2]
                nb += 1
                psv = ps.a.bitcast(BF16)
                for jj in range(4):
                    for ri in range(2):
                        j = jh * 4 + jj
                        idx = jj * 2 + ri
                        P.add('pe', lambda e, psv=psv, idx=idx, Tt=Tt, j=j, ri=ri: e.transpose(
                            psv[:, idx * 128:(idx + 1) * 128], a.PBb.a[:, Tt, j, ri, :, :].rearrange("p m c -> p (m c)"),
                            self.ident.a), reads=[a.PBb, self.ident], writes=[ps])
                self.cp('dve' if nb % 2 else 'act', a.Wp,
                        a.Wp.a[:, Tt, jh * 4:jh * 4 + 4, :, :].rearrange("p j r c -> p (j r c)"),
                        ps, psv) if nb % 2 else \
                    self.act(a.Wp, a.Wp.a[:, Tt, jh * 4:jh * 4 + 4, :, :].rearrange("p j r c -> p (j r c)"),
                             ps, psv, AF.Copy)
        fl = lambda ap: ap.rearrange("p m c -> p (m c)")
        for Tt in range(4):
            for kh in range(2):
                ps = psb[2 + (nb % 2)]
                nb += 1
                for kk in range(4):
                    k = kh * 4 + kk
                    oa = ps.a[:, kk * 128:(kk + 1) * 128]
                    for ri in range(2):
                        self.mm(ps, oa, a.BbQb, fl(a.BbQb.a[:, ri, 4 * Tt:4 * Tt + 4, :]),
                                a.PCb, fl(a.PCb.a[:, k, ri, 4 * Tt:4 * Tt + 4, :]), ri == 0, ri == 1)
                self.tt('dve', a.Kb, a.Kb.a[:, Tt, kh * 4:kh * 4 + 4, :], ps,
                        ps.a.rearrange("p (k c) -> p k c", k=4), self.bmask,
                        self.bmask.a.unsqueeze(1).broadcast_to([128, 4, 128]), ALU.mult)
            self.stt(a.Kb, a.Kb.a[:, Tt, 0, :], self.ident, self.ident.a, self.vec(l, DSK + Tt), a.Kb,
                     a.Kb.a[:, Tt, 0, :], ALU.mult, ALU.add, extra_reads=[self.vecs])
        for gp in range(16):
            Tt, m = gp // 4, gp % 4
            for ri in range(2):
                ps = psb[4 + (nb % 4)]
                nb += 1
                for j in range(8):
                    rhs = a.u.a[32 * m:32 * m + 32, Tt, :].rearrange("p (c j) -> p j c", j=8)[:, j, :]
                    self.mm(ps, ps.a[:, 0:NCH], a.Wp, a.Wp.a[32 * m:32 * m + 32, Tt, j, ri, :], a.u, rhs,
                            j == 0, j == 7, tp=(32 * m, 0))
                if nb % 2:
                    self.cp('dve', a.Sb, a.Sb.a[:, gp, ri, :], ps, ps.a[:, 0:NCH])
                else:
                    self.act(a.Sb, a.Sb.a[:, gp, ri, :], ps, ps.a[:, 0:NCH], AF.Copy)
        Sv = a.Sb.a.rearrange("p g r (s c) -> p g r s c", s=NSEQ)
        Xv = a.Xb.a.rearrange("p g r (s c) -> p g r s c", s=NSEQ)
        Ar4 = Pw.a[:, 8, 0, :].unsqueeze(2).unsqueeze(3).broadcast_to([128, 16, 2, NSEQ])
        Ai3 = Pw.a[:, 8, 1, :].unsqueeze(2).broadcast_to([128, 16, NSEQ])
        nAi3 = a.nAi.a.unsqueeze(2).broadcast_to([128, 16, NSEQ])
        P.add(dv, lambda e: e.memset(a.xs[0].a, 0.0), writes=[a.xs[0]])
        for c in range(CPS):
            xc, xn = a.xs[c % 2], a.xs[(c + 1) % 2]
            self.act(a.Xb, Xv[:, :, :, :, c], xc, xc.a, AF.Copy)
            if c == CPS - 1:
                break
            self.tt(dv, a.t1, a.t1.a, xc, xc.a, Pw, Ar4, ALU.mult)
            self.tt(dv, a.t2, a.t2.a[:, :, 0, :], xc, xc.a[:, :, 1, :], a.nAi, nAi3, ALU.mult)
            self.tt(dv, a.t2, a.t2.a[:, :, 1, :], xc, xc.a[:, :, 0, :], Pw, Ai3, ALU.mult)
            self.tt(dv, a.t1, a.t1.a, a.t1, a.t1.a, a.t2, a.t2.a, ALU.add)
            self.tt(dv, xn, xn.a, a.t1, a.t1.a, a.Sb, Sv[:, :, :, :, c], ALU.add)
        for Tt in range(4):
            for j in range(8):
                ps = psb[nb % 4]
                nb += 1
                for k in range(j + 1):
                    rhs = a.u.a[:, Tt, :].rearrange("p (c j) -> p j c", j=8)[:, j - k, :]
                    self.mm(ps, ps.a[:, 0:NCH], a.Kb, a.Kb.a[:, Tt, k, :], a.u, rhs, k == 0, False)
                for m in range(4):
                    gp = 4 * Tt + m
                    for ri in range(2):
                        self.mm(ps, ps.a[32 * m:32 * m + 32, 0:NCH], a.PCb, a.PCb.a[:, j + 1, ri, gp, :],
                                a.Xb, a.Xb.a[:, gp, ri, :], False, (m == 3 and ri == 1), tp=(0, 32 * m))
                self.act(a.ys, a.ys.a.rearrange("p (c j) -> p j c", j=8)[:, j, :], ps, ps.a[:, 0:NCH],
                         AF.Gelu_apprx_tanh)
            self.dma('sp', self.dr["ysT"], d["ysT"][Tt * 128:(Tt + 1) * 128, :], a.ys, a.ys.a)

    def post_norm_residual(self, l, a, X, goff, dst, t0, square_done=False):
        psb, d = self.psb, self.d
        if not square_done:
            self.act(a.ysq, a.ysq.a, a.Y, a.Y.a, AF.Square)
        self.rms_rstd(psb[7], a.ysq, lambda c: a.ysq.a[:, c, :], 8, 1024.0, a.rt, a.rstd)
        for oc in range(8):
            t = a.tt[oc % 2]
            self.tt('dve', t, t.a, a.Y, a.Y.a[:, oc, :], a.rstd, a.rstd.a, ALU.mult)
            self.stt(X, X.a[:, oc, :], t, t.a, self.vec(l, goff + oc), X, X.a[:, oc, :], ALU.mult, ALU.add,
                     extra_reads=[self.vecs])
        self.dma('sp', self.dr[dst], d[dst].rearrange("(c p) t -> p c t", p=128)[:, :, t0:t0 + 512], X, X.a)

    def m4(self, l, src, dst):
        P, a, d, psb = self.P, self.A4, self.d, self.psb
        xin = d[src].rearrange("(c p) t -> p c t", p=128)
        oin = d["oT"].rearrange("(c p) t -> p c t", p=128)
        yin = d["ysT"].rearrange("(c p) t -> p c t", p=128)
        gin = d["gT"].rearrange("(c p) t -> p c t", p=128)

        def load(tt):
            sl = slice(tt * 512, (tt + 1) * 512)
            self.dma('sp', a.o[tt % 2], a.o[tt % 2].a, self.dr["oT"], oin[:, :, sl])
            self.dma('sp', a.ysb[tt % 2], a.ysb[tt % 2].a, self.dr["ysT"], yin[:, :, sl])
            self.dma('sp', a.g[tt % 2], a.g[tt % 2].a, self.dr["gT"], gin[:, :, sl])
            self.dma('sp', a.X[tt % 2], a.X[tt % 2].a, self.dr[src], xin[:, :, sl])

        nbc = [0]

        def stageA(tt):
            o, ysb, g = a.o[tt % 2], a.ysb[tt % 2], a.g[tt % 2]
            for oc in range(8):
                nb = nbc[0]
                pa, pga, pgb = psb[(3 * nb) % 6], psb[(3 * nb + 1) % 6], psb[(3 * nb + 2) % 6]
                nbc[0] += 1
                sg, yss, m1, m2 = a.sg[oc % 2], a.yss[oc % 2], a.m1[oc % 2], a.m2[oc % 2]
                cs = slice(oc * 128, (oc + 1) * 128)
                for k in range(4):
                    self.mm(pa, pa.a, a.woatt, a.woatt.a[:, k, cs], o, o.a[:, k, :], k == 0, k == 3)
                for k in range(4):
                    self.mm(pga, pga.a, a.wglu, a.wglu.a[:, k, cs], ysb, ysb.a[:, k, :], k == 0, k == 3)
                for k in range(4):
                    self.mm(pgb, pgb.a, a.wglu, a.wglu.a[:, k, 1024 + oc * 128:1024 + (oc + 1) * 128], ysb,
                            ysb.a[:, k, :], k == 0, k == 3)
                self.act(sg, sg.a, pgb, pgb.a, AF.Sigmoid)
                self.tt('dve', yss, yss.a, pga, pga.a, sg, sg.a, ALU.mult)
                self.tt('dve', m1, m1.a, pa, pa.a, g, g.a[:, oc, :], ALU.mult)
                self.tt('dve', m2, m2.a, yss, yss.a, g, g.a[:, 8 + oc, :], ALU.mult)
                self.tt('dve', a.mg, a.mg.a[:, oc, :], m1, m1.a, m2, m2.a, ALU.add)

        def stageB(tt):
            for oc in range(8):
                py = psb[(3 * nbc[0]) % 6]
                nbc[0] += 1
                for k in range(8):
                    self.mm(py, py.a, a.wout, a.wout.a[:, k, oc * 128:(oc + 1) * 128], a.mg, a.mg.a[:, k, :],
                            k == 0, k == 7)
                self.act(a.Y, a.Y.a[:, oc, :], py, py.a, AF.Copy)
            self.act(a.ysq, a.ysq.a, a.Y, a.Y.a, AF.Square)

        def stageC(tt):
            self.post_norm_residual(l, a, a.X[tt % 2], GPOST, dst, tt * 512, square_done=True)

        NT = self.NT
        load(0)
        self.load_w(a.woatt, d["w_oatt"][l], 4, 1024, a.stg)
        self.load_w(a.wglu, d["w_glu"][l], 4, 2048, a.stg)
        self.load_w(a.wout, d["w_out"][l], 8, 1024, a.stg)
        if NT > 1:
            load(1)
        stageA(0)
        stageB(0)
        for tt in range(1, NT):
            stageA(tt)
            stageC(tt - 1)
            if tt + 1 < NT:
                load(tt + 1)
            stageB(tt)
        stageC(NT - 1)

    def f1(self, l, src):
        P, a, d, psb = self.P, self.A5, self.d, self.psb
        self.dma('sp', a.X[0], a.X[0].a, self.dr[src], d[src].rearrange("(c p) t -> p c t", p=128)[:, :, 0:512])
        self.load_w(a.wup, d["w_up"][l], 8, 2 * D_FF, a.stg)
        for fc in range(NFC):
            for k in range(3):
                self.ts('dve', a.dgall, a.dgall.a[:, fc, k, :], self.ident, self.ident.a,
                        self.vec(l, CW + fc * 3 + k), None, ALU.mult, extra_reads=[self.vecs])
        xin = d[src].rearrange("(c p) t -> p c t", p=128)
        aout = d["actT"].rearrange("(c p) t -> p c t", p=128)

        def load(tt):
            self.dma('sp', a.X[tt % 2], a.X[tt % 2].a, self.dr[src], xin[:, :, tt * 512:(tt + 1) * 512])

        gi = 0
        for tt in range(self.NT):
            if tt + 1 < self.NT:
                load(tt + 1)
            X = a.X[tt % 2]
            t0 = tt * 512
            rs = a.rstdF.a[:, t0:t0 + 512]
            self.act(a.sqg, a.sqg.a, X, X.a, AF.Square)
            for c in range(8):
                self.mm(psb[0], psb[0].a, self.ones, self.ones.a, a.sqg, a.sqg.a[:, c, :], c == 0, c == 7)
            self.act(a.rt, a.rt.a, psb[0], psb[0].a, AF.Sqrt, bias=self.epst.a[:, 0:1], scale=1.0 / 1024.0,
                     extra_reads=[self.epst])
            P.add('dve', lambda e, rs=rs: e.reciprocal(out=rs, in_=a.rt.a), reads=[a.rt], writes=[a.rstdF])
            for dc in range(8):
                self.ts('dve', a.sqg, a.sqg.a[:, dc, :], X, X.a[:, dc, :], self.vec(l, GFPRE + dc), None, ALU.mult,
                        extra_reads=[self.vecs])
            seq_start = (tt % self.TPS == 0)
            pend = None

            def conv_stage(fc, Gb, psv, ge_i):
                psc = psb[5 + fc % 2]
                for k in range(3):
                    self.mm(psc, psc.a, a.dgall, a.dgall.a[:, fc, k, :], Gb, Gb.a[:, k:k + 512], k == 0, k == 2)
                ge = a.ge[ge_i % 3]
                self.act(ge, ge.a, psc, psc.a, AF.Gelu_apprx_tanh, bias=self.vec(l, CB + fc),
                         extra_reads=[self.vecs])
                ar = a.act[fc // (NFC // 2)]
                self.tt('dve', ar, ar.a[:, fc % (NFC // 2), :], psv, psv.a, ge, ge.a, ALU.mult)
                if fc % (NFC // 2) == NFC // 2 - 1:
                    hh = fc // (NFC // 2)
                    self.dma('sp', self.dr["actT"], aout[:, hh * 11:(hh + 1) * 11, t0:t0 + 512], ar, ar.a)

            for fc in range(NFC):
                psg, psv = psb[1 + fc % 2], psb[3 + fc % 2]
                Gb = a.Gb[gi % 3]
                gi += 1
                for dc in range(8):
                    self.mm(psg, psg.a, self.wr(a.wup, fc * 256, fc * 256 + 128),
                            a.wup.a[:, dc, fc * 256:fc * 256 + 128], a.sqg, a.sqg.a[:, dc, :], dc == 0, dc == 7)
                for dc in range(8):
                    self.mm(psv, psv.a, self.wr(a.wup, fc * 256 + 128, fc * 256 + 256),
                            a.wup.a[:, dc, fc * 256 + 128:fc * 256 + 256], a.sqg, a.sqg.a[:, dc, :], dc == 0, dc == 7)
                if seq_start:
                    P.add('pool', lambda e, Gb=Gb: e.memset(Gb.a[:, 0:2], 0.0), writes=[Gb])
                else:
                    self.act(Gb, Gb.a[:, 0:2], a.Gh, a.Gh.a[:, fc, :], AF.Copy)
                self.tt('dve', Gb, Gb.a[:, 2:514], psg, psg.a, a.rstdF, rs, ALU.mult)
                self.act(a.Gh, a.Gh.a[:, fc, :], Gb, Gb.a[:, 512:514], AF.Copy)
                if pend is not None:
                    conv_stage(*pend)
                pend = (fc, Gb, psv, gi)
            conv_stage(*pend)

    def f2(self, l, src, dst):
        P, a, d, psb = self.P, self.A6, self.d, self.psb
        xin = d[src].rearrange("(c p) t -> p c t", p=128)
        ain = d["actT"].rearrange("(c p) t -> p c t", p=128)

        def load(tt):
            sl = slice(tt * 512, (tt + 1) * 512)
            self.dma('sp', a.a[tt % 2], a.a[tt % 2].a, self.dr["actT"], ain[:, :, sl])
            self.dma('sp', a.X[tt % 2], a.X[tt % 2].a, self.dr[src], xin[:, :, sl])

        load(0)
        self.load_w(a.wdown, d["w_down"][l], NFC, 1024, a.stg)
        nb = 0
        for tt in range(self.NT):
            if tt + 1 < self.NT:
                load(tt + 1)
            A_, X = a.a[tt % 2], a.X[tt % 2]
            rs = a.rstdF.a[:, tt * 512:(tt + 1) * 512]
            for oc in range(8):
                py = psb[nb % 6]
                nb += 1
                for k in range(NFC):
                    self.mm(py, py.a, self.wr(a.wdown, oc * 128, (oc + 1) * 128), a.wdown.a[:, k, oc * 128:(oc + 1) * 128], A_, A_.a[:, k, :],
                            k == 0, k == NFC - 1)
                self.tt('dve', a.Y, a.Y.a[:, oc, :], py, py.a, a.rstdF, rs, ALU.mult)
            self.post_norm_residual(l, a, X, GFPOST, dst, tt * 512)


def _rope_tables(seq):
    pos = np.arange(seq, dtype=np.float32)
    inv_freq = (np.float32(10000.0) ** (-np.arange(0, 32, 2, dtype=np.float32) / np.float32(32))).astype(np.float32)
    ang = (pos[:, None] * inv_freq[None, :]).astype(np.float32)
    cos, sin = np.cos(ang).astype(np.float32), np.sin(ang).astype(np.float32)
    tab = np.zeros((2, 128, seq), np.float32)
    tab[0, 64:80] = cos.T
    tab[0, 80:96] = cos.T
    tab[1, 64:80] = -sin.T
    tab[1, 80:96] = sin.T
    return tab


def _interleave_up(w):
    L = w.shape[0]
    g = w[:, :, :D_FF].reshape(L, 1024, NFC, 128)
    v = w[:, :, D_FF:].reshape(L, 1024, NFC, 128)
    return np.ascontiguousarray(np.stack([g, v], axis=3).reshape(L, 1024, 2 * D_FF))


def prep_weights(inp, depth, seq):
    f = lambda a: np.ascontiguousarray(np.asarray(a, dtype=np.float32))
    L = depth
    w_in = f(inp["w_in"])[:L]
    o1, o2, o3, o4 = 384, 640, 672, 1184
    kpe = w_in[:, :, o2:o3]
    kpe_sw = np.concatenate([kpe[:, :, 16:32], kpe[:, :, 0:16]], axis=2)
    w_inx = np.concatenate([w_in[:, :, 0:o2], kpe, kpe_sw, w_in[:, :, o3:o4], w_in[:, :, o4:]], axis=2)
    assert w_inx.shape[2] == WINX
    w_uq = f(inp["w_uq"])[:L]
    wq = w_uq.reshape(L, 384, 8, 96)
    w_uqsw = np.concatenate([wq[..., 80:96], wq[..., 64:80]], axis=3).reshape(L, 384, 256)
    w_ukv = f(inp["w_ukv"])[:L].reshape(L, 256, 8, 128)
    w_uk = w_ukv[..., 0:64].reshape(L, 256, 512)
    w_uv = w_ukv[..., 64:128].reshape(L, 256, 512)
    vecs = np.zeros((128, L, NV), np.float32)

    def put(off, arr, n):
        vecs[:, :, off:off + n] = f(arr)[:L].reshape(L, n, 128).transpose(2, 0, 1)

    put(GPRE, inp["g_mix_pre"], 8)
    put(BG, inp["b_gate"], 16)
    put(GQ, inp["g_q"], 3)
    put(GKV, inp["g_kv"], 2)
    put(GPOST, inp["g_mix_post"], 8)
    put(GFPRE, inp["g_ffn_pre"], 8)
    put(GFPOST, inp["g_ffn_post"], 8)
    cw = f(inp["conv_w"])[:L].reshape(L, 3, NFC, 128).transpose(3, 0, 2, 1)
    vecs[:, :, CW:CW + 66] = cw.reshape(128, L, 66)
    put(CB, inp["conv_b"], NFC)
    put(DSK, inp["d_skip"], 4)
    s5v = np.zeros((128, L, 48), np.float32)

    def gl(arr):
        return f(arr)[:L].reshape(L, 16, 2, 64).transpose(2, 3, 0, 1).reshape(128, L, 16)

    s5v[:, :, 0:16] = gl(inp["lam_re"])
    s5v[:, :, 16:32] = gl(inp["lam_im"])
    ls = f(inp["log_step"])[:L].reshape(L, 16, 2)
    s5v[:, :, 32:48] = np.repeat(ls.transpose(2, 0, 1)[:, None], 64, axis=1).reshape(128, L, 16)
    s5m = np.zeros((L, 2, 64, 4, 16, 2, 16), np.float32)
    cr = f(inp["c_re"])[:L].reshape(L, 16, 2, 16, 64)
    ci = f(inp["c_im"])[:L].reshape(L, 16, 2, 16, 64)
    br = f(inp["b_re"])[:L].reshape(L, 16, 2, 64, 16)
    bi = f(inp["b_im"])[:L].reshape(L, 16, 2, 64, 16)
    for g2 in range(2):
        s5m[:, g2, :, 0, :, g2, :] = cr[:, :, g2].transpose(0, 3, 1, 2)
        s5m[:, g2, :, 1, :, g2, :] = ci[:, :, g2].transpose(0, 3, 1, 2)
        s5m[:, g2, :, 2, :, g2, :] = br[:, :, g2].transpose(0, 2, 1, 3)
        s5m[:, g2, :, 3, :, g2, :] = bi[:, :, g2].transpose(0, 2, 1, 3)
    return {
        "w_inx": np.ascontiguousarray(w_inx),
        "w_uq": w_uq, "w_uqsw": np.ascontiguousarray(w_uqsw),
        "w_uk": np.ascontiguousarray(w_uk), "w_uv": np.ascontiguousarray(w_uv),
        "w_oatt": f(inp["w_o_att"])[:L], "w_glu": f(inp["w_glu"])[:L], "w_out": f(inp["w_out"])[:L],
        "w_up": _interleave_up(f(inp["w_up"])[:L]), "w_down": f(inp["w_down"])[:L],
        "vecs": np.ascontiguousarray(vecs.reshape(128, L * NV)),
        "s5v": np.ascontiguousarray(s5v.reshape(128, L * 48)),
        "s5m": np.ascontiguousarray(s5m.reshape(L, 128, 2048)),
        "rope": _rope_tables(seq),
        "ident": np.eye(128, dtype=np.float32),
        "bmask": np.kron(np.eye(4, dtype=np.float32), np.ones((32, 32), np.float32)),
    }


_CACHE = {}


def kernel(**inputs):
    x = np.asarray(inputs["x"], dtype=np.float32)
    B, S, D = x.shape
    ncores = 8
    nseq = B // ncores
    depth = int(np.asarray(inputs["w_in"]).shape[0])
    key = (nseq, S, depth)
    if key not in _CACHE:
        _CACHE[key] = K(nseq=nseq, seq=S, depth=depth).build()
    nc = _CACHE[key]
    wts = prep_weights(inputs, depth, S)
    in_maps = []
    for c in range(ncores):
        xs = x[c * nseq:(c + 1) * nseq].reshape(nseq * S, D)
        m = dict(wts)
        m["xT"] = np.ascontiguousarray(xs.T)
        in_maps.append(m)
    res = run_bass_kernel_spmd(nc, in_maps, core_ids=list(range(ncores)))
    out = np.empty((B, S, D), np.float32)
    for c in range(ncores):
        yT = np.asarray(res.results[c]["yT"], dtype=np.float32)
        out[c * nseq:(c + 1) * nseq] = yT.T.reshape(nseq, S, D)
    return out
```

```python
import contextlib
import math
import numpy as np
import ml_dtypes
import concourse.bass as bass
import concourse.mybir as mybir
from concourse.bass_utils import run_bass_kernel_spmd
from concourse.alu_op_type import AluOpType as ALU

AF = mybir.ActivationFunctionType
F32 = mybir.dt.float32
BF16 = mybir.dt.bfloat16
I32 = mybir.dt.int32
ENGS = ['sp', 'act', 'pool', 'dve', 'pe']

D_MODEL = 1024
N_HEADS = 8
D_FF = 2816
NFC = 22
EPS = 1e-6
NV = 145
GPRE, BG, GQ, GKV, GPOST, GFPRE, GFPOST, CW, CB, DSK = 0, 8, 24, 27, 29, 37, 45, 53, 119, 141
WINX = 3264
ARENA_ELEMS = 196 * 512


class Res:
    def __init__(self, name, dsem=None, multi=False):
        self.name = name
        self.writers = {}
        self.readers = {}
        self.dsem = dsem
        self.nw = 0
        self.multi = multi
        self.a = None


class Op:
    __slots__ = ('eng', 'fn', 'deps', 'needed', 'sig', 'dma', 'key')


class Prog:
    def __init__(self, nc, st):
        self.nc = nc
        self.st = st
        self.ops = []
        self.esem = {e: st.enter_context(nc.semaphore("s_" + e)) for e in ENGS}
        self.allsems = list(self.esem.values())
        self.nsem = len(ENGS)
        self.last = {e: None for e in ENGS}
        self.dma_res = []
        self.bar_deps = []

    def res(self, name, dma=False, multi=False):
        dsem = None
        if dma:
            dsem = self.st.enter_context(self.nc.semaphore("d_" + name))
            self.allsems.append(dsem)
            self.nsem += 1
        r = Res(name, dsem, multi)
        if dma:
            self.dma_res.append(r)
            r.last_dma = None
        return r

    def sb(self, name, shape, dtype, dma=False, multi=False):
        r = self.res(name, dma=dma, multi=multi)
        t = self.st.enter_context(self.nc.sbuf_tensor(name, shape, dtype))
        r.a = t[:]
        return r

    def ps(self, name, shape, dtype):
        r = self.res(name)
        t = self.st.enter_context(self.nc.psum_tensor(name, shape, dtype))
        r.a = t[:]
        return r

    def barrier(self):
        deps = [o for o in self.last.values() if o is not None]
        for r in self.dma_res:
            if r.last_dma is not None:
                deps.append(r.last_dma)
        self.bar_deps = deps

    def add(self, eng, fn, reads=(), writes=(), dma=False):
        op = Op()
        op.eng = eng
        op.fn = fn
        op.dma = dma
        op.needed = False
        op.sig = None
        deps = {}
        for o in self.bar_deps:
            deps[id(o)] = o
        for r in reads:
            for o in r.writers.values():
                deps[id(o)] = o
        for w in writes:
            for o in w.writers.values():
                deps[id(o)] = o
            for o in w.readers.values():
                deps[id(o)] = o
        if dma:
            dres = [w for w in writes if w.dsem is not None]
            assert len(dres) == 1, [w.name for w in writes]
            dres[0].nw += 1
            op.sig = (dres[0].dsem, 16 * dres[0].nw)
            dres[0].last_dma = op
            key = id(dres[0].dsem)
        else:
            key = eng
            self.last[eng] = op
        op.key = key
        op.deps = [o for o in deps.values()
                   if not (o.eng == 'pe' and eng == 'pe' and not o.dma and not dma)]
        for o in op.deps:
            o.needed = True
        for r in reads:
            r.readers[key] = op
        for w in writes:
            if w.multi:
                w.writers[key] = op
            else:
                w.writers = {key: op}
                w.readers = {}
        self.ops.append(op)
        return op

    def finish(self, final_res):
        nc = self.nc
        self.add('sp', None, reads=final_res)
        cnt = {e: 0 for e in ENGS}
        for op in self.ops:
            if not op.dma and op.needed:
                cnt[op.eng] += 1
                op.sig = (self.esem[op.eng], cnt[op.eng])
        per = {e: [o for o in self.ops if o.eng == e] for e in ENGS}
        self.stats = {e: len(per[e]) for e in ENGS}

        def mk(e):
            def body(engobj):
                waited = {}
                for op in per[e]:
                    need = {}
                    for d in op.deps:
                        s, v = d.sig
                        k = id(s)
                        if v > need.get(k, (None, 0))[1]:
                            need[k] = (s, v)
                    for k, (s, v) in need.items():
                        if v > waited.get(k, 0):
                            engobj.wait_ge(s, v)
                            waited[k] = v
                    if op.fn is None:
                        continue
                    ins = op.fn(engobj)
                    if op.dma:
                        ins.then_inc(op.sig[0], 16)
                    elif op.needed:
                        ins.then_inc(op.sig[0], 1)
            return body

        import os
        if os.environ.get("NOCLEAR") != "1":
            for sm in self.allsems:
                nc.gpsimd.sem_clear(sm)
            nc.all_engine_barrier()
        with nc.Block() as block:
            block.sync(mk('sp'))
            block.scalar(mk('act'))
            block.gpsimd(mk('pool'))
            block.vector(mk('dve'))
            block.tensor(mk('pe'))


class Arena:
    def __init__(self, K, name):
        self.K = K
        self.name = name
        self.off = 0

    def alloc(self, name, shape, dtype, dma=False, multi=False, at=None):
        n = 1
        for s in shape:
            n *= s
        nel = n * (2 if dtype in (F32, I32) else 1)
        nel = (nel + 15) // 16 * 16
        off = self.off if at is None else at
        if at is None:
            self.off += nel
        assert off + nel <= ARENA_ELEMS, (self.name, name, off + nel, ARENA_ELEMS)
        v = self.K.arena_t[:, off:off + n * (2 if dtype in (F32, I32) else 1)]
        if dtype in (F32, I32):
            v = v.bitcast(dtype)
        if len(shape) >= 2:
            names = "abcdefg"[:len(shape)]
            pat = "p (" + " ".join(names) + ") -> p " + " ".join(names)
            v = v.rearrange(pat, **{n: sz for n, sz in zip(names[:-1], shape[:-1])})
        r = self.K.P.res(self.name + "_" + name, dma=dma, multi=multi)
        r.a = v
        r.off = off
        return r


class NS:
    pass


class K:
    def __init__(self, nseq=2, seq=2048, depth=4, dump=False, phases=None):
        self.phases = phases
        self.NSEQ = nseq
        self.S = seq
        self.L = depth
        self.T = nseq * seq
        self.NT = self.T // 512
        self.TPS = seq // 512
        self.NCH = self.T // 8
        self.CPS = seq // 8
        self.dump = dump
        self.wl_i = 0
        assert self.NCH <= 512

    def build(self):
        nc = bass.Bass("TRN2", target_bir_lowering=False)
        self.nc = nc
        L, T, S = self.L, self.T, self.S
        d = {}

        def inp(name, shape):
            d[name] = nc.dram_tensor(name, shape, F32, kind="ExternalInput").ap()

        inp("xT", [1024, T])
        inp("w_inx", [L, 1024, WINX])
        inp("w_uq", [L, 384, 768])
        inp("w_uqsw", [L, 384, 256])
        inp("w_uk", [L, 256, 512])
        inp("w_uv", [L, 256, 512])
        inp("w_oatt", [L, 512, 1024])
        inp("w_glu", [L, 512, 2048])
        inp("w_out", [L, 1024, 1024])
        inp("w_up", [L, 1024, 2 * D_FF])
        inp("w_down", [L, D_FF, 1024])
        inp("vecs", [128, L * NV])
        inp("s5v", [128, L * 48])
        inp("s5m", [L, 128, 2048])
        inp("rope", [2, 128, S])
        inp("ident", [128, 128])
        inp("bmask", [128, 128])
        d["yT"] = nc.dram_tensor("yT", [1024, T], F32, kind="ExternalOutput").ap()
        skind = "ExternalOutput" if self.dump else "Internal"

        def scr(name, shape, dt):
            d[name] = nc.dram_tensor(name, shape, dt, kind=skind).ap()

        scr("s1", [1024, T], F32)
        scr("s2", [1024, T], F32)
        scr("qT", [8, 96, T], BF16)
        scr("kT", [8, 96, T], BF16)
        scr("vA", [T, 768], BF16)
        scr("uT", [512, T], BF16)
        scr("gT", [2048, T], BF16)
        scr("oT", [512, T], BF16)
        scr("ysT", [512, T], BF16)
        scr("actT", [D_FF, T], BF16)
        self.d = d
        with contextlib.ExitStack() as st:
            P = Prog(nc, st)
            self.P = P
            self.dr = {n: P.res("dr_" + n, dma=True, multi=True) for n in
                       ["yT", "s1", "s2", "qT", "kT", "vA", "uT", "gT", "oT", "ysT", "actT"]}
            self.dr["xT"] = P.res("dr_xT")
            self.arena_t = st.enter_context(nc.sbuf_tensor("arena", [128, ARENA_ELEMS], BF16))
            self.psb = [P.ps("psb%d" % i, [128, 512], F32) for i in range(8)]
            self.vecs = P.sb("vecs_sb", [128, L * NV], F32, dma=True)
            self.s5v = P.sb("s5v_sb", [128, L * 48], F32, dma=True)
            self.ident = P.sb("identb", [128, 128], BF16, dma=True)
            self.ones = P.sb("onesb", [128, 128], BF16)
            self.bmask = P.sb("bmask_sb", [128, 128], F32, dma=True)
            P.add('sp', lambda e: e.dma_start(out=self.bmask.a, in_=d["bmask"]), writes=[self.bmask], dma=True)
            self.epst = P.sb("epst", [128, 1], F32)
            P.add('sp', lambda e: e.dma_start(out=self.vecs.a, in_=d["vecs"]), writes=[self.vecs], dma=True)
            P.add('sp', lambda e: e.dma_start(out=self.s5v.a, in_=d["s5v"]), writes=[self.s5v], dma=True)
            P.add('pool', lambda e: e.dma_start(out=self.ident.a, in_=d["ident"]), writes=[self.ident], dma=True)
            P.add('dve', lambda e: e.memset(self.ones.a, 1.0), writes=[self.ones])
            P.add('dve', lambda e: e.memset(self.epst.a, EPS), writes=[self.epst])
            self.mk_arenas()
            nupd = 2 * L
            for l in range(L):
                for half in range(2):
                    u = 2 * l + half
                    src = "xT" if u == 0 else ("s1", "s2")[(u - 1) % 2]
                    dst = "yT" if u == nupd - 1 else ("s1", "s2")[u % 2]
                    on = lambda ph: self.phases is None or ph in self.phases
                    if half == 0:
                        if on('m1'):
                            self.m1(l, src)
                            P.barrier()
                        if on('m3'):
                            self.m3_prep(l)
                        if on('m2'):
                            self.m2(l)
                            P.barrier()
                        if on('m3'):
                            self.m3(l)
                            P.barrier()
                        if on('m4'):
                            self.m4(l, src, dst)
                            P.barrier()
                    else:
                        if on('f1'):
                            self.f1(l, src)
                            P.barrier()
                        if on('f2'):
                            self.f2(l, src, dst)
                            P.barrier()
            P.finish([self.dr["yT"]])
        return nc

    def vec(self, l, off, n=1):
        return self.vecs.a[:, l * NV + off:l * NV + off + n]

    def mk_arenas(self):
        T, S, NCH = self.T, self.S, self.NCH
        A = Arena(self, "m1")
        a = NS()
        a.win = A.alloc("win", [8, WINX], BF16, multi=True)
        self.wblocks(a.win, WINX, 1024)
        a.wuq = A.alloc("wuq", [3, 768], BF16, multi=True)
        a.wuqsw = A.alloc("wuqsw", [3, 256], BF16, multi=True)
        a.wuk = A.alloc("wuk", [2, 512], BF16, multi=True)
        a.wuv = A.alloc("wuv", [2, 512], BF16, multi=True)
        a.stg = [A.alloc("stg%d" % i, [1024], F32, dma=True) for i in range(4)]
        a.X = [A.alloc("X%d" % i, [8, 512], F32, dma=True) for i in range(2)]
        a.cs = [A.alloc("cs%d" % i, [2, 512], F32, dma=True) for i in range(2)]
        a.xsq = A.alloc("xsq", [8, 512], BF16)
        a.xg = A.alloc("xg", [8, 512], BF16)
        a.rt = A.alloc("rt", [512], F32)
        a.rstd = A.alloc("rstd", [512], F32)
        a.cq = A.alloc("cq", [3, 512], F32)
        a.ckv = A.alloc("ckv", [2, 512], F32)
        a.sq = A.alloc("sq", [3, 512], BF16)
        a.rq = A.alloc("rq", [512], F32)
        a.cqn = A.alloc("cqn", [3, 512], BF16)
        a.ckvn = A.alloc("ckvn", [2, 512], BF16)
        a.tmp = [A.alloc("tmp%d" % i, [512], F32) for i in range(4)]
        a.kp = A.alloc("kp", [512], F32)
        a.kps = A.alloc("kps", [512], F32)
        a.kr = A.alloc("kr", [512], BF16)
        a.ut = A.alloc("ut", [4, 512], BF16)
        a.gts = [A.alloc("gts%d" % i, [4, 512], BF16) for i in range(2)]
        a.qh = [A.alloc("qh%d" % i, [512], BF16) for i in range(3)]
        a.kh = [A.alloc("kh%d" % i, [512], BF16) for i in range(3)]
        a.vt = A.alloc("vt", [4, 8, 96], BF16)
        self.A1 = a
        A = Arena(self, "m2")
        a = NS()
        a.q = [A.alloc("q%d" % i, [S], BF16, dma=True) for i in range(2)]
        a.k = [A.alloc("k%d" % i, [S], BF16, dma=True) for i in range(2)]
        a.v = [A.alloc("v%d" % i, [S // 128, 768], BF16, dma=True) for i in range(2)]
        a.pT = [A.alloc("pT%d" % i, [512], BF16) for i in range(3)]
        a.rc = [A.alloc("rc%d" % i, [512], F32) for i in range(3)]
        a.hi = [A.alloc("hi%d" % i, [512], BF16) for i in range(3)]
        a.lo = [A.alloc("lo%d" % i, [512], BF16) for i in range(3)]
        a.bc = [A.alloc("bc%d" % i, [512], F32) for i in range(2)]
        a.ot = [A.alloc("ot%d" % i, [512], BF16) for i in range(2)]
        self.A2 = a
        A = Arena(self, "m3")
        a = NS()
        a.u = A.alloc("u", [4, T], BF16, dma=True, multi=True)
        a.Sb = A.alloc("Sb", [16, 2, NCH], BF16)
        a.Xb = A.alloc("Xb", [16, 2, NCH], BF16)
        a.Wp = A.alloc("Wp", [4, 8, 2, 128], BF16)
        a.PCb = A.alloc("PCb", [9, 2, 16, 32], BF16)
        a.Kb = A.alloc("Kb", [4, 8, 128], BF16)
        a.BbQb = A.alloc("BbQb", [2, 16, 32], BF16)
        a.m = A.alloc("m", [4, 16, 32], F32, dma=True)
        a.Bb = A.alloc("Bb", [2, 16, 32], F32)
        a.Pw = A.alloc("Pw", [9, 2, 16], F32)
        a.sm = [A.alloc("sm%d" % i, [16], F32) for i in range(14)]
        a.smi = A.alloc("smi", [16], I32)
        a.fc = A.alloc("fcoef", [2, 16], F32)
        a.nAi = A.alloc("nAi", [16], F32)
        a.nPr = A.alloc("nPr", [9, 16], F32)
        a.td = [A.alloc("td%d" % i, [16, 32], F32) for i in range(4)]
        a.tp = [A.alloc("tp%d" % i, [16, 32], F32) for i in range(4)]
        a.xs = [A.alloc("xs%d" % i, [16, 2, self.NSEQ], F32) for i in range(2)]
        a.t1 = A.alloc("t1", [16, 2, self.NSEQ], F32)
        a.t2 = A.alloc("t2", [16, 2, self.NSEQ], F32)
        off_ys = A.off
        a.ys = A.alloc("ys", [T], BF16)
        a.PBb = A.alloc("PBb", [4, 8, 2, 4, 32], BF16, at=off_ys)
        if 16 * 8 * 2 * 32 > T:
            A.off = off_ys + 16 * 8 * 2 * 32
        self.A3 = a
        A = Arena(self, "m4")
        a = NS()
        a.woatt = A.alloc("woatt", [4, 1024], BF16, multi=True)
        a.stg = [A.alloc("stg%d" % i, [1024], F32, dma=True) for i in range(4)]
        a.wglu = A.alloc("wglu", [4, 2048], BF16, multi=True)
        a.wout = A.alloc("wout", [8, 1024], BF16, multi=True)
        a.o = [A.alloc("o%d" % i, [4, 512], BF16, dma=True) for i in range(2)]
        a.ysb = [A.alloc("ysb%d" % i, [4, 512], BF16, dma=True) for i in range(2)]
        a.g = [A.alloc("g%d" % i, [16, 512], BF16, dma=True) for i in range(2)]
        a.X = [A.alloc("X%d" % i, [8, 512], F32, dma=True) for i in range(2)]
        a.sg = [A.alloc("sg%d" % i, [512], F32) for i in range(2)]
        a.yss = [A.alloc("yss%d" % i, [512], F32) for i in range(2)]
        a.m1 = [A.alloc("m1%d" % i, [512], F32) for i in range(2)]
        a.m2 = [A.alloc("m2%d" % i, [512], F32) for i in range(2)]
        a.mg = A.alloc("mg", [8, 512], BF16)
        a.Y = A.alloc("Y", [8, 512], F32)
        a.ysq = A.alloc("ysq", [8, 512], BF16)
        a.rt = A.alloc("rt", [512], F32)
        a.rstd = A.alloc("rstd", [512], F32)
        a.tt = [A.alloc("tt%d" % i, [512], F32) for i in range(2)]
        self.A4 = a
        A = Arena(self, "f1")
        a = NS()
        rstdF = A.alloc("rstdF", [T], F32)
        a.rstdF = rstdF
        a.wup = A.alloc("wup", [8, 2 * D_FF], BF16, multi=True)
        a.X = [A.alloc("X%d" % i, [8, 512], F32, dma=True) for i in range(2)]
        a.sqg = A.alloc("sqg", [8, 512], BF16)
        a.rt = A.alloc("rt", [512], F32)
        a.Gb = [A.alloc("Gb%d" % i, [514], BF16) for i in range(3)]
        a.Gh = A.alloc("Gh", [NFC, 2], BF16)
        a.ge = [A.alloc("ge%d" % i, [512], BF16) for i in range(3)]
        a.dgall = A.alloc("dgall", [NFC, 3, 128], BF16)
        a.act = [A.alloc("act%d" % i, [NFC // 2, 512], BF16) for i in range(2)]
        a.stg = [A.alloc("stg%d" % i, [1024], F32, dma=True, at=a.act[i // 2].off + (i % 2) * 2048) for i in range(4)]
        self.wblocks(a.wup, 2 * D_FF, 1024)
        self.A5 = a
        A = Arena(self, "f2")
        a = NS()
        A.alloc("rstdF_pad", [T], F32)
        a.rstdF = rstdF
        a.wdown = A.alloc("wdown", [NFC, 1024], BF16, multi=True)
        a.stg = [A.alloc("stg%d" % i, [512], F32, dma=True) for i in range(4)]
        self.wblocks(a.wdown, 1024, 512)
        a.a = [A.alloc("a%d" % i, [NFC, 512], BF16, dma=True) for i in range(2)]
        a.X = [A.alloc("X%d" % i, [8, 512], F32, dma=True) for i in range(2)]
        a.Y = A.alloc("Y", [8, 512], F32)
        a.ysq = A.alloc("ysq", [8, 512], BF16)
        a.rt = A.alloc("rt", [512], F32)
        a.rstd = A.alloc("rstd", [512], F32)
        a.tt = [A.alloc("tt%d" % i, [512], F32) for i in range(2)]
        self.A6 = a

    def wblocks(self, dst, n, W):
        nb = (n + W - 1) // W
        rs = []
        for b in range(nb):
            r = self.P.res(dst.name + "_b%d" % b, multi=True)
            r.a = dst.a
            rs.append(r)
        dst.blocks = rs
        dst.W = W
        return rs

    def wr(self, dst, c0, c1):
        if not hasattr(dst, 'blocks'):
            return [dst]
        return dst.blocks[c0 // dst.W:(c1 - 1) // dst.W + 1]

    def load_w(self, dst, src2d, kc, n, stg):
        W = stg[0].a.shape[-1]
        if hasattr(dst, 'blocks'):
            assert dst.W % W == 0 or W % dst.W == 0
            W = min(W, dst.W)
        for n0 in range(0, n, W):
            n1 = min(n, n0 + W)
            wres = self.wr(dst, n0, n1)
            assert len(wres) == 1
            for c in range(kc):
                i = self.wl_i
                self.wl_i += 1
                sl = stg[i % len(stg)]
                self.dma('sp', sl, sl.a[:, 0:n1 - n0], None, src2d[c * 128:(c + 1) * 128, n0:n1])
                if i % 2 == 0:
                    self.act(wres[0], dst.a[:, c, n0:n1], sl, sl.a[:, 0:n1 - n0], AF.Copy)
                else:
                    self.cp('dve', wres[0], dst.a[:, c, n0:n1], sl, sl.a[:, 0:n1 - n0])

    def mm(self, outr, out_ap, lr, lhsT, rr, rhs, start, stop, tp=None):
        if tp is None:
            fn = lambda e: e.matmul(out_ap, lhsT, rhs, start=start, stop=stop)
        else:
            fn = lambda e: e.matmul(out_ap, lhsT, rhs, start=start, stop=stop, tile_position=tp)
        rd = (lr if isinstance(lr, list) else [lr]) + (rr if isinstance(rr, list) else [rr])
        self.P.add('pe', fn, reads=rd, writes=[outr])

    def act(self, outr, out_ap, inr, in_ap, func, bias=None, scale=None, extra_reads=()):
        kw = {}
        if bias is not None:
            kw['bias'] = bias
        if scale is not None:
            kw['scale'] = scale
        self.P.add('act', lambda e: e.activation(out=out_ap, in_=in_ap, func=func, **kw),
                   reads=[inr] + list(extra_reads), writes=[outr])

    def tt(self, eng, outr, out_ap, r0, in0, r1, in1, op):
        self.P.add(eng, lambda e: e.tensor_tensor(out=out_ap, in0=in0, in1=in1, op=op),
                   reads=[r0, r1], writes=[outr])

    def ts(self, eng, outr, out_ap, r0, in0, s1, s2, op0, op1=None, extra_reads=()):
        if op1 is None:
            fn = lambda e: e.tensor_scalar(out=out_ap, in0=in0, scalar1=s1, scalar2=None, op0=op0)
        else:
            fn = lambda e: e.tensor_scalar(out=out_ap, in0=in0, scalar1=s1, scalar2=s2, op0=op0, op1=op1)
        self.P.add(eng, fn, reads=[r0] + list(extra_reads), writes=[outr])

    def stt(self, outr, out_ap, r0, in0, scalar, r1, in1, op0, op1, extra_reads=()):
        self.P.add('dve', lambda e: e.scalar_tensor_tensor(out=out_ap, in0=in0, scalar=scalar, in1=in1,
                                                           op0=op0, op1=op1),
                   reads=[r0, r1] + list(extra_reads), writes=[outr])

    def cp(self, eng, outr, out_ap, inr, in_ap):
        self.P.add(eng, lambda e: e.tensor_copy(out=out_ap, in_=in_ap), reads=[inr], writes=[outr])

    def dma(self, eng, outr, out_ap, inr, in_ap):
        self.P.add(eng, lambda e: e.dma_start(out=out_ap, in_=in_ap),
                   reads=[inr] if inr is not None else [], writes=[outr], dma=True)

    def rms_rstd(self, ps, sq_res, sq_ap_fn, nchunk, dim, rt, rstd):
        for c in range(nchunk):
            self.mm(ps, ps.a, self.ones, self.ones.a, sq_res, sq_ap_fn(c), c == 0, c == nchunk - 1)
        self.act(rt, rt.a, ps, ps.a, AF.Sqrt, bias=self.epst.a[:, 0:1], scale=1.0 / dim, extra_reads=[self.epst])
        self.P.add('dve', lambda e: e.reciprocal(out=rstd.a, in_=rt.a), reads=[rt], writes=[rstd])

    def m1(self, l, src):
        P, a, d, psb = self.P, self.A1, self.d, self.psb
        xin = d[src].rearrange("(c p) t -> p c t", p=128)
        xr = self.dr[src]
        P.add('pool', lambda e: e.memset(a.vt.a[:, :, :, 64:96], 1.0), writes=[a.vt])
        ropev = d["rope"].rearrange("k p s -> p k s")

        def load(tt):
            X = a.X[tt % 2]
            cs = a.cs[tt % 2]
            self.dma('sp', X, X.a, xr, xin[:, :, tt * 512:(tt + 1) * 512])
            p0 = (tt % self.TPS) * 512
            self.dma('sp', cs, cs.a[64:96, :, :], None, ropev[64:96, :, p0:p0 + 512])

        load(0)
        self.load_w(a.win, d["w_inx"][l], 8, WINX, a.stg)
        self.load_w(a.wuq, d["w_uq"][l], 3, 768, a.stg)
        self.load_w(a.wuqsw, d["w_uqsw"][l], 3, 256, a.stg)
        self.load_w(a.wuk, d["w_uk"][l], 2, 512, a.stg)
        self.load_w(a.wuv, d["w_uv"][l], 2, 512, a.stg)
        zb = [1, 2, 3, 4]
        zi = [0]

        def nextbank():
            b = psb[zb[zi[0] % 4]]
            zi[0] += 1
            return b

        for tt in range(self.NT):
            if tt + 1 < self.NT:
                load(tt + 1)
            X, cs = a.X[tt % 2], a.cs[tt % 2]
            t0 = tt * 512
            cosr = cs.a[64:96, 0, :]
            sinr = cs.a[64:96, 1, :]
            self.act(a.xsq, a.xsq.a, X, X.a, AF.Square)
            for dc in range(8):
                self.ts('dve', a.xg, a.xg.a[:, dc, :], X, X.a[:, dc, :], self.vec(l, GPRE + dc), None, ALU.mult,
                        extra_reads=[self.vecs])
            self.rms_rstd(psb[0], a.xsq, lambda c: a.xsq.a[:, c, :], 8, 1024.0, a.rt, a.rstd)

            def zchunk(ps, c0, m, tp=None, out_ap=None):
                oa = ps.a[0:m, :] if out_ap is None else out_ap
                for dc in range(8):
                    self.mm(ps, oa, self.wr(a.win, c0, c0 + m), a.win.a[:, dc, c0:c0 + m], a.xg, a.xg.a[:, dc, :], dc == 0, dc == 7, tp)

            for c in range(3):
                ps = nextbank()
                zchunk(ps, c * 128, 128)
                self.tt('dve', a.cq, a.cq.a[:, c, :], ps, ps.a, a.rstd, a.rstd.a, ALU.mult)
            for c in range(2):
                ps = nextbank()
                zchunk(ps, 384 + c * 128, 128)
                self.tt('dve', a.ckv, a.ckv.a[:, c, :], ps, ps.a, a.rstd, a.rstd.a, ALU.mult)
            self.act(a.sq, a.sq.a, a.cq, a.cq.a, AF.Square)
            self.rms_rstd(psb[7], a.sq, lambda c: a.sq.a[:, c, :], 3, 384.0, a.rt, a.rq)
            for c in range(3):
                self.stt(a.cqn, a.cqn.a[:, c, :], a.cq, a.cq.a[:, c, :], self.vec(l, GQ + c), a.rq, a.rq.a,
                         ALU.mult, ALU.mult, extra_reads=[self.vecs])
            self.act(a.sq, a.sq.a[:, 0:2, :], a.ckv, a.ckv.a, AF.Square)
            self.rms_rstd(psb[7], a.sq, lambda c: a.sq.a[:, c, :], 2, 256.0, a.rt, a.rq)
            for c in range(2):
                self.stt(a.ckvn, a.ckvn.a[:, c, :], a.ckv, a.ckv.a[:, c, :], self.vec(l, GKV + c), a.rq, a.rq.a,
                         ALU.mult, ALU.mult, extra_reads=[self.vecs])
            zchunk(psb[5], 640, 32, tp=(0, 64), out_ap=psb[5].a[64:96, :])
            zchunk(psb[6], 672, 32, tp=(0, 64), out_ap=psb[6].a[64:96, :])
            self.tt('dve', a.kp, a.kp.a[64:96, :], psb[5], psb[5].a[64:96, :], a.rstd, a.rstd.a[64:96, :], ALU.mult)
            self.tt('dve', a.kps, a.kps.a[64:96, :], psb[6], psb[6].a[64:96, :], a.rstd, a.rstd.a[64:96, :], ALU.mult)
            self.tt('dve', a.kp, a.kp.a[64:96, :], a.kp, a.kp.a[64:96, :], cs, cosr, ALU.mult)
            self.tt('dve', a.kps, a.kps.a[64:96, :], a.kps, a.kps.a[64:96, :], cs, sinr, ALU.mult)
            self.tt('pool', a.kr, a.kr.a[64:96, :], a.kp, a.kp.a[64:96, :], a.kps, a.kps.a[64:96, :], ALU.add)
            for c in range(4):
                ps = nextbank()
                zchunk(ps, 704 + c * 128, 128)
                self.tt('dve', a.ut, a.ut.a[:, c, :].rearrange("p (j cl) -> p cl j", j=8),
                        ps, ps.a.rearrange("p (cl j) -> p cl j", j=8),
                        a.rstd, a.rstd.a.rearrange("p (cl j) -> p cl j", j=8), ALU.mult)
            for c in range(4):
                self.dma('sp', self.dr["uT"],
                         d["uT"][c * 128:(c + 1) * 128, :].rearrange("p (j n) -> p j n", j=8)[:, :, tt * 64:(tt + 1) * 64],
                         a.ut, a.ut.a[:, c, :].rearrange("p (j cl) -> p j cl", j=8))
            for gb in range(4):
                gts = a.gts[gb % 2]
                for gi in range(4):
                    gc = gb * 4 + gi
                    ps = nextbank()
                    zchunk(ps, 1216 + gc * 128, 128)
                    tmp = a.tmp[gc % 4]
                    self.tt('dve', tmp, tmp.a, ps, ps.a, a.rstd, a.rstd.a, ALU.mult)
                    self.act(gts, gts.a[:, gi, :], tmp, tmp.a, AF.Sigmoid, bias=self.vec(l, BG + gc),
                             extra_reads=[self.vecs])
                self.dma('sp', self.dr["gT"],
                         d["gT"].rearrange("(c p) t -> p c t", p=128)[:, gb * 4:(gb + 1) * 4, t0:t0 + 512],
                         gts, gts.a)
            for h in range(8):
                ps = nextbank()
                psw = psb[5 + h % 2]
                for c in range(3):
                    self.mm(ps, ps.a[0:96, :], a.wuq, a.wuq.a[:, c, 96 * h:96 * h + 96], a.cqn, a.cqn.a[:, c, :],
                            c == 0, c == 2)
                for c in range(3):
                    self.mm(psw, psw.a[64:96, :], a.wuqsw, a.wuqsw.a[:, c, 32 * h:32 * h + 32], a.cqn,
                            a.cqn.a[:, c, :], c == 0, c == 2, tp=(0, 64))
                qh = a.qh[h % 3]
                t1, t2 = a.tmp[(2 * h) % 4], a.tmp[(2 * h + 1) % 4]
                self.act(qh, qh.a[0:64, :], ps, ps.a[0:64, :], AF.Copy)
                self.tt('dve', t1, t1.a[64:96, :], ps, ps.a[64:96, :], cs, cosr, ALU.mult)
                self.tt('dve', t2, t2.a[64:96, :], psw, psw.a[64:96, :], cs, sinr, ALU.mult)
                self.tt('pool', qh, qh.a[64:96, :], t1, t1.a[64:96, :], t2, t2.a[64:96, :], ALU.add)
                self.dma('sp', self.dr["qT"], d["qT"][h, :, t0:t0 + 512], qh, qh.a[0:96, :])
            for h in range(8):
                ps = nextbank()
                for c in range(2):
                    self.mm(ps, ps.a[0:64, :], a.wuk, a.wuk.a[:, c, 64 * h:64 * h + 64], a.ckvn, a.ckvn.a[:, c, :],
                            c == 0, c == 1)
                kh = a.kh[h % 3]
                self.act(kh, kh.a[0:64, :], ps, ps.a[0:64, :], AF.Copy)
                self.cp('pool', kh, kh.a[64:96, :], a.kr, a.kr.a[64:96, :])
                self.dma('sp', self.dr["kT"], d["kT"][h, :, t0:t0 + 512], kh, kh.a[0:96, :])
            for tb in range(4):
                ps = nextbank()
                for c in range(2):
                    self.mm(ps, ps.a, a.ckvn, a.ckvn.a[:, c, tb * 128:(tb + 1) * 128], a.wuv, a.wuv.a[:, c, :],
                            c == 0, c == 1)
                self.cp('dve' if tb % 2 else 'act', a.vt, a.vt.a[:, tb, :, 0:64],
                        ps, ps.a.rearrange("p (h c) -> p h c", h=8)) if tb % 2 else \
                    self.act(a.vt, a.vt.a[:, tb, :, 0:64], ps, ps.a.rearrange("p (h c) -> p h c", h=8), AF.Copy)
            self.dma('sp', self.dr["vA"],
                     d["vA"].rearrange("(n p) c -> p n c", p=128)[:, 4 * tt:4 * tt + 4, :],
                     a.vt, a.vt.a.rearrange("p n h c -> p n (h c)"))

    def m2(self, l):
        P, a, d, psb = self.P, self.A2, self.d, self.psb
        S = self.S
        scale = 1.0 / math.sqrt(96.0)
        vAv = d["vA"].rearrange("(n p) c -> p n c", p=128)
        nsb = S // 128
        pairs = [(s, h) for s in range(self.NSEQ) for h in range(8)]

        def loadv(s):
            v = a.v[s % 2]
            self.dma('sp', v, v.a, self.dr["vA"], vAv[:, s * nsb:(s + 1) * nsb, :])

        def loadqk(i):
            s, h = pairs[i]
            q, k = a.q[i % 2], a.k[i % 2]
            self.dma('sp', q, q.a[0:96, :], self.dr["qT"], d["qT"][h, :, s * S:(s + 1) * S])
            self.dma('sp', k, k.a[0:96, :], self.dr["kT"], d["kT"][h, :, s * S:(s + 1) * S])

        blocks = []
        g = 0
        for i, (s, h) in enumerate(pairs):
            for qa in range(S // 512):
                nblk = 4 * qa + 4
                for j in range(nblk):
                    blocks.append((i, s, h, qa, j, nblk, g))
                g += 1
        N = len(blocks)

        def geom(b):
            i, s, h, qa, j, nblk, g = b
            r = j - 4 * qa
            qoff = 128 * r if r > 0 else 0
            return r, qoff, 512 - qoff

        def emitS(idx):
            i, s, h, qa, j, nblk, g = blocks[idx]
            r, qoff, nq = geom(blocks[idx])
            q, k = a.q[i % 2], a.k[i % 2]
            pss = psb[idx % 3]
            self.mm(pss, pss.a[:, 0:nq], k, k.a[0:96, j * 128:(j + 1) * 128],
                    q, q.a[0:96, qa * 512 + qoff:qa * 512 + 512], True, True)

        def emitEP(idx):
            r, qoff, nq = geom(blocks[idx])
            pss, pT = psb[idx % 3], a.pT[idx % 3]
            self.act(pT, pT.a[:, 0:nq], pss, pss.a[:, 0:nq], AF.Exp, scale=scale)
            if r >= 0:
                P.add('dve', lambda e, pT=pT: e.memset(pT.a[64:128, 0:64], 0.0), writes=[pT])

        def emitPV(idx):
            i, s, h, qa, j, nblk, g = blocks[idx]
            r, qoff, nq = geom(blocks[idx])
            v, pT, po = a.v[s % 2], a.pT[idx % 3], psb[POB[g % 3]]
            self.mm(po, po.a[0:96, qoff:512], v, v.a[:, j, h * 96:(h + 1) * 96], pT, pT.a[:, 0:nq],
                    j == 0, j == nblk - 1)

        def tail_front(b):
            g = b[6]
            po, rc, hi, lo = psb[POB[g % 3]], a.rc[g % 3], a.hi[g % 3], a.lo[g % 3]
            P.add('dve', lambda e, rc=rc, po=po: e.reciprocal(out=rc.a[64:65, :], in_=po.a[64:65, :]),
                  reads=[po], writes=[rc])
            self.cp('dve', hi, hi.a[64:65, :], rc, rc.a[64:65, :])
            self.tt('dve', lo, lo.a[64:65, :], rc, rc.a[64:65, :], hi, hi.a[64:65, :], ALU.subtract)

        def tail_back(b):
            i, s, h, qa, j, nblk, g = b
            po, pb = psb[POB[g % 3]], psb[5 + g % 2]
            hi, lo, bc, ot = a.hi[g % 3], a.lo[g % 3], a.bc[g % 2], a.ot[g % 2]
            self.mm(pb, pb.a[0:64, :], self.ones, self.ones.a[64:65, 0:64], hi, hi.a[64:65, :], True, False,
                    tp=(64, 0))
            self.mm(pb, pb.a[0:64, :], self.ones, self.ones.a[64:65, 0:64], lo, lo.a[64:65, :], False, True,
                    tp=(64, 0))
            self.act(bc, bc.a[0:64, :], pb, pb.a[0:64, :], AF.Copy)
            self.tt('dve', ot, ot.a[0:64, :], po, po.a[0:64, :], bc, bc.a[0:64, :], ALU.mult)
            tq = s * S + qa * 512
            self.dma('sp', self.dr["oT"], d["oT"][h * 64:(h + 1) * 64, tq:tq + 512], ot, ot.a[0:64, :])

        POB = [3, 4, 7]
        DEFER = 6
        loadv(0)
        loadqk(0)
        if len(pairs) > 1:
            loadqk(1)
        emitS(0)
        if N > 1:
            emitS(1)
        pending = []
        for idx in range(N):
            b = blocks[idx]
            i, s, h, qa, j, nblk, g = b
            if qa == 0 and j == 0:
                if h == 0 and s + 1 < self.NSEQ:
                    loadv(s + 1)
            emitEP(idx)
            if idx + 2 < N:
                b2 = blocks[idx + 2]
                if b2[3] == 0 and b2[4] == 0 and b2[0] + 1 < len(pairs) and b2[0] >= 1:
                    pass
                emitS(idx + 2)
            emitPV(idx)
            pending = [(pb_, c_ - 1) for (pb_, c_) in pending]
            while pending and pending[0][1] <= 0:
                tail_back(pending.pop(0)[0])
            if j == nblk - 1:
                tail_front(b)
                pending.append((b, DEFER))
                if qa == S // 512 - 1 and i + 2 < len(pairs):
                    loadqk(i + 2)
        for pb_, c_ in pending:
            tail_back(pb_)

    def m3_prep(self, l):
        P, a, d, psb = self.P, self.A3, self.d, self.psb
        T, NCH, CPS, NSEQ = self.T, self.NCH, self.CPS, self.NSEQ
        sv = self.s5v.a[:, l * 48:(l + 1) * 48]
        lre, lim, lst = sv[:, 0:16], sv[:, 16:32], sv[:, 32:48]
        svr = self.s5v
        self.dma('sp', a.m, a.m.a, None, d["s5m"][l].rearrange("p (k g c) -> p k g c", k=4, g=16))
        sm = a.sm
        dv, ac = 'dve', 'act'
        delta, lrd, mag, ang, rr, kf, fr, s1, s2, ch, tA, tB, ca, den = sm
        self.act(delta, delta.a, svr, lst, AF.Exp)
        self.tt(dv, lrd, lrd.a, svr, lre, delta, delta.a, ALU.mult)
        self.act(mag, mag.a, lrd, lrd.a, AF.Exp)
        self.tt(dv, ang, ang.a, svr, lim, delta, delta.a, ALU.mult)
        self.ts(dv, rr, rr.a, ang, ang.a, 1.0 / (2.0 * math.pi), None, ALU.mult)
        self.cp(dv, a.smi, a.smi.a, rr, rr.a)
        self.cp(dv, kf, kf.a, a.smi, a.smi.a)
        self.tt(dv, fr, fr.a, rr, rr.a, kf, kf.a, ALU.subtract)
        self.act(s1, s1.a, fr, fr.a, AF.Sin, scale=math.pi)
        self.act(s2, s2.a, fr, fr.a, AF.Sin, scale=math.pi / 2.0)
        self.tt(dv, tA, tA.a, s2, s2.a, s2, s2.a, ALU.mult)
        self.ts(dv, ch, ch.a, tA, tA.a, -2.0, 1.0, ALU.mult, ALU.add)
        self.tt(dv, tA, tA.a, s1, s1.a, ch, ch.a, ALU.mult)
        Pw = a.Pw
        self.stt(Pw, Pw.a[:, 1, 1, :], tA, tA.a, 2.0, mag, mag.a, ALU.mult, ALU.mult)
        self.tt(dv, tB, tB.a, s1, s1.a, s1, s1.a, ALU.mult)
        self.ts(dv, ca, ca.a, tB, tB.a, -2.0, 1.0, ALU.mult, ALU.add)
        self.tt(dv, Pw, Pw.a[:, 1, 0, :], ca, ca.a, mag, mag.a, ALU.mult)
        P.add(dv, lambda e: e.memset(Pw.a[:, 0, 0, :], 1.0), writes=[Pw])
        P.add(dv, lambda e: e.memset(Pw.a[:, 0, 1, :], 0.0), writes=[Pw])
        ar, ai = Pw.a[:, 1, 0, :], Pw.a[:, 1, 1, :]
        nr = s1
        self.ts(dv, nr, nr.a, Pw, ar, -1.0, None, ALU.add)
        self.tt(dv, tA, tA.a, svr, lre, svr, lre, ALU.mult)
        self.tt(dv, tB, tB.a, svr, lim, svr, lim, ALU.mult)
        self.tt(dv, den, den.a, tA, tA.a, tB, tB.a, ALU.add)
        P.add(dv, lambda e: e.reciprocal(out=den.a, in_=den.a), reads=[den], writes=[den])
        self.tt(dv, tA, tA.a, nr, nr.a, svr, lre, ALU.mult)
        self.tt(dv, tB, tB.a, Pw, ai, svr, lim, ALU.mult)
        self.tt(dv, tA, tA.a, tA, tA.a, tB, tB.a, ALU.add)
        self.tt(dv, a.fc, a.fc.a[:, 0, :], tA, tA.a, den, den.a, ALU.mult)
        self.tt(dv, tA, tA.a, Pw, ai, svr, lre, ALU.mult)
        self.tt(dv, tB, tB.a, nr, nr.a, svr, lim, ALU.mult)
        self.tt(dv, tA, tA.a, tA, tA.a, tB, tB.a, ALU.subtract)
        self.tt(dv, a.fc, a.fc.a[:, 1, :], tA, tA.a, den, den.a, ALU.mult)
        for k in range(2, 9):
            pr, pi = Pw.a[:, k - 1, 0, :], Pw.a[:, k - 1, 1, :]
            self.tt(dv, tA, tA.a, Pw, pr, Pw, ar, ALU.mult)
            self.tt(dv, tB, tB.a, Pw, pi, Pw, ai, ALU.mult)
            self.tt(dv, Pw, Pw.a[:, k, 0, :], tA, tA.a, tB, tB.a, ALU.subtract)
            self.tt(dv, tA, tA.a, Pw, pr, Pw, ai, ALU.mult)
            self.tt(dv, tB, tB.a, Pw, pi, Pw, ar, ALU.mult)
            self.tt(dv, Pw, Pw.a[:, k, 1, :], tA, tA.a, tB, tB.a, ALU.add)
        self.ts(dv, a.nAi, a.nAi.a, Pw, Pw.a[:, 8, 1, :], -1.0, None, ALU.mult)

        def bc32(ap16):
            return ap16.unsqueeze(2).broadcast_to([128, 16, 32])

        Cr, Ci, Br, Bi = a.m.a[:, 0], a.m.a[:, 1], a.m.a[:, 2], a.m.a[:, 3]
        fre, fim = bc32(a.fc.a[:, 0, :]), bc32(a.fc.a[:, 1, :])
        td, tp = a.td, a.tp
        for k in range(9):
            self.ts(dv, a.nPr, a.nPr.a[:, k, :], Pw, Pw.a[:, k, 0, :], -1.0, None, ALU.mult)
        pl = 'pool'
        self.tt(pl, td[0], td[0].a, a.m, Br, a.fc, fre, ALU.mult)
        self.tt(pl, td[1], td[1].a, a.m, Bi, a.fc, fim, ALU.mult)
        self.tt(pl, a.Bb, a.Bb.a[:, 0], td[0], td[0].a, td[1], td[1].a, ALU.subtract)
        self.tt(pl, td[2], td[2].a, a.m, Bi, a.fc, fre, ALU.mult)
        self.tt(pl, td[3], td[3].a, a.m, Br, a.fc, fim, ALU.mult)
        self.tt(pl, a.Bb, a.Bb.a[:, 1], td[2], td[2].a, td[3], td[3].a, ALU.add)
        self.cp('pool', a.BbQb, a.BbQb.a[:, 0], a.Bb, a.Bb.a[:, 0])
        self.cp('pool', a.BbQb, a.BbQb.a[:, 1], a.Bb, a.Bb.a[:, 1])
        for k in range(9):
            pr, pi = bc32(Pw.a[:, k, 0, :]), bc32(Pw.a[:, k, 1, :])
            npr = bc32(a.nPr.a[:, k, :])
            self.tt(pl, td[0], td[0].a, a.m, Cr, Pw, pr, ALU.mult)
            self.tt(pl, td[1], td[1].a, a.m, Ci, Pw, pi, ALU.mult)
            self.tt(pl, a.PCb, a.PCb.a[:, k, 0], td[0], td[0].a, td[1], td[1].a, ALU.subtract)
            self.tt(pl, td[2], td[2].a, a.m, Ci, a.nPr, npr, ALU.mult)
            self.tt(pl, td[3], td[3].a, a.m, Cr, Pw, pi, ALU.mult)
            self.tt(pl, a.PCb, a.PCb.a[:, k, 1], td[2], td[2].a, td[3], td[3].a, ALU.subtract)
        for j in range(8):
            pr, pi = bc32(Pw.a[:, 7 - j, 0, :]), bc32(Pw.a[:, 7 - j, 1, :])
            self.tt(pl, tp[0], tp[0].a, a.Bb, a.Bb.a[:, 0], Pw, pr, ALU.mult)
            self.tt(pl, tp[1], tp[1].a, a.Bb, a.Bb.a[:, 1], Pw, pi, ALU.mult)
            v4 = lambda r: r.a.rearrange("p (t m) c -> p t m c", t=4)
            self.tt(pl, a.PBb, a.PBb.a[:, :, j, 0, :, :], tp[0], v4(tp[0]), tp[1], v4(tp[1]), ALU.subtract)
            self.tt(pl, tp[2], tp[2].a, a.Bb, a.Bb.a[:, 1], Pw, pr, ALU.mult)
            self.tt(pl, tp[3], tp[3].a, a.Bb, a.Bb.a[:, 0], Pw, pi, ALU.mult)
            self.tt(pl, a.PBb, a.PBb.a[:, :, j, 1, :, :], tp[2], v4(tp[2]), tp[3], v4(tp[3]), ALU.add)

    def m3(self, l):
        P, a, d, psb = self.P, self.A3, self.d, self.psb
        T, NCH, CPS, NSEQ = self.T, self.NCH, self.CPS, self.NSEQ
        Pw = a.Pw
        dv = 'dve'
        for Tt in range(4):
            self.dma('sp', a.u, a.u.a[:, Tt, :], self.dr["uT"], d["uT"][Tt * 128:(Tt + 1) * 128, :])
        nb = 0
        for Tt in range(4):
            for jh in range(2):
                ps = psb[nb % 2]
                nb += 1
                psv = ps.a.bitcast(BF16)
                for jj in range(4):
                    for ri in range(2):
                        j = jh * 4 + jj
                        idx = jj * 2 + ri
                        P.add('pe', lambda e, psv=psv, idx=idx, Tt=Tt, j=j, ri=ri: e.transpose(
                            psv[:, idx * 128:(idx + 1) * 128], a.PBb.a[:, Tt, j, ri, :, :].rearrange("p m c -> p (m c)"),
                            self.ident.a), reads=[a.PBb, self.ident], writes=[ps])
                self.cp('dve' if nb % 2 else 'act', a.Wp,
                        a.Wp.a[:, Tt, jh * 4:jh * 4 + 4, :, :].rearrange("p j r c -> p (j r c)"),
                        ps, psv) if nb % 2 else \
                    self.act(a.Wp, a.Wp.a[:, Tt, jh * 4:jh * 4 + 4, :, :].rearrange("p j r c -> p (j r c)"),
                             ps, psv, AF.Copy)
        fl = lambda ap: ap.rearrange("p m c -> p (m c)")
        for Tt in range(4):
            for kh in range(2):
                ps = psb[2 + (nb % 2)]
                nb += 1
                for kk in range(4):
                    k = kh * 4 + kk
                    oa = ps.a[:, kk * 128:(kk + 1) * 128]
                    for ri in range(2):
                        self.mm(ps, oa, a.BbQb, fl(a.BbQb.a[:, ri, 4 * Tt:4 * Tt + 4, :]),
                                a.PCb, fl(a.PCb.a[:, k, ri, 4 * Tt:4 * Tt + 4, :]), ri == 0, ri == 1)
                self.tt('dve', a.Kb, a.Kb.a[:, Tt, kh * 4:kh * 4 + 4, :], ps,
                        ps.a.rearrange("p (k c) -> p k c", k=4), self.bmask,
                        self.bmask.a.unsqueeze(1).broadcast_to([128, 4, 128]), ALU.mult)
            self.stt(a.Kb, a.Kb.a[:, Tt, 0, :], self.ident, self.ident.a, self.vec(l, DSK + Tt), a.Kb,
                     a.Kb.a[:, Tt, 0, :], ALU.mult, ALU.add, extra_reads=[self.vecs])
        for gp in range(16):
            Tt, m = gp // 4, gp % 4
            for ri in range(2):
                ps = psb[4 + (nb % 4)]
                nb += 1
                for j in range(8):
                    rhs = a.u.a[32 * m:32 * m + 32, Tt, j * NCH:(j + 1) * NCH]
                    self.mm(ps, ps.a[:, 0:NCH], a.Wp, a.Wp.a[32 * m:32 * m + 32, Tt, j, ri, :], a.u, rhs,
                            j == 0, j == 7, tp=(32 * m, 0))
                if nb % 2:
                    self.cp('dve', a.Sb, a.Sb.a[:, gp, ri, :], ps, ps.a[:, 0:NCH])
                else:
                    self.act(a.Sb, a.Sb.a[:, gp, ri, :], ps, ps.a[:, 0:NCH], AF.Copy)
        Sv = a.Sb.a.rearrange("p g r (s c) -> p g r s c", s=NSEQ)
        Xv = a.Xb.a.rearrange("p g r (s c) -> p g r s c", s=NSEQ)
        Ar4 = Pw.a[:, 8, 0, :].unsqueeze(2).unsqueeze(3).broadcast_to([128, 16, 2, NSEQ])
        Ai3 = Pw.a[:, 8, 1, :].unsqueeze(2).broadcast_to([128, 16, NSEQ])
        nAi3 = a.nAi.a.unsqueeze(2).broadcast_to([128, 16, NSEQ])
        P.add(dv, lambda e: e.memset(a.xs[0].a, 0.0), writes=[a.xs[0]])
        for c in range(CPS):
            xc, xn = a.xs[c % 2], a.xs[(c + 1) % 2]
            self.act(a.Xb, Xv[:, :, :, :, c], xc, xc.a, AF.Copy)
            if c == CPS - 1:
                break
            self.tt(dv, a.t1, a.t1.a, xc, xc.a, Pw, Ar4, ALU.mult)
            self.tt(dv, a.t2, a.t2.a[:, :, 0, :], xc, xc.a[:, :, 1, :], a.nAi, nAi3, ALU.mult)
            self.tt(dv, a.t2, a.t2.a[:, :, 1, :], xc, xc.a[:, :, 0, :], Pw, Ai3, ALU.mult)
            self.tt(dv, a.t1, a.t1.a, a.t1, a.t1.a, a.t2, a.t2.a, ALU.add)
            self.tt(dv, xn, xn.a, a.t1, a.t1.a, a.Sb, Sv[:, :, :, :, c], ALU.add)
        for Tt in range(4):
            for j in range(8):
                ps = psb[nb % 4]
                nb += 1
                for k in range(j + 1):
                    rhs = a.u.a[:, Tt, (j - k) * NCH:(j - k + 1) * NCH]
                    self.mm(ps, ps.a[:, 0:NCH], a.Kb, a.Kb.a[:, Tt, k, :], a.u, rhs, k == 0, False)
                for m in range(4):
                    gp = 4 * Tt + m
                    for ri in range(2):
                        self.mm(ps, ps.a[32 * m:32 * m + 32, 0:NCH], a.PCb, a.PCb.a[:, j + 1, ri, gp, :],
                                a.Xb, a.Xb.a[:, gp, ri, :], False, (m == 3 and ri == 1), tp=(0, 32 * m))
                self.act(a.ys, a.ys.a.rearrange("p (c j) -> p j c", j=8)[:, j, :], ps, ps.a[:, 0:NCH],
                         AF.Gelu_apprx_tanh)
            self.dma('sp', self.dr["ysT"], d["ysT"][Tt * 128:(Tt + 1) * 128, :], a.ys, a.ys.a)

    def post_norm_residual(self, l, a, X, goff, dst, t0, square_done=False):
        psb, d = self.psb, self.d
        if not square_done:
            self.act(a.ysq, a.ysq.a, a.Y, a.Y.a, AF.Square)
        self.rms_rstd(psb[7], a.ysq, lambda c: a.ysq.a[:, c, :], 8, 1024.0, a.rt, a.rstd)
        for oc in range(8):
            t = a.tt[oc % 2]
            self.tt('dve', t, t.a, a.Y, a.Y.a[:, oc, :], a.rstd, a.rstd.a, ALU.mult)
            self.stt(X, X.a[:, oc, :], t, t.a, self.vec(l, goff + oc), X, X.a[:, oc, :], ALU.mult, ALU.add,
                     extra_reads=[self.vecs])
        self.dma('sp', self.dr[dst], d[dst].rearrange("(c p) t -> p c t", p=128)[:, :, t0:t0 + 512], X, X.a)

    def m4(self, l, src, dst):
        P, a, d, psb = self.P, self.A4, self.d, self.psb
        xin = d[src].rearrange("(c p) t -> p c t", p=128)
        oin = d["oT"].rearrange("(c p) t -> p c t", p=128)
        yin = d["ysT"].rearrange("(c p) t -> p c t", p=128)
        gin = d["gT"].rearrange("(c p) t -> p c t", p=128)

        def load(tt):
            sl = slice(tt * 512, (tt + 1) * 512)
            self.dma('sp', a.o[tt % 2], a.o[tt % 2].a, self.dr["oT"], oin[:, :, sl])
            self.dma('sp', a.ysb[tt % 2], a.ysb[tt % 2].a, self.dr["ysT"], yin[:, :, sl])
            self.dma('sp', a.g[tt % 2], a.g[tt % 2].a, self.dr["gT"], gin[:, :, sl])
            self.dma('sp', a.X[tt % 2], a.X[tt % 2].a, self.dr[src], xin[:, :, sl])

        nbc = [0]

        def stageA(tt):
            o, ysb, g = a.o[tt % 2], a.ysb[tt % 2], a.g[tt % 2]
            for oc in range(8):
                nb = nbc[0]
                pa, pga, pgb = psb[(3 * nb) % 6], psb[(3 * nb + 1) % 6], psb[(3 * nb + 2) % 6]
                nbc[0] += 1
                sg, yss, m1, m2 = a.sg[oc % 2], a.yss[oc % 2], a.m1[oc % 2], a.m2[oc % 2]
                cs = slice(oc * 128, (oc + 1) * 128)
                for k in range(4):
                    self.mm(pa, pa.a, a.woatt, a.woatt.a[:, k, cs], o, o.a[:, k, :], k == 0, k == 3)
                for k in range(4):
                    self.mm(pga, pga.a, a.wglu, a.wglu.a[:, k, cs], ysb, ysb.a[:, k, :], k == 0, k == 3)
                for k in range(4):
                    self.mm(pgb, pgb.a, a.wglu, a.wglu.a[:, k, 1024 + oc * 128:1024 + (oc + 1) * 128], ysb,
                            ysb.a[:, k, :], k == 0, k == 3)
                self.act(sg, sg.a, pgb, pgb.a, AF.Sigmoid)
                self.tt('dve', yss, yss.a, pga, pga.a, sg, sg.a, ALU.mult)
                self.tt('dve', m1, m1.a, pa, pa.a, g, g.a[:, oc, :], ALU.mult)
                self.tt('dve', m2, m2.a, yss, yss.a, g, g.a[:, 8 + oc, :], ALU.mult)
                self.tt('dve', a.mg, a.mg.a[:, oc, :], m1, m1.a, m2, m2.a, ALU.add)

        def stageB(tt):
            for oc in range(8):
                py = psb[(3 * nbc[0]) % 6]
                nbc[0] += 1
                for k in range(8):
                    self.mm(py, py.a, a.wout, a.wout.a[:, k, oc * 128:(oc + 1) * 128], a.mg, a.mg.a[:, k, :],
                            k == 0, k == 7)
                self.act(a.Y, a.Y.a[:, oc, :], py, py.a, AF.Copy)
            self.act(a.ysq, a.ysq.a, a.Y, a.Y.a, AF.Square)

        def stageC(tt):
            self.post_norm_residual(l, a, a.X[tt % 2], GPOST, dst, tt * 512, square_done=True)

        NT = self.NT
        load(0)
        self.load_w(a.woatt, d["w_oatt"][l], 4, 1024, a.stg)
        self.load_w(a.wglu, d["w_glu"][l], 4, 2048, a.stg)
        self.load_w(a.wout, d["w_out"][l], 8, 1024, a.stg)
        if NT > 1:
            load(1)
        stageA(0)
        stageB(0)
        for tt in range(1, NT):
            stageA(tt)
            stageC(tt - 1)
            if tt + 1 < NT:
                load(tt + 1)
            stageB(tt)
        stageC(NT - 1)

    def f1(self, l, src):
        P, a, d, psb = self.P, self.A5, self.d, self.psb
        self.dma('sp', a.X[0], a.X[0].a, self.dr[src], d[src].rearrange("(c p) t -> p c t", p=128)[:, :, 0:512])
        self.load_w(a.wup, d["w_up"][l], 8, 2 * D_FF, a.stg)
        for fc in range(NFC):
            for k in range(3):
                self.ts('dve', a.dgall, a.dgall.a[:, fc, k, :], self.ident, self.ident.a,
                        self.vec(l, CW + fc * 3 + k), None, ALU.mult, extra_reads=[self.vecs])
        xin = d[src].rearrange("(c p) t -> p c t", p=128)
        aout = d["actT"].rearrange("(c p) t -> p c t", p=128)

        def load(tt):
            self.dma('sp', a.X[tt % 2], a.X[tt % 2].a, self.dr[src], xin[:, :, tt * 512:(tt + 1) * 512])

        gi = 0
        for tt in range(self.NT):
            if tt + 1 < self.NT:
                load(tt + 1)
            X = a.X[tt % 2]
            t0 = tt * 512
            rs = a.rstdF.a[:, t0:t0 + 512]
            self.act(a.sqg, a.sqg.a, X, X.a, AF.Square)
            for c in range(8):
                self.mm(psb[0], psb[0].a, self.ones, self.ones.a, a.sqg, a.sqg.a[:, c, :], c == 0, c == 7)
            self.act(a.rt, a.rt.a, psb[0], psb[0].a, AF.Sqrt, bias=self.epst.a[:, 0:1], scale=1.0 / 1024.0,
                     extra_reads=[self.epst])
            P.add('dve', lambda e, rs=rs: e.reciprocal(out=rs, in_=a.rt.a), reads=[a.rt], writes=[a.rstdF])
            for dc in range(8):
                self.ts('dve', a.sqg, a.sqg.a[:, dc, :], X, X.a[:, dc, :], self.vec(l, GFPRE + dc), None, ALU.mult,
                        extra_reads=[self.vecs])
            seq_start = (tt % self.TPS == 0)
            pend = None

            def conv_stage(fc, Gb, psv, ge_i):
                psc = psb[5 + fc % 2]
                for k in range(3):
                    self.mm(psc, psc.a, a.dgall, a.dgall.a[:, fc, k, :], Gb, Gb.a[:, k:k + 512], k == 0, k == 2)
                ge = a.ge[ge_i % 3]
                self.act(ge, ge.a, psc, psc.a, AF.Gelu_apprx_tanh, bias=self.vec(l, CB + fc),
                         extra_reads=[self.vecs])
                ar = a.act[fc // (NFC // 2)]
                self.tt('dve', ar, ar.a[:, fc % (NFC // 2), :], psv, psv.a, ge, ge.a, ALU.mult)
                if fc % (NFC // 2) == NFC // 2 - 1:
                    hh = fc // (NFC // 2)
                    self.dma('sp', self.dr["actT"], aout[:, hh * 11:(hh + 1) * 11, t0:t0 + 512], ar, ar.a)

            for fc in range(NFC):
                psg, psv = psb[1 + fc % 2], psb[3 + fc % 2]
                Gb = a.Gb[gi % 3]
                gi += 1
                for dc in range(8):
                    self.mm(psg, psg.a, self.wr(a.wup, fc * 256, fc * 256 + 128),
                            a.wup.a[:, dc, fc * 256:fc * 256 + 128], a.sqg, a.sqg.a[:, dc, :], dc == 0, dc == 7)
                for dc in range(8):
                    self.mm(psv, psv.a, self.wr(a.wup, fc * 256 + 128, fc * 256 + 256),
                            a.wup.a[:, dc, fc * 256 + 128:fc * 256 + 256], a.sqg, a.sqg.a[:, dc, :], dc == 0, dc == 7)
                if seq_start:
                    P.add('pool', lambda e, Gb=Gb: e.memset(Gb.a[:, 0:2], 0.0), writes=[Gb])
                else:
                    self.act(Gb, Gb.a[:, 0:2], a.Gh, a.Gh.a[:, fc, :], AF.Copy)
                self.tt('dve', Gb, Gb.a[:, 2:514], psg, psg.a, a.rstdF, rs, ALU.mult)
                self.act(a.Gh, a.Gh.a[:, fc, :], Gb, Gb.a[:, 512:514], AF.Copy)
                if pend is not None:
                    conv_stage(*pend)
                pend = (fc, Gb, psv, gi)
            conv_stage(*pend)

    def f2(self, l, src, dst):
        P, a, d, psb = self.P, self.A6, self.d, self.psb
        xin = d[src].rearrange("(c p) t -> p c t", p=128)
        ain = d["actT"].rearrange("(c p) t -> p c t", p=128)

        def load(tt):
            sl = slice(tt * 512, (tt + 1) * 512)
            self.dma('sp', a.a[tt % 2], a.a[tt % 2].a, self.dr["actT"], ain[:, :, sl])
            self.dma('sp', a.X[tt % 2], a.X[tt % 2].a, self.dr[src], xin[:, :, sl])

        load(0)
        self.load_w(a.wdown, d["w_down"][l], NFC, 1024, a.stg)
        nb = 0
        for tt in range(self.NT):
            if tt + 1 < self.NT:
                load(tt + 1)
            A_, X = a.a[tt % 2], a.X[tt % 2]
            rs = a.rstdF.a[:, tt * 512:(tt + 1) * 512]
            for oc in range(8):
                py = psb[nb % 6]
                nb += 1
                for k in range(NFC):
                    self.mm(py, py.a, self.wr(a.wdown, oc * 128, (oc + 1) * 128), a.wdown.a[:, k, oc * 128:(oc + 1) * 128], A_, A_.a[:, k, :],
                            k == 0, k == NFC - 1)
                self.tt('dve', a.Y, a.Y.a[:, oc, :], py, py.a, a.rstdF, rs, ALU.mult)
            self.post_norm_residual(l, a, X, GFPOST, dst, tt * 512)


def _rope_tables(seq):
    pos = np.arange(seq, dtype=np.float32)
    inv_freq = (np.float32(10000.0) ** (-np.arange(0, 32, 2, dtype=np.float32) / np.float32(32))).astype(np.float32)
    ang = (pos[:, None] * inv_freq[None, :]).astype(np.float32)
    cos, sin = np.cos(ang).astype(np.float32), np.sin(ang).astype(np.float32)
    tab = np.zeros((2, 128, seq), np.float32)
    tab[0, 64:80] = cos.T
    tab[0, 80:96] = cos.T
    tab[1, 64:80] = -sin.T
    tab[1, 80:96] = sin.T
    return tab


def _interleave_up(w):
    L = w.shape[0]
    g = w[:, :, :D_FF].reshape(L, 1024, NFC, 128)
    v = w[:, :, D_FF:].reshape(L, 1024, NFC, 128)
    return np.ascontiguousarray(np.stack([g, v], axis=3).reshape(L, 1024, 2 * D_FF))


def prep_weights(inp, depth, seq):
    f = lambda a: np.ascontiguousarray(np.asarray(a, dtype=np.float32))
    L = depth
    w_in = f(inp["w_in"])[:L]
    o1, o2, o3, o4 = 384, 640, 672, 1184
    kpe = w_in[:, :, o2:o3]
    kpe_sw = np.concatenate([kpe[:, :, 16:32], kpe[:, :, 0:16]], axis=2)
    w_inx = np.concatenate([w_in[:, :, 0:o2], kpe, kpe_sw, w_in[:, :, o3:o4], w_in[:, :, o4:]], axis=2)
    assert w_inx.shape[2] == WINX
    w_uq = f(inp["w_uq"])[:L]
    wq = w_uq.reshape(L, 384, 8, 96)
    w_uqsw = np.concatenate([wq[..., 80:96], wq[..., 64:80]], axis=3).reshape(L, 384, 256)
    w_ukv = f(inp["w_ukv"])[:L].reshape(L, 256, 8, 128)
    w_uk = w_ukv[..., 0:64].reshape(L, 256, 512)
    w_uv = w_ukv[..., 64:128].reshape(L, 256, 512)
    vecs = np.zeros((128, L, NV), np.float32)

    def put(off, arr, n):
        vecs[:, :, off:off + n] = f(arr)[:L].reshape(L, n, 128).transpose(2, 0, 1)

    put(GPRE, inp["g_mix_pre"], 8)
    put(BG, inp["b_gate"], 16)
    put(GQ, inp["g_q"], 3)
    put(GKV, inp["g_kv"], 2)
    put(GPOST, inp["g_mix_post"], 8)
    put(GFPRE, inp["g_ffn_pre"], 8)
    put(GFPOST, inp["g_ffn_post"], 8)
    cw = f(inp["conv_w"])[:L].reshape(L, 3, NFC, 128).transpose(3, 0, 2, 1)
    vecs[:, :, CW:CW + 66] = cw.reshape(128, L, 66)
    put(CB, inp["conv_b"], NFC)
    put(DSK, inp["d_skip"], 4)
    s5v = np.zeros((128, L, 48), np.float32)

    def gl(arr):
        return f(arr)[:L].reshape(L, 16, 2, 64).transpose(2, 3, 0, 1).reshape(128, L, 16)

    s5v[:, :, 0:16] = gl(inp["lam_re"])
    s5v[:, :, 16:32] = gl(inp["lam_im"])
    ls = f(inp["log_step"])[:L].reshape(L, 16, 2)
    s5v[:, :, 32:48] = np.repeat(ls.transpose(2, 0, 1)[:, None], 64, axis=1).reshape(128, L, 16)
    s5m = np.zeros((L, 2, 64, 4, 16, 2, 16), np.float32)
    cr = f(inp["c_re"])[:L].reshape(L, 16, 2, 16, 64)
    ci = f(inp["c_im"])[:L].reshape(L, 16, 2, 16, 64)
    br = f(inp["b_re"])[:L].reshape(L, 16, 2, 64, 16)
    bi = f(inp["b_im"])[:L].reshape(L, 16, 2, 64, 16)
    for g2 in range(2):
        s5m[:, g2, :, 0, :, g2, :] = cr[:, :, g2].transpose(0, 3, 1, 2)
        s5m[:, g2, :, 1, :, g2, :] = ci[:, :, g2].transpose(0, 3, 1, 2)
        s5m[:, g2, :, 2, :, g2, :] = br[:, :, g2].transpose(0, 2, 1, 3)
        s5m[:, g2, :, 3, :, g2, :] = bi[:, :, g2].transpose(0, 2, 1, 3)
    return {
        "w_inx": np.ascontiguousarray(w_inx),
        "w_uq": w_uq, "w_uqsw": np.ascontiguousarray(w_uqsw),
        "w_uk": np.ascontiguousarray(w_uk), "w_uv": np.ascontiguousarray(w_uv),
        "w_oatt": f(inp["w_o_att"])[:L], "w_glu": f(inp["w_glu"])[:L], "w_out": f(inp["w_out"])[:L],
        "w_up": _interleave_up(f(inp["w_up"])[:L]), "w_down": f(inp["w_down"])[:L],
        "vecs": np.ascontiguousarray(vecs.reshape(128, L * NV)),
        "s5v": np.ascontiguousarray(s5v.reshape(128, L * 48)),
        "s5m": np.ascontiguousarray(s5m.reshape(L, 128, 2048)),
        "rope": _rope_tables(seq),
        "ident": np.eye(128, dtype=np.float32),
        "bmask": np.kron(np.eye(4, dtype=np.float32), np.ones((32, 32), np.float32)),
    }


_CACHE = {}


def kernel(**inputs):
    x = np.asarray(inputs["x"], dtype=np.float32)
    B, S, D = x.shape
    ncores = 8
    nseq = B // ncores
    depth = int(np.asarray(inputs["w_in"]).shape[0])
    key = (nseq, S, depth)
    if key not in _CACHE:
        _CACHE[key] = K(nseq=nseq, seq=S, depth=depth).build()
    nc = _CACHE[key]
    wts = prep_weights(inputs, depth, S)
    in_maps = []
    for c in range(ncores):
        xs = x[c * nseq:(c + 1) * nseq].reshape(nseq * S, D)
        m = dict(wts)
        m["xT"] = np.ascontiguousarray(xs.T)
        in_maps.append(m)
    res = run_bass_kernel_spmd(nc, in_maps, core_ids=list(range(ncores)))
    out = np.empty((B, S, D), np.float32)
    for c in range(ncores):
        yT = np.asarray(res.results[c]["yT"], dtype=np.float32)
        out[c * nseq:(c + 1) * nseq] = yT.T.reshape(nseq, S, D)
    return out
```

```python
import contextlib
import math
import numpy as np
import ml_dtypes
import concourse.bass as bass
import concourse.mybir as mybir
from concourse.bass_utils import run_bass_kernel_spmd
from concourse.alu_op_type import AluOpType as ALU

AF = mybir.ActivationFunctionType
F32 = mybir.dt.float32
BF16 = mybir.dt.bfloat16
I32 = mybir.dt.int32
ENGS = ['sp', 'act', 'pool', 'dve', 'pe']

D_MODEL = 1024
N_HEADS = 8
D_FF = 2816
NFC = 22
EPS = 1e-6
NV = 145
GPRE, BG, GQ, GKV, GPOST, GFPRE, GFPOST, CW, CB, DSK = 0, 8, 24, 27, 29, 37, 45, 53, 119, 141
WINX = 3264
ARENA_ELEMS = 196 * 512


class Res:
    def __init__(self, name, dsem=None, multi=False):
        self.name = name
        self.writers = {}
        self.readers = {}
        self.dsem = dsem
        self.nw = 0
        self.multi = multi
        self.a = None


class Op:
    __slots__ = ('eng', 'fn', 'deps', 'needed', 'sig', 'dma', 'key')


class Prog:
    def __init__(self, nc, st):
        self.nc = nc
        self.st = st
        self.ops = []
        self.esem = {e: st.enter_context(nc.semaphore("s_" + e)) for e in ENGS}
        self.allsems = list(self.esem.values())
        self.nsem = len(ENGS)
        self.last = {e: None for e in ENGS}
        self.dma_res = []
        self.bar_deps = []

    def res(self, name, dma=False, multi=False):
        dsem = None
        if dma:
            dsem = self.st.enter_context(self.nc.semaphore("d_" + name))
            self.allsems.append(dsem)
            self.nsem += 1
        r = Res(name, dsem, multi)
        if dma:
            self.dma_res.append(r)
            r.last_dma = None
        return r

    def sb(self, name, shape, dtype, dma=False, multi=False):
        r = self.res(name, dma=dma, multi=multi)
        t = self.st.enter_context(self.nc.sbuf_tensor(name, shape, dtype))
        r.a = t[:]
        return r

    def ps(self, name, shape, dtype):
        r = self.res(name)
        t = self.st.enter_context(self.nc.psum_tensor(name, shape, dtype))
        r.a = t[:]
        return r

    def barrier(self):
        deps = [o for o in self.last.values() if o is not None]
        for r in self.dma_res:
            if r.last_dma is not None:
                deps.append(r.last_dma)
        self.bar_deps = deps

    def add(self, eng, fn, reads=(), writes=(), dma=False):
        op = Op()
        op.eng = eng
        op.fn = fn
        op.dma = dma
        op.needed = False
        op.sig = None
        deps = {}
        for o in self.bar_deps:
            deps[id(o)] = o
        for r in reads:
            for o in r.writers.values():
                deps[id(o)] = o
        for w in writes:
            for o in w.writers.values():
                deps[id(o)] = o
            for o in w.readers.values():
                deps[id(o)] = o
        if dma:
            dres = [w for w in writes if w.dsem is not None]
            assert len(dres) == 1, [w.name for w in writes]
            dres[0].nw += 1
            op.sig = (dres[0].dsem, 16 * dres[0].nw)
            dres[0].last_dma = op
            key = id(dres[0].dsem)
        else:
            key = eng
            self.last[eng] = op
        op.key = key
        op.deps = [o for o in deps.values()
                   if not (o.eng == 'pe' and eng == 'pe' and not o.dma and not dma)]
        for o in op.deps:
            o.needed = True
        for r in reads:
            r.readers[key] = op
        for w in writes:
            if w.multi:
                w.writers[key] = op
            else:
                w.writers = {key: op}
                w.readers = {}
        self.ops.append(op)
        return op

    def finish(self, final_res):
        nc = self.nc
        self.add('sp', None, reads=final_res)
        cnt = {e: 0 for e in ENGS}
        for op in self.ops:
            if not op.dma and op.needed:
                cnt[op.eng] += 1
                op.sig = (self.esem[op.eng], cnt[op.eng])
        per = {e: [o for o in self.ops if o.eng == e] for e in ENGS}
        self.stats = {e: len(per[e]) for e in ENGS}

        def mk(e):
            def body(engobj):
                waited = {}
                for op in per[e]:
                    need = {}
                    for d in op.deps:
                        s, v = d.sig
                        k = id(s)
                        if v > need.get(k, (None, 0))[1]:
                            need[k] = (s, v)
                    for k, (s, v) in need.items():
                        if v > waited.get(k, 0):
                            engobj.wait_ge(s, v)
                            waited[k] = v
                    if op.fn is None:
                        continue
                    ins = op.fn(engobj)
                    if op.dma:
                        ins.then_inc(op.sig[0], 16)
                    elif op.needed:
                        ins.then_inc(op.sig[0], 1)
            return body

        import os
        if os.environ.get("NOCLEAR") != "1":
            for sm in self.allsems:
                nc.gpsimd.sem_clear(sm)
            nc.all_engine_barrier()
        with nc.Block() as block:
            block.sync(mk('sp'))
            block.scalar(mk('act'))
            block.gpsimd(mk('pool'))
            block.vector(mk('dve'))
            block.tensor(mk('pe'))


class Arena:
    def __init__(self, K, name):
        self.K = K
        self.name = name
        self.off = 0

    def alloc(self, name, shape, dtype, dma=False, multi=False, at=None):
        n = 1
        for s in shape:
            n *= s
        nel = n * (2 if dtype in (F32, I32) else 1)
        nel = (nel + 15) // 16 * 16
        off = self.off if at is None else at
        if at is None:
            self.off += nel
        assert off + nel <= ARENA_ELEMS, (self.name, name, off + nel, ARENA_ELEMS)
        v = self.K.arena_t[:, off:off + n * (2 if dtype in (F32, I32) else 1)]
        if dtype in (F32, I32):
            v = v.bitcast(dtype)
        if len(shape) >= 2:
            names = "abcdefg"[:len(shape)]
            pat = "p (" + " ".join(names) + ") -> p " + " ".join(names)
            v = v.rearrange(pat, **{n: sz for n, sz in zip(names[:-1], shape[:-1])})
        r = self.K.P.res(self.name + "_" + name, dma=dma, multi=multi)
        r.a = v
        r.off = off
        return r


class NS:
    pass


class K:
    def __init__(self, nseq=2, seq=2048, depth=4, dump=False, phases=None):
        self.phases = phases
        self.NSEQ = nseq
        self.S = seq
        self.L = depth
        self.T = nseq * seq
        self.NT = self.T // 512
        self.TPS = seq // 512
        self.NCH = self.T // 8
        self.CPS = seq // 8
        self.G = 4
        self.dump = dump
        self.wl_i = 0
        assert self.NCH <= 512

    def build(self):
        nc = bass.Bass("TRN2", target_bir_lowering=False)
        self.nc = nc
        L, T, S = self.L, self.T, self.S
        d = {}

        def inp(name, shape):
            d[name] = nc.dram_tensor(name, shape, F32, kind="ExternalInput").ap()

        inp("xT", [1024, T])
        inp("w_inx", [L, 1024, WINX])
        inp("w_uq", [L, 384, 768])
        inp("w_uqsw", [L, 384, 256])
        inp("w_uk", [L, 256, 512])
        inp("w_uv", [L, 256, 512])
        inp("w_oatt", [L, 512, 1024])
        inp("w_glu", [L, 512, 2048])
        inp("w_out", [L, 1024, 1024])
        inp("w_up", [L, 1024, 2 * D_FF])
        inp("w_down", [L, D_FF, 1024])
        inp("vecs", [128, L * NV])
        inp("s5v", [128, L * 48])
        inp("s5m", [L, 128, 2048])
        inp("rope", [2, 128, S])
        inp("ident", [128, 128])
        inp("bmask", [128, 128])
        d["yT"] = nc.dram_tensor("yT", [1024, T], F32, kind="ExternalOutput").ap()
        skind = "ExternalOutput" if self.dump else "Internal"

        def scr(name, shape, dt):
            d[name] = nc.dram_tensor(name, shape, dt, kind=skind).ap()

        scr("s1", [1024, T], F32)
        scr("s2", [1024, T], F32)
        scr("qT", [8, 96, T], BF16)
        scr("kT", [8, 96, T], BF16)
        scr("vA", [T, 768], BF16)
        scr("uT", [512, T], BF16)
        scr("gT", [2048, T], BF16)
        scr("oT", [512, T], BF16)
        scr("ysT", [512, T], BF16)
        scr("actT", [D_FF, T], BF16)
        self.d = d
        with contextlib.ExitStack() as st:
            P = Prog(nc, st)
            self.P = P
            self.dr = {n: P.res("dr_" + n, dma=True, multi=True) for n in
                       ["yT", "s1", "s2", "qT", "kT", "vA", "uT", "gT", "oT", "ysT", "actT"]}
            self.dr["xT"] = P.res("dr_xT")
            self.arena_t = st.enter_context(nc.sbuf_tensor("arena", [128, ARENA_ELEMS], BF16))
            self.psb = [P.ps("psb%d" % i, [128, 512], F32) for i in range(8)]
            self.vecs = P.sb("vecs_sb", [128, L * NV], F32, dma=True)
            self.s5v = P.sb("s5v_sb", [128, L * 48], F32, dma=True)
            self.ident = P.sb("identb", [128, 128], BF16, dma=True)
            self.ones = P.sb("onesb", [128, 128], BF16)
            self.bmask = P.sb("bmask_sb", [128, 128], F32, dma=True)
            P.add('sp', lambda e: e.dma_start(out=self.bmask.a, in_=d["bmask"]), writes=[self.bmask], dma=True)
            self.epst = P.sb("epst", [128, 1], F32)
            P.add('sp', lambda e: e.dma_start(out=self.vecs.a, in_=d["vecs"]), writes=[self.vecs], dma=True)
            P.add('sp', lambda e: e.dma_start(out=self.s5v.a, in_=d["s5v"]), writes=[self.s5v], dma=True)
            P.add('pool', lambda e: e.dma_start(out=self.ident.a, in_=d["ident"]), writes=[self.ident], dma=True)
            P.add('dve', lambda e: e.memset(self.ones.a, 1.0), writes=[self.ones])
            P.add('dve', lambda e: e.memset(self.epst.a, EPS), writes=[self.epst])
            self.mk_arenas()
            nupd = 2 * L
            for l in range(L):
                for half in range(2):
                    u = 2 * l + half
                    src = "xT" if u == 0 else ("s1", "s2")[(u - 1) % 2]
                    dst = "yT" if u == nupd - 1 else ("s1", "s2")[u % 2]
                    on = lambda ph: self.phases is None or ph in self.phases
                    if half == 0:
                        if on('m1'):
                            self.m1(l, src)
                            P.barrier()
                        if on('m3'):
                            self.m3_prep(l)
                        if on('m2'):
                            self.m2(l)
                            P.barrier()
                        if on('m3'):
                            self.m3(l)
                            P.barrier()
                        if on('m4'):
                            self.m4(l, src, dst)
                            P.barrier()
                    else:
                        if on('f1'):
                            self.f1(l, src)
                            P.barrier()
                        if on('f2'):
                            self.f2(l, src, dst)
                            P.barrier()
            P.finish([self.dr["yT"]])
        return nc

    def vec(self, l, off, n=1):
        return self.vecs.a[:, l * NV + off:l * NV + off + n]

    def mk_arenas(self):
        T, S, NCH = self.T, self.S, self.NCH
        A = Arena(self, "m1")
        a = NS()
        a.win = A.alloc("win", [8, WINX], BF16, multi=True)
        self.wblocks(a.win, WINX, 1024)
        a.wuq = A.alloc("wuq", [3, 768], BF16, multi=True)
        a.wuqsw = A.alloc("wuqsw", [3, 256], BF16, multi=True)
        a.wuk = A.alloc("wuk", [2, 512], BF16, multi=True)
        a.wuv = A.alloc("wuv", [2, 512], BF16, multi=True)
        a.stg = [A.alloc("stg%d" % i, [1024], F32, dma=True) for i in range(4)]
        a.X = [A.alloc("X%d" % i, [8, 512], F32, dma=True) for i in range(2)]
        a.cs = [A.alloc("cs%d" % i, [2, 512], F32, dma=True) for i in range(2)]
        a.xsq = A.alloc("xsq", [8, 512], BF16)
        a.xg = A.alloc("xg", [8, 512], BF16)
        a.rt = A.alloc("rt", [512], F32)
        a.rstd = A.alloc("rstd", [512], F32)
        a.cq = A.alloc("cq", [3, 512], F32)
        a.ckv = A.alloc("ckv", [2, 512], F32)
        a.sq = A.alloc("sq", [3, 512], BF16)
        a.rq = A.alloc("rq", [512], F32)
        a.cqn = A.alloc("cqn", [3, 512], BF16)
        a.ckvn = A.alloc("ckvn", [2, 512], BF16)
        a.tmp = [A.alloc("tmp%d" % i, [512], F32) for i in range(4)]
        a.kp = A.alloc("kp", [512], F32)
        a.kps = A.alloc("kps", [512], F32)
        a.kr = A.alloc("kr", [512], BF16)
        a.ut = A.alloc("ut", [4, 512], BF16)
        a.gts = [A.alloc("gts%d" % i, [4, 512], BF16) for i in range(2)]
        a.qh = [A.alloc("qh%d" % i, [512], BF16) for i in range(3)]
        a.kh = [A.alloc("kh%d" % i, [512], BF16) for i in range(3)]
        a.vt = A.alloc("vt", [4, 8, 96], BF16)
        self.A1 = a
        A = Arena(self, "m2")
        a = NS()
        a.q = [A.alloc("q%d" % i, [S], BF16, dma=True) for i in range(2)]
        a.k = [A.alloc("k%d" % i, [S], BF16, dma=True) for i in range(2)]
        a.v = [A.alloc("v%d" % i, [S // 128, 768], BF16, dma=True) for i in range(2)]
        a.pT = [A.alloc("pT%d" % i, [512], BF16) for i in range(3)]
        a.rc = [A.alloc("rc%d" % i, [512], F32) for i in range(3)]
        a.hi = [A.alloc("hi%d" % i, [512], BF16) for i in range(3)]
        a.lo = [A.alloc("lo%d" % i, [512], BF16) for i in range(3)]
        a.bc = [A.alloc("bc%d" % i, [512], F32) for i in range(2)]
        a.ot = [A.alloc("ot%d" % i, [512], BF16) for i in range(2)]
        self.A2 = a
        A = Arena(self, "m3")
        a = NS()
        a.u = A.alloc("u", [4, T], BF16, dma=True, multi=True)
        a.Sb = A.alloc("Sb", [16, 2, NCH], BF16)
        a.Xb = A.alloc("Xb", [16, 2, NCH], BF16)
        a.Wp = A.alloc("Wp", [4, 8, 2, 128], BF16)
        a.PCb = A.alloc("PCb", [9, 2, 16, 32], BF16)
        a.Kb = A.alloc("Kb", [4, 8, 128], BF16)
        a.BbQb = A.alloc("BbQb", [2, 16, 32], BF16)
        a.m = A.alloc("m", [4, 16, 32], F32, dma=True)
        a.Bb = A.alloc("Bb", [2, 16, 32], F32)
        a.Pw = A.alloc("Pw", [9, 2, 16], F32)
        a.sm = [A.alloc("sm%d" % i, [16], F32) for i in range(14)]
        a.smi = A.alloc("smi", [16], I32)
        a.fc = A.alloc("fcoef", [2, 16], F32)
        a.nAi = A.alloc("nAi", [16], F32)
        a.nPr = A.alloc("nPr", [9, 16], F32)
        a.td = [A.alloc("td%d" % i, [16, 32], F32) for i in range(4)]
        G = self.G
        LS = self.CPS // G
        a.xs = [A.alloc("xs%d" % i, [16, 2, self.NSEQ * G], F32) for i in range(2)]
        a.t1 = A.alloc("t1", [16, 2, self.NSEQ * G], F32)
        a.t2 = A.alloc("t2", [16, 2, self.NSEQ * G], F32)
        a.ee = A.alloc("ee", [16, 2, self.NSEQ * G], F32)
        a.Q = A.alloc("Q", [8, 2, 16], F32)
        a.Apow = A.alloc("Apow", [16, 2, LS], F32)
        off_ys = A.off
        a.ys = A.alloc("ys", [T], BF16)
        a.PBb = A.alloc("PBb", [4, 8, 2, 4, 32], BF16, at=off_ys)
        if 16 * 8 * 2 * 32 > T:
            A.off = off_ys + 16 * 8 * 2 * 32
        self.A3 = a
        A = Arena(self, "m4")
        a = NS()
        a.woatt = A.alloc("woatt", [4, 1024], BF16, multi=True)
        a.stg = [A.alloc("stg%d" % i, [1024], F32, dma=True) for i in range(4)]
        a.wglu = A.alloc("wglu", [4, 2048], BF16, multi=True)
        a.wout = A.alloc("wout", [8, 1024], BF16, multi=True)
        a.o = [A.alloc("o%d" % i, [4, 512], BF16, dma=True) for i in range(2)]
        a.ysb = [A.alloc("ysb%d" % i, [4, 512], BF16, dma=True) for i in range(2)]
        a.g = [A.alloc("g%d" % i, [16, 512], BF16, dma=True) for i in range(2)]
        a.X = [A.alloc("X%d" % i, [8, 512], F32, dma=True) for i in range(2)]
        a.sg = [A.alloc("sg%d" % i, [512], F32) for i in range(2)]
        a.yss = [A.alloc("yss%d" % i, [512], F32) for i in range(2)]
        a.m1 = [A.alloc("m1%d" % i, [512], F32) for i in range(2)]
        a.m2 = [A.alloc("m2%d" % i, [512], F32) for i in range(2)]
        a.mg = A.alloc("mg", [8, 512], BF16)
        a.Y = A.alloc("Y", [8, 512], F32)
        a.ysq = A.alloc("ysq", [8, 512], BF16)
        a.rt = A.alloc("rt", [512], F32)
        a.rstd = A.alloc("rstd", [512], F32)
        a.tt = [A.alloc("tt%d" % i, [512], F32) for i in range(2)]
        self.A4 = a
        A = Arena(self, "f1")
        a = NS()
        rstdF = A.alloc("rstdF", [T], F32)
        a.rstdF = rstdF
        a.wup = A.alloc("wup", [8, 2 * D_FF], BF16, multi=True)
        a.X = [A.alloc("X%d" % i, [8, 512], F32, dma=True) for i in range(2)]
        a.sqg = A.alloc("sqg", [8, 512], BF16)
        a.rt = A.alloc("rt", [512], F32)
        a.Gb = [A.alloc("Gb%d" % i, [514], BF16) for i in range(3)]
        a.Gh = A.alloc("Gh", [NFC, 2], BF16)
        a.ge = [A.alloc("ge%d" % i, [512], BF16) for i in range(3)]
        a.dgall = A.alloc("dgall", [NFC, 3, 128], BF16)
        a.act = [A.alloc("act%d" % i, [NFC // 2, 512], BF16) for i in range(2)]
        a.stg = [A.alloc("stg%d" % i, [1024], F32, dma=True, at=a.act[i // 2].off + (i % 2) * 2048) for i in range(4)]
        self.wblocks(a.wup, 2 * D_FF, 1024)
        self.A5 = a
        A = Arena(self, "f2")
        a = NS()
        A.alloc("rstdF_pad", [T], F32)
        a.rstdF = rstdF
        a.wdown = A.alloc("wdown", [NFC, 1024], BF16, multi=True)
        a.stg = [A.alloc("stg%d" % i, [512], F32, dma=True) for i in range(4)]
        self.wblocks(a.wdown, 1024, 512)
        a.a = [A.alloc("a%d" % i, [NFC, 512], BF16, dma=True) for i in range(2)]
        a.X = [A.alloc("X%d" % i, [8, 512], F32, dma=True) for i in range(2)]
        a.Y = A.alloc("Y", [8, 512], F32)
        a.ysq = A.alloc("ysq", [8, 512], BF16)
        a.rt = A.alloc("rt", [512], F32)
        a.rstd = A.alloc("rstd", [512], F32)
        a.tt = [A.alloc("tt%d" % i, [512], F32) for i in range(2)]
        self.A6 = a

    def wblocks(self, dst, n, W):
        nb = (n + W - 1) // W
        rs = []
        for b in range(nb):
            r = self.P.res(dst.name + "_b%d" % b, multi=True)
            r.a = dst.a
            rs.append(r)
        dst.blocks = rs
        dst.W = W
        return rs

    def wr(self, dst, c0, c1):
        if not hasattr(dst, 'blocks'):
            return [dst]
        return dst.blocks[c0 // dst.W:(c1 - 1) // dst.W + 1]

    def load_w(self, dst, src2d, kc, n, stg):
        W = stg[0].a.shape[-1]
        if hasattr(dst, 'blocks'):
            assert dst.W % W == 0 or W % dst.W == 0
            W = min(W, dst.W)
        for n0 in range(0, n, W):
            n1 = min(n, n0 + W)
            wres = self.wr(dst, n0, n1)
            assert len(wres) == 1
            for c in range(kc):
                i = self.wl_i
                self.wl_i += 1
                sl = stg[i % len(stg)]
                self.dma('sp', sl, sl.a[:, 0:n1 - n0], None, src2d[c * 128:(c + 1) * 128, n0:n1])
                if i % 2 == 0:
                    self.act(wres[0], dst.a[:, c, n0:n1], sl, sl.a[:, 0:n1 - n0], AF.Copy)
                else:
                    self.cp('dve', wres[0], dst.a[:, c, n0:n1], sl, sl.a[:, 0:n1 - n0])

    def mm(self, outr, out_ap, lr, lhsT, rr, rhs, start, stop, tp=None):
        if tp is None:
            fn = lambda e: e.matmul(out_ap, lhsT, rhs, start=start, stop=stop)
        else:
            fn = lambda e: e.matmul(out_ap, lhsT, rhs, start=start, stop=stop, tile_position=tp)
        rd = (lr if isinstance(lr, list) else [lr]) + (rr if isinstance(rr, list) else [rr])
        self.P.add('pe', fn, reads=rd, writes=[outr])

    def act(self, outr, out_ap, inr, in_ap, func, bias=None, scale=None, extra_reads=()):
        kw = {}
        if bias is not None:
            kw['bias'] = bias
        if scale is not None:
            kw['scale'] = scale
        self.P.add('act', lambda e: e.activation(out=out_ap, in_=in_ap, func=func, **kw),
                   reads=[inr] + list(extra_reads), writes=[outr])

    def tt(self, eng, outr, out_ap, r0, in0, r1, in1, op):
        self.P.add(eng, lambda e: e.tensor_tensor(out=out_ap, in0=in0, in1=in1, op=op),
                   reads=[r0, r1], writes=[outr])

    def ts(self, eng, outr, out_ap, r0, in0, s1, s2, op0, op1=None, extra_reads=()):
        if op1 is None:
            fn = lambda e: e.tensor_scalar(out=out_ap, in0=in0, scalar1=s1, scalar2=None, op0=op0)
        else:
            fn = lambda e: e.tensor_scalar(out=out_ap, in0=in0, scalar1=s1, scalar2=s2, op0=op0, op1=op1)
        self.P.add(eng, fn, reads=[r0] + list(extra_reads), writes=[outr])

    def stt(self, outr, out_ap, r0, in0, scalar, r1, in1, op0, op1, extra_reads=()):
        self.P.add('dve', lambda e: e.scalar_tensor_tensor(out=out_ap, in0=in0, scalar=scalar, in1=in1,
                                                           op0=op0, op1=op1),
                   reads=[r0, r1] + list(extra_reads), writes=[outr])

    def cp(self, eng, outr, out_ap, inr, in_ap):
        self.P.add(eng, lambda e: e.tensor_copy(out=out_ap, in_=in_ap), reads=[inr], writes=[outr])

    def dma(self, eng, outr, out_ap, inr, in_ap):
        self.P.add(eng, lambda e: e.dma_start(out=out_ap, in_=in_ap),
                   reads=[inr] if inr is not None else [], writes=[outr], dma=True)

    def rms_rstd(self, ps, sq_res, sq_ap_fn, nchunk, dim, rt, rstd):
        for c in range(nchunk):
            self.mm(ps, ps.a, self.ones, self.ones.a, sq_res, sq_ap_fn(c), c == 0, c == nchunk - 1)
        self.act(rt, rt.a, ps, ps.a, AF.Sqrt, bias=self.epst.a[:, 0:1], scale=1.0 / dim, extra_reads=[self.epst])
        self.P.add('dve', lambda e: e.reciprocal(out=rstd.a, in_=rt.a), reads=[rt], writes=[rstd])

    def m1(self, l, src):
        P, a, d, psb = self.P, self.A1, self.d, self.psb
        xin = d[src].rearrange("(c p) t -> p c t", p=128)
        xr = self.dr[src]
        P.add('pool', lambda e: e.memset(a.vt.a[:, :, :, 64:96], 1.0), writes=[a.vt])
        ropev = d["rope"].rearrange("k p s -> p k s")

        def load(tt):
            X = a.X[tt % 2]
            cs = a.cs[tt % 2]
            self.dma('sp', X, X.a, xr, xin[:, :, tt * 512:(tt + 1) * 512])
            p0 = (tt % self.TPS) * 512
            self.dma('sp', cs, cs.a[64:96, :, :], None, ropev[64:96, :, p0:p0 + 512])

        load(0)
        self.load_w(a.win, d["w_inx"][l], 8, WINX, a.stg)
        self.load_w(a.wuq, d["w_uq"][l], 3, 768, a.stg)
        self.load_w(a.wuqsw, d["w_uqsw"][l], 3, 256, a.stg)
        self.load_w(a.wuk, d["w_uk"][l], 2, 512, a.stg)
        self.load_w(a.wuv, d["w_uv"][l], 2, 512, a.stg)
        zb = [1, 2, 3, 4]
        zi = [0]

        def nextbank():
            b = psb[zb[zi[0] % 4]]
            zi[0] += 1
            return b

        for tt in range(self.NT):
            if tt + 1 < self.NT:
                load(tt + 1)
            X, cs = a.X[tt % 2], a.cs[tt % 2]
            t0 = tt * 512
            cosr = cs.a[64:96, 0, :]
            sinr = cs.a[64:96, 1, :]
            self.act(a.xsq, a.xsq.a, X, X.a, AF.Square)
            for dc in range(8):
                self.ts('dve', a.xg, a.xg.a[:, dc, :], X, X.a[:, dc, :], self.vec(l, GPRE + dc), None, ALU.mult,
                        extra_reads=[self.vecs])
            self.rms_rstd(psb[0], a.xsq, lambda c: a.xsq.a[:, c, :], 8, 1024.0, a.rt, a.rstd)

            def zchunk(ps, c0, m, tp=None, out_ap=None):
                oa = ps.a[0:m, :] if out_ap is None else out_ap
                for dc in range(8):
                    self.mm(ps, oa, self.wr(a.win, c0, c0 + m), a.win.a[:, dc, c0:c0 + m], a.xg, a.xg.a[:, dc, :], dc == 0, dc == 7, tp)

            for c in range(3):
                ps = nextbank()
                zchunk(ps, c * 128, 128)
                self.tt('dve', a.cq, a.cq.a[:, c, :], ps, ps.a, a.rstd, a.rstd.a, ALU.mult)
            for c in range(2):
                ps = nextbank()
                zchunk(ps, 384 + c * 128, 128)
                self.tt('dve', a.ckv, a.ckv.a[:, c, :], ps, ps.a, a.rstd, a.rstd.a, ALU.mult)
            self.act(a.sq, a.sq.a, a.cq, a.cq.a, AF.Square)
            self.rms_rstd(psb[7], a.sq, lambda c: a.sq.a[:, c, :], 3, 384.0, a.rt, a.rq)
            for c in range(3):
                self.stt(a.cqn, a.cqn.a[:, c, :], a.cq, a.cq.a[:, c, :], self.vec(l, GQ + c), a.rq, a.rq.a,
                         ALU.mult, ALU.mult, extra_reads=[self.vecs])
            self.act(a.sq, a.sq.a[:, 0:2, :], a.ckv, a.ckv.a, AF.Square)
            self.rms_rstd(psb[7], a.sq, lambda c: a.sq.a[:, c, :], 2, 256.0, a.rt, a.rq)
            for c in range(2):
                self.stt(a.ckvn, a.ckvn.a[:, c, :], a.ckv, a.ckv.a[:, c, :], self.vec(l, GKV + c), a.rq, a.rq.a,
                         ALU.mult, ALU.mult, extra_reads=[self.vecs])
            zchunk(psb[5], 640, 32, tp=(0, 64), out_ap=psb[5].a[64:96, :])
            zchunk(psb[6], 672, 32, tp=(0, 64), out_ap=psb[6].a[64:96, :])
            self.tt('dve', a.kp, a.kp.a[64:96, :], psb[5], psb[5].a[64:96, :], a.rstd, a.rstd.a[64:96, :], ALU.mult)
            self.tt('dve', a.kps, a.kps.a[64:96, :], psb[6], psb[6].a[64:96, :], a.rstd, a.rstd.a[64:96, :], ALU.mult)
            self.tt('dve', a.kp, a.kp.a[64:96, :], a.kp, a.kp.a[64:96, :], cs, cosr, ALU.mult)
            self.tt('dve', a.kps, a.kps.a[64:96, :], a.kps, a.kps.a[64:96, :], cs, sinr, ALU.mult)
            self.tt('pool', a.kr, a.kr.a[64:96, :], a.kp, a.kp.a[64:96, :], a.kps, a.kps.a[64:96, :], ALU.add)
            for c in range(4):
                ps = nextbank()
                zchunk(ps, 704 + c * 128, 128)
                self.tt('dve', a.ut, a.ut.a[:, c, :].rearrange("p (j cl) -> p cl j", j=8),
                        ps, ps.a.rearrange("p (cl j) -> p cl j", j=8),
                        a.rstd, a.rstd.a.rearrange("p (cl j) -> p cl j", j=8), ALU.mult)
            for c in range(4):
                self.dma('sp', self.dr["uT"],
                         d["uT"][c * 128:(c + 1) * 128, :].rearrange("p (j n) -> p j n", j=8)[:, :, tt * 64:(tt + 1) * 64],
                         a.ut, a.ut.a[:, c, :].rearrange("p (j cl) -> p j cl", j=8))
            for gb in range(4):
                gts = a.gts[gb % 2]
                for gi in range(4):
                    gc = gb * 4 + gi
                    ps = nextbank()
                    zchunk(ps, 1216 + gc * 128, 128)
                    tmp = a.tmp[gc % 4]
                    self.tt('dve', tmp, tmp.a, ps, ps.a, a.rstd, a.rstd.a, ALU.mult)
                    self.act(gts, gts.a[:, gi, :], tmp, tmp.a, AF.Sigmoid, bias=self.vec(l, BG + gc),
                             extra_reads=[self.vecs])
                self.dma('sp', self.dr["gT"],
                         d["gT"].rearrange("(c p) t -> p c t", p=128)[:, gb * 4:(gb + 1) * 4, t0:t0 + 512],
                         gts, gts.a)
            for h in range(8):
                ps = nextbank()
                psw = psb[5 + h % 2]
                for c in range(3):
                    self.mm(ps, ps.a[0:96, :], a.wuq, a.wuq.a[:, c, 96 * h:96 * h + 96], a.cqn, a.cqn.a[:, c, :],
                            c == 0, c == 2)
                for c in range(3):
                    self.mm(psw, psw.a[64:96, :], a.wuqsw, a.wuqsw.a[:, c, 32 * h:32 * h + 32], a.cqn,
                            a.cqn.a[:, c, :], c == 0, c == 2, tp=(0, 64))
                qh = a.qh[h % 3]
                t1, t2 = a.tmp[(2 * h) % 4], a.tmp[(2 * h + 1) % 4]
                self.act(qh, qh.a[0:64, :], ps, ps.a[0:64, :], AF.Copy)
                self.tt('dve', t1, t1.a[64:96, :], ps, ps.a[64:96, :], cs, cosr, ALU.mult)
                self.tt('dve', t2, t2.a[64:96, :], psw, psw.a[64:96, :], cs, sinr, ALU.mult)
                self.tt('pool', qh, qh.a[64:96, :], t1, t1.a[64:96, :], t2, t2.a[64:96, :], ALU.add)
                self.dma('sp', self.dr["qT"], d["qT"][h, :, t0:t0 + 512], qh, qh.a[0:96, :])
            for h in range(8):
                ps = nextbank()
                for c in range(2):
                    self.mm(ps, ps.a[0:64, :], a.wuk, a.wuk.a[:, c, 64 * h:64 * h + 64], a.ckvn, a.ckvn.a[:, c, :],
                            c == 0, c == 1)
                kh = a.kh[h % 3]
                self.act(kh, kh.a[0:64, :], ps, ps.a[0:64, :], AF.Copy)
                self.cp('pool', kh, kh.a[64:96, :], a.kr, a.kr.a[64:96, :])
                self.dma('sp', self.dr["kT"], d["kT"][h, :, t0:t0 + 512], kh, kh.a[0:96, :])
            for tb in range(4):
                ps = nextbank()
                for c in range(2):
                    self.mm(ps, ps.a, a.ckvn, a.ckvn.a[:, c, tb * 128:(tb + 1) * 128], a.wuv, a.wuv.a[:, c, :],
                            c == 0, c == 1)
                self.cp('dve' if tb % 2 else 'act', a.vt, a.vt.a[:, tb, :, 0:64],
                        ps, ps.a.rearrange("p (h c) -> p h c", h=8)) if tb % 2 else \
                    self.act(a.vt, a.vt.a[:, tb, :, 0:64], ps, ps.a.rearrange("p (h c) -> p h c", h=8), AF.Copy)
            self.dma('sp', self.dr["vA"],
                     d["vA"].rearrange("(n p) c -> p n c", p=128)[:, 4 * tt:4 * tt + 4, :],
                     a.vt, a.vt.a.rearrange("p n h c -> p n (h c)"))

    def m2(self, l):
        P, a, d, psb = self.P, self.A2, self.d, self.psb
        S = self.S
        scale = 1.0 / math.sqrt(96.0)
        vAv = d["vA"].rearrange("(n p) c -> p n c", p=128)
        nsb = S // 128
        pairs = [(s, h) for s in range(self.NSEQ) for h in range(8)]

        def loadv(s):
            v = a.v[s % 2]
            self.dma('sp', v, v.a, self.dr["vA"], vAv[:, s * nsb:(s + 1) * nsb, :])

        def loadqk(i):
            s, h = pairs[i]
            q, k = a.q[i % 2], a.k[i % 2]
            self.dma('sp', q, q.a[0:96, :], self.dr["qT"], d["qT"][h, :, s * S:(s + 1) * S])
            self.dma('sp', k, k.a[0:96, :], self.dr["kT"], d["kT"][h, :, s * S:(s + 1) * S])

        blocks = []
        g = 0
        for i, (s, h) in enumerate(pairs):
            for qa in range(S // 512):
                nblk = 4 * qa + 4
                for j in range(nblk):
                    blocks.append((i, s, h, qa, j, nblk, g))
                g += 1
        N = len(blocks)

        def geom(b):
            i, s, h, qa, j, nblk, g = b
            r = j - 4 * qa
            qoff = 128 * r if r > 0 else 0
            return r, qoff, 512 - qoff

        def emitS(idx):
            i, s, h, qa, j, nblk, g = blocks[idx]
            r, qoff, nq = geom(blocks[idx])
            q, k = a.q[i % 2], a.k[i % 2]
            pss = psb[idx % 3]
            self.mm(pss, pss.a[:, 0:nq], k, k.a[0:96, j * 128:(j + 1) * 128],
                    q, q.a[0:96, qa * 512 + qoff:qa * 512 + 512], True, True)

        def emitEP(idx):
            r, qoff, nq = geom(blocks[idx])
            pss, pT = psb[idx % 3], a.pT[idx % 3]
            self.act(pT, pT.a[:, 0:nq], pss, pss.a[:, 0:nq], AF.Exp, scale=scale)
            if r >= 0:
                P.add('dve', lambda e, pT=pT: e.memset(pT.a[64:128, 0:64], 0.0), writes=[pT])

        def emitPV(idx):
            i, s, h, qa, j, nblk, g = blocks[idx]
            r, qoff, nq = geom(blocks[idx])
            v, pT, po = a.v[s % 2], a.pT[idx % 3], psb[POB[g % 3]]
            self.mm(po, po.a[0:96, qoff:512], v, v.a[:, j, h * 96:(h + 1) * 96], pT, pT.a[:, 0:nq],
                    j == 0, j == nblk - 1)

        def tail_front(b):
            g = b[6]
            po, rc, hi, lo = psb[POB[g % 3]], a.rc[g % 3], a.hi[g % 3], a.lo[g % 3]
            P.add('dve', lambda e, rc=rc, po=po: e.reciprocal(out=rc.a[64:65, :], in_=po.a[64:65, :]),
                  reads=[po], writes=[rc])
            self.cp('dve', hi, hi.a[64:65, :], rc, rc.a[64:65, :])
            self.tt('dve', lo, lo.a[64:65, :], rc, rc.a[64:65, :], hi, hi.a[64:65, :], ALU.subtract)

        def tail_back(b):
            i, s, h, qa, j, nblk, g = b
            po, pb = psb[POB[g % 3]], psb[5 + g % 2]
            hi, lo, bc, ot = a.hi[g % 3], a.lo[g % 3], a.bc[g % 2], a.ot[g % 2]
            self.mm(pb, pb.a[0:64, :], self.ones, self.ones.a[64:65, 0:64], hi, hi.a[64:65, :], True, False,
                    tp=(64, 0))
            self.mm(pb, pb.a[0:64, :], self.ones, self.ones.a[64:65, 0:64], lo, lo.a[64:65, :], False, True,
                    tp=(64, 0))
            self.act(bc, bc.a[0:64, :], pb, pb.a[0:64, :], AF.Copy)
            self.tt('dve', ot, ot.a[0:64, :], po, po.a[0:64, :], bc, bc.a[0:64, :], ALU.mult)
            tq = s * S + qa * 512
            self.dma('sp', self.dr["oT"], d["oT"][h * 64:(h + 1) * 64, tq:tq + 512], ot, ot.a[0:64, :])

        POB = [3, 4, 7]
        DEFER = 6
        loadv(0)
        loadqk(0)
        if len(pairs) > 1:
            loadqk(1)
        emitS(0)
        if N > 1:
            emitS(1)
        pending = []
        for idx in range(N):
            b = blocks[idx]
            i, s, h, qa, j, nblk, g = b
            if qa == 0 and j == 0:
                if h == 0 and s + 1 < self.NSEQ:
                    loadv(s + 1)
            emitEP(idx)
            if idx + 2 < N:
                b2 = blocks[idx + 2]
                if b2[3] == 0 and b2[4] == 0 and b2[0] + 1 < len(pairs) and b2[0] >= 1:
                    pass
                emitS(idx + 2)
            emitPV(idx)
            pending = [(pb_, c_ - 1) for (pb_, c_) in pending]
            while pending and pending[0][1] <= 0:
                tail_back(pending.pop(0)[0])
            if j == nblk - 1:
                tail_front(b)
                pending.append((b, DEFER))
                if qa == S // 512 - 1 and i + 2 < len(pairs):
                    loadqk(i + 2)
        for pb_, c_ in pending:
            tail_back(pb_)

    def m3_prep(self, l):
        P, a, d, psb = self.P, self.A3, self.d, self.psb
        T, NCH, CPS, NSEQ = self.T, self.NCH, self.CPS, self.NSEQ
        sv = self.s5v.a[:, l * 48:(l + 1) * 48]
        lre, lim, lst = sv[:, 0:16], sv[:, 16:32], sv[:, 32:48]
        svr = self.s5v
        self.dma('sp', a.m, a.m.a, None, d["s5m"][l].rearrange("p (k g c) -> p k g c", k=4, g=16))
        sm = a.sm
        dv, ac = 'dve', 'act'
        delta, lrd, mag, ang, rr, kf, fr, s1, s2, ch, tA, tB, ca, den = sm
        self.act(delta, delta.a, svr, lst, AF.Exp)
        self.tt(dv, lrd, lrd.a, svr, lre, delta, delta.a, ALU.mult)
        self.act(mag, mag.a, lrd, lrd.a, AF.Exp)
        self.tt(dv, ang, ang.a, svr, lim, delta, delta.a, ALU.mult)
        self.ts(dv, rr, rr.a, ang, ang.a, 1.0 / (2.0 * math.pi), None, ALU.mult)
        self.cp(dv, a.smi, a.smi.a, rr, rr.a)
        self.cp(dv, kf, kf.a, a.smi, a.smi.a)
        self.tt(dv, fr, fr.a, rr, rr.a, kf, kf.a, ALU.subtract)
        self.act(s1, s1.a, fr, fr.a, AF.Sin, scale=math.pi)
        self.act(s2, s2.a, fr, fr.a, AF.Sin, scale=math.pi / 2.0)
        self.tt(dv, tA, tA.a, s2, s2.a, s2, s2.a, ALU.mult)
        self.ts(dv, ch, ch.a, tA, tA.a, -2.0, 1.0, ALU.mult, ALU.add)
        self.tt(dv, tA, tA.a, s1, s1.a, ch, ch.a, ALU.mult)
        Pw = a.Pw
        self.stt(Pw, Pw.a[:, 1, 1, :], tA, tA.a, 2.0, mag, mag.a, ALU.mult, ALU.mult)
        self.tt(dv, tB, tB.a, s1, s1.a, s1, s1.a, ALU.mult)
        self.ts(dv, ca, ca.a, tB, tB.a, -2.0, 1.0, ALU.mult, ALU.add)
        self.tt(dv, Pw, Pw.a[:, 1, 0, :], ca, ca.a, mag, mag.a, ALU.mult)
        P.add(dv, lambda e: e.memset(Pw.a[:, 0, 0, :], 1.0), writes=[Pw])
        P.add(dv, lambda e: e.memset(Pw.a[:, 0, 1, :], 0.0), writes=[Pw])
        ar, ai = Pw.a[:, 1, 0, :], Pw.a[:, 1, 1, :]
        nr = s1
        self.ts(dv, nr, nr.a, Pw, ar, -1.0, None, ALU.add)
        self.tt(dv, tA, tA.a, svr, lre, svr, lre, ALU.mult)
        self.tt(dv, tB, tB.a, svr, lim, svr, lim, ALU.mult)
        self.tt(dv, den, den.a, tA, tA.a, tB, tB.a, ALU.add)
        P.add(dv, lambda e: e.reciprocal(out=den.a, in_=den.a), reads=[den], writes=[den])
        self.tt(dv, tA, tA.a, nr, nr.a, svr, lre, ALU.mult)
        self.tt(dv, tB, tB.a, Pw, ai, svr, lim, ALU.mult)
        self.tt(dv, tA, tA.a, tA, tA.a, tB, tB.a, ALU.add)
        self.tt(dv, a.fc, a.fc.a[:, 0, :], tA, tA.a, den, den.a, ALU.mult)
        self.tt(dv, tA, tA.a, Pw, ai, svr, lre, ALU.mult)
        self.tt(dv, tB, tB.a, nr, nr.a, svr, lim, ALU.mult)
        self.tt(dv, tA, tA.a, tA, tA.a, tB, tB.a, ALU.subtract)
        self.tt(dv, a.fc, a.fc.a[:, 1, :], tA, tA.a, den, den.a, ALU.mult)
        for k in range(2, 9):
            pr, pi = Pw.a[:, k - 1, 0, :], Pw.a[:, k - 1, 1, :]
            self.tt(dv, tA, tA.a, Pw, pr, Pw, ar, ALU.mult)
            self.tt(dv, tB, tB.a, Pw, pi, Pw, ai, ALU.mult)
            self.tt(dv, Pw, Pw.a[:, k, 0, :], tA, tA.a, tB, tB.a, ALU.subtract)
            self.tt(dv, tA, tA.a, Pw, pr, Pw, ai, ALU.mult)
            self.tt(dv, tB, tB.a, Pw, pi, Pw, ar, ALU.mult)
            self.tt(dv, Pw, Pw.a[:, k, 1, :], tA, tA.a, tB, tB.a, ALU.add)
        self.ts(dv, a.nAi, a.nAi.a, Pw, Pw.a[:, 8, 1, :], -1.0, None, ALU.mult)

        def bc32(ap16):
            return ap16.unsqueeze(2).broadcast_to([128, 16, 32])

        Cr, Ci, Br, Bi = a.m.a[:, 0], a.m.a[:, 1], a.m.a[:, 2], a.m.a[:, 3]
        fre, fim = bc32(a.fc.a[:, 0, :]), bc32(a.fc.a[:, 1, :])
        td, tp = a.td, a.td
        for k in range(9):
            self.ts(dv, a.nPr, a.nPr.a[:, k, :], Pw, Pw.a[:, k, 0, :], -1.0, None, ALU.mult)
        pl = 'pool'
        self.tt(pl, td[0], td[0].a, a.m, Br, a.fc, fre, ALU.mult)
        self.tt(pl, td[1], td[1].a, a.m, Bi, a.fc, fim, ALU.mult)
        self.tt(pl, a.Bb, a.Bb.a[:, 0], td[0], td[0].a, td[1], td[1].a, ALU.subtract)
        self.tt(pl, td[2], td[2].a, a.m, Bi, a.fc, fre, ALU.mult)
        self.tt(pl, td[3], td[3].a, a.m, Br, a.fc, fim, ALU.mult)
        self.tt(pl, a.Bb, a.Bb.a[:, 1], td[2], td[2].a, td[3], td[3].a, ALU.add)
        self.cp('pool', a.BbQb, a.BbQb.a[:, 0], a.Bb, a.Bb.a[:, 0])
        self.cp('pool', a.BbQb, a.BbQb.a[:, 1], a.Bb, a.Bb.a[:, 1])
        for k in range(9):
            pr, pi = bc32(Pw.a[:, k, 0, :]), bc32(Pw.a[:, k, 1, :])
            npr = bc32(a.nPr.a[:, k, :])
            self.tt(pl, td[0], td[0].a, a.m, Cr, Pw, pr, ALU.mult)
            self.tt(pl, td[1], td[1].a, a.m, Ci, Pw, pi, ALU.mult)
            self.tt(pl, a.PCb, a.PCb.a[:, k, 0], td[0], td[0].a, td[1], td[1].a, ALU.subtract)
            self.tt(pl, td[2], td[2].a, a.m, Ci, a.nPr, npr, ALU.mult)
            self.tt(pl, td[3], td[3].a, a.m, Cr, Pw, pi, ALU.mult)
            self.tt(pl, a.PCb, a.PCb.a[:, k, 1], td[2], td[2].a, td[3], td[3].a, ALU.subtract)
        for j in range(8):
            pr, pi = bc32(Pw.a[:, 7 - j, 0, :]), bc32(Pw.a[:, 7 - j, 1, :])
            self.tt(pl, tp[0], tp[0].a, a.Bb, a.Bb.a[:, 0], Pw, pr, ALU.mult)
            self.tt(pl, tp[1], tp[1].a, a.Bb, a.Bb.a[:, 1], Pw, pi, ALU.mult)
            v4 = lambda r: r.a.rearrange("p (t m) c -> p t m c", t=4)
            self.tt(pl, a.PBb, a.PBb.a[:, :, j, 0, :, :], tp[0], v4(tp[0]), tp[1], v4(tp[1]), ALU.subtract)
            self.tt(pl, tp[2], tp[2].a, a.Bb, a.Bb.a[:, 1], Pw, pr, ALU.mult)
            self.tt(pl, tp[3], tp[3].a, a.Bb, a.Bb.a[:, 0], Pw, pi, ALU.mult)
            self.tt(pl, a.PBb, a.PBb.a[:, :, j, 1, :, :], tp[2], v4(tp[2]), tp[3], v4(tp[3]), ALU.add)
        LS = CPS // self.G
        nk = LS.bit_length() - 1
        assert (1 << nk) == LS
        Q = a.Q
        sA, sB = a.sm[0], a.sm[1]
        self.cp(pl, Q, Q.a[:, 0, :, :], Pw, Pw.a[:, 8, :, :])
        for k in range(1, nk + 1):
            qr, qi = Q.a[:, k - 1, 0, :], Q.a[:, k - 1, 1, :]
            self.tt(pl, sA, sA.a, Q, qr, Q, qr, ALU.mult)
            self.tt(pl, sB, sB.a, Q, qi, Q, qi, ALU.mult)
            self.tt(pl, Q, Q.a[:, k, 0, :], sA, sA.a, sB, sB.a, ALU.subtract)
            self.tt(pl, sA, sA.a, Q, qr, Q, qi, ALU.mult)
            self.tt(pl, Q, Q.a[:, k, 1, :], sA, sA.a, sA, sA.a, ALU.add)
        Ap = a.Apow
        P.add(pl, lambda e: e.memset(Ap.a[:, :, 0, 0:1], 1.0), writes=[Ap])
        P.add(pl, lambda e: e.memset(Ap.a[:, :, 1, 0:1], 0.0), writes=[Ap])
        for k in range(nk):
            n = 1 << k
            qr = Q.a[:, k, 0, :].unsqueeze(2).broadcast_to([128, 16, n])
            qi = Q.a[:, k, 1, :].unsqueeze(2).broadcast_to([128, 16, n])
            sr, si = Ap.a[:, :, 0, 0:n], Ap.a[:, :, 1, 0:n]
            u0, u1 = tp[0].a[:, :, 0:n], tp[1].a[:, :, 0:n]
            self.tt(pl, tp[0], u0, Ap, sr, Q, qr, ALU.mult)
            self.tt(pl, tp[1], u1, Ap, si, Q, qi, ALU.mult)
            self.tt(pl, Ap, Ap.a[:, :, 0, n:2 * n], tp[0], u0, tp[1], u1, ALU.subtract)
            self.tt(pl, tp[0], u0, Ap, sr, Q, qi, ALU.mult)
            self.tt(pl, tp[1], u1, Ap, si, Q, qr, ALU.mult)
            self.tt(pl, Ap, Ap.a[:, :, 1, n:2 * n], tp[0], u0, tp[1], u1, ALU.add)

    def m3(self, l):
        P, a, d, psb = self.P, self.A3, self.d, self.psb
        T, NCH, CPS, NSEQ = self.T, self.NCH, self.CPS, self.NSEQ
        Pw = a.Pw
        dv = 'dve'
        for Tt in range(4):
            self.dma('sp', a.u, a.u.a[:, Tt, :], self.dr["uT"], d["uT"][Tt * 128:(Tt + 1) * 128, :])
        nb = 0
        for Tt in range(4):
            for jh in range(2):
                ps = psb[nb % 2]
                nb += 1
                psv = ps.a.bitcast(BF16)
                for jj in range(4):
                    for ri in range(2):
                        j = jh * 4 + jj
                        idx = jj * 2 + ri
                        P.add('pe', lambda e, psv=psv, idx=idx, Tt=Tt, j=j, ri=ri: e.transpose(
                            psv[:, idx * 128:(idx + 1) * 128], a.PBb.a[:, Tt, j, ri, :, :].rearrange("p m c -> p (m c)"),
                            self.ident.a), reads=[a.PBb, self.ident], writes=[ps])
                self.cp('dve' if nb % 2 else 'act', a.Wp,
                        a.Wp.a[:, Tt, jh * 4:jh * 4 + 4, :, :].rearrange("p j r c -> p (j r c)"),
                        ps, psv) if nb % 2 else \
                    self.act(a.Wp, a.Wp.a[:, Tt, jh * 4:jh * 4 + 4, :, :].rearrange("p j r c -> p (j r c)"),
                             ps, psv, AF.Copy)
        fl = lambda ap: ap.rearrange("p m c -> p (m c)")
        for Tt in range(4):
            for kh in range(2):
                ps = psb[2 + (nb % 2)]
                nb += 1
                for kk in range(4):
                    k = kh * 4 + kk
                    oa = ps.a[:, kk * 128:(kk + 1) * 128]
                    for ri in range(2):
                        self.mm(ps, oa, a.BbQb, fl(a.BbQb.a[:, ri, 4 * Tt:4 * Tt + 4, :]),
                                a.PCb, fl(a.PCb.a[:, k, ri, 4 * Tt:4 * Tt + 4, :]), ri == 0, ri == 1)
                self.tt('dve', a.Kb, a.Kb.a[:, Tt, kh * 4:kh * 4 + 4, :], ps,
                        ps.a.rearrange("p (k c) -> p k c", k=4), self.bmask,
                        self.bmask.a.unsqueeze(1).broadcast_to([128, 4, 128]), ALU.mult)
            self.stt(a.Kb, a.Kb.a[:, Tt, 0, :], self.ident, self.ident.a, self.vec(l, DSK + Tt), a.Kb,
                     a.Kb.a[:, Tt, 0, :], ALU.mult, ALU.add, extra_reads=[self.vecs])
        for gp in range(16):
            Tt, m = gp // 4, gp % 4
            for ri in range(2):
                ps = psb[4 + (nb % 4)]
                nb += 1
                for j in range(8):
                    rhs = a.u.a[32 * m:32 * m + 32, Tt, j * NCH:(j + 1) * NCH]
                    self.mm(ps, ps.a[:, 0:NCH], a.Wp, a.Wp.a[32 * m:32 * m + 32, Tt, j, ri, :], a.u, rhs,
                            j == 0, j == 7, tp=(32 * m, 0))
                if nb % 2:
                    self.cp('dve', a.Sb, a.Sb.a[:, gp, ri, :], ps, ps.a[:, 0:NCH])
                else:
                    self.act(a.Sb, a.Sb.a[:, gp, ri, :], ps, ps.a[:, 0:NCH], AF.Copy)
        G = self.G
        NS2 = NSEQ * G
        LS = CPS // G
        nk = LS.bit_length() - 1
        Sv = a.Sb.a.rearrange("p g r (s c) -> p g r s c", s=NS2)
        Xv = a.Xb.a.rearrange("p g r (s c) -> p g r s c", s=NS2)
        Ar4 = Pw.a[:, 8, 0, :].unsqueeze(2).unsqueeze(3).broadcast_to([128, 16, 2, NS2])
        Ai3 = Pw.a[:, 8, 1, :].unsqueeze(2).broadcast_to([128, 16, NS2])
        nAi3 = a.nAi.a.unsqueeze(2).broadcast_to([128, 16, NS2])
        P.add(dv, lambda e: e.memset(a.xs[0].a, 0.0), writes=[a.xs[0]])
        for c in range(LS):
            xc, xn = a.xs[c % 2], a.xs[(c + 1) % 2]
            self.act(a.Xb, Xv[:, :, :, :, c], xc, xc.a, AF.Copy)
            self.tt(dv, a.t1, a.t1.a, xc, xc.a, Pw, Ar4, ALU.mult)
            self.tt(dv, a.t2, a.t2.a[:, :, 0, :], xc, xc.a[:, :, 1, :], a.nAi, nAi3, ALU.mult)
            self.tt(dv, a.t2, a.t2.a[:, :, 1, :], xc, xc.a[:, :, 0, :], Pw, Ai3, ALU.mult)
            self.tt(dv, a.t1, a.t1.a, a.t1, a.t1.a, a.t2, a.t2.a, ALU.add)
            self.tt(dv, xn, xn.a, a.t1, a.t1.a, a.Sb, Sv[:, :, :, :, c], ALU.add)
        xe = a.xs[LS % 2]
        xev = xe.a.rearrange("p g r (s k) -> p g r s k", k=G)
        eev = a.ee.a.rearrange("p g r (s k) -> p g r s k", k=G)
        QL = a.Q.a[:, nk, :, :]
        QLr = QL[:, 0, :].unsqueeze(2).broadcast_to([128, 16, NSEQ])
        QLi = QL[:, 1, :].unsqueeze(2).broadcast_to([128, 16, NSEQ])
        P.add(dv, lambda e: e.memset(a.ee.a, 0.0), writes=[a.ee])
        s0, s1 = a.t1, a.t2
        s0v = s0.a.rearrange("p g r (s k) -> p g r s k", k=G)
        s1v = s1.a.rearrange("p g r (s k) -> p g r s k", k=G)
        for k in range(1, G):
            er, ei = eev[:, :, 0, :, k - 1], eev[:, :, 1, :, k - 1]
            self.tt(dv, s0, s0v[:, :, 0, :, 0], a.ee, er, a.Q, QLr, ALU.mult)
            self.tt(dv, s1, s1v[:, :, 0, :, 0], a.ee, ei, a.Q, QLi, ALU.mult)
            self.tt(dv, s0, s0v[:, :, 0, :, 0], s0, s0v[:, :, 0, :, 0], s1, s1v[:, :, 0, :, 0], ALU.subtract)
            self.tt(dv, a.ee, eev[:, :, 0, :, k], s0, s0v[:, :, 0, :, 0], xe, xev[:, :, 0, :, k - 1], ALU.add)
            self.tt(dv, s0, s0v[:, :, 1, :, 0], a.ee, er, a.Q, QLi, ALU.mult)
            self.tt(dv, s1, s1v[:, :, 1, :, 0], a.ee, ei, a.Q, QLr, ALU.mult)
            self.tt(dv, s0, s0v[:, :, 1, :, 0], s0, s0v[:, :, 1, :, 0], s1, s1v[:, :, 1, :, 0], ALU.add)
            self.tt(dv, a.ee, eev[:, :, 1, :, k], s0, s0v[:, :, 1, :, 0], xe, xev[:, :, 1, :, k - 1], ALU.add)
        nel = 16 * (G - 1) * LS
        scr = self.arena_t[:, a.Sb.off:a.Sb.off + 4 * nel].bitcast(F32)
        tA = scr[:, 0:nel].rearrange("p (g k c) -> p g k c", g=16, k=G - 1)
        tB = scr[:, nel:2 * nel].rearrange("p (g k c) -> p g k c", g=16, k=G - 1)
        X6 = a.Xb.a.rearrange("p g r (s k c) -> p g r s k c", s=NSEQ, k=G)
        shp = [128, 16, G - 1, LS]
        Apr = a.Apow.a[:, :, 0, :].unsqueeze(2).broadcast_to(shp)
        Api = a.Apow.a[:, :, 1, :].unsqueeze(2).broadcast_to(shp)
        for sq in range(NSEQ):
            er = eev[:, :, 0, sq, 1:G].unsqueeze(3).broadcast_to(shp)
            ei = eev[:, :, 1, sq, 1:G].unsqueeze(3).broadcast_to(shp)
            Xr, Xi = X6[:, :, 0, sq, 1:G, :], X6[:, :, 1, sq, 1:G, :]
            self.tt(dv, a.Sb, tA, a.Apow, Apr, a.ee, er, ALU.mult)
            self.tt(dv, a.Sb, tB, a.Apow, Api, a.ee, ei, ALU.mult)
            self.tt(dv, a.Sb, tA, a.Sb, tA, a.Sb, tB, ALU.subtract)
            self.tt(dv, a.Xb, Xr, a.Xb, Xr, a.Sb, tA, ALU.add)
            self.tt(dv, a.Sb, tA, a.Apow, Apr, a.ee, ei, ALU.mult)
            self.tt(dv, a.Sb, tB, a.Apow, Api, a.ee, er, ALU.mult)
            self.tt(dv, a.Sb, tA, a.Sb, tA, a.Sb, tB, ALU.add)
            self.tt(dv, a.Xb, Xi, a.Xb, Xi, a.Sb, tA, ALU.add)
        for Tt in range(4):
            for j in range(8):
                ps = psb[nb % 4]
                nb += 1
                for k in range(j + 1):
                    rhs = a.u.a[:, Tt, (j - k) * NCH:(j - k + 1) * NCH]
                    self.mm(ps, ps.a[:, 0:NCH], a.Kb, a.Kb.a[:, Tt, k, :], a.u, rhs, k == 0, False)
                for m in range(4):
                    gp = 4 * Tt + m
                    for ri in range(2):
                        self.mm(ps, ps.a[32 * m:32 * m + 32, 0:NCH], a.PCb, a.PCb.a[:, j + 1, ri, gp, :],
                                a.Xb, a.Xb.a[:, gp, ri, :], False, (m == 3 and ri == 1), tp=(0, 32 * m))
                self.act(a.ys, a.ys.a.rearrange("p (c j) -> p j c", j=8)[:, j, :], ps, ps.a[:, 0:NCH],
                         AF.Gelu_apprx_tanh)
            self.dma('sp', self.dr["ysT"], d["ysT"][Tt * 128:(Tt + 1) * 128, :], a.ys, a.ys.a)

    def post_norm_residual(self, l, a, X, goff, dst, t0, square_done=False):
        psb, d = self.psb, self.d
        if not square_done:
            self.act(a.ysq, a.ysq.a, a.Y, a.Y.a, AF.Square)
        self.rms_rstd(psb[7], a.ysq, lambda c: a.ysq.a[:, c, :], 8, 1024.0, a.rt, a.rstd)
        for oc in range(8):
            t = a.tt[oc % 2]
            self.tt('dve', t, t.a, a.Y, a.Y.a[:, oc, :], a.rstd, a.rstd.a, ALU.mult)
            self.stt(X, X.a[:, oc, :], t, t.a, self.vec(l, goff + oc), X, X.a[:, oc, :], ALU.mult, ALU.add,
                     extra_reads=[self.vecs])
        self.dma('sp', self.dr[dst], d[dst].rearrange("(c p) t -> p c t", p=128)[:, :, t0:t0 + 512], X, X.a)

    def m4(self, l, src, dst):
        P, a, d, psb = self.P, self.A4, self.d, self.psb
        xin = d[src].rearrange("(c p) t -> p c t", p=128)
        oin = d["oT"].rearrange("(c p) t -> p c t", p=128)
        yin = d["ysT"].rearrange("(c p) t -> p c t", p=128)
        gin = d["gT"].rearrange("(c p) t -> p c t", p=128)

        def load(tt):
            sl = slice(tt * 512, (tt + 1) * 512)
            self.dma('sp', a.o[tt % 2], a.o[tt % 2].a, self.dr["oT"], oin[:, :, sl])
            self.dma('sp', a.ysb[tt % 2], a.ysb[tt % 2].a, self.dr["ysT"], yin[:, :, sl])
            self.dma('sp', a.g[tt % 2], a.g[tt % 2].a, self.dr["gT"], gin[:, :, sl])
            self.dma('sp', a.X[tt % 2], a.X[tt % 2].a, self.dr[src], xin[:, :, sl])

        nbc = [0]

        def stageA(tt):
            o, ysb, g = a.o[tt % 2], a.ysb[tt % 2], a.g[tt % 2]
            for oc in range(8):
                nb = nbc[0]
                pa, pga, pgb = psb[(3 * nb) % 6], psb[(3 * nb + 1) % 6], psb[(3 * nb + 2) % 6]
                nbc[0] += 1
                sg, yss, m1, m2 = a.sg[oc % 2], a.yss[oc % 2], a.m1[oc % 2], a.m2[oc % 2]
                cs = slice(oc * 128, (oc + 1) * 128)
                for k in range(4):
                    self.mm(pa, pa.a, a.woatt, a.woatt.a[:, k, cs], o, o.a[:, k, :], k == 0, k == 3)
                for k in range(4):
                    self.mm(pga, pga.a, a.wglu, a.wglu.a[:, k, cs], ysb, ysb.a[:, k, :], k == 0, k == 3)
                for k in range(4):
                    self.mm(pgb, pgb.a, a.wglu, a.wglu.a[:, k, 1024 + oc * 128:1024 + (oc + 1) * 128], ysb,
                            ysb.a[:, k, :], k == 0, k == 3)
                self.act(sg, sg.a, pgb, pgb.a, AF.Sigmoid)
                self.tt('dve', yss, yss.a, pga, pga.a, sg, sg.a, ALU.mult)
                self.tt('dve', m1, m1.a, pa, pa.a, g, g.a[:, oc, :], ALU.mult)
                self.tt('dve', m2, m2.a, yss, yss.a, g, g.a[:, 8 + oc, :], ALU.mult)
                self.tt('dve', a.mg, a.mg.a[:, oc, :], m1, m1.a, m2, m2.a, ALU.add)

        def stageB(tt):
            for oc in range(8):
                py = psb[(3 * nbc[0]) % 6]
                nbc[0] += 1
                for k in range(8):
                    self.mm(py, py.a, a.wout, a.wout.a[:, k, oc * 128:(oc + 1) * 128], a.mg, a.mg.a[:, k, :],
                            k == 0, k == 7)
                self.act(a.Y, a.Y.a[:, oc, :], py, py.a, AF.Copy)
            self.act(a.ysq, a.ysq.a, a.Y, a.Y.a, AF.Square)

        def stageC(tt):
            self.post_norm_residual(l, a, a.X[tt % 2], GPOST, dst, tt * 512, square_done=True)

        NT = self.NT
        load(0)
        self.load_w(a.woatt, d["w_oatt"][l], 4, 1024, a.stg)
        self.load_w(a.wglu, d["w_glu"][l], 4, 2048, a.stg)
        self.load_w(a.wout, d["w_out"][l], 8, 1024, a.stg)
        if NT > 1:
            load(1)
        stageA(0)
        stageB(0)
        for tt in range(1, NT):
            stageA(tt)
            stageC(tt - 1)
            if tt + 1 < NT:
                load(tt + 1)
            stageB(tt)
        stageC(NT - 1)

    def f1(self, l, src):
        P, a, d, psb = self.P, self.A5, self.d, self.psb
        self.dma('sp', a.X[0], a.X[0].a, self.dr[src], d[src].rearrange("(c p) t -> p c t", p=128)[:, :, 0:512])
        self.load_w(a.wup, d["w_up"][l], 8, 2 * D_FF, a.stg)
        for fc in range(NFC):
            for k in range(3):
                self.ts('dve', a.dgall, a.dgall.a[:, fc, k, :], self.ident, self.ident.a,
                        self.vec(l, CW + fc * 3 + k), None, ALU.mult, extra_reads=[self.vecs])
        xin = d[src].rearrange("(c p) t -> p c t", p=128)
        aout = d["actT"].rearrange("(c p) t -> p c t", p=128)

        def load(tt):
            self.dma('sp', a.X[tt % 2], a.X[tt % 2].a, self.dr[src], xin[:, :, tt * 512:(tt + 1) * 512])

        gi = 0
        for tt in range(self.NT):
            if tt + 1 < self.NT:
                load(tt + 1)
            X = a.X[tt % 2]
            t0 = tt * 512
            rs = a.rstdF.a[:, t0:t0 + 512]
            self.act(a.sqg, a.sqg.a, X, X.a, AF.Square)
            for c in range(8):
                self.mm(psb[0], psb[0].a, self.ones, self.ones.a, a.sqg, a.sqg.a[:, c, :], c == 0, c == 7)
            self.act(a.rt, a.rt.a, psb[0], psb[0].a, AF.Sqrt, bias=self.epst.a[:, 0:1], scale=1.0 / 1024.0,
                     extra_reads=[self.epst])
            P.add('dve', lambda e, rs=rs: e.reciprocal(out=rs, in_=a.rt.a), reads=[a.rt], writes=[a.rstdF])
            for dc in range(8):
                self.ts('dve', a.sqg, a.sqg.a[:, dc, :], X, X.a[:, dc, :], self.vec(l, GFPRE + dc), None, ALU.mult,
                        extra_reads=[self.vecs])
            seq_start = (tt % self.TPS == 0)
            pend = None

            def conv_stage(fc, Gb, psv, ge_i):
                psc = psb[5 + fc % 2]
                for k in range(3):
                    self.mm(psc, psc.a, a.dgall, a.dgall.a[:, fc, k, :], Gb, Gb.a[:, k:k + 512], k == 0, k == 2)
                ge = a.ge[ge_i % 3]
                self.act(ge, ge.a, psc, psc.a, AF.Gelu_apprx_tanh, bias=self.vec(l, CB + fc),
                         extra_reads=[self.vecs])
                ar = a.act[fc // (NFC // 2)]
                self.tt('dve', ar, ar.a[:, fc % (NFC // 2), :], psv, psv.a, ge, ge.a, ALU.mult)
                if fc % (NFC // 2) == NFC // 2 - 1:
                    hh = fc // (NFC // 2)
                    self.dma('sp', self.dr["actT"], aout[:, hh * 11:(hh + 1) * 11, t0:t0 + 512], ar, ar.a)

            for fc in range(NFC):
                psg, psv = psb[1 + fc % 2], psb[3 + fc % 2]
                Gb = a.Gb[gi % 3]
                gi += 1
                for dc in range(8):
                    self.mm(psg, psg.a, self.wr(a.wup, fc * 256, fc * 256 + 128),
                            a.wup.a[:, dc, fc * 256:fc * 256 + 128], a.sqg, a.sqg.a[:, dc, :], dc == 0, dc == 7)
                for dc in range(8):
                    self.mm(psv, psv.a, self.wr(a.wup, fc * 256 + 128, fc * 256 + 256),
                            a.wup.a[:, dc, fc * 256 + 128:fc * 256 + 256], a.sqg, a.sqg.a[:, dc, :], dc == 0, dc == 7)
                if seq_start:
                    P.add('pool', lambda e, Gb=Gb: e.memset(Gb.a[:, 0:2], 0.0), writes=[Gb])
                else:
                    self.act(Gb, Gb.a[:, 0:2], a.Gh, a.Gh.a[:, fc, :], AF.Copy)
                self.tt('dve', Gb, Gb.a[:, 2:514], psg, psg.a, a.rstdF, rs, ALU.mult)
                self.act(a.Gh, a.Gh.a[:, fc, :], Gb, Gb.a[:, 512:514], AF.Copy)
                if pend is not None:
                    conv_stage(*pend)
                pend = (fc, Gb, psv, gi)
            conv_stage(*pend)

    def f2(self, l, src, dst):
        P, a, d, psb = self.P, self.A6, self.d, self.psb
        xin = d[src].rearrange("(c p) t -> p c t", p=128)
        ain = d["actT"].rearrange("(c p) t -> p c t", p=128)

        def load(tt):
            sl = slice(tt * 512, (tt + 1) * 512)
            self.dma('sp', a.a[tt % 2], a.a[tt % 2].a, self.dr["actT"], ain[:, :, sl])
            self.dma('sp', a.X[tt % 2], a.X[tt % 2].a, self.dr[src], xin[:, :, sl])

        load(0)
        self.load_w(a.wdown, d["w_down"][l], NFC, 1024, a.stg)
        nb = 0
        for tt in range(self.NT):
            if tt + 1 < self.NT:
                load(tt + 1)
            A_, X = a.a[tt % 2], a.X[tt % 2]
            rs = a.rstdF.a[:, tt * 512:(tt + 1) * 512]
            for oc in range(8):
                py = psb[nb % 6]
                nb += 1
                for k in range(NFC):
                    self.mm(py, py.a, self.wr(a.wdown, oc * 128, (oc + 1) * 128), a.wdown.a[:, k, oc * 128:(oc + 1) * 128], A_, A_.a[:, k, :],
                            k == 0, k == NFC - 1)
                self.tt('dve', a.Y, a.Y.a[:, oc, :], py, py.a, a.rstdF, rs, ALU.mult)
            self.post_norm_residual(l, a, X, GFPOST, dst, tt * 512)


def _rope_tables(seq):
    pos = np.arange(seq, dtype=np.float32)
    inv_freq = (np.float32(10000.0) ** (-np.arange(0, 32, 2, dtype=np.float32) / np.float32(32))).astype(np.float32)
    ang = (pos[:, None] * inv_freq[None, :]).astype(np.float32)
    cos, sin = np.cos(ang).astype(np.float32), np.sin(ang).astype(np.float32)
    tab = np.zeros((2, 128, seq), np.float32)
    tab[0, 64:80] = cos.T
    tab[0, 80:96] = cos.T
    tab[1, 64:80] = -sin.T
    tab[1, 80:96] = sin.T
    return tab


def _interleave_up(w):
    L = w.shape[0]
    g = w[:, :, :D_FF].reshape(L, 1024, NFC, 128)
    v = w[:, :, D_FF:].reshape(L, 1024, NFC, 128)
    return np.ascontiguousarray(np.stack([g, v], axis=3).reshape(L, 1024, 2 * D_FF))


def prep_weights(inp, depth, seq):
    f = lambda a: np.ascontiguousarray(np.asarray(a, dtype=np.float32))
    L = depth
    w_in = f(inp["w_in"])[:L]
    o1, o2, o3, o4 = 384, 640, 672, 1184
    kpe = w_in[:, :, o2:o3]
    kpe_sw = np.concatenate([kpe[:, :, 16:32], kpe[:, :, 0:16]], axis=2)
    w_inx = np.concatenate([w_in[:, :, 0:o2], kpe, kpe_sw, w_in[:, :, o3:o4], w_in[:, :, o4:]], axis=2)
    assert w_inx.shape[2] == WINX
    w_uq = f(inp["w_uq"])[:L]
    wq = w_uq.reshape(L, 384, 8, 96)
    w_uqsw = np.concatenate([wq[..., 80:96], wq[..., 64:80]], axis=3).reshape(L, 384, 256)
    w_ukv = f(inp["w_ukv"])[:L].reshape(L, 256, 8, 128)
    w_uk = w_ukv[..., 0:64].reshape(L, 256, 512)
    w_uv = w_ukv[..., 64:128].reshape(L, 256, 512)
    vecs = np.zeros((128, L, NV), np.float32)

    def put(off, arr, n):
        vecs[:, :, off:off + n] = f(arr)[:L].reshape(L, n, 128).transpose(2, 0, 1)

    put(GPRE, inp["g_mix_pre"], 8)
    put(BG, inp["b_gate"], 16)
    put(GQ, inp["g_q"], 3)
    put(GKV, inp["g_kv"], 2)
    put(GPOST, inp["g_mix_post"], 8)
    put(GFPRE, inp["g_ffn_pre"], 8)
    put(GFPOST, inp["g_ffn_post"], 8)
    cw = f(inp["conv_w"])[:L].reshape(L, 3, NFC, 128).transpose(3, 0, 2, 1)
    vecs[:, :, CW:CW + 66] = cw.reshape(128, L, 66)
    put(CB, inp["conv_b"], NFC)
    put(DSK, inp["d_skip"], 4)
    s5v = np.zeros((128, L, 48), np.float32)

    def gl(arr):
        return f(arr)[:L].reshape(L, 16, 2, 64).transpose(2, 3, 0, 1).reshape(128, L, 16)

    s5v[:, :, 0:16] = gl(inp["lam_re"])
    s5v[:, :, 16:32] = gl(inp["lam_im"])
    ls = f(inp["log_step"])[:L].reshape(L, 16, 2)
    s5v[:, :, 32:48] = np.repeat(ls.transpose(2, 0, 1)[:, None], 64, axis=1).reshape(128, L, 16)
    s5m = np.zeros((L, 2, 64, 4, 16, 2, 16), np.float32)
    cr = f(inp["c_re"])[:L].reshape(L, 16, 2, 16, 64)
    ci = f(inp["c_im"])[:L].reshape(L, 16, 2, 16, 64)
    br = f(inp["b_re"])[:L].reshape(L, 16, 2, 64, 16)
    bi = f(inp["b_im"])[:L].reshape(L, 16, 2, 64, 16)
    for g2 in range(2):
        s5m[:, g2, :, 0, :, g2, :] = cr[:, :, g2].transpose(0, 3, 1, 2)
        s5m[:, g2, :, 1, :, g2, :] = ci[:, :, g2].transpose(0, 3, 1, 2)
        s5m[:, g2, :, 2, :, g2, :] = br[:, :, g2].transpose(0, 2, 1, 3)
        s5m[:, g2, :, 3, :, g2, :] = bi[:, :, g2].transpose(0, 2, 1, 3)
    return {
        "w_inx": np.ascontiguousarray(w_inx),
        "w_uq": w_uq, "w_uqsw": np.ascontiguousarray(w_uqsw),
        "w_uk": np.ascontiguousarray(w_uk), "w_uv": np.ascontiguousarray(w_uv),
        "w_oatt": f(inp["w_o_att"])[:L], "w_glu": f(inp["w_glu"])[:L], "w_out": f(inp["w_out"])[:L],
        "w_up": _interleave_up(f(inp["w_up"])[:L]), "w_down": f(inp["w_down"])[:L],
        "vecs": np.ascontiguousarray(vecs.reshape(128, L * NV)),
        "s5v": np.ascontiguousarray(s5v.reshape(128, L * 48)),
        "s5m": np.ascontiguousarray(s5m.reshape(L, 128, 2048)),
        "rope": _rope_tables(seq),
        "ident": np.eye(128, dtype=np.float32),
        "bmask": np.kron(np.eye(4, dtype=np.float32), np.ones((32, 32), np.float32)),
    }


_CACHE = {}


def kernel(**inputs):
    x = np.asarray(inputs["x"], dtype=np.float32)
    B, S, D = x.shape
    ncores = 8
    nseq = B // ncores
    depth = int(np.asarray(inputs["w_in"]).shape[0])
    key = (nseq, S, depth)
    if key not in _CACHE:
        _CACHE[key] = K(nseq=nseq, seq=S, depth=depth).build()
    nc = _CACHE[key]
    wts = prep_weights(inputs, depth, S)
    in_maps = []
    for c in range(ncores):
        xs = x[c * nseq:(c + 1) * nseq].reshape(nseq * S, D)
        m = dict(wts)
        m["xT"] = np.ascontiguousarray(xs.T)
        in_maps.append(m)
    res = run_bass_kernel_spmd(nc, in_maps, core_ids=list(range(ncores)))
    out = np.empty((B, S, D), np.float32)
    for c in range(ncores):
        yT = np.asarray(res.results[c]["yT"], dtype=np.float32)
        out[c * nseq:(c + 1) * nseq] = yT.T.reshape(nseq, S, D)
    return out
```

```python
import contextlib
import math
import numpy as np
import ml_dtypes
import concourse.bass as bass
import concourse.mybir as mybir
from concourse.bass_utils import run_bass_kernel_spmd
from concourse.alu_op_type import AluOpType as ALU

AF = mybir.ActivationFunctionType
F32 = mybir.dt.float32
BF16 = mybir.dt.bfloat16
I32 = mybir.dt.int32
ENGS = ['sp', 'act', 'pool', 'dve', 'pe']

D_MODEL = 1024
N_HEADS = 8
D_FF = 2816
NFC = 22
EPS = 1e-6
NV = 145
GPRE, BG, GQ, GKV, GPOST, GFPRE, GFPOST, CW, CB, DSK = 0, 8, 24, 27, 29, 37, 45, 53, 119, 141
WINX = 3264
ARENA_ELEMS = 196 * 512


class Res:
    def __init__(self, name, dsem=None, multi=False):
        self.name = name
        self.writers = {}
        self.readers = {}
        self.dsem = dsem
        self.nw = 0
        self.multi = multi
        self.a = None


class Op:
    __slots__ = ('eng', 'fn', 'deps', 'needed', 'sig', 'dma', 'key')


class Prog:
    def __init__(self, nc, st):
        self.nc = nc
        self.st = st
        self.ops = []
        self.esem = {e: st.enter_context(nc.semaphore("s_" + e)) for e in ENGS}
        self.allsems = list(self.esem.values())
        self.nsem = len(ENGS)
        self.last = {e: None for e in ENGS}
        self.dma_res = []
        self.bar_deps = []

    def res(self, name, dma=False, multi=False):
        dsem = None
        if dma:
            dsem = self.st.enter_context(self.nc.semaphore("d_" + name))
            self.allsems.append(dsem)
            self.nsem += 1
        r = Res(name, dsem, multi)
        if dma:
            self.dma_res.append(r)
            r.last_dma = None
        return r

    def sb(self, name, shape, dtype, dma=False, multi=False):
        r = self.res(name, dma=dma, multi=multi)
        t = self.st.enter_context(self.nc.sbuf_tensor(name, shape, dtype))
        r.a = t[:]
        return r

    def ps(self, name, shape, dtype):
        r = self.res(name)
        t = self.st.enter_context(self.nc.psum_tensor(name, shape, dtype))
        r.a = t[:]
        return r

    def barrier(self):
        deps = [o for o in self.last.values() if o is not None]
        for r in self.dma_res:
            if r.last_dma is not None:
                deps.append(r.last_dma)
        self.bar_deps = deps

    def add(self, eng, fn, reads=(), writes=(), dma=False):
        op = Op()
        op.eng = eng
        op.fn = fn
        op.dma = dma
        op.needed = False
        op.sig = None
        deps = {}
        for o in self.bar_deps:
            deps[id(o)] = o
        for r in reads:
            for o in r.writers.values():
                deps[id(o)] = o
        for w in writes:
            for o in w.writers.values():
                deps[id(o)] = o
            for o in w.readers.values():
                deps[id(o)] = o
        if dma:
            dres = [w for w in writes if w.dsem is not None]
            assert len(dres) == 1, [w.name for w in writes]
            dres[0].nw += 1
            op.sig = (dres[0].dsem, 16 * dres[0].nw)
            dres[0].last_dma = op
            key = id(dres[0].dsem)
        else:
            key = eng
            self.last[eng] = op
        op.key = key
        op.deps = [o for o in deps.values()
                   if not (o.eng == 'pe' and eng == 'pe' and not o.dma and not dma)]
        for o in op.deps:
            o.needed = True
        for r in reads:
            r.readers[key] = op
        for w in writes:
            if w.multi:
                w.writers[key] = op
            else:
                w.writers = {key: op}
                w.readers = {}
        self.ops.append(op)
        return op

    def finish(self, final_res):
        nc = self.nc
        self.add('sp', None, reads=final_res)
        cnt = {e: 0 for e in ENGS}
        for op in self.ops:
            if not op.dma and op.needed:
                cnt[op.eng] += 1
                op.sig = (self.esem[op.eng], cnt[op.eng])
        per = {e: [o for o in self.ops if o.eng == e] for e in ENGS}
        self.stats = {e: len(per[e]) for e in ENGS}

        def mk(e):
            def body(engobj):
                waited = {}
                for op in per[e]:
                    need = {}
                    for d in op.deps:
                        s, v = d.sig
                        k = id(s)
                        if v > need.get(k, (None, 0))[1]:
                            need[k] = (s, v)
                    for k, (s, v) in need.items():
                        if v > waited.get(k, 0):
                            engobj.wait_ge(s, v)
                            waited[k] = v
                    if op.fn is None:
                        continue
                    ins = op.fn(engobj)
                    if op.dma:
                        ins.then_inc(op.sig[0], 16)
                    elif op.needed:
                        ins.then_inc(op.sig[0], 1)
            return body

        import os
        if os.environ.get("NOCLEAR") != "1":
            for sm in self.allsems:
                nc.gpsimd.sem_clear(sm)
            nc.all_engine_barrier()
        with nc.Block() as block:
            block.sync(mk('sp'))
            block.scalar(mk('act'))
            block.gpsimd(mk('pool'))
            block.vector(mk('dve'))
            block.tensor(mk('pe'))


class Arena:
    def __init__(self, K, name):
        self.K = K
        self.name = name
        self.off = 0

    def alloc(self, name, shape, dtype, dma=False, multi=False, at=None):
        n = 1
        for s in shape:
            n *= s
        nel = n * (2 if dtype in (F32, I32) else 1)
        nel = (nel + 15) // 16 * 16
        off = self.off if at is None else at
        if at is None:
            self.off += nel
        assert off + nel <= ARENA_ELEMS, (self.name, name, off + nel, ARENA_ELEMS)
        v = self.K.arena_t[:, off:off + n * (2 if dtype in (F32, I32) else 1)]
        if dtype in (F32, I32):
            v = v.bitcast(dtype)
        if len(shape) >= 2:
            names = "abcdefg"[:len(shape)]
            pat = "p (" + " ".join(names) + ") -> p " + " ".join(names)
            v = v.rearrange(pat, **{n: sz for n, sz in zip(names[:-1], shape[:-1])})
        r = self.K.P.res(self.name + "_" + name, dma=dma, multi=multi)
        r.a = v
        r.off = off
        return r


class NS:
    pass


class K:
    def __init__(self, nseq=2, seq=2048, depth=4, dump=False, phases=None):
        self.phases = phases
        self.NSEQ = nseq
        self.S = seq
        self.L = depth
        self.T = nseq * seq
        self.NT = self.T // 512
        self.TPS = seq // 512
        self.NCH = self.T // 8
        self.CPS = seq // 8
        self.G = 4
        self.dump = dump
        self.wl_i = 0
        assert self.NCH <= 512

    def build(self):
        nc = bass.Bass("TRN2", target_bir_lowering=False)
        self.nc = nc
        L, T, S = self.L, self.T, self.S
        d = {}

        def inp(name, shape):
            d[name] = nc.dram_tensor(name, shape, F32, kind="ExternalInput").ap()

        inp("xT", [1024, T])
        inp("w_inx", [L, 1024, WINX])
        inp("w_uq", [L, 384, 768])
        inp("w_uqsw", [L, 384, 256])
        inp("w_uk", [L, 256, 512])
        inp("w_uv", [L, 256, 512])
        inp("w_oatt", [L, 512, 1024])
        inp("w_glu", [L, 512, 2048])
        inp("w_out", [L, 1024, 1024])
        inp("w_up", [L, 1024, 2 * D_FF])
        inp("w_down", [L, D_FF, 1024])
        inp("vecs", [128, L * NV])
        inp("s5v", [128, L * 48])
        inp("s5m", [L, 128, 2048])
        inp("rope", [2, 128, S])
        inp("ident", [128, 128])
        inp("bmask", [128, 128])
        d["yT"] = nc.dram_tensor("yT", [1024, T], F32, kind="ExternalOutput").ap()
        skind = "ExternalOutput" if self.dump else "Internal"

        def scr(name, shape, dt):
            d[name] = nc.dram_tensor(name, shape, dt, kind=skind).ap()

        scr("s1", [1024, T], F32)
        scr("s2", [1024, T], F32)
        scr("qT", [8, 96, T], BF16)
        scr("kT", [8, 96, T], BF16)
        scr("vA", [T, 768], BF16)
        scr("uT", [512, T], BF16)
        scr("gT", [2048, T], BF16)
        scr("oT", [512, T], BF16)
        scr("ysT", [512, T], BF16)
        scr("actT", [D_FF, T], BF16)
        self.d = d
        with contextlib.ExitStack() as st:
            P = Prog(nc, st)
            self.P = P
            self.dr = {n: P.res("dr_" + n, dma=True, multi=True) for n in
                       ["yT", "s1", "s2", "qT", "kT", "vA", "uT", "gT", "oT", "ysT", "actT"]}
            self.dr["xT"] = P.res("dr_xT")
            self.arena_t = st.enter_context(nc.sbuf_tensor("arena", [128, ARENA_ELEMS], BF16))
            self.psb = [P.ps("psb%d" % i, [128, 512], F32) for i in range(8)]
            self.vecs = P.sb("vecs_sb", [128, L * NV], F32, dma=True)
            self.s5v = P.sb("s5v_sb", [128, L * 48], F32, dma=True)
            self.ident = P.sb("identb", [128, 128], BF16, dma=True)
            self.ones = P.sb("onesb", [128, 128], BF16)
            self.bmask = P.sb("bmask_sb", [128, 128], F32, dma=True)
            P.add('sp', lambda e: e.dma_start(out=self.bmask.a, in_=d["bmask"]), writes=[self.bmask], dma=True)
            self.epst = P.sb("epst", [128, 1], F32)
            P.add('sp', lambda e: e.dma_start(out=self.vecs.a, in_=d["vecs"]), writes=[self.vecs], dma=True)
            P.add('sp', lambda e: e.dma_start(out=self.s5v.a, in_=d["s5v"]), writes=[self.s5v], dma=True)
            P.add('pool', lambda e: e.dma_start(out=self.ident.a, in_=d["ident"]), writes=[self.ident], dma=True)
            P.add('dve', lambda e: e.memset(self.ones.a, 1.0), writes=[self.ones])
            P.add('dve', lambda e: e.memset(self.epst.a, EPS), writes=[self.epst])
            self.mk_arenas()
            nupd = 2 * L
            for l in range(L):
                for half in range(2):
                    u = 2 * l + half
                    src = "xT" if u == 0 else ("s1", "s2")[(u - 1) % 2]
                    dst = "yT" if u == nupd - 1 else ("s1", "s2")[u % 2]
                    on = lambda ph: self.phases is None or ph in self.phases
                    if half == 0:
                        if on('m1'):
                            self.m1(l, src)
                            P.barrier()
                        if on('m3'):
                            self.m3_prep(l)
                        if on('m2'):
                            self.m2(l)
                            P.barrier()
                        if on('m3'):
                            self.m3(l)
                            P.barrier()
                        if on('m4'):
                            self.m4(l, src, dst)
                            P.barrier()
                    else:
                        if on('f1'):
                            self.f1(l, src)
                            P.barrier()
                        if on('f2'):
                            self.f2(l, src, dst)
                            P.barrier()
            P.finish([self.dr["yT"]])
        return nc

    def vec(self, l, off, n=1):
        return self.vecs.a[:, l * NV + off:l * NV + off + n]

    def mk_arenas(self):
        T, S, NCH = self.T, self.S, self.NCH
        A = Arena(self, "m1")
        a = NS()
        a.win = A.alloc("win", [8, WINX], BF16, multi=True)
        self.wblocks(a.win, WINX, 1024)
        a.wuq = A.alloc("wuq", [3, 768], BF16, multi=True)
        a.wuqsw = A.alloc("wuqsw", [3, 256], BF16, multi=True)
        a.wuk = A.alloc("wuk", [2, 512], BF16, multi=True)
        a.wuv = A.alloc("wuv", [2, 512], BF16, multi=True)
        a.stg = [A.alloc("stg%d" % i, [1024], F32, dma=True) for i in range(4)]
        a.X = [A.alloc("X%d" % i, [8, 512], F32, dma=True) for i in range(2)]
        a.cs = [A.alloc("cs%d" % i, [2, 512], F32, dma=True) for i in range(2)]
        a.xsq = A.alloc("xsq", [8, 512], BF16)
        a.xg = A.alloc("xg", [8, 512], BF16)
        a.rt = A.alloc("rt", [512], F32)
        a.rstd = A.alloc("rstd", [512], F32)
        a.cq = A.alloc("cq", [3, 512], F32)
        a.ckv = A.alloc("ckv", [2, 512], F32)
        a.sq = A.alloc("sq", [3, 512], BF16)
        a.rq = A.alloc("rq", [512], F32)
        a.cqn = A.alloc("cqn", [3, 512], BF16)
        a.ckvn = A.alloc("ckvn", [2, 512], BF16)
        a.tmp = [A.alloc("tmp%d" % i, [512], F32) for i in range(4)]
        a.kp = A.alloc("kp", [512], F32)
        a.kps = A.alloc("kps", [512], F32)
        a.kr = A.alloc("kr", [512], BF16)
        a.ut = A.alloc("ut", [4, 512], BF16)
        a.gts = [A.alloc("gts%d" % i, [4, 512], BF16) for i in range(2)]
        a.qh = [A.alloc("qh%d" % i, [512], BF16) for i in range(3)]
        a.kh = [A.alloc("kh%d" % i, [512], BF16) for i in range(3)]
        a.vt = A.alloc("vt", [4, 8, 96], BF16)
        self.A1 = a
        A = Arena(self, "m2")
        a = NS()
        a.q = [A.alloc("q%d" % i, [S], BF16, dma=True) for i in range(2)]
        a.k = [A.alloc("k%d" % i, [S], BF16, dma=True) for i in range(2)]
        a.v = [A.alloc("v%d" % i, [S // 128, 768], BF16, dma=True) for i in range(2)]
        a.pT = [A.alloc("pT%d" % i, [512], BF16) for i in range(3)]
        a.rc = [A.alloc("rc%d" % i, [512], F32) for i in range(3)]
        a.hi = [A.alloc("hi%d" % i, [512], BF16) for i in range(3)]
        a.lo = [A.alloc("lo%d" % i, [512], BF16) for i in range(3)]
        a.bc = [A.alloc("bc%d" % i, [512], F32) for i in range(2)]
        a.ot = [A.alloc("ot%d" % i, [512], BF16) for i in range(2)]
        self.A2 = a
        A = Arena(self, "m3")
        a = NS()
        a.u = A.alloc("u", [4, T], BF16, dma=True, multi=True)
        a.Sb = A.alloc("Sb", [16, 2, NCH], BF16)
        a.Xb = A.alloc("Xb", [16, 2, NCH], BF16)
        a.Wp = A.alloc("Wp", [4, 8, 2, 128], BF16)
        a.PCb = A.alloc("PCb", [9, 2, 16, 32], BF16)
        a.Kb = A.alloc("Kb", [4, 8, 128], BF16)
        a.BbQb = A.alloc("BbQb", [2, 16, 32], BF16)
        a.m = A.alloc("m", [4, 16, 32], F32, dma=True)
        a.Bb = A.alloc("Bb", [2, 16, 32], F32)
        a.Pw = A.alloc("Pw", [9, 2, 16], F32)
        a.sm = [A.alloc("sm%d" % i, [16], F32) for i in range(14)]
        a.smi = A.alloc("smi", [16], I32)
        a.fc = A.alloc("fcoef", [2, 16], F32)
        a.nAi = A.alloc("nAi", [16], F32)
        a.nPr = A.alloc("nPr", [9, 16], F32)
        a.td = [A.alloc("td%d" % i, [16, 32], F32) for i in range(4)]
        G = self.G
        LS = self.CPS // G
        a.xs = [A.alloc("xs%d" % i, [16, 2, self.NSEQ * G], F32) for i in range(2)]
        a.t1 = A.alloc("t1", [16, 2, self.NSEQ * G], F32)
        a.t2 = A.alloc("t2", [16, 2, self.NSEQ * G], F32)
        a.ee = A.alloc("ee", [16, 2, self.NSEQ * G], F32)
        a.Q = A.alloc("Q", [8, 2, 16], F32)
        a.Apow = A.alloc("Apow", [16, 2, LS], F32)
        off_ys = A.off
        a.ys = A.alloc("ys", [T], BF16)
        a.PBb = A.alloc("PBb", [4, 8, 2, 4, 32], BF16, at=off_ys)
        if 16 * 8 * 2 * 32 > T:
            A.off = off_ys + 16 * 8 * 2 * 32
        self.A3 = a
        A = Arena(self, "m4")
        a = NS()
        a.woatt = A.alloc("woatt", [4, 1024], BF16, multi=True)
        a.stg = [A.alloc("stg%d" % i, [1024], F32, dma=True) for i in range(2)]
        a.wglu = A.alloc("wglu", [4, 2048], BF16, multi=True)
        a.wout = A.alloc("wout", [8, 1024], BF16, multi=True)
        a.o = [A.alloc("o%d" % i, [4, 512], BF16, dma=True) for i in range(2)]
        a.ysb = [A.alloc("ysb%d" % i, [4, 512], BF16, dma=True) for i in range(2)]
        a.g = [A.alloc("g%d" % i, [16, 512], BF16, dma=True) for i in range(2)]
        a.X = [A.alloc("X%d" % i, [8, 512], F32, dma=True) for i in range(2)]
        a.sg = [A.alloc("sg%d" % i, [512], F32) for i in range(1)]
        a.yss = [A.alloc("yss%d" % i, [512], F32) for i in range(1)]
        a.m1 = [A.alloc("m1%d" % i, [512], F32) for i in range(1)]
        a.m2 = [A.alloc("m2%d" % i, [512], F32) for i in range(1)]
        a.mg = A.alloc("mg", [8, 512], BF16)
        a.Y2 = [A.alloc("Y%d" % i, [8, 512], F32) for i in range(2)]
        a.ysq = A.alloc("ysq", [8, 512], BF16)
        a.rt = A.alloc("rt", [512], F32)
        a.rstd = A.alloc("rstd", [512], F32)
        a.tt = [A.alloc("tt%d" % i, [512], F32) for i in range(2)]
        self.A4 = a
        A = Arena(self, "f1")
        a = NS()
        rstdF = A.alloc("rstdF", [T], F32)
        a.rstdF = rstdF
        a.wup = A.alloc("wup", [8, 2 * D_FF], BF16, multi=True)
        a.X = [A.alloc("X%d" % i, [8, 512], F32, dma=True) for i in range(2)]
        a.sqg = A.alloc("sqg", [8, 512], BF16)
        a.rt = A.alloc("rt", [512], F32)
        a.Gb = [A.alloc("Gb%d" % i, [514], BF16) for i in range(3)]
        a.Gh = A.alloc("Gh", [NFC, 2], BF16)
        a.ge = [A.alloc("ge%d" % i, [512], BF16) for i in range(3)]
        a.dgall = A.alloc("dgall", [NFC, 3, 128], BF16)
        a.act = [A.alloc("act%d" % i, [NFC // 2, 512], BF16) for i in range(2)]
        a.stg = [A.alloc("stg%d" % i, [1024], F32, dma=True, at=a.act[i // 2].off + (i % 2) * 2048) for i in range(4)]
        self.wblocks(a.wup, 2 * D_FF, 1024)
        self.A5 = a
        A = Arena(self, "f2")
        a = NS()
        A.alloc("rstdF_pad", [T], F32)
        a.rstdF = rstdF
        a.wdown = A.alloc("wdown", [NFC, 1024], BF16, multi=True)
        a.stg = [A.alloc("stg%d" % i, [512], F32, dma=True) for i in range(4)]
        self.wblocks(a.wdown, 1024, 512)
        a.a = [A.alloc("a%d" % i, [NFC, 512], BF16, dma=True) for i in range(2)]
        a.X = [A.alloc("X%d" % i, [8, 512], F32, dma=True) for i in range(2)]
        a.Y = A.alloc("Y", [8, 512], F32)
        a.ysq = A.alloc("ysq", [8, 512], BF16)
        a.rt = A.alloc("rt", [512], F32)
        a.rstd = A.alloc("rstd", [512], F32)
        a.tt = [A.alloc("tt%d" % i, [512], F32) for i in range(2)]
        self.A6 = a

    def wblocks(self, dst, n, W):
        nb = (n + W - 1) // W
        rs = []
        for b in range(nb):
            r = self.P.res(dst.name + "_b%d" % b, multi=True)
            r.a = dst.a
            rs.append(r)
        dst.blocks = rs
        dst.W = W
        return rs

    def wr(self, dst, c0, c1):
        if not hasattr(dst, 'blocks'):
            return [dst]
        return dst.blocks[c0 // dst.W:(c1 - 1) // dst.W + 1]

    def load_w(self, dst, src2d, kc, n, stg):
        W = stg[0].a.shape[-1]
        if hasattr(dst, 'blocks'):
            assert dst.W % W == 0 or W % dst.W == 0
            W = min(W, dst.W)
        for n0 in range(0, n, W):
            n1 = min(n, n0 + W)
            wres = self.wr(dst, n0, n1)
            assert len(wres) == 1
            for c in range(kc):
                i = self.wl_i
                self.wl_i += 1
                sl = stg[i % len(stg)]
                self.dma('sp', sl, sl.a[:, 0:n1 - n0], None, src2d[c * 128:(c + 1) * 128, n0:n1])
                if i % 2 == 0:
                    self.act(wres[0], dst.a[:, c, n0:n1], sl, sl.a[:, 0:n1 - n0], AF.Copy)
                else:
                    self.cp('dve', wres[0], dst.a[:, c, n0:n1], sl, sl.a[:, 0:n1 - n0])

    def mm(self, outr, out_ap, lr, lhsT, rr, rhs, start, stop, tp=None):
        if tp is None:
            fn = lambda e: e.matmul(out_ap, lhsT, rhs, start=start, stop=stop)
        else:
            fn = lambda e: e.matmul(out_ap, lhsT, rhs, start=start, stop=stop, tile_position=tp)
        rd = (lr if isinstance(lr, list) else [lr]) + (rr if isinstance(rr, list) else [rr])
        self.P.add('pe', fn, reads=rd, writes=[outr])

    def act(self, outr, out_ap, inr, in_ap, func, bias=None, scale=None, extra_reads=()):
        kw = {}
        if bias is not None:
            kw['bias'] = bias
        if scale is not None:
            kw['scale'] = scale
        self.P.add('act', lambda e: e.activation(out=out_ap, in_=in_ap, func=func, **kw),
                   reads=[inr] + list(extra_reads), writes=[outr])

    def tt(self, eng, outr, out_ap, r0, in0, r1, in1, op):
        self.P.add(eng, lambda e: e.tensor_tensor(out=out_ap, in0=in0, in1=in1, op=op),
                   reads=[r0, r1], writes=[outr])

    def ts(self, eng, outr, out_ap, r0, in0, s1, s2, op0, op1=None, extra_reads=()):
        if op1 is None:
            fn = lambda e: e.tensor_scalar(out=out_ap, in0=in0, scalar1=s1, scalar2=None, op0=op0)
        else:
            fn = lambda e: e.tensor_scalar(out=out_ap, in0=in0, scalar1=s1, scalar2=s2, op0=op0, op1=op1)
        self.P.add(eng, fn, reads=[r0] + list(extra_reads), writes=[outr])

    def stt(self, outr, out_ap, r0, in0, scalar, r1, in1, op0, op1, extra_reads=()):
        self.P.add('dve', lambda e: e.scalar_tensor_tensor(out=out_ap, in0=in0, scalar=scalar, in1=in1,
                                                           op0=op0, op1=op1),
                   reads=[r0, r1] + list(extra_reads), writes=[outr])

    def cp(self, eng, outr, out_ap, inr, in_ap):
        self.P.add(eng, lambda e: e.tensor_copy(out=out_ap, in_=in_ap), reads=[inr], writes=[outr])

    def dma(self, eng, outr, out_ap, inr, in_ap):
        self.P.add(eng, lambda e: e.dma_start(out=out_ap, in_=in_ap),
                   reads=[inr] if inr is not None else [], writes=[outr], dma=True)

    def rms_rstd(self, ps, sq_res, sq_ap_fn, nchunk, dim, rt, rstd):
        for c in range(nchunk):
            self.mm(ps, ps.a, self.ones, self.ones.a, sq_res, sq_ap_fn(c), c == 0, c == nchunk - 1)
        self.act(rt, rt.a, ps, ps.a, AF.Sqrt, bias=self.epst.a[:, 0:1], scale=1.0 / dim, extra_reads=[self.epst])
        self.P.add('dve', lambda e: e.reciprocal(out=rstd.a, in_=rt.a), reads=[rt], writes=[rstd])

    def m1(self, l, src):
        P, a, d, psb = self.P, self.A1, self.d, self.psb
        xin = d[src].rearrange("(c p) t -> p c t", p=128)
        xr = self.dr[src]
        P.add('pool', lambda e: e.memset(a.vt.a[:, :, :, 64:96], 1.0), writes=[a.vt])
        ropev = d["rope"].rearrange("k p s -> p k s")

        def load(tt):
            X = a.X[tt % 2]
            cs = a.cs[tt % 2]
            self.dma('sp', X, X.a, xr, xin[:, :, tt * 512:(tt + 1) * 512])
            p0 = (tt % self.TPS) * 512
            self.dma('sp', cs, cs.a[64:96, :, :], None, ropev[64:96, :, p0:p0 + 512])

        load(0)
        self.load_w(a.win, d["w_inx"][l], 8, WINX, a.stg)
        self.load_w(a.wuq, d["w_uq"][l], 3, 768, a.stg)
        self.load_w(a.wuqsw, d["w_uqsw"][l], 3, 256, a.stg)
        self.load_w(a.wuk, d["w_uk"][l], 2, 512, a.stg)
        self.load_w(a.wuv, d["w_uv"][l], 2, 512, a.stg)
        zb = [1, 2, 3, 4]
        zi = [0]

        def nextbank():
            b = psb[zb[zi[0] % 4]]
            zi[0] += 1
            return b

        for tt in range(self.NT):
            if tt + 1 < self.NT:
                load(tt + 1)
            X, cs = a.X[tt % 2], a.cs[tt % 2]
            t0 = tt * 512
            cosr = cs.a[64:96, 0, :]
            sinr = cs.a[64:96, 1, :]
            self.act(a.xsq, a.xsq.a, X, X.a, AF.Square)
            for dc in range(8):
                self.ts('dve', a.xg, a.xg.a[:, dc, :], X, X.a[:, dc, :], self.vec(l, GPRE + dc), None, ALU.mult,
                        extra_reads=[self.vecs])
            self.rms_rstd(psb[0], a.xsq, lambda c: a.xsq.a[:, c, :], 8, 1024.0, a.rt, a.rstd)

            def zchunk(ps, c0, m, tp=None, out_ap=None):
                oa = ps.a[0:m, :] if out_ap is None else out_ap
                for dc in range(8):
                    self.mm(ps, oa, self.wr(a.win, c0, c0 + m), a.win.a[:, dc, c0:c0 + m], a.xg, a.xg.a[:, dc, :], dc == 0, dc == 7, tp)

            for c in range(3):
                ps = nextbank()
                zchunk(ps, c * 128, 128)
                self.tt('dve', a.cq, a.cq.a[:, c, :], ps, ps.a, a.rstd, a.rstd.a, ALU.mult)
            for c in range(2):
                ps = nextbank()
                zchunk(ps, 384 + c * 128, 128)
                self.tt('dve', a.ckv, a.ckv.a[:, c, :], ps, ps.a, a.rstd, a.rstd.a, ALU.mult)
            self.act(a.sq, a.sq.a, a.cq, a.cq.a, AF.Square)
            self.rms_rstd(psb[7], a.sq, lambda c: a.sq.a[:, c, :], 3, 384.0, a.rt, a.rq)
            for c in range(3):
                self.stt(a.cqn, a.cqn.a[:, c, :], a.cq, a.cq.a[:, c, :], self.vec(l, GQ + c), a.rq, a.rq.a,
                         ALU.mult, ALU.mult, extra_reads=[self.vecs])
            self.act(a.sq, a.sq.a[:, 0:2, :], a.ckv, a.ckv.a, AF.Square)
            self.rms_rstd(psb[7], a.sq, lambda c: a.sq.a[:, c, :], 2, 256.0, a.rt, a.rq)
            for c in range(2):
                self.stt(a.ckvn, a.ckvn.a[:, c, :], a.ckv, a.ckv.a[:, c, :], self.vec(l, GKV + c), a.rq, a.rq.a,
                         ALU.mult, ALU.mult, extra_reads=[self.vecs])
            zchunk(psb[5], 640, 32, tp=(0, 64), out_ap=psb[5].a[64:96, :])
            zchunk(psb[6], 672, 32, tp=(0, 64), out_ap=psb[6].a[64:96, :])
            self.tt('dve', a.kp, a.kp.a[64:96, :], psb[5], psb[5].a[64:96, :], a.rstd, a.rstd.a[64:96, :], ALU.mult)
            self.tt('dve', a.kps, a.kps.a[64:96, :], psb[6], psb[6].a[64:96, :], a.rstd, a.rstd.a[64:96, :], ALU.mult)
            self.tt('dve', a.kp, a.kp.a[64:96, :], a.kp, a.kp.a[64:96, :], cs, cosr, ALU.mult)
            self.tt('dve', a.kps, a.kps.a[64:96, :], a.kps, a.kps.a[64:96, :], cs, sinr, ALU.mult)
            self.tt('pool', a.kr, a.kr.a[64:96, :], a.kp, a.kp.a[64:96, :], a.kps, a.kps.a[64:96, :], ALU.add)
            for c in range(4):
                ps = nextbank()
                zchunk(ps, 704 + c * 128, 128)
                self.tt('dve', a.ut, a.ut.a[:, c, :].rearrange("p (j cl) -> p cl j", j=8),
                        ps, ps.a.rearrange("p (cl j) -> p cl j", j=8),
                        a.rstd, a.rstd.a.rearrange("p (cl j) -> p cl j", j=8), ALU.mult)
            for c in range(4):
                self.dma('sp', self.dr["uT"],
                         d["uT"][c * 128:(c + 1) * 128, :].rearrange("p (j n) -> p j n", j=8)[:, :, tt * 64:(tt + 1) * 64],
                         a.ut, a.ut.a[:, c, :].rearrange("p (j cl) -> p j cl", j=8))
            for gb in range(4):
                gts = a.gts[gb % 2]
                for gi in range(4):
                    gc = gb * 4 + gi
                    ps = nextbank()
                    zchunk(ps, 1216 + gc * 128, 128)
                    tmp = a.tmp[gc % 4]
                    self.tt('dve', tmp, tmp.a, ps, ps.a, a.rstd, a.rstd.a, ALU.mult)
                    self.act(gts, gts.a[:, gi, :], tmp, tmp.a, AF.Sigmoid, bias=self.vec(l, BG + gc),
                             extra_reads=[self.vecs])
                self.dma('sp', self.dr["gT"],
                         d["gT"].rearrange("(c p) t -> p c t", p=128)[:, gb * 4:(gb + 1) * 4, t0:t0 + 512],
                         gts, gts.a)
            for h in range(8):
                ps = nextbank()
                psw = psb[5 + h % 2]
                for c in range(3):
                    self.mm(ps, ps.a[0:96, :], a.wuq, a.wuq.a[:, c, 96 * h:96 * h + 96], a.cqn, a.cqn.a[:, c, :],
                            c == 0, c == 2)
                for c in range(3):
                    self.mm(psw, psw.a[64:96, :], a.wuqsw, a.wuqsw.a[:, c, 32 * h:32 * h + 32], a.cqn,
                            a.cqn.a[:, c, :], c == 0, c == 2, tp=(0, 64))
                qh = a.qh[h % 3]
                t1, t2 = a.tmp[(2 * h) % 4], a.tmp[(2 * h + 1) % 4]
                self.act(qh, qh.a[0:64, :], ps, ps.a[0:64, :], AF.Copy)
                self.tt('dve', t1, t1.a[64:96, :], ps, ps.a[64:96, :], cs, cosr, ALU.mult)
                self.tt('dve', t2, t2.a[64:96, :], psw, psw.a[64:96, :], cs, sinr, ALU.mult)
                self.tt('pool', qh, qh.a[64:96, :], t1, t1.a[64:96, :], t2, t2.a[64:96, :], ALU.add)
                self.dma('sp', self.dr["qT"], d["qT"][h, :, t0:t0 + 512], qh, qh.a[0:96, :])
            for h in range(8):
                ps = nextbank()
                for c in range(2):
                    self.mm(ps, ps.a[0:64, :], a.wuk, a.wuk.a[:, c, 64 * h:64 * h + 64], a.ckvn, a.ckvn.a[:, c, :],
                            c == 0, c == 1)
                kh = a.kh[h % 3]
                self.act(kh, kh.a[0:64, :], ps, ps.a[0:64, :], AF.Copy)
                self.cp('pool', kh, kh.a[64:96, :], a.kr, a.kr.a[64:96, :])
                self.dma('sp', self.dr["kT"], d["kT"][h, :, t0:t0 + 512], kh, kh.a[0:96, :])
            for tb in range(4):
                ps = nextbank()
                for c in range(2):
                    self.mm(ps, ps.a, a.ckvn, a.ckvn.a[:, c, tb * 128:(tb + 1) * 128], a.wuv, a.wuv.a[:, c, :],
                            c == 0, c == 1)
                self.cp('dve' if tb % 2 else 'act', a.vt, a.vt.a[:, tb, :, 0:64],
                        ps, ps.a.rearrange("p (h c) -> p h c", h=8)) if tb % 2 else \
                    self.act(a.vt, a.vt.a[:, tb, :, 0:64], ps, ps.a.rearrange("p (h c) -> p h c", h=8), AF.Copy)
            self.dma('sp', self.dr["vA"],
                     d["vA"].rearrange("(n p) c -> p n c", p=128)[:, 4 * tt:4 * tt + 4, :],
                     a.vt, a.vt.a.rearrange("p n h c -> p n (h c)"))

    def m2(self, l):
        P, a, d, psb = self.P, self.A2, self.d, self.psb
        S = self.S
        scale = 1.0 / math.sqrt(96.0)
        vAv = d["vA"].rearrange("(n p) c -> p n c", p=128)
        nsb = S // 128
        pairs = [(s, h) for s in range(self.NSEQ) for h in range(8)]

        def loadv(s):
            v = a.v[s % 2]
            self.dma('sp', v, v.a, self.dr["vA"], vAv[:, s * nsb:(s + 1) * nsb, :])

        def loadqk(i):
            s, h = pairs[i]
            q, k = a.q[i % 2], a.k[i % 2]
            self.dma('sp', q, q.a[0:96, :], self.dr["qT"], d["qT"][h, :, s * S:(s + 1) * S])
            self.dma('sp', k, k.a[0:96, :], self.dr["kT"], d["kT"][h, :, s * S:(s + 1) * S])

        blocks = []
        g = 0
        for i, (s, h) in enumerate(pairs):
            for qa in range(S // 512):
                nblk = 4 * qa + 4
                for j in range(nblk):
                    blocks.append((i, s, h, qa, j, nblk, g))
                g += 1
        N = len(blocks)

        def geom(b):
            i, s, h, qa, j, nblk, g = b
            r = j - 4 * qa
            qoff = 128 * r if r > 0 else 0
            return r, qoff, 512 - qoff

        def emitS(idx):
            i, s, h, qa, j, nblk, g = blocks[idx]
            r, qoff, nq = geom(blocks[idx])
            q, k = a.q[i % 2], a.k[i % 2]
            pss = psb[idx % 3]
            self.mm(pss, pss.a[:, 0:nq], k, k.a[0:96, j * 128:(j + 1) * 128],
                    q, q.a[0:96, qa * 512 + qoff:qa * 512 + 512], True, True)

        def emitEP(idx):
            r, qoff, nq = geom(blocks[idx])
            pss, pT = psb[idx % 3], a.pT[idx % 3]
            self.act(pT, pT.a[:, 0:nq], pss, pss.a[:, 0:nq], AF.Exp, scale=scale)
            if r >= 0:
                P.add('dve', lambda e, pT=pT: e.memset(pT.a[64:128, 0:64], 0.0), writes=[pT])

        def emitPV(idx):
            i, s, h, qa, j, nblk, g = blocks[idx]
            r, qoff, nq = geom(blocks[idx])
            v, pT, po = a.v[s % 2], a.pT[idx % 3], psb[POB[g % 3]]
            self.mm(po, po.a[0:96, qoff:512], v, v.a[:, j, h * 96:(h + 1) * 96], pT, pT.a[:, 0:nq],
                    j == 0, j == nblk - 1)

        def tail_front(b):
            g = b[6]
            po, rc, hi, lo = psb[POB[g % 3]], a.rc[g % 3], a.hi[g % 3], a.lo[g % 3]
            P.add('dve', lambda e, rc=rc, po=po: e.reciprocal(out=rc.a[64:65, :], in_=po.a[64:65, :]),
                  reads=[po], writes=[rc])
            self.cp('dve', hi, hi.a[64:65, :], rc, rc.a[64:65, :])
            self.tt('dve', lo, lo.a[64:65, :], rc, rc.a[64:65, :], hi, hi.a[64:65, :], ALU.subtract)

        def tail_back(b):
            i, s, h, qa, j, nblk, g = b
            po, pb = psb[POB[g % 3]], psb[5 + g % 2]
            hi, lo, bc, ot = a.hi[g % 3], a.lo[g % 3], a.bc[g % 2], a.ot[g % 2]
            self.mm(pb, pb.a[0:64, :], self.ones, self.ones.a[64:65, 0:64], hi, hi.a[64:65, :], True, False,
                    tp=(64, 0))
            self.mm(pb, pb.a[0:64, :], self.ones, self.ones.a[64:65, 0:64], lo, lo.a[64:65, :], False, True,
                    tp=(64, 0))
            self.act(bc, bc.a[0:64, :], pb, pb.a[0:64, :], AF.Copy)
            self.tt('dve', ot, ot.a[0:64, :], po, po.a[0:64, :], bc, bc.a[0:64, :], ALU.mult)
            tq = s * S + qa * 512
            self.dma('sp', self.dr["oT"], d["oT"][h * 64:(h + 1) * 64, tq:tq + 512], ot, ot.a[0:64, :])

        POB = [3, 4, 7]
        DEFER = 6
        loadv(0)
        loadqk(0)
        if len(pairs) > 1:
            loadqk(1)
        emitS(0)
        if N > 1:
            emitS(1)
        pending = []
        for idx in range(N):
            b = blocks[idx]
            i, s, h, qa, j, nblk, g = b
            if qa == 0 and j == 0:
                if h == 0 and s + 1 < self.NSEQ:
                    loadv(s + 1)
            emitEP(idx)
            if idx + 2 < N:
                b2 = blocks[idx + 2]
                if b2[3] == 0 and b2[4] == 0 and b2[0] + 1 < len(pairs) and b2[0] >= 1:
                    pass
                emitS(idx + 2)
            emitPV(idx)
            pending = [(pb_, c_ - 1) for (pb_, c_) in pending]
            while pending and pending[0][1] <= 0:
                tail_back(pending.pop(0)[0])
            if j == nblk - 1:
                tail_front(b)
                pending.append((b, DEFER))
                if qa == S // 512 - 1 and i + 2 < len(pairs):
                    loadqk(i + 2)
        for pb_, c_ in pending:
            tail_back(pb_)

    def m3_prep(self, l):
        P, a, d, psb = self.P, self.A3, self.d, self.psb
        T, NCH, CPS, NSEQ = self.T, self.NCH, self.CPS, self.NSEQ
        sv = self.s5v.a[:, l * 48:(l + 1) * 48]
        lre, lim, lst = sv[:, 0:16], sv[:, 16:32], sv[:, 32:48]
        svr = self.s5v
        self.dma('sp', a.m, a.m.a, None, d["s5m"][l].rearrange("p (k g c) -> p k g c", k=4, g=16))
        sm = a.sm
        dv, ac = 'dve', 'act'
        delta, lrd, mag, ang, rr, kf, fr, s1, s2, ch, tA, tB, ca, den = sm
        self.act(delta, delta.a, svr, lst, AF.Exp)
        self.tt(dv, lrd, lrd.a, svr, lre, delta, delta.a, ALU.mult)
        self.act(mag, mag.a, lrd, lrd.a, AF.Exp)
        self.tt(dv, ang, ang.a, svr, lim, delta, delta.a, ALU.mult)
        self.ts(dv, rr, rr.a, ang, ang.a, 1.0 / (2.0 * math.pi), None, ALU.mult)
        self.cp(dv, a.smi, a.smi.a, rr, rr.a)
        self.cp(dv, kf, kf.a, a.smi, a.smi.a)
        self.tt(dv, fr, fr.a, rr, rr.a, kf, kf.a, ALU.subtract)
        self.act(s1, s1.a, fr, fr.a, AF.Sin, scale=math.pi)
        self.act(s2, s2.a, fr, fr.a, AF.Sin, scale=math.pi / 2.0)
        self.tt(dv, tA, tA.a, s2, s2.a, s2, s2.a, ALU.mult)
        self.ts(dv, ch, ch.a, tA, tA.a, -2.0, 1.0, ALU.mult, ALU.add)
        self.tt(dv, tA, tA.a, s1, s1.a, ch, ch.a, ALU.mult)
        Pw = a.Pw
        self.stt(Pw, Pw.a[:, 1, 1, :], tA, tA.a, 2.0, mag, mag.a, ALU.mult, ALU.mult)
        self.tt(dv, tB, tB.a, s1, s1.a, s1, s1.a, ALU.mult)
        self.ts(dv, ca, ca.a, tB, tB.a, -2.0, 1.0, ALU.mult, ALU.add)
        self.tt(dv, Pw, Pw.a[:, 1, 0, :], ca, ca.a, mag, mag.a, ALU.mult)
        P.add(dv, lambda e: e.memset(Pw.a[:, 0, 0, :], 1.0), writes=[Pw])
        P.add(dv, lambda e: e.memset(Pw.a[:, 0, 1, :], 0.0), writes=[Pw])
        ar, ai = Pw.a[:, 1, 0, :], Pw.a[:, 1, 1, :]
        nr = s1
        self.ts(dv, nr, nr.a, Pw, ar, -1.0, None, ALU.add)
        self.tt(dv, tA, tA.a, svr, lre, svr, lre, ALU.mult)
        self.tt(dv, tB, tB.a, svr, lim, svr, lim, ALU.mult)
        self.tt(dv, den, den.a, tA, tA.a, tB, tB.a, ALU.add)
        P.add(dv, lambda e: e.reciprocal(out=den.a, in_=den.a), reads=[den], writes=[den])
        self.tt(dv, tA, tA.a, nr, nr.a, svr, lre, ALU.mult)
        self.tt(dv, tB, tB.a, Pw, ai, svr, lim, ALU.mult)
        self.tt(dv, tA, tA.a, tA, tA.a, tB, tB.a, ALU.add)
        self.tt(dv, a.fc, a.fc.a[:, 0, :], tA, tA.a, den, den.a, ALU.mult)
        self.tt(dv, tA, tA.a, Pw, ai, svr, lre, ALU.mult)
        self.tt(dv, tB, tB.a, nr, nr.a, svr, lim, ALU.mult)
        self.tt(dv, tA, tA.a, tA, tA.a, tB, tB.a, ALU.subtract)
        self.tt(dv, a.fc, a.fc.a[:, 1, :], tA, tA.a, den, den.a, ALU.mult)
        for k in range(2, 9):
            pr, pi = Pw.a[:, k - 1, 0, :], Pw.a[:, k - 1, 1, :]
            self.tt(dv, tA, tA.a, Pw, pr, Pw, ar, ALU.mult)
            self.tt(dv, tB, tB.a, Pw, pi, Pw, ai, ALU.mult)
            self.tt(dv, Pw, Pw.a[:, k, 0, :], tA, tA.a, tB, tB.a, ALU.subtract)
            self.tt(dv, tA, tA.a, Pw, pr, Pw, ai, ALU.mult)
            self.tt(dv, tB, tB.a, Pw, pi, Pw, ar, ALU.mult)
            self.tt(dv, Pw, Pw.a[:, k, 1, :], tA, tA.a, tB, tB.a, ALU.add)
        self.ts(dv, a.nAi, a.nAi.a, Pw, Pw.a[:, 8, 1, :], -1.0, None, ALU.mult)

        def bc32(ap16):
            return ap16.unsqueeze(2).broadcast_to([128, 16, 32])

        Cr, Ci, Br, Bi = a.m.a[:, 0], a.m.a[:, 1], a.m.a[:, 2], a.m.a[:, 3]
        fre, fim = bc32(a.fc.a[:, 0, :]), bc32(a.fc.a[:, 1, :])
        td, tp = a.td, a.td
        for k in range(9):
            self.ts(dv, a.nPr, a.nPr.a[:, k, :], Pw, Pw.a[:, k, 0, :], -1.0, None, ALU.mult)
        pl = 'pool'
        self.tt(pl, td[0], td[0].a, a.m, Br, a.fc, fre, ALU.mult)
        self.tt(pl, td[1], td[1].a, a.m, Bi, a.fc, fim, ALU.mult)
        self.tt(pl, a.Bb, a.Bb.a[:, 0], td[0], td[0].a, td[1], td[1].a, ALU.subtract)
        self.tt(pl, td[2], td[2].a, a.m, Bi, a.fc, fre, ALU.mult)
        self.tt(pl, td[3], td[3].a, a.m, Br, a.fc, fim, ALU.mult)
        self.tt(pl, a.Bb, a.Bb.a[:, 1], td[2], td[2].a, td[3], td[3].a, ALU.add)
        self.cp('pool', a.BbQb, a.BbQb.a[:, 0], a.Bb, a.Bb.a[:, 0])
        self.cp('pool', a.BbQb, a.BbQb.a[:, 1], a.Bb, a.Bb.a[:, 1])
        for k in range(9):
            pr, pi = bc32(Pw.a[:, k, 0, :]), bc32(Pw.a[:, k, 1, :])
            npr = bc32(a.nPr.a[:, k, :])
            self.tt(pl, td[0], td[0].a, a.m, Cr, Pw, pr, ALU.mult)
            self.tt(pl, td[1], td[1].a, a.m, Ci, Pw, pi, ALU.mult)
            self.tt(pl, a.PCb, a.PCb.a[:, k, 0], td[0], td[0].a, td[1], td[1].a, ALU.subtract)
            self.tt(pl, td[2], td[2].a, a.m, Ci, a.nPr, npr, ALU.mult)
            self.tt(pl, td[3], td[3].a, a.m, Cr, Pw, pi, ALU.mult)
            self.tt(pl, a.PCb, a.PCb.a[:, k, 1], td[2], td[2].a, td[3], td[3].a, ALU.subtract)
        for j in range(8):
            pr, pi = bc32(Pw.a[:, 7 - j, 0, :]), bc32(Pw.a[:, 7 - j, 1, :])
            self.tt(pl, tp[0], tp[0].a, a.Bb, a.Bb.a[:, 0], Pw, pr, ALU.mult)
            self.tt(pl, tp[1], tp[1].a, a.Bb, a.Bb.a[:, 1], Pw, pi, ALU.mult)
            v4 = lambda r: r.a.rearrange("p (t m) c -> p t m c", t=4)
            self.tt(pl, a.PBb, a.PBb.a[:, :, j, 0, :, :], tp[0], v4(tp[0]), tp[1], v4(tp[1]), ALU.subtract)
            self.tt(pl, tp[2], tp[2].a, a.Bb, a.Bb.a[:, 1], Pw, pr, ALU.mult)
            self.tt(pl, tp[3], tp[3].a, a.Bb, a.Bb.a[:, 0], Pw, pi, ALU.mult)
            self.tt(pl, a.PBb, a.PBb.a[:, :, j, 1, :, :], tp[2], v4(tp[2]), tp[3], v4(tp[3]), ALU.add)
        LS = CPS // self.G
        nk = LS.bit_length() - 1
        assert (1 << nk) == LS
        Q = a.Q
        sA, sB = a.sm[0], a.sm[1]
        self.cp(pl, Q, Q.a[:, 0, :, :], Pw, Pw.a[:, 8, :, :])
        for k in range(1, nk + 1):
            qr, qi = Q.a[:, k - 1, 0, :], Q.a[:, k - 1, 1, :]
            self.tt(pl, sA, sA.a, Q, qr, Q, qr, ALU.mult)
            self.tt(pl, sB, sB.a, Q, qi, Q, qi, ALU.mult)
            self.tt(pl, Q, Q.a[:, k, 0, :], sA, sA.a, sB, sB.a, ALU.subtract)
            self.tt(pl, sA, sA.a, Q, qr, Q, qi, ALU.mult)
            self.tt(pl, Q, Q.a[:, k, 1, :], sA, sA.a, sA, sA.a, ALU.add)
        Ap = a.Apow
        P.add(pl, lambda e: e.memset(Ap.a[:, :, 0, 0:1], 1.0), writes=[Ap])
        P.add(pl, lambda e: e.memset(Ap.a[:, :, 1, 0:1], 0.0), writes=[Ap])
        for k in range(nk):
            n = 1 << k
            qr = Q.a[:, k, 0, :].unsqueeze(2).broadcast_to([128, 16, n])
            qi = Q.a[:, k, 1, :].unsqueeze(2).broadcast_to([128, 16, n])
            sr, si = Ap.a[:, :, 0, 0:n], Ap.a[:, :, 1, 0:n]
            u0, u1 = tp[0].a[:, :, 0:n], tp[1].a[:, :, 0:n]
            self.tt(pl, tp[0], u0, Ap, sr, Q, qr, ALU.mult)
            self.tt(pl, tp[1], u1, Ap, si, Q, qi, ALU.mult)
            self.tt(pl, Ap, Ap.a[:, :, 0, n:2 * n], tp[0], u0, tp[1], u1, ALU.subtract)
            self.tt(pl, tp[0], u0, Ap, sr, Q, qi, ALU.mult)
            self.tt(pl, tp[1], u1, Ap, si, Q, qr, ALU.mult)
            self.tt(pl, Ap, Ap.a[:, :, 1, n:2 * n], tp[0], u0, tp[1], u1, ALU.add)

    def m3(self, l):
        P, a, d, psb = self.P, self.A3, self.d, self.psb
        T, NCH, CPS, NSEQ = self.T, self.NCH, self.CPS, self.NSEQ
        Pw = a.Pw
        dv = 'dve'
        for Tt in range(4):
            self.dma('sp', a.u, a.u.a[:, Tt, :], self.dr["uT"], d["uT"][Tt * 128:(Tt + 1) * 128, :])
        nb = 0
        for Tt in range(4):
            for jh in range(2):
                ps = psb[nb % 2]
                nb += 1
                psv = ps.a.bitcast(BF16)
                for jj in range(4):
                    for ri in range(2):
                        j = jh * 4 + jj
                        idx = jj * 2 + ri
                        P.add('pe', lambda e, psv=psv, idx=idx, Tt=Tt, j=j, ri=ri: e.transpose(
                            psv[:, idx * 128:(idx + 1) * 128], a.PBb.a[:, Tt, j, ri, :, :].rearrange("p m c -> p (m c)"),
                            self.ident.a), reads=[a.PBb, self.ident], writes=[ps])
                self.cp('dve' if nb % 2 else 'act', a.Wp,
                        a.Wp.a[:, Tt, jh * 4:jh * 4 + 4, :, :].rearrange("p j r c -> p (j r c)"),
                        ps, psv) if nb % 2 else \
                    self.act(a.Wp, a.Wp.a[:, Tt, jh * 4:jh * 4 + 4, :, :].rearrange("p j r c -> p (j r c)"),
                             ps, psv, AF.Copy)
        fl = lambda ap: ap.rearrange("p m c -> p (m c)")
        for Tt in range(4):
            for kh in range(2):
                ps = psb[2 + (nb % 2)]
                nb += 1
                for kk in range(4):
                    k = kh * 4 + kk
                    oa = ps.a[:, kk * 128:(kk + 1) * 128]
                    for ri in range(2):
                        self.mm(ps, oa, a.BbQb, fl(a.BbQb.a[:, ri, 4 * Tt:4 * Tt + 4, :]),
                                a.PCb, fl(a.PCb.a[:, k, ri, 4 * Tt:4 * Tt + 4, :]), ri == 0, ri == 1)
                self.tt('dve', a.Kb, a.Kb.a[:, Tt, kh * 4:kh * 4 + 4, :], ps,
                        ps.a.rearrange("p (k c) -> p k c", k=4), self.bmask,
                        self.bmask.a.unsqueeze(1).broadcast_to([128, 4, 128]), ALU.mult)
            self.stt(a.Kb, a.Kb.a[:, Tt, 0, :], self.ident, self.ident.a, self.vec(l, DSK + Tt), a.Kb,
                     a.Kb.a[:, Tt, 0, :], ALU.mult, ALU.add, extra_reads=[self.vecs])
        for gp in range(16):
            Tt, m = gp // 4, gp % 4
            for ri in range(2):
                ps = psb[4 + (nb % 4)]
                nb += 1
                for j in range(8):
                    rhs = a.u.a[32 * m:32 * m + 32, Tt, j * NCH:(j + 1) * NCH]
                    self.mm(ps, ps.a[:, 0:NCH], a.Wp, a.Wp.a[32 * m:32 * m + 32, Tt, j, ri, :], a.u, rhs,
                            j == 0, j == 7, tp=(32 * m, 0))
                if nb % 2:
                    self.cp('dve', a.Sb, a.Sb.a[:, gp, ri, :], ps, ps.a[:, 0:NCH])
                else:
                    self.act(a.Sb, a.Sb.a[:, gp, ri, :], ps, ps.a[:, 0:NCH], AF.Copy)
        G = self.G
        NS2 = NSEQ * G
        LS = CPS // G
        nk = LS.bit_length() - 1
        Sv = a.Sb.a.rearrange("p g r (s c) -> p g r s c", s=NS2)
        Xv = a.Xb.a.rearrange("p g r (s c) -> p g r s c", s=NS2)
        Ar4 = Pw.a[:, 8, 0, :].unsqueeze(2).unsqueeze(3).broadcast_to([128, 16, 2, NS2])
        Ai3 = Pw.a[:, 8, 1, :].unsqueeze(2).broadcast_to([128, 16, NS2])
        nAi3 = a.nAi.a.unsqueeze(2).broadcast_to([128, 16, NS2])
        P.add(dv, lambda e: e.memset(a.xs[0].a, 0.0), writes=[a.xs[0]])
        for c in range(LS):
            xc, xn = a.xs[c % 2], a.xs[(c + 1) % 2]
            self.act(a.Xb, Xv[:, :, :, :, c], xc, xc.a, AF.Copy)
            self.tt(dv, a.t1, a.t1.a, xc, xc.a, Pw, Ar4, ALU.mult)
            self.tt(dv, a.t2, a.t2.a[:, :, 0, :], xc, xc.a[:, :, 1, :], a.nAi, nAi3, ALU.mult)
            self.tt(dv, a.t2, a.t2.a[:, :, 1, :], xc, xc.a[:, :, 0, :], Pw, Ai3, ALU.mult)
            self.tt(dv, a.t1, a.t1.a, a.t1, a.t1.a, a.t2, a.t2.a, ALU.add)
            self.tt(dv, xn, xn.a, a.t1, a.t1.a, a.Sb, Sv[:, :, :, :, c], ALU.add)
        xe = a.xs[LS % 2]
        xev = xe.a.rearrange("p g r (s k) -> p g r s k", k=G)
        eev = a.ee.a.rearrange("p g r (s k) -> p g r s k", k=G)
        QL = a.Q.a[:, nk, :, :]
        QLr = QL[:, 0, :].unsqueeze(2).broadcast_to([128, 16, NSEQ])
        QLi = QL[:, 1, :].unsqueeze(2).broadcast_to([128, 16, NSEQ])
        P.add(dv, lambda e: e.memset(a.ee.a, 0.0), writes=[a.ee])
        s0, s1 = a.t1, a.t2
        s0v = s0.a.rearrange("p g r (s k) -> p g r s k", k=G)
        s1v = s1.a.rearrange("p g r (s k) -> p g r s k", k=G)
        for k in range(1, G):
            er, ei = eev[:, :, 0, :, k - 1], eev[:, :, 1, :, k - 1]
            self.tt(dv, s0, s0v[:, :, 0, :, 0], a.ee, er, a.Q, QLr, ALU.mult)
            self.tt(dv, s1, s1v[:, :, 0, :, 0], a.ee, ei, a.Q, QLi, ALU.mult)
            self.tt(dv, s0, s0v[:, :, 0, :, 0], s0, s0v[:, :, 0, :, 0], s1, s1v[:, :, 0, :, 0], ALU.subtract)
            self.tt(dv, a.ee, eev[:, :, 0, :, k], s0, s0v[:, :, 0, :, 0], xe, xev[:, :, 0, :, k - 1], ALU.add)
            self.tt(dv, s0, s0v[:, :, 1, :, 0], a.ee, er, a.Q, QLi, ALU.mult)
            self.tt(dv, s1, s1v[:, :, 1, :, 0], a.ee, ei, a.Q, QLr, ALU.mult)
            self.tt(dv, s0, s0v[:, :, 1, :, 0], s0, s0v[:, :, 1, :, 0], s1, s1v[:, :, 1, :, 0], ALU.add)
            self.tt(dv, a.ee, eev[:, :, 1, :, k], s0, s0v[:, :, 1, :, 0], xe, xev[:, :, 1, :, k - 1], ALU.add)
        nel = 16 * (G - 1) * LS
        scr = self.arena_t[:, a.Sb.off:a.Sb.off + 4 * nel].bitcast(F32)
        tA = scr[:, 0:nel].rearrange("p (g k c) -> p g k c", g=16, k=G - 1)
        tB = scr[:, nel:2 * nel].rearrange("p (g k c) -> p g k c", g=16, k=G - 1)
        X6 = a.Xb.a.rearrange("p g r (s k c) -> p g r s k c", s=NSEQ, k=G)
        shp = [128, 16, G - 1, LS]
        Apr = a.Apow.a[:, :, 0, :].unsqueeze(2).broadcast_to(shp)
        Api = a.Apow.a[:, :, 1, :].unsqueeze(2).broadcast_to(shp)
        for sq in range(NSEQ):
            er = eev[:, :, 0, sq, 1:G].unsqueeze(3).broadcast_to(shp)
            ei = eev[:, :, 1, sq, 1:G].unsqueeze(3).broadcast_to(shp)
            Xr, Xi = X6[:, :, 0, sq, 1:G, :], X6[:, :, 1, sq, 1:G, :]
            self.tt(dv, a.Sb, tA, a.Apow, Apr, a.ee, er, ALU.mult)
            self.tt(dv, a.Sb, tB, a.Apow, Api, a.ee, ei, ALU.mult)
            self.tt(dv, a.Sb, tA, a.Sb, tA, a.Sb, tB, ALU.subtract)
            self.tt(dv, a.Xb, Xr, a.Xb, Xr, a.Sb, tA, ALU.add)
            self.tt(dv, a.Sb, tA, a.Apow, Apr, a.ee, ei, ALU.mult)
            self.tt(dv, a.Sb, tB, a.Apow, Api, a.ee, er, ALU.mult)
            self.tt(dv, a.Sb, tA, a.Sb, tA, a.Sb, tB, ALU.add)
            self.tt(dv, a.Xb, Xi, a.Xb, Xi, a.Sb, tA, ALU.add)
        for Tt in range(4):
            for j in range(8):
                ps = psb[nb % 4]
                nb += 1
                for k in range(j + 1):
                    rhs = a.u.a[:, Tt, (j - k) * NCH:(j - k + 1) * NCH]
                    self.mm(ps, ps.a[:, 0:NCH], a.Kb, a.Kb.a[:, Tt, k, :], a.u, rhs, k == 0, False)
                for m in range(4):
                    gp = 4 * Tt + m
                    for ri in range(2):
                        self.mm(ps, ps.a[32 * m:32 * m + 32, 0:NCH], a.PCb, a.PCb.a[:, j + 1, ri, gp, :],
                                a.Xb, a.Xb.a[:, gp, ri, :], False, ri == 1, tp=(0, 32 * m))
                self.act(a.ys, a.ys.a.rearrange("p (c j) -> p j c", j=8)[:, j, :], ps, ps.a[:, 0:NCH],
                         AF.Gelu_apprx_tanh)
            self.dma('sp', self.dr["ysT"], d["ysT"][Tt * 128:(Tt + 1) * 128, :], a.ys, a.ys.a)

    def post_norm_residual(self, l, a, X, goff, dst, t0, square_done=False, Y=None):
        psb, d = self.psb, self.d
        Y = a.Y if Y is None else Y
        if not square_done:
            self.act(a.ysq, a.ysq.a, Y, Y.a, AF.Square)
        self.rms_rstd(psb[7], a.ysq, lambda c: a.ysq.a[:, c, :], 8, 1024.0, a.rt, a.rstd)
        for oc in range(8):
            t = a.tt[oc % 2]
            self.tt('dve', t, t.a, Y, Y.a[:, oc, :], a.rstd, a.rstd.a, ALU.mult)
            self.stt(X, X.a[:, oc, :], t, t.a, self.vec(l, goff + oc), X, X.a[:, oc, :], ALU.mult, ALU.add,
                     extra_reads=[self.vecs])
        self.dma('sp', self.dr[dst], d[dst].rearrange("(c p) t -> p c t", p=128)[:, :, t0:t0 + 512], X, X.a)

    def m4(self, l, src, dst):
        P, a, d, psb = self.P, self.A4, self.d, self.psb
        xin = d[src].rearrange("(c p) t -> p c t", p=128)
        oin = d["oT"].rearrange("(c p) t -> p c t", p=128)
        yin = d["ysT"].rearrange("(c p) t -> p c t", p=128)
        gin = d["gT"].rearrange("(c p) t -> p c t", p=128)

        def load(tt):
            sl = slice(tt * 512, (tt + 1) * 512)
            self.dma('sp', a.o[tt % 2], a.o[tt % 2].a, self.dr["oT"], oin[:, :, sl])
            self.dma('sp', a.ysb[tt % 2], a.ysb[tt % 2].a, self.dr["ysT"], yin[:, :, sl])
            self.dma('sp', a.g[tt % 2], a.g[tt % 2].a, self.dr["gT"], gin[:, :, sl])
            self.dma('sp', a.X[tt % 2], a.X[tt % 2].a, self.dr[src], xin[:, :, sl])

        nbc = [0]

        def stageA(tt):
            o, ysb, g = a.o[tt % 2], a.ysb[tt % 2], a.g[tt % 2]
            for oc in range(8):
                nb = nbc[0]
                pa, pga, pgb = psb[(3 * nb) % 6], psb[(3 * nb + 1) % 6], psb[(3 * nb + 2) % 6]
                nbc[0] += 1
                sg, yss, m1, m2 = a.sg[0], a.yss[0], a.m1[0], a.m2[0]
                cs = slice(oc * 128, (oc + 1) * 128)
                for k in range(4):
                    self.mm(pa, pa.a, a.woatt, a.woatt.a[:, k, cs], o, o.a[:, k, :], k == 0, k == 3)
                for k in range(4):
                    self.mm(pga, pga.a, a.wglu, a.wglu.a[:, k, cs], ysb, ysb.a[:, k, :], k == 0, k == 3)
                for k in range(4):
                    self.mm(pgb, pgb.a, a.wglu, a.wglu.a[:, k, 1024 + oc * 128:1024 + (oc + 1) * 128], ysb,
                            ysb.a[:, k, :], k == 0, k == 3)
                self.act(sg, sg.a, pgb, pgb.a, AF.Sigmoid)
                self.tt('dve', yss, yss.a, pga, pga.a, sg, sg.a, ALU.mult)
                self.tt('dve', m1, m1.a, pa, pa.a, g, g.a[:, oc, :], ALU.mult)
                self.tt('dve', m2, m2.a, yss, yss.a, g, g.a[:, 8 + oc, :], ALU.mult)
                self.tt('dve', a.mg, a.mg.a[:, oc, :], m1, m1.a, m2, m2.a, ALU.add)

        def stageB(tt):
            for oc in range(8):
                py = psb[(3 * nbc[0]) % 6]
                nbc[0] += 1
                for k in range(8):
                    self.mm(py, py.a, a.wout, a.wout.a[:, k, oc * 128:(oc + 1) * 128], a.mg, a.mg.a[:, k, :],
                            k == 0, k == 7)
                self.act(a.Y2[tt % 2], a.Y2[tt % 2].a[:, oc, :], py, py.a, AF.Copy)
            self.act(a.ysq, a.ysq.a, a.Y2[tt % 2], a.Y2[tt % 2].a, AF.Square)

        def stageC(tt):
            self.post_norm_residual(l, a, a.X[tt % 2], GPOST, dst, tt * 512, square_done=True, Y=a.Y2[tt % 2])

        NT = self.NT
        load(0)
        self.load_w(a.woatt, d["w_oatt"][l], 4, 1024, a.stg)
        self.load_w(a.wglu, d["w_glu"][l], 4, 2048, a.stg)
        self.load_w(a.wout, d["w_out"][l], 8, 1024, a.stg)
        if NT > 1:
            load(1)
        stageA(0)
        stageB(0)
        for tt in range(1, NT):
            stageA(tt)
            stageC(tt - 1)
            if tt + 1 < NT:
                load(tt + 1)
            stageB(tt)
        stageC(NT - 1)

    def f1(self, l, src):
        P, a, d, psb = self.P, self.A5, self.d, self.psb
        self.dma('sp', a.X[0], a.X[0].a, self.dr[src], d[src].rearrange("(c p) t -> p c t", p=128)[:, :, 0:512])
        self.load_w(a.wup, d["w_up"][l], 8, 2 * D_FF, a.stg)
        for fc in range(NFC):
            for k in range(3):
                self.ts('dve', a.dgall, a.dgall.a[:, fc, k, :], self.ident, self.ident.a,
                        self.vec(l, CW + fc * 3 + k), None, ALU.mult, extra_reads=[self.vecs])
        xin = d[src].rearrange("(c p) t -> p c t", p=128)
        aout = d["actT"].rearrange("(c p) t -> p c t", p=128)

        def load(tt):
            self.dma('sp', a.X[tt % 2], a.X[tt % 2].a, self.dr[src], xin[:, :, tt * 512:(tt + 1) * 512])

        gi = 0
        for tt in range(self.NT):
            if tt + 1 < self.NT:
                load(tt + 1)
            X = a.X[tt % 2]
            t0 = tt * 512
            rs = a.rstdF.a[:, t0:t0 + 512]
            self.act(a.sqg, a.sqg.a, X, X.a, AF.Square)
            for c in range(8):
                self.mm(psb[0], psb[0].a, self.ones, self.ones.a, a.sqg, a.sqg.a[:, c, :], c == 0, c == 7)
            self.act(a.rt, a.rt.a, psb[0], psb[0].a, AF.Sqrt, bias=self.epst.a[:, 0:1], scale=1.0 / 1024.0,
                     extra_reads=[self.epst])
            P.add('dve', lambda e, rs=rs: e.reciprocal(out=rs, in_=a.rt.a), reads=[a.rt], writes=[a.rstdF])
            for dc in range(8):
                self.ts('dve', a.sqg, a.sqg.a[:, dc, :], X, X.a[:, dc, :], self.vec(l, GFPRE + dc), None, ALU.mult,
                        extra_reads=[self.vecs])
            seq_start = (tt % self.TPS == 0)
            pend = None

            def conv_stage(fc, Gb, psv, ge_i):
                psc = psb[5 + fc % 2]
                for k in range(3):
                    self.mm(psc, psc.a, a.dgall, a.dgall.a[:, fc, k, :], Gb, Gb.a[:, k:k + 512], k == 0, k == 2)
                ge = a.ge[ge_i % 3]
                self.act(ge, ge.a, psc, psc.a, AF.Gelu_apprx_tanh, bias=self.vec(l, CB + fc),
                         extra_reads=[self.vecs])
                ar = a.act[fc // (NFC // 2)]
                self.tt('dve', ar, ar.a[:, fc % (NFC // 2), :], psv, psv.a, ge, ge.a, ALU.mult)
                if fc % (NFC // 2) == NFC // 2 - 1:
                    hh = fc // (NFC // 2)
                    self.dma('sp', self.dr["actT"], aout[:, hh * 11:(hh + 1) * 11, t0:t0 + 512], ar, ar.a)

            for fc in range(NFC):
                psg, psv = psb[1 + fc % 2], psb[3 + fc % 2]
                Gb = a.Gb[gi % 3]
                gi += 1
                for dc in range(8):
                    self.mm(psg, psg.a, self.wr(a.wup, fc * 256, fc * 256 + 128),
                            a.wup.a[:, dc, fc * 256:fc * 256 + 128], a.sqg, a.sqg.a[:, dc, :], dc == 0, dc == 7)
                for dc in range(8):
                    self.mm(psv, psv.a, self.wr(a.wup, fc * 256 + 128, fc * 256 + 256),
                            a.wup.a[:, dc, fc * 256 + 128:fc * 256 + 256], a.sqg, a.sqg.a[:, dc, :], dc == 0, dc == 7)
                if seq_start:
                    P.add('pool', lambda e, Gb=Gb: e.memset(Gb.a[:, 0:2], 0.0), writes=[Gb])
                else:
                    self.act(Gb, Gb.a[:, 0:2], a.Gh, a.Gh.a[:, fc, :], AF.Copy)
                self.tt('dve', Gb, Gb.a[:, 2:514], psg, psg.a, a.rstdF, rs, ALU.mult)
                self.act(a.Gh, a.Gh.a[:, fc, :], Gb, Gb.a[:, 512:514], AF.Copy)
                if pend is not None:
                    conv_stage(*pend)
                pend = (fc, Gb, psv, gi)
            conv_stage(*pend)

    def f2(self, l, src, dst):
        P, a, d, psb = self.P, self.A6, self.d, self.psb
        xin = d[src].rearrange("(c p) t -> p c t", p=128)
        ain = d["actT"].rearrange("(c p) t -> p c t", p=128)

        def load(tt):
            sl = slice(tt * 512, (tt + 1) * 512)
            self.dma('sp', a.a[tt % 2], a.a[tt % 2].a, self.dr["actT"], ain[:, :, sl])
            self.dma('sp', a.X[tt % 2], a.X[tt % 2].a, self.dr[src], xin[:, :, sl])

        load(0)
        self.load_w(a.wdown, d["w_down"][l], NFC, 1024, a.stg)
        nbc = [0]
        NT = self.NT

        def mms(tt, oc):
            A_ = a.a[tt % 2]
            py = psb[nbc[0] % 6]
            nbc[0] += 1
            for k in range(NFC):
                self.mm(py, py.a, self.wr(a.wdown, oc * 128, (oc + 1) * 128),
                        a.wdown.a[:, k, oc * 128:(oc + 1) * 128], A_, A_.a[:, k, :], k == 0, k == NFC - 1)
            return py

        def evac(tt, oc, py):
            rs = a.rstdF.a[:, tt * 512:(tt + 1) * 512]
            self.tt('dve', a.Y, a.Y.a[:, oc, :], py, py.a, a.rstdF, rs, ALU.mult)

        for tt in range(NT):
            if tt == 0 and NT > 1:
                load(1)
            held = []
            for oc in range(8):
                py = mms(tt, oc)
                if tt > 0 and oc < 2:
                    held.append((oc, py))
                    if oc == 1:
                        self.post_norm_residual(l, a, a.X[(tt - 1) % 2], GFPOST, dst, (tt - 1) * 512,
                                                square_done=True)
                        if tt + 1 < NT:
                            load(tt + 1)
                        for oc_, py_ in held:
                            evac(tt, oc_, py_)
                else:
                    evac(tt, oc, py)
            self.act(a.ysq, a.ysq.a, a.Y, a.Y.a, AF.Square)
        self.post_norm_residual(l, a, a.X[(NT - 1) % 2], GFPOST, dst, (NT - 1) * 512, square_done=True)


def _rope_tables(seq):
    pos = np.arange(seq, dtype=np.float32)
    inv_freq = (np.float32(10000.0) ** (-np.arange(0, 32, 2, dtype=np.float32) / np.float32(32))).astype(np.float32)
    ang = (pos[:, None] * inv_freq[None, :]).astype(np.float32)
    cos, sin = np.cos(ang).astype(np.float32), np.sin(ang).astype(np.float32)
    tab = np.zeros((2, 128, seq), np.float32)
    tab[0, 64:80] = cos.T
    tab[0, 80:96] = cos.T
    tab[1, 64:80] = -sin.T
    tab[1, 80:96] = sin.T
    return tab


def _interleave_up(w):
    L = w.shape[0]
    g = w[:, :, :D_FF].reshape(L, 1024, NFC, 128)
    v = w[:, :, D_FF:].reshape(L, 1024, NFC, 128)
    return np.ascontiguousarray(np.stack([g, v], axis=3).reshape(L, 1024, 2 * D_FF))


def prep_weights(inp, depth, seq):
    f = lambda a: np.ascontiguousarray(np.asarray(a, dtype=np.float32))
    L = depth
    w_in = f(inp["w_in"])[:L]
    o1, o2, o3, o4 = 384, 640, 672, 1184
    kpe = w_in[:, :, o2:o3]
    kpe_sw = np.concatenate([kpe[:, :, 16:32], kpe[:, :, 0:16]], axis=2)
    w_inx = np.concatenate([w_in[:, :, 0:o2], kpe, kpe_sw, w_in[:, :, o3:o4], w_in[:, :, o4:]], axis=2)
    assert w_inx.shape[2] == WINX
    w_uq = f(inp["w_uq"])[:L]
    wq = w_uq.reshape(L, 384, 8, 96)
    w_uqsw = np.concatenate([wq[..., 80:96], wq[..., 64:80]], axis=3).reshape(L, 384, 256)
    w_ukv = f(inp["w_ukv"])[:L].reshape(L, 256, 8, 128)
    w_uk = w_ukv[..., 0:64].reshape(L, 256, 512)
    w_uv = w_ukv[..., 64:128].reshape(L, 256, 512)
    vecs = np.zeros((128, L, NV), np.float32)

    def put(off, arr, n):
        vecs[:, :, off:off + n] = f(arr)[:L].reshape(L, n, 128).transpose(2, 0, 1)

    put(GPRE, inp["g_mix_pre"], 8)
    put(BG, inp["b_gate"], 16)
    put(GQ, inp["g_q"], 3)
    put(GKV, inp["g_kv"], 2)
    put(GPOST, inp["g_mix_post"], 8)
    put(GFPRE, inp["g_ffn_pre"], 8)
    put(GFPOST, inp["g_ffn_post"], 8)
    cw = f(inp["conv_w"])[:L].reshape(L, 3, NFC, 128).transpose(3, 0, 2, 1)
    vecs[:, :, CW:CW + 66] = cw.reshape(128, L, 66)
    put(CB, inp["conv_b"], NFC)
    put(DSK, inp["d_skip"], 4)
    s5v = np.zeros((128, L, 48), np.float32)

    def gl(arr):
        return f(arr)[:L].reshape(L, 16, 2, 64).transpose(2, 3, 0, 1).reshape(128, L, 16)

    s5v[:, :, 0:16] = gl(inp["lam_re"])
    s5v[:, :, 16:32] = gl(inp["lam_im"])
    ls = f(inp["log_step"])[:L].reshape(L, 16, 2)
    s5v[:, :, 32:48] = np.repeat(ls.transpose(2, 0, 1)[:, None], 64, axis=1).reshape(128, L, 16)
    s5m = np.zeros((L, 2, 64, 4, 16, 2, 16), np.float32)
    cr = f(inp["c_re"])[:L].reshape(L, 16, 2, 16, 64)
    ci = f(inp["c_im"])[:L].reshape(L, 16, 2, 16, 64)
    br = f(inp["b_re"])[:L].reshape(L, 16, 2, 64, 16)
    bi = f(inp["b_im"])[:L].reshape(L, 16, 2, 64, 16)
    for g2 in range(2):
        s5m[:, g2, :, 0, :, g2, :] = cr[:, :, g2].transpose(0, 3, 1, 2)
        s5m[:, g2, :, 1, :, g2, :] = ci[:, :, g2].transpose(0, 3, 1, 2)
        s5m[:, g2, :, 2, :, g2, :] = br[:, :, g2].transpose(0, 2, 1, 3)
        s5m[:, g2, :, 3, :, g2, :] = bi[:, :, g2].transpose(0, 2, 1, 3)
    return {
        "w_inx": np.ascontiguousarray(w_inx),
        "w_uq": w_uq, "w_uqsw": np.ascontiguousarray(w_uqsw),
        "w_uk": np.ascontiguousarray(w_uk), "w_uv": np.ascontiguousarray(w_uv),
        "w_oatt": f(inp["w_o_att"])[:L], "w_glu": f(inp["w_glu"])[:L], "w_out": f(inp["w_out"])[:L],
        "w_up": _interleave_up(f(inp["w_up"])[:L]), "w_down": f(inp["w_down"])[:L],
        "vecs": np.ascontiguousarray(vecs.reshape(128, L * NV)),
        "s5v": np.ascontiguousarray(s5v.reshape(128, L * 48)),
        "s5m": np.ascontiguousarray(s5m.reshape(L, 128, 2048)),
        "rope": _rope_tables(seq),
        "ident": np.eye(128, dtype=np.float32),
        "bmask": np.kron(np.eye(4, dtype=np.float32), np.ones((32, 32), np.float32)),
    }


_CACHE = {}


def kernel(**inputs):
    x = np.asarray(inputs["x"], dtype=np.float32)
    B, S, D = x.shape
    ncores = 8
    nseq = B // ncores
    depth = int(np.asarray(inputs["w_in"]).shape[0])
    key = (nseq, S, depth)
    if key not in _CACHE:
        _CACHE[key] = K(nseq=nseq, seq=S, depth=depth).build()
    nc = _CACHE[key]
    wts = prep_weights(inputs, depth, S)
    in_maps = []
    for c in range(ncores):
        xs = x[c * nseq:(c + 1) * nseq].reshape(nseq * S, D)
        m = dict(wts)
        m["xT"] = np.ascontiguousarray(xs.T)
        in_maps.append(m)
    res = run_bass_kernel_spmd(nc, in_maps, core_ids=list(range(ncores)))
    out = np.empty((B, S, D), np.float32)
    for c in range(ncores):
        yT = np.asarray(res.results[c]["yT"], dtype=np.float32)
        out[c * nseq:(c + 1) * nseq] = yT.T.reshape(nseq, S, D)
    return out
```

```python
import contextlib
import math
import numpy as np
import ml_dtypes
import concourse.bass as bass
import concourse.mybir as mybir
from concourse.bass_utils import run_bass_kernel_spmd
from concourse.alu_op_type import AluOpType as ALU

AF = mybir.ActivationFunctionType
F32 = mybir.dt.float32
BF16 = mybir.dt.bfloat16
I32 = mybir.dt.int32
ENGS = ['sp', 'act', 'pool', 'dve', 'pe']

D_MODEL = 1024
N_HEADS = 8
D_FF = 2816
NFC = 22
EPS = 1e-6
NV = 145
GPRE, BG, GQ, GKV, GPOST, GFPRE, GFPOST, CW, CB, DSK = 0, 8, 24, 27, 29, 37, 45, 53, 119, 141
WINX = 3264
ARENA_ELEMS = 196 * 512


class Res:
    def __init__(self, name, dsem=None, multi=False):
        self.name = name
        self.writers = {}
        self.readers = {}
        self.dsem = dsem
        self.nw = 0
        self.multi = multi
        self.a = None


class Op:
    __slots__ = ('eng', 'fn', 'deps', 'needed', 'sig', 'dma', 'key')


class Prog:
    def __init__(self, nc, st):
        self.nc = nc
        self.st = st
        self.ops = []
        self.esem = {e: st.enter_context(nc.semaphore("s_" + e)) for e in ENGS}
        self.allsems = list(self.esem.values())
        self.nsem = len(ENGS)
        self.last = {e: None for e in ENGS}
        self.dma_res = []
        self.bar_deps = []

    def res(self, name, dma=False, multi=False):
        dsem = None
        if dma:
            dsem = self.st.enter_context(self.nc.semaphore("d_" + name))
            self.allsems.append(dsem)
            self.nsem += 1
        r = Res(name, dsem, multi)
        if dma:
            self.dma_res.append(r)
            r.last_dma = None
        return r

    def sb(self, name, shape, dtype, dma=False, multi=False):
        r = self.res(name, dma=dma, multi=multi)
        t = self.st.enter_context(self.nc.sbuf_tensor(name, shape, dtype))
        r.a = t[:]
        return r

    def ps(self, name, shape, dtype):
        r = self.res(name)
        t = self.st.enter_context(self.nc.psum_tensor(name, shape, dtype))
        r.a = t[:]
        return r

    def barrier(self):
        deps = [o for o in self.last.values() if o is not None]
        for r in self.dma_res:
            if r.last_dma is not None:
                deps.append(r.last_dma)
        self.bar_deps = deps

    def add(self, eng, fn, reads=(), writes=(), dma=False):
        op = Op()
        op.eng = eng
        op.fn = fn
        op.dma = dma
        op.needed = False
        op.sig = None
        deps = {}
        for o in self.bar_deps:
            deps[id(o)] = o
        for r in reads:
            for o in r.writers.values():
                deps[id(o)] = o
        for w in writes:
            for o in w.writers.values():
                deps[id(o)] = o
            for o in w.readers.values():
                deps[id(o)] = o
        if dma:
            dres = [w for w in writes if w.dsem is not None]
            assert len(dres) == 1, [w.name for w in writes]
            dres[0].nw += 1
            op.sig = (dres[0].dsem, 16 * dres[0].nw)
            dres[0].last_dma = op
            key = id(dres[0].dsem)
        else:
            key = eng
            self.last[eng] = op
        op.key = key
        op.deps = [o for o in deps.values()
                   if not (o.eng == 'pe' and eng == 'pe' and not o.dma and not dma)]
        for o in op.deps:
            o.needed = True
        for r in reads:
            r.readers[key] = op
        for w in writes:
            if w.multi:
                w.writers[key] = op
            else:
                w.writers = {key: op}
                w.readers = {}
        self.ops.append(op)
        return op

    def finish(self, final_res):
        nc = self.nc
        self.add('sp', None, reads=final_res)
        cnt = {e: 0 for e in ENGS}
        for op in self.ops:
            if not op.dma and op.needed:
                cnt[op.eng] += 1
                op.sig = (self.esem[op.eng], cnt[op.eng])
        per = {e: [o for o in self.ops if o.eng == e] for e in ENGS}
        self.stats = {e: len(per[e]) for e in ENGS}

        def mk(e):
            def body(engobj):
                waited = {}
                for op in per[e]:
                    need = {}
                    for d in op.deps:
                        s, v = d.sig
                        k = id(s)
                        if v > need.get(k, (None, 0))[1]:
                            need[k] = (s, v)
                    for k, (s, v) in need.items():
                        if v > waited.get(k, 0):
                            engobj.wait_ge(s, v)
                            waited[k] = v
                    if op.fn is None:
                        continue
                    ins = op.fn(engobj)
                    if op.dma:
                        ins.then_inc(op.sig[0], 16)
                    elif op.needed:
                        ins.then_inc(op.sig[0], 1)
            return body

        import os
        if os.environ.get("NOCLEAR") != "1":
            for sm in self.allsems:
                nc.gpsimd.sem_clear(sm)
            nc.all_engine_barrier()
        with nc.Block() as block:
            block.sync(mk('sp'))
            block.scalar(mk('act'))
            block.gpsimd(mk('pool'))
            block.vector(mk('dve'))
            block.tensor(mk('pe'))


class Arena:
    def __init__(self, K, name):
        self.K = K
        self.name = name
        self.off = 0

    def alloc(self, name, shape, dtype, dma=False, multi=False, at=None):
        n = 1
        for s in shape:
            n *= s
        nel = n * (2 if dtype in (F32, I32) else 1)
        nel = (nel + 15) // 16 * 16
        off = self.off if at is None else at
        if at is None:
            self.off += nel
        assert off + nel <= ARENA_ELEMS, (self.name, name, off + nel, ARENA_ELEMS)
        v = self.K.arena_t[:, off:off + n * (2 if dtype in (F32, I32) else 1)]
        if dtype in (F32, I32):
            v = v.bitcast(dtype)
        if len(shape) >= 2:
            names = "abcdefg"[:len(shape)]
            pat = "p (" + " ".join(names) + ") -> p " + " ".join(names)
            v = v.rearrange(pat, **{n: sz for n, sz in zip(names[:-1], shape[:-1])})
        r = self.K.P.res(self.name + "_" + name, dma=dma, multi=multi)
        r.a = v
        r.off = off
        return r


class NS:
    pass


class K:
    def __init__(self, nseq=2, seq=2048, depth=4, dump=False, phases=None):
        self.phases = phases
        self.NSEQ = nseq
        self.S = seq
        self.L = depth
        self.T = nseq * seq
        self.NT = self.T // 512
        self.TPS = seq // 512
        self.NCH = self.T // 8
        self.CPS = seq // 8
        self.G = 4
        self.dump = dump
        self.wl_i = 0
        assert self.NCH <= 512

    def build(self):
        nc = bass.Bass("TRN2", target_bir_lowering=False)
        self.nc = nc
        L, T, S = self.L, self.T, self.S
        d = {}

        def inp(name, shape):
            d[name] = nc.dram_tensor(name, shape, F32, kind="ExternalInput").ap()

        inp("xT", [1024, T])
        inp("w_inx", [L, 1024, WINX])
        inp("w_uq", [L, 384, 768])
        inp("w_uqsw", [L, 384, 256])
        inp("w_uk", [L, 256, 512])
        inp("w_uv", [L, 256, 512])
        inp("w_oatt", [L, 512, 1024])
        inp("w_glu", [L, 512, 2048])
        inp("w_out", [L, 1024, 1024])
        inp("w_up", [L, 1024, 2 * D_FF])
        inp("w_down", [L, D_FF, 1024])
        inp("vecs", [128, L * NV])
        inp("s5v", [128, L * 48])
        inp("s5m", [L, 128, 2048])
        inp("rope", [2, 128, S])
        inp("ident", [128, 128])
        inp("bmask", [128, 128])
        d["yT"] = nc.dram_tensor("yT", [1024, T], F32, kind="ExternalOutput").ap()
        skind = "ExternalOutput" if self.dump else "Internal"

        def scr(name, shape, dt):
            d[name] = nc.dram_tensor(name, shape, dt, kind=skind).ap()

        scr("s1", [1024, T], F32)
        scr("s2", [1024, T], F32)
        scr("qT", [8, 96, T], BF16)
        scr("kT", [8, 96, T], BF16)
        scr("vA", [T, 768], BF16)
        scr("uT", [512, T], BF16)
        scr("gT", [2048, T], BF16)
        scr("oT", [512, T], BF16)
        scr("ysT", [512, T], BF16)
        scr("actT", [D_FF, T], BF16)
        self.d = d
        with contextlib.ExitStack() as st:
            P = Prog(nc, st)
            self.P = P
            self.dr = {n: P.res("dr_" + n, dma=True, multi=True) for n in
                       ["yT", "s1", "s2", "qT", "kT", "vA", "uT", "gT", "oT", "ysT", "actT"]}
            self.dr["xT"] = P.res("dr_xT")
            self.arena_t = st.enter_context(nc.sbuf_tensor("arena", [128, ARENA_ELEMS], BF16))
            self.psb = [P.ps("psb%d" % i, [128, 512], F32) for i in range(8)]
            self.vecs = P.sb("vecs_sb", [128, L * NV], F32, dma=True)
            self.s5v = P.sb("s5v_sb", [128, L * 48], F32, dma=True)
            self.ident = P.sb("identb", [128, 128], BF16, dma=True)
            self.ones = P.sb("onesb", [128, 128], BF16)
            self.bmask = P.sb("bmask_sb", [128, 128], F32, dma=True)
            P.add('sp', lambda e: e.dma_start(out=self.bmask.a, in_=d["bmask"]), writes=[self.bmask], dma=True)
            self.epst = P.sb("epst", [128, 1], F32)
            P.add('sp', lambda e: e.dma_start(out=self.vecs.a, in_=d["vecs"]), writes=[self.vecs], dma=True)
            P.add('sp', lambda e: e.dma_start(out=self.s5v.a, in_=d["s5v"]), writes=[self.s5v], dma=True)
            P.add('pool', lambda e: e.dma_start(out=self.ident.a, in_=d["ident"]), writes=[self.ident], dma=True)
            P.add('dve', lambda e: e.memset(self.ones.a, 1.0), writes=[self.ones])
            P.add('dve', lambda e: e.memset(self.epst.a, EPS), writes=[self.epst])
            self.mk_arenas()
            nupd = 2 * L
            for l in range(L):
                for half in range(2):
                    u = 2 * l + half
                    src = "xT" if u == 0 else ("s1", "s2")[(u - 1) % 2]
                    dst = "yT" if u == nupd - 1 else ("s1", "s2")[u % 2]
                    on = lambda ph: self.phases is None or ph in self.phases
                    if half == 0:
                        if on('m1'):
                            self.m1(l, src)
                            P.barrier()
                        if on('m3'):
                            self.m3_prep(l)
                        if on('m2'):
                            self.m2(l)
                            P.barrier()
                        if on('m3'):
                            self.m3(l)
                            P.barrier()
                        if on('m4'):
                            self.m4(l, src, dst)
                            P.barrier()
                    else:
                        if on('f1'):
                            self.f1(l, src)
                            P.barrier()
                        if on('f2'):
                            self.f2(l, src, dst)
                            P.barrier()
            P.finish([self.dr["yT"]])
        return nc

    def vec(self, l, off, n=1):
        return self.vecs.a[:, l * NV + off:l * NV + off + n]

    def mk_arenas(self):
        T, S, NCH = self.T, self.S, self.NCH
        A = Arena(self, "m1")
        a = NS()
        a.win = A.alloc("win", [8, WINX], BF16, multi=True)
        self.wblocks(a.win, WINX, 1024)
        a.wuq = A.alloc("wuq", [3, 768], BF16, multi=True)
        a.wuqsw = A.alloc("wuqsw", [3, 256], BF16, multi=True)
        a.wuk = A.alloc("wuk", [2, 512], BF16, multi=True)
        a.wuv = A.alloc("wuv", [2, 512], BF16, multi=True)
        a.stg = [A.alloc("stg%d" % i, [1024], F32, dma=True) for i in range(4)]
        a.X = [A.alloc("X%d" % i, [8, 512], F32, dma=True) for i in range(2)]
        a.cs = [A.alloc("cs%d" % i, [2, 512], F32, dma=True) for i in range(2)]
        a.xsq = A.alloc("xsq", [8, 512], BF16)
        a.xg = A.alloc("xg", [8, 512], BF16)
        a.rt = A.alloc("rt", [512], F32)
        a.rstd = A.alloc("rstd", [512], F32)
        a.cq = A.alloc("cq", [3, 512], F32)
        a.ckv = A.alloc("ckv", [2, 512], F32)
        a.sq = A.alloc("sq", [3, 512], BF16)
        a.rq = A.alloc("rq", [512], F32)
        a.cqn = A.alloc("cqn", [3, 512], BF16)
        a.ckvn = A.alloc("ckvn", [2, 512], BF16)
        a.tmp = [A.alloc("tmp%d" % i, [512], F32) for i in range(4)]
        a.kp = A.alloc("kp", [512], F32)
        a.kps = A.alloc("kps", [512], F32)
        a.kr = A.alloc("kr", [512], BF16)
        a.ut = A.alloc("ut", [4, 512], BF16)
        a.gts = [A.alloc("gts%d" % i, [4, 512], BF16) for i in range(2)]
        a.qh = [A.alloc("qh%d" % i, [512], BF16) for i in range(3)]
        a.kh = [A.alloc("kh%d" % i, [512], BF16) for i in range(3)]
        a.vt = A.alloc("vt", [4, 8, 96], BF16)
        self.A1 = a
        A = Arena(self, "m2")
        a = NS()
        a.q = [A.alloc("q%d" % i, [S], BF16, dma=True) for i in range(2)]
        a.k = [A.alloc("k%d" % i, [S], BF16, dma=True) for i in range(2)]
        a.v = [A.alloc("v%d" % i, [S // 128, 768], BF16, dma=True) for i in range(2)]
        a.pT = [A.alloc("pT%d" % i, [512], BF16) for i in range(3)]
        a.rc = [A.alloc("rc%d" % i, [512], F32) for i in range(3)]
        a.hi = [A.alloc("hi%d" % i, [512], BF16) for i in range(3)]
        a.lo = [A.alloc("lo%d" % i, [512], BF16) for i in range(3)]
        a.bc = [A.alloc("bc%d" % i, [512], F32) for i in range(2)]
        a.ot = [A.alloc("ot%d" % i, [512], BF16) for i in range(2)]
        self.A2 = a
        A = Arena(self, "m3")
        a = NS()
        a.u = A.alloc("u", [4, T], BF16, dma=True, multi=True)
        a.Sb = A.alloc("Sb", [16, 2, NCH], BF16)
        a.Xb = A.alloc("Xb", [16, 2, NCH], BF16)
        a.Wp = A.alloc("Wp", [4, 8, 2, 128], BF16)
        a.PCb = A.alloc("PCb", [9, 2, 16, 32], BF16)
        a.Kb = A.alloc("Kb", [4, 8, 128], BF16)
        a.BbQb = A.alloc("BbQb", [2, 16, 32], BF16)
        a.m = A.alloc("m", [4, 16, 32], F32, dma=True)
        a.Bb = A.alloc("Bb", [2, 16, 32], F32)
        a.Pw = A.alloc("Pw", [9, 2, 16], F32)
        a.sm = [A.alloc("sm%d" % i, [16], F32) for i in range(14)]
        a.smi = A.alloc("smi", [16], I32)
        a.fc = A.alloc("fcoef", [2, 16], F32)
        a.nAi = A.alloc("nAi", [16], F32)
        a.nPr = A.alloc("nPr", [9, 16], F32)
        a.td = [A.alloc("td%d" % i, [16, 32], F32) for i in range(4)]
        G = self.G
        LS = self.CPS // G
        a.xs = [A.alloc("xs%d" % i, [16, 2, self.NSEQ * G], F32) for i in range(2)]
        a.t1 = A.alloc("t1", [16, 2, self.NSEQ * G], F32)
        a.t2 = A.alloc("t2", [16, 2, self.NSEQ * G], F32)
        a.ee = A.alloc("ee", [16, 2, self.NSEQ * G], F32)
        a.Q = A.alloc("Q", [8, 2, 16], F32)
        a.Apow = A.alloc("Apow", [16, 2, LS], F32)
        off_ys = A.off
        a.ys = A.alloc("ys", [T], BF16)
        a.PBb = A.alloc("PBb", [4, 8, 2, 4, 32], BF16, at=off_ys)
        if 16 * 8 * 2 * 32 > T:
            A.off = off_ys + 16 * 8 * 2 * 32
        self.A3 = a
        A = Arena(self, "m4")
        a = NS()
        a.woatt = A.alloc("woatt", [4, 1024], BF16, multi=True)
        a.stg = [A.alloc("stg%d" % i, [1024], F32, dma=True) for i in range(2)]
        a.wglu = A.alloc("wglu", [4, 2048], BF16, multi=True)
        a.wout = A.alloc("wout", [8, 1024], BF16, multi=True)
        a.o = [A.alloc("o%d" % i, [4, 512], BF16, dma=True) for i in range(2)]
        a.ysb = [A.alloc("ysb%d" % i, [4, 512], BF16, dma=True) for i in range(2)]
        a.g = [A.alloc("g%d" % i, [16, 512], BF16, dma=True) for i in range(2)]
        a.X = [A.alloc("X%d" % i, [8, 512], F32, dma=True) for i in range(2)]
        a.sg = [A.alloc("sg%d" % i, [512], F32) for i in range(1)]
        a.yss = [A.alloc("yss%d" % i, [512], F32) for i in range(1)]
        a.m1 = [A.alloc("m1%d" % i, [512], F32) for i in range(1)]
        a.m2 = [A.alloc("m2%d" % i, [512], F32) for i in range(1)]
        a.mg = A.alloc("mg", [8, 512], BF16)
        a.Y2 = [A.alloc("Y%d" % i, [8, 512], F32) for i in range(2)]
        a.ysq = A.alloc("ysq", [8, 512], BF16)
        a.rt = A.alloc("rt", [512], F32)
        a.rstd = A.alloc("rstd", [512], F32)
        a.tt = [A.alloc("tt%d" % i, [512], F32) for i in range(2)]
        self.A4 = a
        A = Arena(self, "f1")
        a = NS()
        rstdF = A.alloc("rstdF", [T], F32)
        a.rstdF = rstdF
        a.wup = A.alloc("wup", [8, 2 * D_FF], BF16, multi=True)
        a.X = [A.alloc("X%d" % i, [8, 512], F32, dma=True) for i in range(1)]
        a.stg = [A.alloc("stg%d" % i, [1024], F32, dma=True) for i in range(2)]
        a.sqg = A.alloc("sqg", [8, 512], BF16)
        a.rt = A.alloc("rt", [512], F32)
        a.Gb = [A.alloc("Gb%d" % i, [514], BF16) for i in range(3)]
        a.Gh = A.alloc("Gh", [NFC, 2], BF16)
        a.ge = [A.alloc("ge%d" % i, [512], BF16) for i in range(3)]
        a.dgall = A.alloc("dgall", [NFC, 3, 128], BF16)
        a.act = [A.alloc("act%d" % i, [NFC // 2, 512], BF16) for i in range(2)]
        self.wblocks(a.wup, 2 * D_FF, 1024)
        self.A5 = a
        A = Arena(self, "f2")
        a = NS()
        A.alloc("rstdF_pad", [T], F32)
        a.rstdF = rstdF
        a.wdown = A.alloc("wdown", [NFC, 1024], BF16, multi=True)
        a.stg = [A.alloc("stg%d" % i, [512], F32, dma=True) for i in range(4)]
        self.wblocks(a.wdown, 1024, 512)
        a.a = [A.alloc("a%d" % i, [NFC, 512], BF16, dma=True) for i in range(2)]
        a.X = [A.alloc("X%d" % i, [8, 512], F32, dma=True) for i in range(2)]
        a.Y = A.alloc("Y", [8, 512], F32)
        a.ysq = A.alloc("ysq", [8, 512], BF16)
        a.rt = A.alloc("rt", [512], F32)
        a.rstd = A.alloc("rstd", [512], F32)
        a.tt = [A.alloc("tt%d" % i, [512], F32) for i in range(2)]
        self.A6 = a

    def wblocks(self, dst, n, W):
        nb = (n + W - 1) // W
        rs = []
        for b in range(nb):
            r = self.P.res(dst.name + "_b%d" % b, multi=True)
            r.a = dst.a
            rs.append(r)
        dst.blocks = rs
        dst.W = W
        return rs

    def wr(self, dst, c0, c1):
        if not hasattr(dst, 'blocks'):
            return [dst]
        return dst.blocks[c0 // dst.W:(c1 - 1) // dst.W + 1]

    def load_w(self, dst, src2d, kc, n, stg, only_block=None):
        W = stg[0].a.shape[-1]
        if hasattr(dst, 'blocks'):
            assert dst.W % W == 0 or W % dst.W == 0
            W = min(W, dst.W)
        for bi, n0 in enumerate(range(0, n, W)):
            if only_block is not None and bi != only_block:
                continue
            n1 = min(n, n0 + W)
            wres = self.wr(dst, n0, n1)
            assert len(wres) == 1
            for c in range(kc):
                i = self.wl_i
                self.wl_i += 1
                sl = stg[i % len(stg)]
                self.dma('sp', sl, sl.a[:, 0:n1 - n0], None, src2d[c * 128:(c + 1) * 128, n0:n1])
                if i % 2 == 0:
                    self.act(wres[0], dst.a[:, c, n0:n1], sl, sl.a[:, 0:n1 - n0], AF.Copy)
                else:
                    self.cp('dve', wres[0], dst.a[:, c, n0:n1], sl, sl.a[:, 0:n1 - n0])

    def lazy_loader(self, items, first):
        items = list(items)
        for _ in range(min(first, len(items))):
            items.pop(0)()

        def pump(n=1):
            for _ in range(n):
                if items:
                    items.pop(0)()
        return pump

    def mm(self, outr, out_ap, lr, lhsT, rr, rhs, start, stop, tp=None):
        if tp is None:
            fn = lambda e: e.matmul(out_ap, lhsT, rhs, start=start, stop=stop)
        else:
            fn = lambda e: e.matmul(out_ap, lhsT, rhs, start=start, stop=stop, tile_position=tp)
        rd = (lr if isinstance(lr, list) else [lr]) + (rr if isinstance(rr, list) else [rr])
        self.P.add('pe', fn, reads=rd, writes=[outr])

    def act(self, outr, out_ap, inr, in_ap, func, bias=None, scale=None, extra_reads=()):
        kw = {}
        if bias is not None:
            kw['bias'] = bias
        if scale is not None:
            kw['scale'] = scale
        self.P.add('act', lambda e: e.activation(out=out_ap, in_=in_ap, func=func, **kw),
                   reads=[inr] + list(extra_reads), writes=[outr])

    def tt(self, eng, outr, out_ap, r0, in0, r1, in1, op):
        self.P.add(eng, lambda e: e.tensor_tensor(out=out_ap, in0=in0, in1=in1, op=op),
                   reads=[r0, r1], writes=[outr])

    def ts(self, eng, outr, out_ap, r0, in0, s1, s2, op0, op1=None, extra_reads=()):
        if op1 is None:
            fn = lambda e: e.tensor_scalar(out=out_ap, in0=in0, scalar1=s1, scalar2=None, op0=op0)
        else:
            fn = lambda e: e.tensor_scalar(out=out_ap, in0=in0, scalar1=s1, scalar2=s2, op0=op0, op1=op1)
        self.P.add(eng, fn, reads=[r0] + list(extra_reads), writes=[outr])

    def stt(self, outr, out_ap, r0, in0, scalar, r1, in1, op0, op1, extra_reads=()):
        self.P.add('dve', lambda e: e.scalar_tensor_tensor(out=out_ap, in0=in0, scalar=scalar, in1=in1,
                                                           op0=op0, op1=op1),
                   reads=[r0, r1] + list(extra_reads), writes=[outr])

    def cp(self, eng, outr, out_ap, inr, in_ap):
        self.P.add(eng, lambda e: e.tensor_copy(out=out_ap, in_=in_ap), reads=[inr], writes=[outr])

    def dma(self, eng, outr, out_ap, inr, in_ap):
        self.P.add(eng, lambda e: e.dma_start(out=out_ap, in_=in_ap),
                   reads=[inr] if inr is not None else [], writes=[outr], dma=True)

    def rms_rstd(self, ps, sq_res, sq_ap_fn, nchunk, dim, rt, rstd):
        for c in range(nchunk):
            self.mm(ps, ps.a, self.ones, self.ones.a, sq_res, sq_ap_fn(c), c == 0, c == nchunk - 1)
        self.act(rt, rt.a, ps, ps.a, AF.Sqrt, bias=self.epst.a[:, 0:1], scale=1.0 / dim, extra_reads=[self.epst])
        self.P.add('dve', lambda e: e.reciprocal(out=rstd.a, in_=rt.a), reads=[rt], writes=[rstd])

    def m1(self, l, src):
        P, a, d, psb = self.P, self.A1, self.d, self.psb
        xin = d[src].rearrange("(c p) t -> p c t", p=128)
        xr = self.dr[src]
        P.add('pool', lambda e: e.memset(a.vt.a[:, :, :, 64:96], 1.0), writes=[a.vt])
        ropev = d["rope"].rearrange("k p s -> p k s")

        def load(tt):
            X = a.X[tt % 2]
            cs = a.cs[tt % 2]
            self.dma('sp', X, X.a, xr, xin[:, :, tt * 512:(tt + 1) * 512])
            p0 = (tt % self.TPS) * 512
            self.dma('sp', cs, cs.a[64:96, :, :], None, ropev[64:96, :, p0:p0 + 512])

        load(0)
        items = [(lambda b=b: self.load_w(a.win, d["w_inx"][l], 8, WINX, a.stg, only_block=b))
                 for b in range(len(a.win.blocks))]
        items += [lambda: self.load_w(a.wuq, d["w_uq"][l], 3, 768, a.stg),
                  lambda: self.load_w(a.wuqsw, d["w_uqsw"][l], 3, 256, a.stg),
                  lambda: self.load_w(a.wuk, d["w_uk"][l], 2, 512, a.stg),
                  lambda: self.load_w(a.wuv, d["w_uv"][l], 2, 512, a.stg)]
        pump = self.lazy_loader(items, 2)
        zb = [1, 2, 3, 4]
        zi = [0]

        def nextbank():
            b = psb[zb[zi[0] % 4]]
            zi[0] += 1
            return b

        for tt in range(self.NT):
            if tt + 1 < self.NT:
                load(tt + 1)
            X, cs = a.X[tt % 2], a.cs[tt % 2]
            t0 = tt * 512
            cosr = cs.a[64:96, 0, :]
            sinr = cs.a[64:96, 1, :]
            self.act(a.xsq, a.xsq.a, X, X.a, AF.Square)
            for dc in range(8):
                self.ts('dve', a.xg, a.xg.a[:, dc, :], X, X.a[:, dc, :], self.vec(l, GPRE + dc), None, ALU.mult,
                        extra_reads=[self.vecs])
            self.rms_rstd(psb[0], a.xsq, lambda c: a.xsq.a[:, c, :], 8, 1024.0, a.rt, a.rstd)

            def zchunk(ps, c0, m, tp=None, out_ap=None):
                oa = ps.a[0:m, :] if out_ap is None else out_ap
                for dc in range(8):
                    self.mm(ps, oa, self.wr(a.win, c0, c0 + m), a.win.a[:, dc, c0:c0 + m], a.xg, a.xg.a[:, dc, :], dc == 0, dc == 7, tp)

            for c in range(3):
                ps = nextbank()
                zchunk(ps, c * 128, 128)
                self.tt('dve', a.cq, a.cq.a[:, c, :], ps, ps.a, a.rstd, a.rstd.a, ALU.mult)
            for c in range(2):
                ps = nextbank()
                zchunk(ps, 384 + c * 128, 128)
                self.tt('dve', a.ckv, a.ckv.a[:, c, :], ps, ps.a, a.rstd, a.rstd.a, ALU.mult)
            self.act(a.sq, a.sq.a, a.cq, a.cq.a, AF.Square)
            self.rms_rstd(psb[7], a.sq, lambda c: a.sq.a[:, c, :], 3, 384.0, a.rt, a.rq)
            for c in range(3):
                self.stt(a.cqn, a.cqn.a[:, c, :], a.cq, a.cq.a[:, c, :], self.vec(l, GQ + c), a.rq, a.rq.a,
                         ALU.mult, ALU.mult, extra_reads=[self.vecs])
            self.act(a.sq, a.sq.a[:, 0:2, :], a.ckv, a.ckv.a, AF.Square)
            self.rms_rstd(psb[7], a.sq, lambda c: a.sq.a[:, c, :], 2, 256.0, a.rt, a.rq)
            for c in range(2):
                self.stt(a.ckvn, a.ckvn.a[:, c, :], a.ckv, a.ckv.a[:, c, :], self.vec(l, GKV + c), a.rq, a.rq.a,
                         ALU.mult, ALU.mult, extra_reads=[self.vecs])
            pump()
            zchunk(psb[5], 640, 32, tp=(0, 64), out_ap=psb[5].a[64:96, :])
            zchunk(psb[6], 672, 32, tp=(0, 64), out_ap=psb[6].a[64:96, :])
            self.tt('dve', a.kp, a.kp.a[64:96, :], psb[5], psb[5].a[64:96, :], a.rstd, a.rstd.a[64:96, :], ALU.mult)
            self.tt('dve', a.kps, a.kps.a[64:96, :], psb[6], psb[6].a[64:96, :], a.rstd, a.rstd.a[64:96, :], ALU.mult)
            self.tt('dve', a.kp, a.kp.a[64:96, :], a.kp, a.kp.a[64:96, :], cs, cosr, ALU.mult)
            self.tt('dve', a.kps, a.kps.a[64:96, :], a.kps, a.kps.a[64:96, :], cs, sinr, ALU.mult)
            self.tt('pool', a.kr, a.kr.a[64:96, :], a.kp, a.kp.a[64:96, :], a.kps, a.kps.a[64:96, :], ALU.add)
            for c in range(4):
                ps = nextbank()
                zchunk(ps, 704 + c * 128, 128)
                self.tt('dve', a.ut, a.ut.a[:, c, :].rearrange("p (j cl) -> p cl j", j=8),
                        ps, ps.a.rearrange("p (cl j) -> p cl j", j=8),
                        a.rstd, a.rstd.a.rearrange("p (cl j) -> p cl j", j=8), ALU.mult)
            for c in range(4):
                self.dma('sp', self.dr["uT"],
                         d["uT"][c * 128:(c + 1) * 128, :].rearrange("p (j n) -> p j n", j=8)[:, :, tt * 64:(tt + 1) * 64],
                         a.ut, a.ut.a[:, c, :].rearrange("p (j cl) -> p j cl", j=8))
            pump()
            for gb in range(4):
                pump()
                gts = a.gts[gb % 2]
                for gi in range(4):
                    gc = gb * 4 + gi
                    ps = nextbank()
                    zchunk(ps, 1216 + gc * 128, 128)
                    tmp = a.tmp[gc % 4]
                    self.tt('dve', tmp, tmp.a, ps, ps.a, a.rstd, a.rstd.a, ALU.mult)
                    self.act(gts, gts.a[:, gi, :], tmp, tmp.a, AF.Sigmoid, bias=self.vec(l, BG + gc),
                             extra_reads=[self.vecs])
                self.dma('sp', self.dr["gT"],
                         d["gT"].rearrange("(c p) t -> p c t", p=128)[:, gb * 4:(gb + 1) * 4, t0:t0 + 512],
                         gts, gts.a)
            pump(8)
            for h in range(8):
                ps = nextbank()
                psw = psb[5 + h % 2]
                for c in range(3):
                    self.mm(ps, ps.a[0:96, :], a.wuq, a.wuq.a[:, c, 96 * h:96 * h + 96], a.cqn, a.cqn.a[:, c, :],
                            c == 0, c == 2)
                for c in range(3):
                    self.mm(psw, psw.a[64:96, :], a.wuqsw, a.wuqsw.a[:, c, 32 * h:32 * h + 32], a.cqn,
                            a.cqn.a[:, c, :], c == 0, c == 2, tp=(0, 64))
                qh = a.qh[h % 3]
                t1, t2 = a.tmp[(2 * h) % 4], a.tmp[(2 * h + 1) % 4]
                self.act(qh, qh.a[0:64, :], ps, ps.a[0:64, :], AF.Copy)
                self.tt('dve', t1, t1.a[64:96, :], ps, ps.a[64:96, :], cs, cosr, ALU.mult)
                self.tt('dve', t2, t2.a[64:96, :], psw, psw.a[64:96, :], cs, sinr, ALU.mult)
                self.tt('pool', qh, qh.a[64:96, :], t1, t1.a[64:96, :], t2, t2.a[64:96, :], ALU.add)
                self.dma('sp', self.dr["qT"], d["qT"][h, :, t0:t0 + 512], qh, qh.a[0:96, :])
            for h in range(8):
                ps = nextbank()
                for c in range(2):
                    self.mm(ps, ps.a[0:64, :], a.wuk, a.wuk.a[:, c, 64 * h:64 * h + 64], a.ckvn, a.ckvn.a[:, c, :],
                            c == 0, c == 1)
                kh = a.kh[h % 3]
                self.act(kh, kh.a[0:64, :], ps, ps.a[0:64, :], AF.Copy)
                self.cp('pool', kh, kh.a[64:96, :], a.kr, a.kr.a[64:96, :])
                self.dma('sp', self.dr["kT"], d["kT"][h, :, t0:t0 + 512], kh, kh.a[0:96, :])
            for tb in range(4):
                ps = nextbank()
                for c in range(2):
                    self.mm(ps, ps.a, a.ckvn, a.ckvn.a[:, c, tb * 128:(tb + 1) * 128], a.wuv, a.wuv.a[:, c, :],
                            c == 0, c == 1)
                self.cp('dve' if tb % 2 else 'act', a.vt, a.vt.a[:, tb, :, 0:64],
                        ps, ps.a.rearrange("p (h c) -> p h c", h=8)) if tb % 2 else \
                    self.act(a.vt, a.vt.a[:, tb, :, 0:64], ps, ps.a.rearrange("p (h c) -> p h c", h=8), AF.Copy)
            self.dma('sp', self.dr["vA"],
                     d["vA"].rearrange("(n p) c -> p n c", p=128)[:, 4 * tt:4 * tt + 4, :],
                     a.vt, a.vt.a.rearrange("p n h c -> p n (h c)"))

    def m2(self, l):
        P, a, d, psb = self.P, self.A2, self.d, self.psb
        S = self.S
        scale = 1.0 / math.sqrt(96.0)
        vAv = d["vA"].rearrange("(n p) c -> p n c", p=128)
        nsb = S // 128
        pairs = [(s, h) for s in range(self.NSEQ) for h in range(8)]

        def loadv(s):
            v = a.v[s % 2]
            self.dma('sp', v, v.a, self.dr["vA"], vAv[:, s * nsb:(s + 1) * nsb, :])

        def loadqk(i):
            s, h = pairs[i]
            q, k = a.q[i % 2], a.k[i % 2]
            self.dma('sp', q, q.a[0:96, :], self.dr["qT"], d["qT"][h, :, s * S:(s + 1) * S])
            self.dma('sp', k, k.a[0:96, :], self.dr["kT"], d["kT"][h, :, s * S:(s + 1) * S])

        blocks = []
        g = 0
        for i, (s, h) in enumerate(pairs):
            for qa in range(S // 512):
                nblk = 4 * qa + 4
                for j in range(nblk):
                    blocks.append((i, s, h, qa, j, nblk, g))
                g += 1
        N = len(blocks)

        def geom(b):
            i, s, h, qa, j, nblk, g = b
            r = j - 4 * qa
            qoff = 128 * r if r > 0 else 0
            return r, qoff, 512 - qoff

        def emitS(idx):
            i, s, h, qa, j, nblk, g = blocks[idx]
            r, qoff, nq = geom(blocks[idx])
            q, k = a.q[i % 2], a.k[i % 2]
            pss = psb[idx % 3]
            self.mm(pss, pss.a[:, 0:nq], k, k.a[0:96, j * 128:(j + 1) * 128],
                    q, q.a[0:96, qa * 512 + qoff:qa * 512 + 512], True, True)

        def emitEP(idx):
            r, qoff, nq = geom(blocks[idx])
            pss, pT = psb[idx % 3], a.pT[idx % 3]
            self.act(pT, pT.a[:, 0:nq], pss, pss.a[:, 0:nq], AF.Exp, scale=scale)
            if r >= 0:
                P.add('dve', lambda e, pT=pT: e.memset(pT.a[64:128, 0:64], 0.0), writes=[pT])

        def emitPV(idx):
            i, s, h, qa, j, nblk, g = blocks[idx]
            r, qoff, nq = geom(blocks[idx])
            v, pT, po = a.v[s % 2], a.pT[idx % 3], psb[POB[g % 3]]
            self.mm(po, po.a[0:96, qoff:512], v, v.a[:, j, h * 96:(h + 1) * 96], pT, pT.a[:, 0:nq],
                    j == 0, j == nblk - 1)

        def tail_front(b):
            g = b[6]
            po, rc, hi, lo = psb[POB[g % 3]], a.rc[g % 3], a.hi[g % 3], a.lo[g % 3]
            P.add('dve', lambda e, rc=rc, po=po: e.reciprocal(out=rc.a[64:65, :], in_=po.a[64:65, :]),
                  reads=[po], writes=[rc])
            self.cp('dve', hi, hi.a[64:65, :], rc, rc.a[64:65, :])
            self.tt('dve', lo, lo.a[64:65, :], rc, rc.a[64:65, :], hi, hi.a[64:65, :], ALU.subtract)

        def tail_back(b):
            i, s, h, qa, j, nblk, g = b
            po, pb = psb[POB[g % 3]], psb[5 + g % 2]
            hi, lo, bc, ot = a.hi[g % 3], a.lo[g % 3], a.bc[g % 2], a.ot[g % 2]
            self.mm(pb, pb.a[0:64, :], self.ones, self.ones.a[64:65, 0:64], hi, hi.a[64:65, :], True, False,
                    tp=(64, 0))
            self.mm(pb, pb.a[0:64, :], self.ones, self.ones.a[64:65, 0:64], lo, lo.a[64:65, :], False, True,
                    tp=(64, 0))
            self.act(bc, bc.a[0:64, :], pb, pb.a[0:64, :], AF.Copy)
            self.tt('dve', ot, ot.a[0:64, :], po, po.a[0:64, :], bc, bc.a[0:64, :], ALU.mult)
            tq = s * S + qa * 512
            self.dma('sp', self.dr["oT"], d["oT"][h * 64:(h + 1) * 64, tq:tq + 512], ot, ot.a[0:64, :])

        POB = [3, 4, 7]
        DEFER = 6
        loadv(0)
        loadqk(0)
        if len(pairs) > 1:
            loadqk(1)
        emitS(0)
        if N > 1:
            emitS(1)
        pending = []
        for idx in range(N):
            b = blocks[idx]
            i, s, h, qa, j, nblk, g = b
            if qa == 0 and j == 0:
                if h == 0 and s + 1 < self.NSEQ:
                    loadv(s + 1)
            emitEP(idx)
            if idx + 2 < N:
                b2 = blocks[idx + 2]
                if b2[3] == 0 and b2[4] == 0 and b2[0] + 1 < len(pairs) and b2[0] >= 1:
                    pass
                emitS(idx + 2)
            emitPV(idx)
            pending = [(pb_, c_ - 1) for (pb_, c_) in pending]
            while pending and pending[0][1] <= 0:
                tail_back(pending.pop(0)[0])
            if j == nblk - 1:
                tail_front(b)
                pending.append((b, DEFER))
                if qa == S // 512 - 1 and i + 2 < len(pairs):
                    loadqk(i + 2)
        for pb_, c_ in pending:
            tail_back(pb_)

    def m3_prep(self, l):
        P, a, d, psb = self.P, self.A3, self.d, self.psb
        T, NCH, CPS, NSEQ = self.T, self.NCH, self.CPS, self.NSEQ
        sv = self.s5v.a[:, l * 48:(l + 1) * 48]
        lre, lim, lst = sv[:, 0:16], sv[:, 16:32], sv[:, 32:48]
        svr = self.s5v
        self.dma('sp', a.m, a.m.a, None, d["s5m"][l].rearrange("p (k g c) -> p k g c", k=4, g=16))
        sm = a.sm
        dv, ac = 'dve', 'act'
        delta, lrd, mag, ang, rr, kf, fr, s1, s2, ch, tA, tB, ca, den = sm
        self.act(delta, delta.a, svr, lst, AF.Exp)
        self.tt(dv, lrd, lrd.a, svr, lre, delta, delta.a, ALU.mult)
        self.act(mag, mag.a, lrd, lrd.a, AF.Exp)
        self.tt(dv, ang, ang.a, svr, lim, delta, delta.a, ALU.mult)
        self.ts(dv, rr, rr.a, ang, ang.a, 1.0 / (2.0 * math.pi), None, ALU.mult)
        self.cp(dv, a.smi, a.smi.a, rr, rr.a)
        self.cp(dv, kf, kf.a, a.smi, a.smi.a)
        self.tt(dv, fr, fr.a, rr, rr.a, kf, kf.a, ALU.subtract)
        self.act(s1, s1.a, fr, fr.a, AF.Sin, scale=math.pi)
        self.act(s2, s2.a, fr, fr.a, AF.Sin, scale=math.pi / 2.0)
        self.tt(dv, tA, tA.a, s2, s2.a, s2, s2.a, ALU.mult)
        self.ts(dv, ch, ch.a, tA, tA.a, -2.0, 1.0, ALU.mult, ALU.add)
        self.tt(dv, tA, tA.a, s1, s1.a, ch, ch.a, ALU.mult)
        Pw = a.Pw
        self.stt(Pw, Pw.a[:, 1, 1, :], tA, tA.a, 2.0, mag, mag.a, ALU.mult, ALU.mult)
        self.tt(dv, tB, tB.a, s1, s1.a, s1, s1.a, ALU.mult)
        self.ts(dv, ca, ca.a, tB, tB.a, -2.0, 1.0, ALU.mult, ALU.add)
        self.tt(dv, Pw, Pw.a[:, 1, 0, :], ca, ca.a, mag, mag.a, ALU.mult)
        P.add(dv, lambda e: e.memset(Pw.a[:, 0, 0, :], 1.0), writes=[Pw])
        P.add(dv, lambda e: e.memset(Pw.a[:, 0, 1, :], 0.0), writes=[Pw])
        ar, ai = Pw.a[:, 1, 0, :], Pw.a[:, 1, 1, :]
        nr = s1
        self.ts(dv, nr, nr.a, Pw, ar, -1.0, None, ALU.add)
        self.tt(dv, tA, tA.a, svr, lre, svr, lre, ALU.mult)
        self.tt(dv, tB, tB.a, svr, lim, svr, lim, ALU.mult)
        self.tt(dv, den, den.a, tA, tA.a, tB, tB.a, ALU.add)
        P.add(dv, lambda e: e.reciprocal(out=den.a, in_=den.a), reads=[den], writes=[den])
        self.tt(dv, tA, tA.a, nr, nr.a, svr, lre, ALU.mult)
        self.tt(dv, tB, tB.a, Pw, ai, svr, lim, ALU.mult)
        self.tt(dv, tA, tA.a, tA, tA.a, tB, tB.a, ALU.add)
        self.tt(dv, a.fc, a.fc.a[:, 0, :], tA, tA.a, den, den.a, ALU.mult)
        self.tt(dv, tA, tA.a, Pw, ai, svr, lre, ALU.mult)
        self.tt(dv, tB, tB.a, nr, nr.a, svr, lim, ALU.mult)
        self.tt(dv, tA, tA.a, tA, tA.a, tB, tB.a, ALU.subtract)
        self.tt(dv, a.fc, a.fc.a[:, 1, :], tA, tA.a, den, den.a, ALU.mult)
        for k in range(2, 9):
            pr, pi = Pw.a[:, k - 1, 0, :], Pw.a[:, k - 1, 1, :]
            self.tt(dv, tA, tA.a, Pw, pr, Pw, ar, ALU.mult)
            self.tt(dv, tB, tB.a, Pw, pi, Pw, ai, ALU.mult)
            self.tt(dv, Pw, Pw.a[:, k, 0, :], tA, tA.a, tB, tB.a, ALU.subtract)
            self.tt(dv, tA, tA.a, Pw, pr, Pw, ai, ALU.mult)
            self.tt(dv, tB, tB.a, Pw, pi, Pw, ar, ALU.mult)
            self.tt(dv, Pw, Pw.a[:, k, 1, :], tA, tA.a, tB, tB.a, ALU.add)
        self.ts(dv, a.nAi, a.nAi.a, Pw, Pw.a[:, 8, 1, :], -1.0, None, ALU.mult)

        def bc32(ap16):
            return ap16.unsqueeze(2).broadcast_to([128, 16, 32])

        Cr, Ci, Br, Bi = a.m.a[:, 0], a.m.a[:, 1], a.m.a[:, 2], a.m.a[:, 3]
        fre, fim = bc32(a.fc.a[:, 0, :]), bc32(a.fc.a[:, 1, :])
        td, tp = a.td, a.td
        for k in range(9):
            self.ts(dv, a.nPr, a.nPr.a[:, k, :], Pw, Pw.a[:, k, 0, :], -1.0, None, ALU.mult)
        pl = 'pool'
        self.tt(pl, td[0], td[0].a, a.m, Br, a.fc, fre, ALU.mult)
        self.tt(pl, td[1], td[1].a, a.m, Bi, a.fc, fim, ALU.mult)
        self.tt(pl, a.Bb, a.Bb.a[:, 0], td[0], td[0].a, td[1], td[1].a, ALU.subtract)
        self.tt(pl, td[2], td[2].a, a.m, Bi, a.fc, fre, ALU.mult)
        self.tt(pl, td[3], td[3].a, a.m, Br, a.fc, fim, ALU.mult)
        self.tt(pl, a.Bb, a.Bb.a[:, 1], td[2], td[2].a, td[3], td[3].a, ALU.add)
        self.cp('pool', a.BbQb, a.BbQb.a[:, 0], a.Bb, a.Bb.a[:, 0])
        self.cp('pool', a.BbQb, a.BbQb.a[:, 1], a.Bb, a.Bb.a[:, 1])
        for k in range(9):
            pr, pi = bc32(Pw.a[:, k, 0, :]), bc32(Pw.a[:, k, 1, :])
            npr = bc32(a.nPr.a[:, k, :])
            self.tt(pl, td[0], td[0].a, a.m, Cr, Pw, pr, ALU.mult)
            self.tt(pl, td[1], td[1].a, a.m, Ci, Pw, pi, ALU.mult)
            self.tt(pl, a.PCb, a.PCb.a[:, k, 0], td[0], td[0].a, td[1], td[1].a, ALU.subtract)
            self.tt(pl, td[2], td[2].a, a.m, Ci, a.nPr, npr, ALU.mult)
            self.tt(pl, td[3], td[3].a, a.m, Cr, Pw, pi, ALU.mult)
            self.tt(pl, a.PCb, a.PCb.a[:, k, 1], td[2], td[2].a, td[3], td[3].a, ALU.subtract)
        for j in range(8):
            pr, pi = bc32(Pw.a[:, 7 - j, 0, :]), bc32(Pw.a[:, 7 - j, 1, :])
            self.tt(pl, tp[0], tp[0].a, a.Bb, a.Bb.a[:, 0], Pw, pr, ALU.mult)
            self.tt(pl, tp[1], tp[1].a, a.Bb, a.Bb.a[:, 1], Pw, pi, ALU.mult)
            v4 = lambda r: r.a.rearrange("p (t m) c -> p t m c", t=4)
            self.tt(pl, a.PBb, a.PBb.a[:, :, j, 0, :, :], tp[0], v4(tp[0]), tp[1], v4(tp[1]), ALU.subtract)
            self.tt(pl, tp[2], tp[2].a, a.Bb, a.Bb.a[:, 1], Pw, pr, ALU.mult)
            self.tt(pl, tp[3], tp[3].a, a.Bb, a.Bb.a[:, 0], Pw, pi, ALU.mult)
            self.tt(pl, a.PBb, a.PBb.a[:, :, j, 1, :, :], tp[2], v4(tp[2]), tp[3], v4(tp[3]), ALU.add)
        LS = CPS // self.G
        nk = LS.bit_length() - 1
        assert (1 << nk) == LS
        Q = a.Q
        sA, sB = a.sm[0], a.sm[1]
        self.cp(pl, Q, Q.a[:, 0, :, :], Pw, Pw.a[:, 8, :, :])
        for k in range(1, nk + 1):
            qr, qi = Q.a[:, k - 1, 0, :], Q.a[:, k - 1, 1, :]
            self.tt(pl, sA, sA.a, Q, qr, Q, qr, ALU.mult)
            self.tt(pl, sB, sB.a, Q, qi, Q, qi, ALU.mult)
            self.tt(pl, Q, Q.a[:, k, 0, :], sA, sA.a, sB, sB.a, ALU.subtract)
            self.tt(pl, sA, sA.a, Q, qr, Q, qi, ALU.mult)
            self.tt(pl, Q, Q.a[:, k, 1, :], sA, sA.a, sA, sA.a, ALU.add)
        Ap = a.Apow
        P.add(pl, lambda e: e.memset(Ap.a[:, :, 0, 0:1], 1.0), writes=[Ap])
        P.add(pl, lambda e: e.memset(Ap.a[:, :, 1, 0:1], 0.0), writes=[Ap])
        for k in range(nk):
            n = 1 << k
            qr = Q.a[:, k, 0, :].unsqueeze(2).broadcast_to([128, 16, n])
            qi = Q.a[:, k, 1, :].unsqueeze(2).broadcast_to([128, 16, n])
            sr, si = Ap.a[:, :, 0, 0:n], Ap.a[:, :, 1, 0:n]
            u0, u1 = tp[0].a[:, :, 0:n], tp[1].a[:, :, 0:n]
            self.tt(pl, tp[0], u0, Ap, sr, Q, qr, ALU.mult)
            self.tt(pl, tp[1], u1, Ap, si, Q, qi, ALU.mult)
            self.tt(pl, Ap, Ap.a[:, :, 0, n:2 * n], tp[0], u0, tp[1], u1, ALU.subtract)
            self.tt(pl, tp[0], u0, Ap, sr, Q, qi, ALU.mult)
            self.tt(pl, tp[1], u1, Ap, si, Q, qr, ALU.mult)
            self.tt(pl, Ap, Ap.a[:, :, 1, n:2 * n], tp[0], u0, tp[1], u1, ALU.add)

    def m3(self, l):
        P, a, d, psb = self.P, self.A3, self.d, self.psb
        T, NCH, CPS, NSEQ = self.T, self.NCH, self.CPS, self.NSEQ
        Pw = a.Pw
        dv = 'dve'
        for Tt in range(4):
            self.dma('sp', a.u, a.u.a[:, Tt, :], self.dr["uT"], d["uT"][Tt * 128:(Tt + 1) * 128, :])
        nb = 0
        for Tt in range(4):
            for jh in range(2):
                ps = psb[nb % 2]
                nb += 1
                psv = ps.a.bitcast(BF16)
                for jj in range(4):
                    for ri in range(2):
                        j = jh * 4 + jj
                        idx = jj * 2 + ri
                        P.add('pe', lambda e, psv=psv, idx=idx, Tt=Tt, j=j, ri=ri: e.transpose(
                            psv[:, idx * 128:(idx + 1) * 128], a.PBb.a[:, Tt, j, ri, :, :].rearrange("p m c -> p (m c)"),
                            self.ident.a), reads=[a.PBb, self.ident], writes=[ps])
                self.cp('dve' if nb % 2 else 'act', a.Wp,
                        a.Wp.a[:, Tt, jh * 4:jh * 4 + 4, :, :].rearrange("p j r c -> p (j r c)"),
                        ps, psv) if nb % 2 else \
                    self.act(a.Wp, a.Wp.a[:, Tt, jh * 4:jh * 4 + 4, :, :].rearrange("p j r c -> p (j r c)"),
                             ps, psv, AF.Copy)
        fl = lambda ap: ap.rearrange("p m c -> p (m c)")
        for Tt in range(4):
            for kh in range(2):
                ps = psb[2 + (nb % 2)]
                nb += 1
                for kk in range(4):
                    k = kh * 4 + kk
                    oa = ps.a[:, kk * 128:(kk + 1) * 128]
                    for ri in range(2):
                        self.mm(ps, oa, a.BbQb, fl(a.BbQb.a[:, ri, 4 * Tt:4 * Tt + 4, :]),
                                a.PCb, fl(a.PCb.a[:, k, ri, 4 * Tt:4 * Tt + 4, :]), ri == 0, ri == 1)
                self.tt('dve', a.Kb, a.Kb.a[:, Tt, kh * 4:kh * 4 + 4, :], ps,
                        ps.a.rearrange("p (k c) -> p k c", k=4), self.bmask,
                        self.bmask.a.unsqueeze(1).broadcast_to([128, 4, 128]), ALU.mult)
            self.stt(a.Kb, a.Kb.a[:, Tt, 0, :], self.ident, self.ident.a, self.vec(l, DSK + Tt), a.Kb,
                     a.Kb.a[:, Tt, 0, :], ALU.mult, ALU.add, extra_reads=[self.vecs])
        for gp in range(16):
            Tt, m = gp // 4, gp % 4
            for ri in range(2):
                ps = psb[4 + (nb % 4)]
                nb += 1
                for j in range(8):
                    rhs = a.u.a[32 * m:32 * m + 32, Tt, j * NCH:(j + 1) * NCH]
                    self.mm(ps, ps.a[:, 0:NCH], a.Wp, a.Wp.a[32 * m:32 * m + 32, Tt, j, ri, :], a.u, rhs,
                            j == 0, j == 7, tp=(32 * m, 0))
                if nb % 2:
                    self.cp('dve', a.Sb, a.Sb.a[:, gp, ri, :], ps, ps.a[:, 0:NCH])
                else:
                    self.act(a.Sb, a.Sb.a[:, gp, ri, :], ps, ps.a[:, 0:NCH], AF.Copy)
        G = self.G
        NS2 = NSEQ * G
        LS = CPS // G
        nk = LS.bit_length() - 1
        Sv = a.Sb.a.rearrange("p g r (s c) -> p g r s c", s=NS2)
        Xv = a.Xb.a.rearrange("p g r (s c) -> p g r s c", s=NS2)
        Ar4 = Pw.a[:, 8, 0, :].unsqueeze(2).unsqueeze(3).broadcast_to([128, 16, 2, NS2])
        Ai3 = Pw.a[:, 8, 1, :].unsqueeze(2).broadcast_to([128, 16, NS2])
        nAi3 = a.nAi.a.unsqueeze(2).broadcast_to([128, 16, NS2])
        P.add(dv, lambda e: e.memset(a.xs[0].a, 0.0), writes=[a.xs[0]])
        for c in range(LS):
            xc, xn = a.xs[c % 2], a.xs[(c + 1) % 2]
            self.act(a.Xb, Xv[:, :, :, :, c], xc, xc.a, AF.Copy)
            self.tt(dv, a.t1, a.t1.a, xc, xc.a, Pw, Ar4, ALU.mult)
            self.tt(dv, a.t2, a.t2.a[:, :, 0, :], xc, xc.a[:, :, 1, :], a.nAi, nAi3, ALU.mult)
            self.tt(dv, a.t2, a.t2.a[:, :, 1, :], xc, xc.a[:, :, 0, :], Pw, Ai3, ALU.mult)
            self.tt(dv, a.t1, a.t1.a, a.t1, a.t1.a, a.t2, a.t2.a, ALU.add)
            self.tt(dv, xn, xn.a, a.t1, a.t1.a, a.Sb, Sv[:, :, :, :, c], ALU.add)
        xe = a.xs[LS % 2]
        xev = xe.a.rearrange("p g r (s k) -> p g r s k", k=G)
        eev = a.ee.a.rearrange("p g r (s k) -> p g r s k", k=G)
        QL = a.Q.a[:, nk, :, :]
        QLr = QL[:, 0, :].unsqueeze(2).broadcast_to([128, 16, NSEQ])
        QLi = QL[:, 1, :].unsqueeze(2).broadcast_to([128, 16, NSEQ])
        P.add(dv, lambda e: e.memset(a.ee.a, 0.0), writes=[a.ee])
        s0, s1 = a.t1, a.t2
        s0v = s0.a.rearrange("p g r (s k) -> p g r s k", k=G)
        s1v = s1.a.rearrange("p g r (s k) -> p g r s k", k=G)
        for k in range(1, G):
            er, ei = eev[:, :, 0, :, k - 1], eev[:, :, 1, :, k - 1]
            self.tt(dv, s0, s0v[:, :, 0, :, 0], a.ee, er, a.Q, QLr, ALU.mult)
            self.tt(dv, s1, s1v[:, :, 0, :, 0], a.ee, ei, a.Q, QLi, ALU.mult)
            self.tt(dv, s0, s0v[:, :, 0, :, 0], s0, s0v[:, :, 0, :, 0], s1, s1v[:, :, 0, :, 0], ALU.subtract)
            self.tt(dv, a.ee, eev[:, :, 0, :, k], s0, s0v[:, :, 0, :, 0], xe, xev[:, :, 0, :, k - 1], ALU.add)
            self.tt(dv, s0, s0v[:, :, 1, :, 0], a.ee, er, a.Q, QLi, ALU.mult)
            self.tt(dv, s1, s1v[:, :, 1, :, 0], a.ee, ei, a.Q, QLr, ALU.mult)
            self.tt(dv, s0, s0v[:, :, 1, :, 0], s0, s0v[:, :, 1, :, 0], s1, s1v[:, :, 1, :, 0], ALU.add)
            self.tt(dv, a.ee, eev[:, :, 1, :, k], s0, s0v[:, :, 1, :, 0], xe, xev[:, :, 1, :, k - 1], ALU.add)
        nel = 16 * (G - 1) * LS
        scr = self.arena_t[:, a.Sb.off:a.Sb.off + 4 * nel].bitcast(F32)
        tA = scr[:, 0:nel].rearrange("p (g k c) -> p g k c", g=16, k=G - 1)
        tB = scr[:, nel:2 * nel].rearrange("p (g k c) -> p g k c", g=16, k=G - 1)
        X6 = a.Xb.a.rearrange("p g r (s k c) -> p g r s k c", s=NSEQ, k=G)
        shp = [128, 16, G - 1, LS]
        Apr = a.Apow.a[:, :, 0, :].unsqueeze(2).broadcast_to(shp)
        Api = a.Apow.a[:, :, 1, :].unsqueeze(2).broadcast_to(shp)
        for sq in range(NSEQ):
            er = eev[:, :, 0, sq, 1:G].unsqueeze(3).broadcast_to(shp)
            ei = eev[:, :, 1, sq, 1:G].unsqueeze(3).broadcast_to(shp)
            Xr, Xi = X6[:, :, 0, sq, 1:G, :], X6[:, :, 1, sq, 1:G, :]
            self.tt(dv, a.Sb, tA, a.Apow, Apr, a.ee, er, ALU.mult)
            self.tt(dv, a.Sb, tB, a.Apow, Api, a.ee, ei, ALU.mult)
            self.tt(dv, a.Sb, tA, a.Sb, tA, a.Sb, tB, ALU.subtract)
            self.tt(dv, a.Xb, Xr, a.Xb, Xr, a.Sb, tA, ALU.add)
            self.tt(dv, a.Sb, tA, a.Apow, Apr, a.ee, ei, ALU.mult)
            self.tt(dv, a.Sb, tB, a.Apow, Api, a.ee, er, ALU.mult)
            self.tt(dv, a.Sb, tA, a.Sb, tA, a.Sb, tB, ALU.add)
            self.tt(dv, a.Xb, Xi, a.Xb, Xi, a.Sb, tA, ALU.add)
        for Tt in range(4):
            for j in range(8):
                ps = psb[nb % 4]
                nb += 1
                for k in range(j + 1):
                    rhs = a.u.a[:, Tt, (j - k) * NCH:(j - k + 1) * NCH]
                    self.mm(ps, ps.a[:, 0:NCH], a.Kb, a.Kb.a[:, Tt, k, :], a.u, rhs, k == 0, False)
                for m in range(4):
                    gp = 4 * Tt + m
                    for ri in range(2):
                        self.mm(ps, ps.a[32 * m:32 * m + 32, 0:NCH], a.PCb, a.PCb.a[:, j + 1, ri, gp, :],
                                a.Xb, a.Xb.a[:, gp, ri, :], False, ri == 1, tp=(0, 32 * m))
                self.act(a.ys, a.ys.a.rearrange("p (c j) -> p j c", j=8)[:, j, :], ps, ps.a[:, 0:NCH],
                         AF.Gelu_apprx_tanh)
            self.dma('sp', self.dr["ysT"], d["ysT"][Tt * 128:(Tt + 1) * 128, :], a.ys, a.ys.a)

    def post_norm_residual(self, l, a, X, goff, dst, t0, square_done=False, Y=None):
        psb, d = self.psb, self.d
        Y = a.Y if Y is None else Y
        if not square_done:
            self.act(a.ysq, a.ysq.a, Y, Y.a, AF.Square)
        self.rms_rstd(psb[7], a.ysq, lambda c: a.ysq.a[:, c, :], 8, 1024.0, a.rt, a.rstd)
        for oc in range(8):
            t = a.tt[oc % 2]
            self.tt('dve', t, t.a, Y, Y.a[:, oc, :], a.rstd, a.rstd.a, ALU.mult)
            self.stt(X, X.a[:, oc, :], t, t.a, self.vec(l, goff + oc), X, X.a[:, oc, :], ALU.mult, ALU.add,
                     extra_reads=[self.vecs])
        self.dma('sp', self.dr[dst], d[dst].rearrange("(c p) t -> p c t", p=128)[:, :, t0:t0 + 512], X, X.a)

    def m4(self, l, src, dst):
        P, a, d, psb = self.P, self.A4, self.d, self.psb
        xin = d[src].rearrange("(c p) t -> p c t", p=128)
        oin = d["oT"].rearrange("(c p) t -> p c t", p=128)
        yin = d["ysT"].rearrange("(c p) t -> p c t", p=128)
        gin = d["gT"].rearrange("(c p) t -> p c t", p=128)

        def load(tt):
            sl = slice(tt * 512, (tt + 1) * 512)
            self.dma('sp', a.o[tt % 2], a.o[tt % 2].a, self.dr["oT"], oin[:, :, sl])
            self.dma('sp', a.ysb[tt % 2], a.ysb[tt % 2].a, self.dr["ysT"], yin[:, :, sl])
            self.dma('sp', a.g[tt % 2], a.g[tt % 2].a, self.dr["gT"], gin[:, :, sl])
            self.dma('sp', a.X[tt % 2], a.X[tt % 2].a, self.dr[src], xin[:, :, sl])

        nbc = [0]

        def stageA(tt):
            o, ysb, g = a.o[tt % 2], a.ysb[tt % 2], a.g[tt % 2]
            for oc in range(8):
                nb = nbc[0]
                pa, pga, pgb = psb[(3 * nb) % 6], psb[(3 * nb + 1) % 6], psb[(3 * nb + 2) % 6]
                nbc[0] += 1
                sg, yss, m1, m2 = a.sg[0], a.yss[0], a.m1[0], a.m2[0]
                cs = slice(oc * 128, (oc + 1) * 128)
                for k in range(4):
                    self.mm(pa, pa.a, a.woatt, a.woatt.a[:, k, cs], o, o.a[:, k, :], k == 0, k == 3)
                for k in range(4):
                    self.mm(pga, pga.a, a.wglu, a.wglu.a[:, k, cs], ysb, ysb.a[:, k, :], k == 0, k == 3)
                for k in range(4):
                    self.mm(pgb, pgb.a, a.wglu, a.wglu.a[:, k, 1024 + oc * 128:1024 + (oc + 1) * 128], ysb,
                            ysb.a[:, k, :], k == 0, k == 3)
                self.act(sg, sg.a, pgb, pgb.a, AF.Sigmoid)
                self.tt('dve', yss, yss.a, pga, pga.a, sg, sg.a, ALU.mult)
                self.tt('dve', m1, m1.a, pa, pa.a, g, g.a[:, oc, :], ALU.mult)
                self.tt('dve', m2, m2.a, yss, yss.a, g, g.a[:, 8 + oc, :], ALU.mult)
                self.tt('dve', a.mg, a.mg.a[:, oc, :], m1, m1.a, m2, m2.a, ALU.add)

        def stageB(tt):
            for oc in range(8):
                py = psb[(3 * nbc[0]) % 6]
                nbc[0] += 1
                for k in range(8):
                    self.mm(py, py.a, a.wout, a.wout.a[:, k, oc * 128:(oc + 1) * 128], a.mg, a.mg.a[:, k, :],
                            k == 0, k == 7)
                self.act(a.Y2[tt % 2], a.Y2[tt % 2].a[:, oc, :], py, py.a, AF.Copy)
            self.act(a.ysq, a.ysq.a, a.Y2[tt % 2], a.Y2[tt % 2].a, AF.Square)

        def stageC(tt):
            self.post_norm_residual(l, a, a.X[tt % 2], GPOST, dst, tt * 512, square_done=True, Y=a.Y2[tt % 2])

        NT = self.NT
        load(0)
        self.load_w(a.woatt, d["w_oatt"][l], 4, 1024, a.stg)
        self.load_w(a.wglu, d["w_glu"][l], 4, 2048, a.stg)
        self.load_w(a.wout, d["w_out"][l], 8, 1024, a.stg)
        if NT > 1:
            load(1)
        stageA(0)
        stageB(0)
        for tt in range(1, NT):
            stageA(tt)
            stageC(tt - 1)
            if tt + 1 < NT:
                load(tt + 1)
            stageB(tt)
        stageC(NT - 1)

    def f1(self, l, src):
        P, a, d, psb = self.P, self.A5, self.d, self.psb
        self.dma('sp', a.X[0], a.X[0].a, self.dr[src], d[src].rearrange("(c p) t -> p c t", p=128)[:, :, 0:512])
        items = [(lambda b=b: self.load_w(a.wup, d["w_up"][l], 8, 2 * D_FF, a.stg, only_block=b))
                 for b in range(len(a.wup.blocks))]
        pump = self.lazy_loader(items, 2)
        a.dgr = [self.P.res("dgall_fc%d_%d" % (fc, l)) for fc in range(NFC)]
        xin = d[src].rearrange("(c p) t -> p c t", p=128)
        aout = d["actT"].rearrange("(c p) t -> p c t", p=128)

        def load(tt):
            self.dma('sp', a.X[0], a.X[0].a, self.dr[src], xin[:, :, tt * 512:(tt + 1) * 512])

        gi = 0
        for tt in range(self.NT):
            X = a.X[0]
            t0 = tt * 512
            rs = a.rstdF.a[:, t0:t0 + 512]
            self.act(a.sqg, a.sqg.a, X, X.a, AF.Square)
            for c in range(8):
                self.mm(psb[0], psb[0].a, self.ones, self.ones.a, a.sqg, a.sqg.a[:, c, :], c == 0, c == 7)
            self.act(a.rt, a.rt.a, psb[0], psb[0].a, AF.Sqrt, bias=self.epst.a[:, 0:1], scale=1.0 / 1024.0,
                     extra_reads=[self.epst])
            P.add('dve', lambda e, rs=rs: e.reciprocal(out=rs, in_=a.rt.a), reads=[a.rt], writes=[a.rstdF])
            for dc in range(8):
                self.ts('dve', a.sqg, a.sqg.a[:, dc, :], X, X.a[:, dc, :], self.vec(l, GFPRE + dc), None, ALU.mult,
                        extra_reads=[self.vecs])
            if tt + 1 < self.NT:
                load(tt + 1)
            seq_start = (tt % self.TPS == 0)
            pend = None

            def conv_stage(fc, Gb, psv, ge_i):
                psc = psb[5 + fc % 2]
                for k in range(3):
                    self.mm(psc, psc.a, a.dgr[fc], a.dgall.a[:, fc, k, :], Gb, Gb.a[:, k:k + 512], k == 0, k == 2)
                ge = a.ge[ge_i % 3]
                self.act(ge, ge.a, psc, psc.a, AF.Gelu_apprx_tanh, bias=self.vec(l, CB + fc),
                         extra_reads=[self.vecs])
                ar = a.act[fc // (NFC // 2)]
                self.tt('dve', ar, ar.a[:, fc % (NFC // 2), :], psv, psv.a, ge, ge.a, ALU.mult)
                if fc % (NFC // 2) == NFC // 2 - 1:
                    hh = fc // (NFC // 2)
                    self.dma('sp', self.dr["actT"], aout[:, hh * 11:(hh + 1) * 11, t0:t0 + 512], ar, ar.a)

            for fc in range(NFC):
                psg, psv = psb[1 + fc % 2], psb[3 + fc % 2]
                Gb = a.Gb[gi % 3]
                gi += 1
                if tt == 0:
                    if fc % 4 == 0:
                        pump()
                    for k in range(3):
                        self.ts('dve', a.dgr[fc], a.dgall.a[:, fc, k, :], self.ident, self.ident.a,
                                self.vec(l, CW + fc * 3 + k), None, ALU.mult, extra_reads=[self.vecs])
                for dc in range(8):
                    self.mm(psg, psg.a, self.wr(a.wup, fc * 256, fc * 256 + 128),
                            a.wup.a[:, dc, fc * 256:fc * 256 + 128], a.sqg, a.sqg.a[:, dc, :], dc == 0, dc == 7)
                for dc in range(8):
                    self.mm(psv, psv.a, self.wr(a.wup, fc * 256 + 128, fc * 256 + 256),
                            a.wup.a[:, dc, fc * 256 + 128:fc * 256 + 256], a.sqg, a.sqg.a[:, dc, :], dc == 0, dc == 7)
                if seq_start:
                    P.add('pool', lambda e, Gb=Gb: e.memset(Gb.a[:, 0:2], 0.0), writes=[Gb])
                else:
                    self.act(Gb, Gb.a[:, 0:2], a.Gh, a.Gh.a[:, fc, :], AF.Copy)
                self.tt('dve', Gb, Gb.a[:, 2:514], psg, psg.a, a.rstdF, rs, ALU.mult)
                self.act(a.Gh, a.Gh.a[:, fc, :], Gb, Gb.a[:, 512:514], AF.Copy)
                if pend is not None:
                    conv_stage(*pend)
                pend = (fc, Gb, psv, gi)
            conv_stage(*pend)

    def f2(self, l, src, dst):
        P, a, d, psb = self.P, self.A6, self.d, self.psb
        xin = d[src].rearrange("(c p) t -> p c t", p=128)
        ain = d["actT"].rearrange("(c p) t -> p c t", p=128)

        def load(tt):
            sl = slice(tt * 512, (tt + 1) * 512)
            self.dma('sp', a.a[tt % 2], a.a[tt % 2].a, self.dr["actT"], ain[:, :, sl])
            self.dma('sp', a.X[tt % 2], a.X[tt % 2].a, self.dr[src], xin[:, :, sl])

        load(0)
        items = [(lambda b=b: self.load_w(a.wdown, d["w_down"][l], NFC, 1024, a.stg, only_block=b))
                 for b in range(len(a.wdown.blocks))]
        pump = self.lazy_loader(items, 1)
        nbc = [0]
        NT = self.NT

        def mms(tt, oc):
            A_ = a.a[tt % 2]
            py = psb[nbc[0] % 6]
            nbc[0] += 1
            for k in range(NFC):
                self.mm(py, py.a, self.wr(a.wdown, oc * 128, (oc + 1) * 128),
                        a.wdown.a[:, k, oc * 128:(oc + 1) * 128], A_, A_.a[:, k, :], k == 0, k == NFC - 1)
            return py

        def evac(tt, oc, py):
            rs = a.rstdF.a[:, tt * 512:(tt + 1) * 512]
            self.tt('dve', a.Y, a.Y.a[:, oc, :], py, py.a, a.rstdF, rs, ALU.mult)

        for tt in range(NT):
            if tt == 0 and NT > 1:
                load(1)
            held = []
            for oc in range(8):
                if tt == 0 and oc == 1:
                    pump(4)
                py = mms(tt, oc)
                if tt > 0 and oc < 2:
                    held.append((oc, py))
                    if oc == 1:
                        self.post_norm_residual(l, a, a.X[(tt - 1) % 2], GFPOST, dst, (tt - 1) * 512,
                                                square_done=True)
                        if tt + 1 < NT:
                            load(tt + 1)
                        for oc_, py_ in held:
                            evac(tt, oc_, py_)
                else:
                    evac(tt, oc, py)
            self.act(a.ysq, a.ysq.a, a.Y, a.Y.a, AF.Square)
        self.post_norm_residual(l, a, a.X[(NT - 1) % 2], GFPOST, dst, (NT - 1) * 512, square_done=True)


def _rope_tables(seq):
    pos = np.arange(seq, dtype=np.float32)
    inv_freq = (np.float32(10000.0) ** (-np.arange(0, 32, 2, dtype=np.float32) / np.float32(32))).astype(np.float32)
    ang = (pos[:, None] * inv_freq[None, :]).astype(np.float32)
    cos, sin = np.cos(ang).astype(np.float32), np.sin(ang).astype(np.float32)
    tab = np.zeros((2, 128, seq), np.float32)
    tab[0, 64:80] = cos.T
    tab[0, 80:96] = cos.T
    tab[1, 64:80] = -sin.T
    tab[1, 80:96] = sin.T
    return tab


def _interleave_up(w):
    L = w.shape[0]
    g = w[:, :, :D_FF].reshape(L, 1024, NFC, 128)
    v = w[:, :, D_FF:].reshape(L, 1024, NFC, 128)
    return np.ascontiguousarray(np.stack([g, v], axis=3).reshape(L, 1024, 2 * D_FF))


def prep_weights(inp, depth, seq):
    f = lambda a: np.ascontiguousarray(np.asarray(a, dtype=np.float32))
    L = depth
    w_in = f(inp["w_in"])[:L]
    o1, o2, o3, o4 = 384, 640, 672, 1184
    kpe = w_in[:, :, o2:o3]
    kpe_sw = np.concatenate([kpe[:, :, 16:32], kpe[:, :, 0:16]], axis=2)
    w_inx = np.concatenate([w_in[:, :, 0:o2], kpe, kpe_sw, w_in[:, :, o3:o4], w_in[:, :, o4:]], axis=2)
    assert w_inx.shape[2] == WINX
    w_uq = f(inp["w_uq"])[:L]
    wq = w_uq.reshape(L, 384, 8, 96)
    w_uqsw = np.concatenate([wq[..., 80:96], wq[..., 64:80]], axis=3).reshape(L, 384, 256)
    w_ukv = f(inp["w_ukv"])[:L].reshape(L, 256, 8, 128)
    w_uk = w_ukv[..., 0:64].reshape(L, 256, 512)
    w_uv = w_ukv[..., 64:128].reshape(L, 256, 512)
    vecs = np.zeros((128, L, NV), np.float32)

    def put(off, arr, n):
        vecs[:, :, off:off + n] = f(arr)[:L].reshape(L, n, 128).transpose(2, 0, 1)

    put(GPRE, inp["g_mix_pre"], 8)
    put(BG, inp["b_gate"], 16)
    put(GQ, inp["g_q"], 3)
    put(GKV, inp["g_kv"], 2)
    put(GPOST, inp["g_mix_post"], 8)
    put(GFPRE, inp["g_ffn_pre"], 8)
    put(GFPOST, inp["g_ffn_post"], 8)
    cw = f(inp["conv_w"])[:L].reshape(L, 3, NFC, 128).transpose(3, 0, 2, 1)
    vecs[:, :, CW:CW + 66] = cw.reshape(128, L, 66)
    put(CB, inp["conv_b"], NFC)
    put(DSK, inp["d_skip"], 4)
    s5v = np.zeros((128, L, 48), np.float32)

    def gl(arr):
        return f(arr)[:L].reshape(L, 16, 2, 64).transpose(2, 3, 0, 1).reshape(128, L, 16)

    s5v[:, :, 0:16] = gl(inp["lam_re"])
    s5v[:, :, 16:32] = gl(inp["lam_im"])
    ls = f(inp["log_step"])[:L].reshape(L, 16, 2)
    s5v[:, :, 32:48] = np.repeat(ls.transpose(2, 0, 1)[:, None], 64, axis=1).reshape(128, L, 16)
    s5m = np.zeros((L, 2, 64, 4, 16, 2, 16), np.float32)
    cr = f(inp["c_re"])[:L].reshape(L, 16, 2, 16, 64)
    ci = f(inp["c_im"])[:L].reshape(L, 16, 2, 16, 64)
    br = f(inp["b_re"])[:L].reshape(L, 16, 2, 64, 16)
    bi = f(inp["b_im"])[:L].reshape(L, 16, 2, 64, 16)
    for g2 in range(2):
        s5m[:, g2, :, 0, :, g2, :] = cr[:, :, g2].transpose(0, 3, 1, 2)
        s5m[:, g2, :, 1, :, g2, :] = ci[:, :, g2].transpose(0, 3, 1, 2)
        s5m[:, g2, :, 2, :, g2, :] = br[:, :, g2].transpose(0, 2, 1, 3)
        s5m[:, g2, :, 3, :, g2, :] = bi[:, :, g2].transpose(0, 2, 1, 3)
    return {
        "w_inx": np.ascontiguousarray(w_inx),
        "w_uq": w_uq, "w_uqsw": np.ascontiguousarray(w_uqsw),
        "w_uk": np.ascontiguousarray(w_uk), "w_uv": np.ascontiguousarray(w_uv),
        "w_oatt": f(inp["w_o_att"])[:L], "w_glu": f(inp["w_glu"])[:L], "w_out": f(inp["w_out"])[:L],
        "w_up": _interleave_up(f(inp["w_up"])[:L]), "w_down": f(inp["w_down"])[:L],
        "vecs": np.ascontiguousarray(vecs.reshape(128, L * NV)),
        "s5v": np.ascontiguousarray(s5v.reshape(128, L * 48)),
        "s5m": np.ascontiguousarray(s5m.reshape(L, 128, 2048)),
        "rope": _rope_tables(seq),
        "ident": np.eye(128, dtype=np.float32),
        "bmask": np.kron(np.eye(4, dtype=np.float32), np.ones((32, 32), np.float32)),
    }


_CACHE = {}


def kernel(**inputs):
    x = np.asarray(inputs["x"], dtype=np.float32)
    B, S, D = x.shape
    ncores = 8
    nseq = B // ncores
    depth = int(np.asarray(inputs["w_in"]).shape[0])
    key = (nseq, S, depth)
    if key not in _CACHE:
        _CACHE[key] = K(nseq=nseq, seq=S, depth=depth).build()
    nc = _CACHE[key]
    wts = prep_weights(inputs, depth, S)
    in_maps = []
    for c in range(ncores):
        xs = x[c * nseq:(c + 1) * nseq].reshape(nseq * S, D)
        m = dict(wts)
        m["xT"] = np.ascontiguousarray(xs.T)
        in_maps.append(m)
    res = run_bass_kernel_spmd(nc, in_maps, core_ids=list(range(ncores)))
    out = np.empty((B, S, D), np.float32)
    for c in range(ncores):
        yT = np.asarray(res.results[c]["yT"], dtype=np.float32)
        out[c * nseq:(c + 1) * nseq] = yT.T.reshape(nseq, S, D)
    return out
```
